# Optimizing a Trainium2 kernel written in Bass

```python
import math
import jax
import jax.numpy as jnp
from jax import lax
import numpy as np


D_MODEL = 1024
BATCH = 2
SEQ = 8192
DEPTH = 4

D_MIX = D_MODEL
GLA_HEADS = 4
GLA_DK = 32
GLA_DV = 64
GLA_W = GLA_HEADS * GLA_DV
GLA_LOWRANK = 16
GLA_GATE_NORM = 16.0
GLA_CHUNK = 32
S5_GROUPS = 16
S5_CH = 16
S5_W = S5_GROUPS * S5_CH
S5_STATE = 64
S5_DT_MIN = 1e-3
S5_DT_MAX = 1e-1
NSA_HEADS = 8
NSA_KV = 2
NSA_HPG = NSA_HEADS // NSA_KV
NSA_DH = 64
NSA_W = NSA_HEADS * NSA_DH
N_BRANCH = 3
CMP_LEN = 32
CMP_STRIDE = 16
CMP_HIDDEN = 256
SEL_BLOCK = 64
SEL_TOPK = 16
WINDOW = 512
Q_BLOCK = 128
RMS_EPS = 1e-6
NEG_INF = -1e30
FORCE_SCORE = 1e4

IN_SIZES = (GLA_HEADS * GLA_DK, GLA_HEADS * GLA_DK, GLA_W, GLA_LOWRANK, GLA_W,
            S5_W, S5_W,
            NSA_W, N_BRANCH * 2 * NSA_KV * NSA_DH, NSA_HEADS * N_BRANCH, NSA_W)
D_IN = sum(IN_SIZES)
IN_SPLITS = tuple(int(v) for v in np.cumsum(IN_SIZES)[:-1])

kernel_name = 'hybrid_gla_s5_nsa_parallel_heads'


def rmsnorm(x, g):
    xf = x.astype(jnp.float32)
    y = xf * lax.rsqrt(jnp.mean(xf * xf, axis=-1, keepdims=True) + RMS_EPS)
    return (y * g.astype(jnp.float32)).astype(x.dtype)


def masked_softmax(s, mask):
    s = jnp.where(mask, s.astype(jnp.float32), NEG_INF)
    return jax.nn.softmax(s, axis=-1) * mask


def gla_mixer(q, k, v, lr, gate, w2, b2, onorm_g):
    B, T, _ = q.shape
    H, C = GLA_HEADS, GLA_CHUNK
    N = T // C
    f32 = jnp.float32
    glog = jax.nn.log_sigmoid(jnp.matmul(lr, w2).astype(f32) + b2.astype(f32)) / GLA_GATE_NORM

    def heads(a, d):
        return a.astype(f32).reshape(B, N, C, H, d).transpose(0, 3, 1, 2, 4)

    qh = heads(q, GLA_DK) * (GLA_DK ** -0.5)
    kh = heads(k, GLA_DK)
    vh = heads(v, GLA_DV)
    bcum = jnp.cumsum(heads(glog, GLA_DK), axis=3)
    blast = bcum[:, :, :, -1:, :]
    causal = np.tril(np.ones((C, C), dtype=bool))[:, :, None]
    rel = bcum[:, :, :, :, None, :] - bcum[:, :, :, None, :, :]
    decay = jnp.exp(jnp.where(causal, rel, -jnp.inf))
    scores = jnp.einsum('bhnid,bhnjd,bhnijd->bhnij', qh, kh, decay)
    o_intra = jnp.einsum('bhnij,bhnjd->bhnid', scores, vh)
    chunk_kv = jnp.einsum('bhncd,bhnce->bhnde', kh * jnp.exp(blast - bcum), vh)
    chunk_decay = jnp.exp(blast[:, :, :, 0, :])

    def step(S, inp):
        dec, kv = inp
        return dec[..., None] * S + kv, S

    S0 = jnp.zeros((B, H, GLA_DK, GLA_DV), f32)
    _, S_prev = lax.scan(step, S0, (jnp.moveaxis(chunk_decay, 2, 0), jnp.moveaxis(chunk_kv, 2, 0)))
    S_prev = jnp.moveaxis(S_prev, 0, 2)
    o_inter = jnp.einsum('bhncd,bhnde->bhnce', qh * jnp.exp(bcum), S_prev)
    o = (o_intra + o_inter).transpose(0, 2, 3, 1, 4).reshape(B, T, H, GLA_DV)
    o = rmsnorm(o, onorm_g).reshape(B, T, GLA_W)
    return o * jax.nn.silu(gate.astype(f32))


def _complex_affine_combine(e1, e2):
    a1r, a1i, b1r, b1i = e1
    a2r, a2i, b2r, b2i = e2
    return (a1r * a2r - a1i * a2i,
            a1r * a2i + a1i * a2r,
            a2r * b1r - a2i * b1i + b2r,
            a2r * b1i + a2i * b1r + b2i)


def s5_mixer(u, gate, lam_re, lam_im, log_step, b_re, b_im, c_re, c_im, d_skip, glu_w, glu_b):
    B, T, _ = u.shape
    f32 = jnp.float32
    uf = u.astype(f32).reshape(B, T, S5_GROUPS, S5_CH)
    lr = jnp.minimum(lam_re.astype(f32), -1e-4)
    li = lam_im.astype(f32)
    dt = jnp.exp(log_step.astype(f32))[:, None]
    mag = jnp.exp(lr * dt)
    ar, ai = mag * jnp.cos(li * dt), mag * jnp.sin(li * dt)
    den = lr * lr + li * li
    fr = ((ar - 1.0) * lr + ai * li) / den
    fi = (ai * lr - (ar - 1.0) * li) / den
    br, bim = b_re.astype(f32), b_im.astype(f32)
    bbar_r = fr[..., None] * br - fi[..., None] * bim
    bbar_i = fr[..., None] * bim + fi[..., None] * br
    bu_r = jnp.einsum('btgc,gpc->btgp', uf, bbar_r)
    bu_i = jnp.einsum('btgc,gpc->btgp', uf, bbar_i)
    a_r = jnp.broadcast_to(ar, bu_r.shape)
    a_i = jnp.broadcast_to(ai, bu_r.shape)
    _, _, xr, xi = lax.associative_scan(_complex_affine_combine, (a_r, a_i, bu_r, bu_i), axis=1)
    y = (jnp.einsum('btgp,gcp->btgc', xr, c_re.astype(f32))
         - jnp.einsum('btgp,gcp->btgc', xi, c_im.astype(f32))
         + d_skip.astype(f32) * uf)
    y = jax.nn.gelu(y.reshape(B, T, S5_W))
    z = jnp.matmul(y, glu_w.astype(f32)) + glu_b.astype(f32)
    y = z[..., :S5_W] * jax.nn.sigmoid(z[..., S5_W:])
    return y * jax.nn.silu(gate.astype(f32))


def nsa_mixer(q, kv, gate_logit, gate, gate_b, qn_g, kn_g, cmp_pos, cmp_w1, cmp_b1, cmp_w2, cmp_b2):
    B, T, _ = q.shape
    G, Hg, dh = NSA_KV, NSA_HPG, NSA_DH
    f32 = jnp.float32
    qh = rmsnorm(q.reshape(B, T, G, Hg, dh), qn_g) * (dh ** -0.5)
    kv = kv.reshape(B, T, N_BRANCH, 2, G, dh)
    n_cmp = (T - CMP_LEN) // CMP_STRIDE + 1
    blk_idx = np.arange(n_cmp)[:, None] * CMP_STRIDE + np.arange(CMP_LEN)[None, :]
    cmp_start = blk_idx[:, 0]
    cmp_end = blk_idx[:, -1]

    def compress(a, j):
        blocks = a[:, blk_idx] + cmp_pos[j][:, None, :]
        blocks = blocks.transpose(0, 1, 3, 2, 4).reshape(B, n_cmp, G, CMP_LEN * dh)
        hid = jax.nn.gelu(jnp.matmul(blocks, cmp_w1[j]) + cmp_b1[j])
        return jnp.matmul(hid, cmp_w2[j]) + cmp_b2[j]

    k_cmp = rmsnorm(compress(kv[:, :, 0, 0], 0), kn_g[0])
    v_cmp = compress(kv[:, :, 0, 1], 1)
    n_sel = T // SEL_BLOCK
    top_k = min(SEL_TOPK, n_sel)
    sel_start = np.arange(n_sel) * SEL_BLOCK
    overlap = jnp.asarray(((cmp_start[:, None] < sel_start[None, :] + SEL_BLOCK)
                           & (cmp_end[:, None] >= sel_start[None, :])).astype(np.float32))

    def to_blocks(a):
        return a.reshape(B, n_sel, SEL_BLOCK, G, dh).transpose(0, 3, 1, 2, 4)

    k_sel_blk = to_blocks(rmsnorm(kv[:, :, 1, 0], kn_g[1]))
    v_sel_blk = to_blocks(kv[:, :, 1, 1])
    pad = ((0, 0), (WINDOW, 0), (0, 0), (0, 0))
    k_win = jnp.pad(rmsnorm(kv[:, :, 2, 0], kn_g[2]), pad)
    v_win = jnp.pad(kv[:, :, 2, 1], pad)
    gates = jax.nn.sigmoid(gate_logit.astype(f32) + gate_b.astype(f32)).reshape(B, T, G, Hg, N_BRANCH)
    n_qb = T // Q_BLOCK
    q_blocks = qh.reshape(B, n_qb, Q_BLOCK, G, Hg, dh).swapaxes(0, 1)
    g_blocks = gates.reshape(B, n_qb, Q_BLOCK, G, Hg, N_BRANCH).swapaxes(0, 1)
    bi = jnp.arange(B)[:, None, None, None]
    gi = jnp.arange(G)[None, :, None, None]
    sel_ids = np.arange(n_sel)

    def block_fn(args):
        qb, gb, i = args
        t = i * Q_BLOCK + jnp.arange(Q_BLOCK)
        s_c = jnp.einsum('bqghd,bcgd->bgqhc', qb, k_cmp, preferred_element_type=f32)
        p_c = masked_softmax(s_c, (cmp_end[None, :] <= t[:, None])[None, None, :, None, :])
        o_c = jnp.einsum('bgqhc,bcgd->bgqhd', p_c, v_cmp.astype(f32))
        imp = jnp.einsum('bgqhc,cs->bgqs', p_c, overlap)
        cur = (t // SEL_BLOCK)[:, None]
        forced = (sel_ids[None, :] == 0) | (sel_ids[None, :] == cur) | (sel_ids[None, :] == cur - 1)
        imp = jnp.where(forced, FORCE_SCORE, imp)
        imp = jnp.where(sel_ids[None, :] <= cur, imp, NEG_INF)
        top_val, top_idx = lax.top_k(imp, top_k)
        k_g = k_sel_blk[bi, gi, top_idx]
        v_g = v_sel_blk[bi, gi, top_idx]
        tok = top_idx[..., None] * SEL_BLOCK + jnp.arange(SEL_BLOCK)
        m_s = (top_val > 0.5 * NEG_INF)[..., None] & (tok <= t[None, None, :, None, None])
        s_s = jnp.einsum('bqghd,bgqkld->bgqhkl', qb, k_g, preferred_element_type=f32)
        shp = s_s.shape
        p_s = masked_softmax(s_s.reshape(shp[:4] + (-1,)), m_s.reshape(B, G, Q_BLOCK, 1, -1)).reshape(shp)
        o_s = jnp.einsum('bgqhkl,bgqkld->bgqhd', p_s, v_g.astype(f32))
        kw = lax.dynamic_slice_in_dim(k_win, i * Q_BLOCK, Q_BLOCK + WINDOW, axis=1)
        vw = lax.dynamic_slice_in_dim(v_win, i * Q_BLOCK, Q_BLOCK + WINDOW, axis=1)
        kpos = i * Q_BLOCK - WINDOW + jnp.arange(Q_BLOCK + WINDOW)
        m_w = (kpos[None, :] >= 0) & (kpos[None, :] <= t[:, None]) & (kpos[None, :] > t[:, None] - WINDOW)
        s_w = jnp.einsum('bqghd,bkgd->bgqhk', qb, kw, preferred_element_type=f32)
        p_w = masked_softmax(s_w, m_w[None, None, :, None, :])
        o_w = jnp.einsum('bgqhk,bkgd->bgqhd', p_w, vw.astype(f32))
        gb = gb.transpose(0, 2, 1, 3, 4)
        o = gb[..., 0:1] * o_c + gb[..., 1:2] * o_s + gb[..., 2:3] * o_w
        return o.transpose(0, 2, 1, 3, 4)

    o = lax.map(block_fn, (q_blocks, g_blocks, jnp.arange(n_qb)))
    o = o.swapaxes(0, 1).reshape(B, T, NSA_W)
    return o * jax.nn.silu(gate.astype(f32))


def setup_inputs(seed: int = 0) -> dict:
    key = jax.random.key(seed)
    ks = jax.random.split(key, 32)
    L = DEPTH
    f32 = jnp.float32

    def nrm(k, shape, s):
        return jax.random.normal(k, shape, f32) * s

    x = nrm(ks[0], (BATCH, SEQ, D_MODEL), 1.0)
    norm_g = 1.0 + nrm(ks[1], (L, D_MODEL), 0.01)
    w_in = nrm(ks[2], (L, D_MODEL, D_IN), D_MODEL ** -0.5)
    gla_w2 = nrm(ks[3], (L, GLA_LOWRANK, GLA_HEADS * GLA_DK), GLA_LOWRANK ** -0.5)
    gla_b2 = nrm(ks[4], (L, GLA_HEADS * GLA_DK), 0.1)
    gla_onorm = 1.0 + nrm(ks[5], (L, GLA_DV), 0.01)
    n_idx = jnp.arange(S5_STATE, dtype=f32)
    s5_lam_re = -0.5 + nrm(ks[6], (L, S5_GROUPS, S5_STATE), 0.01)
    s5_lam_im = math.pi * n_idx + nrm(ks[7], (L, S5_GROUPS, S5_STATE), 0.01)
    s5_log_step = jax.random.uniform(ks[8], (L, S5_GROUPS), f32, math.log(S5_DT_MIN), math.log(S5_DT_MAX))
    s5_b_re = nrm(ks[9], (L, S5_GROUPS, S5_STATE, S5_CH), (2 * S5_CH) ** -0.5)
    s5_b_im = nrm(ks[10], (L, S5_GROUPS, S5_STATE, S5_CH), (2 * S5_CH) ** -0.5)
    s5_c_re = nrm(ks[11], (L, S5_GROUPS, S5_CH, S5_STATE), (2 * S5_STATE) ** -0.5)
    s5_c_im = nrm(ks[12], (L, S5_GROUPS, S5_CH, S5_STATE), (2 * S5_STATE) ** -0.5)
    s5_d = nrm(ks[13], (L, S5_GROUPS, S5_CH), 1.0)
    s5_glu_w = nrm(ks[14], (L, S5_W, 2 * S5_W), S5_W ** -0.5)
    s5_glu_b = nrm(ks[15], (L, 2 * S5_W), 0.02)
    nsa_gate_b = nrm(ks[16], (L, NSA_HEADS * N_BRANCH), 0.1)
    nsa_qn = 1.0 + nrm(ks[17], (L, NSA_DH), 0.01)
    nsa_kn = 1.0 + nrm(ks[18], (L, N_BRANCH, NSA_DH), 0.01)
    nsa_cmp_pos = nrm(ks[19], (L, 2, CMP_LEN, NSA_DH), 0.1)
    nsa_cmp_w1 = nrm(ks[20], (L, 2, CMP_LEN * NSA_DH, CMP_HIDDEN), (CMP_LEN * NSA_DH) ** -0.5)
    nsa_cmp_b1 = nrm(ks[21], (L, 2, CMP_HIDDEN), 0.02)
    nsa_cmp_w2 = nrm(ks[22], (L, 2, CMP_HIDDEN, NSA_DH), CMP_HIDDEN ** -0.5)
    nsa_cmp_b2 = nrm(ks[23], (L, 2, NSA_DH), 0.02)
    w_out = nrm(ks[24], (L, D_MIX, D_MODEL), D_MIX ** -0.5 * (2 * DEPTH) ** -0.5)
    return {'x': x, 'norm_g': norm_g, 'w_in': w_in,
            'gla_w2': gla_w2, 'gla_b2': gla_b2, 'gla_onorm': gla_onorm,
            's5_lam_re': s5_lam_re, 's5_lam_im': s5_lam_im, 's5_log_step': s5_log_step,
            's5_b_re': s5_b_re, 's5_b_im': s5_b_im, 's5_c_re': s5_c_re, 's5_c_im': s5_c_im,
            's5_d': s5_d, 's5_glu_w': s5_glu_w, 's5_glu_b': s5_glu_b,
            'nsa_gate_b': nsa_gate_b, 'nsa_qn': nsa_qn, 'nsa_kn': nsa_kn, 'nsa_cmp_pos': nsa_cmp_pos,
            'nsa_cmp_w1': nsa_cmp_w1, 'nsa_cmp_b1': nsa_cmp_b1, 'nsa_cmp_w2': nsa_cmp_w2,
            'nsa_cmp_b2': nsa_cmp_b2, 'w_out': w_out}


def reference(x, norm_g, w_in, gla_w2, gla_b2, gla_onorm, s5_lam_re, s5_lam_im, s5_log_step,
              s5_b_re, s5_b_im, s5_c_re, s5_c_im, s5_d, s5_glu_w, s5_glu_b, nsa_gate_b, nsa_qn,
              nsa_kn, nsa_cmp_pos, nsa_cmp_w1, nsa_cmp_b1, nsa_cmp_w2, nsa_cmp_b2, w_out):
    for l in range(DEPTH):
        h = rmsnorm(x, norm_g[l])
        proj = jnp.matmul(h, w_in[l])
        (gq, gk, gv, glr, gg, su, sg, nq, nkv, ngl, ng) = jnp.split(proj, IN_SPLITS, axis=-1)
        y_gla = gla_mixer(gq, gk, gv, glr, gg, gla_w2[l], gla_b2[l], gla_onorm[l])
        y_s5 = s5_mixer(su, sg, s5_lam_re[l], s5_lam_im[l], s5_log_step[l], s5_b_re[l], s5_b_im[l],
                        s5_c_re[l], s5_c_im[l], s5_d[l], s5_glu_w[l], s5_glu_b[l])
        y_nsa = nsa_mixer(nq, nkv, ngl, ng, nsa_gate_b[l], nsa_qn[l], nsa_kn[l], nsa_cmp_pos[l],
                          nsa_cmp_w1[l], nsa_cmp_b1[l], nsa_cmp_w2[l], nsa_cmp_b2[l])
        mix = jnp.concatenate([y_gla, y_s5, y_nsa], axis=-1).astype(x.dtype)
        x = x + jnp.matmul(mix, w_out[l])
    return x
```

```python
from contextlib import ExitStack
import numpy as np
import concourse.bass as bass
import concourse.mybir as mybir
from concourse.bass_utils import run_bass_kernel_spmd

F32 = mybir.dt.float32
BF16 = mybir.dt.bfloat16
AF = mybir.ActivationFunctionType
ALU = mybir.AluOpType
AX = mybir.AxisListType

NCORES = 8
D_MODEL = 1024
BATCH = 2
SEQ = 8192
DEPTH = 4
D_IN = 3112
D_IN_PAD = 3200
RMS_EPS = 1e-6


class LT:
    def __init__(self, ap, name=""):
        self.ap = ap
        self.name = name
        self.w = None
        self.r = []
        self.dsem = None
        self.dcnt = 0

    def __getitem__(self, idx):
        return self.ap[idx]


class Prog:
    ENGS = ("pe", "dve", "act", "pool", "sp")

    def __init__(self, nc, stack):
        self.nc = nc
        self.stack = stack
        self.q = {e: [] for e in self.ENGS}
        self.sem = {e: stack.enter_context(nc.semaphore("s_" + e)) for e in self.ENGS}
        self.cnt = {e: 0 for e in self.ENGS}
        self.seen = {e: {} for e in self.ENGS}
        self.out_events = []
        self.nsem = 0
        self.ntile = 0

    def sb(self, shape, dt, name=None):
        self.ntile += 1
        name = "sb_" + (name or f"t{self.ntile}")
        t = self.stack.enter_context(self.nc.sbuf_tensor(name, list(shape), dt))
        return LT(t, name)

    def ps(self, shape, dt=F32, name=None):
        self.ntile += 1
        name = "ps_" + (name or f"p{self.ntile}")
        t = self.stack.enter_context(self.nc.psum_tensor(name, list(shape), dt))
        return LT(t, name)

    def sub(self, lt, idx, name=""):
        return LT(lt.ap[idx], name or lt.name)

    def _dsem(self, t):
        if t.dsem is None:
            self.nsem += 1
            t.dsem = self.stack.enter_context(self.nc.semaphore(f"d{self.nsem}"))
        return t.dsem

    def _deps(self, eng, reads, writes):
        evs = []
        for t in reads:
            if t.w is not None:
                evs.append(t.w)
        for t in writes:
            if t.w is not None:
                evs.append(t.w)
            evs.extend(t.r)
        agg = {}
        for sem, val, tile, src in evs:
            if tile is not None:
                val = max(val, tile.dcnt * 16)
            if src == "pe" and eng == "pe":
                continue
            k = id(sem)
            if k not in agg or agg[k][1] < val:
                agg[k] = (sem, val)
        waits = []
        seen = self.seen[eng]
        for k, (sem, val) in agg.items():
            if seen.get(k, 0) >= val:
                continue
            seen[k] = val
            waits.append((sem, val))
        return waits

    def op(self, eng, fn, reads=(), writes=()):
        waits = self._deps(eng, reads, writes)
        self.cnt[eng] += 1
        ev = (self.sem[eng], self.cnt[eng], None, eng)
        self.q[eng].append((waits, fn, (self.sem[eng], 1)))
        for t in reads:
            t.r.append(ev)
        for t in writes:
            t.w = ev
            t.r = []
        return ev

    def dma(self, queue, fn, sbt, reads=(), writes=(), is_out=False):
        waits = self._deps(queue, reads, writes)
        sem = self._dsem(sbt)
        sbt.dcnt += 1
        ev = (sem, sbt.dcnt * 16, sbt, "dma")
        self.q[queue].append((waits, fn, (sem, 16)))
        for t in reads:
            t.r.append(ev)
        for t in writes:
            t.w = ev
            t.r = []
        if is_out:
            self.out_events.append(ev)
        return ev

    def emit(self):
        nc = self.nc
        fin = []
        seen = {}
        for sem, val, tile, _ in self.out_events:
            v = tile.dcnt * 16
            seen[id(sem)] = (sem, v)
        fin = list(seen.values())
        engmap = {"pe": "tensor", "dve": "vector", "act": "scalar", "pool": "gpsimd", "sp": "sync"}
        with nc.Block() as block:
            for e in self.ENGS:
                items = self.q[e]
                extra = fin if e == "sp" else []

                def body(engine, items=items, extra=extra):
                    for waits, fn, inc in items:
                        for sem, val in waits:
                            engine.wait_ge(sem, val)
                        ins = fn(engine)
                        ins.then_inc(inc[0], inc[1])
                    for sem, val in extra:
                        engine.wait_ge(sem, val)

                if items or extra:
                    getattr(block, engmap[e])(body)


def _run(nc, in_maps):
    res = run_bass_kernel_spmd(nc, in_maps, core_ids=list(range(NCORES)))
    return res.results


NT = 2048
KC = 8
NFC = D_IN_PAD // 128
NF32 = 5


def build_op(do_out, do_in):
    nc = bass.Bass("TRN2", target_bir_lowering=False)
    din = lambda name, shape: nc.dram_tensor(name, list(shape), F32, kind="ExternalInput").ap()
    xT = din("xT", [128, KC, NT])
    if do_out:
        mixT = din("mixT", [128, KC, NT])
        wout = din("wout", [128, KC, D_MODEL])
        yssmT = din("yssmT", [128, 2, NT])
        uT = din("uT", [128, 2, NT])
        sgT = din("sgT", [128, 2, NT])
        dskd = din("dsk", [128, 2])
        gluwd = din("gluw", [128, 2, 512])
        glubd = din("glub", [128, 4])
        xoT = nc.dram_tensor("xoT", [128, KC, NT], F32, kind="ExternalOutput").ap()
    if do_in:
        win = din("win", [NFC, 128, KC, 128])
        gin = din("gin", [128, KC])
        projT = nc.dram_tensor("projT", [NFC, 128, NT], F32, kind="ExternalOutput").ap()
    with ExitStack() as st:
        P = Prog(nc, st)
        xs = [P.sb([128, NT], F32, f"x{k}") for k in range(KC)]
        actb = [P.sb([128, NT], BF16, f"ab{k}") for k in range(KC)]
        Fb = [P.sb([128, NT], F32, f"F{i}") for i in range(4)]
        wstg = [P.sb([128, D_MODEL], F32, f"wstg{i}") for i in range(2)]
        banks = [P.ps([128, 512], F32, f"bk{i}") for i in range(8)]
        for k in range(KC):
            P.dma("sp", lambda e, k=k: e.dma_start(out=xs[k][:], in_=xT[:, k, :]), xs[k], writes=[xs[k]])
        nb = 0
        if do_out:
            wob = [P.sb([128, D_MODEL], BF16, f"wob{k}") for k in range(KC)]
            ge = [P.sb([128, NT], BF16, f"ge{k}") for k in range(2)]
            dsk = P.sb([128, 2], F32, "dsk")
            glub = P.sb([128, 4], F32, "glub")
            gluwf = P.sb([128, 2, 512], F32, "gluwf")
            gluwb = P.sb([128, 2, 512], BF16, "gluwb")
            P.dma("sp", lambda e: e.dma_start(out=dsk[:], in_=dskd), dsk, writes=[dsk])
            P.dma("sp", lambda e: e.dma_start(out=glub[:], in_=glubd), glub, writes=[glub])
            P.dma("sp", lambda e: e.dma_start(out=gluwf[:], in_=gluwd), gluwf, writes=[gluwf])
            P.op("act", lambda e: e.copy(out=gluwb[:], in_=gluwf[:]), reads=[gluwf], writes=[gluwb])
            for k in range(KC):
                w = wstg[k % 2]
                P.dma("sp", lambda e, k=k, w=w: e.dma_start(out=w[:], in_=wout[:, k, :]), w, writes=[w])
                P.op("act", lambda e, k=k, w=w: e.copy(out=wob[k][:], in_=w[:]), reads=[w], writes=[wob[k]])
            for i, k in enumerate((0, 1, 4, 5, 6, 7)):
                s_ = Fb[i % 2]
                P.dma("pool", lambda e, k=k, s_=s_: e.dma_start(out=s_[:], in_=mixT[:, k, :]), s_, writes=[s_])
                P.op("dve", lambda e, k=k, s_=s_: e.tensor_copy(out=actb[k][:], in_=s_[:]), reads=[s_], writes=[actb[k]])
            F0, F1, F2, F3 = Fb
            for kc in range(2):
                P.dma("pool", lambda e, kc=kc: e.dma_start(out=F0[:], in_=yssmT[:, kc, :]), F0, writes=[F0])
                P.dma("sp", lambda e, kc=kc: e.dma_start(out=F1[:], in_=uT[:, kc, :]), F1, writes=[F1])
                P.op("dve", lambda e, kc=kc: e.scalar_tensor_tensor(
                    out=F0[:], in0=F1[:], scalar=dsk[:, kc:kc + 1], in1=F0[:], op0=ALU.mult, op1=ALU.add),
                    reads=[F0, F1, dsk], writes=[F0])
                P.op("act", lambda e: e.activation(out=F2[:], in_=F0[:], func=AF.Square), reads=[F0], writes=[F2])
                P.op("dve", lambda e: e.tensor_scalar(out=F2[:], in0=F2[:], scalar1=0.044715, scalar2=1.0, op0=ALU.mult,
                                                      op1=ALU.add), reads=[F2], writes=[F2])
                P.op("dve", lambda e: e.tensor_tensor(out=F2[:], in0=F2[:], in1=F0[:], op=ALU.mult), reads=[F2, F0], writes=[F2])
                P.op("act", lambda e: e.activation(out=F2[:], in_=F2[:], func=AF.Tanh, scale=0.7978845608028654),
                     reads=[F2], writes=[F2])
                P.op("dve", lambda e: e.tensor_scalar(out=F2[:], in0=F2[:], scalar1=0.5, scalar2=0.5, op0=ALU.mult,
                                                      op1=ALU.add), reads=[F2], writes=[F2])
                P.op("dve", lambda e, kc=kc: e.tensor_tensor(out=ge[kc][:], in0=F2[:], in1=F0[:], op=ALU.mult),
                     reads=[F2, F0], writes=[ge[kc]])
            for fcp in range(2):
                P.dma("pool", lambda e, fcp=fcp: e.dma_start(out=F1[:], in_=sgT[:, fcp, :]), F1, writes=[F1])
                P.op("act", lambda e: e.activation(out=F1[:], in_=F1[:], func=AF.Silu), reads=[F1], writes=[F1])
                for tt in range(4):
                    tsl = slice(tt * 512, (tt + 1) * 512)
                    bA, bB = banks[nb % 8], banks[(nb + 1) % 8]
                    nb += 2
                    for kc in range(2):
                        P.op("pe", lambda e, kc=kc, fcp=fcp, tsl=tsl, bA=bA: e.matmul(
                            bA[:], gluwb[:, kc, fcp * 128:(fcp + 1) * 128], ge[kc][:, tsl], start=(kc == 0), stop=(kc == 1)),
                            reads=[gluwb, ge[kc]], writes=[bA])
                    for kc in range(2):
                        P.op("pe", lambda e, kc=kc, fcp=fcp, tsl=tsl, bB=bB: e.matmul(
                            bB[:], gluwb[:, kc, (fcp + 2) * 128:(fcp + 3) * 128], ge[kc][:, tsl], start=(kc == 0),
                            stop=(kc == 1)), reads=[gluwb, ge[kc]], writes=[bB])
                    P.op("act", lambda e, fcp=fcp, tsl=tsl, bB=bB: e.activation(
                        out=F3[:, tsl], in_=bB[:], func=AF.Sigmoid, bias=glub[:, fcp + 2:fcp + 3]),
                        reads=[bB, glub], writes=[F3])
                    P.op("dve", lambda e, fcp=fcp, tsl=tsl, bA=bA: e.scalar_tensor_tensor(
                        out=F3[:, tsl], in0=bA[:], scalar=glub[:, fcp:fcp + 1], in1=F3[:, tsl], op0=ALU.add, op1=ALU.mult),
                        reads=[bA, glub, F3], writes=[F3])
                P.op("dve", lambda e, fcp=fcp: e.tensor_tensor(out=actb[2 + fcp][:], in0=F3[:], in1=F1[:], op=ALU.mult),
                     reads=[F3, F1], writes=[actb[2 + fcp]])
            for fo in range(KC):
                for tt in range(NT // 512):
                    bk = banks[nb % 8]
                    nb += 1
                    for k in range(KC):
                        P.op("pe", lambda e, k=k, fo=fo, tt=tt, bk=bk: e.matmul(
                            bk[:], wob[k][:, fo * 128:(fo + 1) * 128], actb[k][:, tt * 512:(tt + 1) * 512],
                            start=(k == 0), stop=(k == KC - 1)),
                            reads=[wob[k], actb[k]], writes=[bk])
                    P.op("dve", lambda e, fo=fo, tt=tt, bk=bk: e.tensor_tensor(
                        out=xs[fo][:, tt * 512:(tt + 1) * 512], in0=bk[:], in1=xs[fo][:, tt * 512:(tt + 1) * 512],
                        op=ALU.add), reads=[bk, xs[fo]], writes=[xs[fo]])
                P.dma("sp", lambda e, fo=fo: e.dma_start(out=xoT[:, fo, :], in_=xs[fo][:]), xs[fo],
                      reads=[xs[fo]], is_out=True)
        if do_in:
            gt = P.sb([128, KC], F32, "gt")
            P.dma("sp", lambda e: e.dma_start(out=gt[:], in_=gin), gt, writes=[gt])
            ones = P.sb([128, 128], F32, "ones")
            P.op("pool", lambda e: e.memset(ones[:], 1.0), writes=[ones])
            rstd = P.sb([128, NT], F32, "rstd")
            sq = Fb[0:2]
            sbk = banks[0:4]
            for k in range(KC):
                s = sq[k % 2]
                P.op("act", lambda e, k=k, s=s: e.activation(out=s[:], in_=xs[k][:], func=AF.Square),
                     reads=[xs[k]], writes=[s])
                for tt in range(4):
                    P.op("pe", lambda e, k=k, s=s, tt=tt: e.matmul(
                        sbk[tt][:], ones[:], s[:, tt * 512:(tt + 1) * 512], start=(k == 0), stop=(k == KC - 1)),
                        reads=[ones, s], writes=[sbk[tt]])
                P.op("dve", lambda e, k=k: e.tensor_scalar(
                    out=actb[k][:], in0=xs[k][:], scalar1=gt[:, k:k + 1], scalar2=None, op0=ALU.mult),
                    reads=[xs[k], gt], writes=[actb[k]])
            epst = P.sb([128, 1], F32, "eps")
            P.op("pool", lambda e: e.memset(epst[:], RMS_EPS), writes=[epst])
            for tt in range(4):
                P.op("act", lambda e, tt=tt: e.activation(
                    out=rstd[:, tt * 512:(tt + 1) * 512], in_=sbk[tt][:], func=AF.Sqrt,
                    scale=1.0 / D_MODEL, bias=epst[:]), reads=[sbk[tt], epst], writes=[rstd])
            P.op("dve", lambda e: e.reciprocal(out=rstd[:], in_=rstd[:]), reads=[rstd], writes=[rstd])
            wb = [P.sb([128, KC, 128], BF16, f"wb{i}") for i in range(2)]
            ot = Fb[2:4]
            nb = 4
            for fc in range(NFC):
                w = wstg[fc % 2]
                wbb = wb[fc % 2]
                o = ot[fc % 2]
                P.dma("pool", lambda e, fc=fc, w=w: e.dma_start(
                    out=w[:], in_=win[fc].rearrange("p k f -> p (k f)")), w, writes=[w])
                f32c = fc < NF32
                w3 = w[:].rearrange("p (k f) -> p k f", k=KC)
                if f32c:
                    P.op("dve", lambda e, w3=w3: e.tensor_tensor(
                        out=w3, in0=w3, in1=gt[:].unsqueeze(2).to_broadcast([128, KC, 128]), op=ALU.mult),
                        reads=[w, gt], writes=[w])
                else:
                    P.op("act", lambda e, w=w, wbb=wbb: e.copy(out=wbb[:].rearrange("p k f -> p (k f)"), in_=w[:]),
                         reads=[w], writes=[wbb])
                for tt in range(4):
                    bk = banks[4 + nb % 4]
                    nb += 1
                    for k in range(KC):
                        if f32c:
                            P.op("pe", lambda e, k=k, tt=tt, bk=bk, w3=w3: e.matmul(
                                bk[:], w3[:, k, :], xs[k][:, tt * 512:(tt + 1) * 512],
                                start=(k == 0), stop=(k == KC - 1)), reads=[w, xs[k]], writes=[bk])
                        else:
                            P.op("pe", lambda e, k=k, tt=tt, bk=bk, wbb=wbb: e.matmul(
                                bk[:], wbb[:, k, :], actb[k][:, tt * 512:(tt + 1) * 512],
                                start=(k == 0), stop=(k == KC - 1)), reads=[wbb, actb[k]], writes=[bk])
                    P.op("dve", lambda e, tt=tt, bk=bk, o=o: e.tensor_tensor(
                        out=o[:, tt * 512:(tt + 1) * 512], in0=bk[:], in1=rstd[:, tt * 512:(tt + 1) * 512],
                        op=ALU.mult), reads=[bk, rstd], writes=[o])
                P.dma("sp", lambda e, fc=fc, o=o: e.dma_start(out=projT[fc], in_=o[:]), o, reads=[o], is_out=True)
        P.emit()
    return nc


T = SEQ
NTILE = T // 128
NCHUNK = T // 128
GLA_G = 4


def gla_consts():
    j = np.arange(128)[:, None]
    i = np.arange(128)[None, :]
    cm = np.zeros((128, 3, 128), np.float32)
    cm[:, 0, :] = (j <= i)
    cm[:, 1, :] = (j > i)
    cm[:, 2, 0:4] = 1.0
    return cm


def build_gla():
    nc = bass.Bass("TRN2", target_bir_lowering=False)
    qT = nc.dram_tensor("qT", [32, T], F32, kind="ExternalInput").ap()
    kT = nc.dram_tensor("kT", [32, T], F32, kind="ExternalInput").ap()
    lrT1 = nc.dram_tensor("lrT1", [17, T], F32, kind="ExternalInput").ap()
    ktok = nc.dram_tensor("ktok", [128, NTILE, 32], F32, kind="ExternalInput").ap()
    vtok = nc.dram_tensor("vtok", [128, NTILE, 64], F32, kind="ExternalInput").ap()
    gtok = nc.dram_tensor("gtok", [128, NTILE, 64], F32, kind="ExternalInput").ap()
    w2b = nc.dram_tensor("w2b", [17, 32], F32, kind="ExternalInput").ap()
    onb = nc.dram_tensor("onb", [128, 64], F32, kind="ExternalInput").ap()
    cm = nc.dram_tensor("cm", [128, 3, 128], F32, kind="ExternalInput").ap()
    y = nc.dram_tensor("y", [128, NTILE, 64], F32, kind="ExternalOutput").ap()
    NG = NTILE // GLA_G
    with ExitStack() as st:
        P = Prog(nc, st)
        cmt = P.sb([128, 3, 128], F32, "cmt")
        w2t = P.sb([17, 32], F32, "w2t")
        onbt = P.sb([128, 64], F32, "onbt")
        ktokt = P.sb([128, NTILE, 32], F32, "ktokt")
        vb = P.sb([128, NTILE, 64], F32, "vb")
        qeT = P.sb([32, T], F32, "qeT")
        scm = P.sb([128, NTILE, 128], F32, "scm")
        kv_all = P.sb([32, NCHUNK, 64], F32, "kv_all")
        S_bf = P.sb([32, NCHUNK + 1, 64], F32, "S_bf")
        dec_all = P.sb([32, NCHUNK], F32, "dec_all")
        epst = P.sb([128, 1], F32, "eps")
        qTg = [P.sb([32, 512], F32, f"qTg{i}") for i in range(2)]
        kTg = [P.sb([32, 512], F32, f"kTg{i}") for i in range(2)]
        lrg = [P.sb([17, 512], F32, f"lrg{i}") for i in range(2)]
        e1 = P.sb([128, 128], F32, "e1")
        lt = P.sb([128, 128], F32, "lt")
        eqTt = P.sb([32, 512], F32, "eqTt")
        ekTt = P.sb([32, 512], F32, "ekTt")
        ekd = P.sb([128, 128], F32, "ekd")
        keT = P.sb([32, 512], F32, "keT")
        kd = P.sb([128, 4, 32], F32, "kd")
        osb = P.sb([128, 4, 64], F32, "osb")
        sq = P.sb([128, 4, 64], F32, "sq")
        ss = P.sb([128, 4], F32, "ss")
        gg = [P.sb([128, 4, 64], F32, f"gg{i}") for i in range(2)]
        yt = [P.sb([128, 4, 64], F32, f"yt{i}") for i in range(2)]
        bz = P.ps([128, 512], F32, "bz")
        zps = P.sub(bz, (slice(None), slice(0, 128)), "zps")
        sups = P.sub(bz, (slice(None), slice(128, 256)), "sups")
        bT = P.ps([32, 512], F32, "bT")
        bL = P.ps([32, 512], F32, "bL")
        bs = P.ps([128, 512], F32, "bs")
        bkv = P.ps([32, 512], F32, "bkv")
        bo = [P.ps([128, 512], F32, f"bo{i}") for i in range(2)]

        P.dma("sp", lambda e: e.dma_start(out=cmt[:], in_=cm), cmt, writes=[cmt])
        P.dma("sp", lambda e: e.dma_start(out=w2t[:], in_=w2b), w2t, writes=[w2t])
        P.dma("sp", lambda e: e.dma_start(out=onbt[:], in_=onb), onbt, writes=[onbt])
        P.dma("sp", lambda e: e.dma_start(out=ktokt[:], in_=ktok), ktokt, writes=[ktokt])
        P.op("pool", lambda e: e.memset(epst[:], RMS_EPS), writes=[epst])
        for i in range(4):
            P.dma("pool", lambda e, i=i: e.dma_start(out=vb[:, i * 16:(i + 1) * 16, :], in_=vtok[:, i * 16:(i + 1) * 16, :]),
                  vb, writes=[vb])
        LTm = cmt[:, 0, :]
        UTm = cmt[:, 1, :]
        BDs = cmt[:, 2, 0:1]
        for g in range(NG):
            qg, kg, lg = qTg[g % 2], kTg[g % 2], lrg[g % 2]
            tok = slice(g * 512, (g + 1) * 512)
            P.dma("sp", lambda e, qg=qg, tok=tok: e.dma_start(out=qg[:], in_=qT[:, tok]), qg, writes=[qg])
            P.dma("sp", lambda e, kg=kg, tok=tok: e.dma_start(out=kg[:], in_=kT[:, tok]), kg, writes=[kg])
            P.dma("sp", lambda e, lg=lg, tok=tok: e.dma_start(out=lg[:], in_=lrT1[:, tok]), lg, writes=[lg])
            for t in range(4):
                P.op("pe", lambda e, t=t, lg=lg: e.matmul(zps[:, t * 32:(t + 1) * 32], lg[:, t * 128:(t + 1) * 128],
                                                        w2t[:], start=True, stop=True), reads=[lg, w2t], writes=[zps])
            P.op("act", lambda e: e.activation(out=e1[:], in_=zps[:], func=AF.Exp, scale=-1.0), reads=[zps], writes=[e1])
            P.op("act", lambda e: e.activation(out=lt[:], in_=e1[:], func=AF.Ln, bias=1.0), reads=[e1], writes=[lt])
            for t in range(4):
                lsl = lt[:, t * 32:(t + 1) * 32]
                P.op("pe", lambda e, t=t, lsl=lsl: e.matmul(sups[:, t * 32:(t + 1) * 32], UTm, lsl, start=True, stop=True),
                     reads=[cmt, lt], writes=[sups])
                P.op("pe", lambda e, t=t, lsl=lsl: e.matmul(bT[:, t * 128:(t + 1) * 128], lsl, LTm, start=True, stop=True),
                     reads=[cmt, lt], writes=[bT])
                P.op("pe", lambda e, t=t, lsl=lsl: e.matmul(bL[:, t:t + 1], lsl, BDs, start=True, stop=True),
                     reads=[cmt, lt], writes=[bL])
            P.op("act", lambda e: e.activation(out=eqTt[:], in_=bT[:], func=AF.Exp, scale=-1.0 / 16), reads=[bT], writes=[eqTt])
            P.op("act", lambda e: e.activation(out=ekTt[:], in_=bT[:], func=AF.Exp, scale=1.0 / 16), reads=[bT], writes=[ekTt])
            P.op("act", lambda e: e.activation(out=ekd[:], in_=sups[:], func=AF.Exp, scale=-1.0 / 16), reads=[sups], writes=[ekd])
            P.op("act", lambda e, g=g: e.activation(out=dec_all[:, 4 * g:4 * g + 4], in_=bL[:, 0:4], func=AF.Exp,
                                                   scale=-1.0 / 16), reads=[bL], writes=[dec_all])
            P.op("dve", lambda e, qg=qg, tok=tok: e.scalar_tensor_tensor(
                out=qeT[:, tok], in0=qg[:], scalar=32 ** -0.5, in1=eqTt[:], op0=ALU.mult, op1=ALU.mult),
                reads=[qg, eqTt], writes=[qeT])
            P.op("dve", lambda e, kg=kg: e.tensor_tensor(out=keT[:], in0=kg[:], in1=ekTt[:], op=ALU.mult),
                 reads=[kg, ekTt], writes=[keT])
            P.op("dve", lambda e, g=g: e.tensor_tensor(
                out=kd[:], in0=ktokt[:, 4 * g:4 * g + 4, :], in1=ekd[:].rearrange("p (t d) -> p t d", t=4), op=ALU.mult),
                reads=[ktokt, ekd], writes=[kd])
            for t in range(4):
                P.op("pe", lambda e, t=t, g=g: e.matmul(
                    bs[:, t * 128:(t + 1) * 128], keT[:, t * 128:(t + 1) * 128],
                    qeT[:, (4 * g + t) * 128:(4 * g + t + 1) * 128], start=True, stop=True),
                    reads=[keT, qeT], writes=[bs])
            for t in range(4):
                P.op("pe", lambda e, t=t, g=g: e.matmul(
                    bkv[:, t * 64:(t + 1) * 64], kd[:, t, :], vb[:, 4 * g + t, :], start=True, stop=True),
                    reads=[kd, vb], writes=[bkv])
            P.op("dve", lambda e, g=g: e.tensor_tensor(
                out=scm[:, 4 * g:4 * g + 4, :], in0=bs[:].rearrange("p (t i) -> p t i", t=4),
                in1=cmt[:, 0:1, :].to_broadcast([128, 4, 128]), op=ALU.mult), reads=[bs, cmt], writes=[scm])
            P.op("act", lambda e, g=g: e.copy(out=kv_all[:, 4 * g:4 * g + 4, :],
                                             in_=bkv[:, 0:256].rearrange("p (c e) -> p c e", c=4)),
                 reads=[bkv], writes=[kv_all])
        P.op("pool", lambda e: e.memset(S_bf[:, 0, :], 0.0), writes=[S_bf])
        for ee in range(64):
            P.op("dve", lambda e, ee=ee: e.tensor_tensor_scan(
                out=S_bf[:, 1:NCHUNK + 1, ee], data0=dec_all[:], data1=kv_all[:, :, ee], initial=0.0,
                op0=ALU.mult, op1=ALU.add), reads=[dec_all, kv_all], writes=[S_bf])
        for g in range(NG):
            b = bo[g % 2]
            gt_, yy = gg[g % 2], yt[g % 2]
            P.dma("sp", lambda e, g=g, gt_=gt_: e.dma_start(out=gt_[:], in_=gtok[:, 4 * g:4 * g + 4, :]), gt_, writes=[gt_])
            for t in range(4):
                P.op("pe", lambda e, t=t, g=g, b=b: e.matmul(
                    b[:, t * 64:(t + 1) * 64], scm[:, 4 * g + t, :], vb[:, 4 * g + t, :], start=True, stop=False),
                    reads=[scm, vb], writes=[b])
                P.op("pe", lambda e, t=t, g=g, b=b: e.matmul(
                    b[:, t * 64:(t + 1) * 64], qeT[:, (4 * g + t) * 128:(4 * g + t + 1) * 128],
                    S_bf[:, 4 * g + t, :], start=False, stop=True), reads=[qeT, S_bf], writes=[b])
            P.op("act", lambda e, b=b: e.copy(out=osb[:], in_=b[:, 0:256].rearrange("p (t e) -> p t e", t=4)),
                 reads=[b], writes=[osb])
            P.op("dve", lambda e: e.tensor_tensor(out=sq[:], in0=osb[:], in1=osb[:], op=ALU.mult), reads=[osb], writes=[sq])
            P.op("dve", lambda e: e.tensor_reduce(out=ss[:], in_=sq[:], axis=AX.X, op=ALU.add), reads=[sq], writes=[ss])
            P.op("act", lambda e: e.activation(out=ss[:], in_=ss[:], func=AF.Sqrt, scale=1.0 / 64, bias=epst[:]),
                 reads=[ss, epst], writes=[ss])
            P.op("dve", lambda e: e.reciprocal(out=ss[:], in_=ss[:]), reads=[ss], writes=[ss])
            P.op("act", lambda e, gt_=gt_: e.activation(out=gt_[:], in_=gt_[:], func=AF.Silu), reads=[gt_], writes=[gt_])
            P.op("dve", lambda e, gt_=gt_: e.tensor_tensor(
                out=gt_[:], in0=gt_[:], in1=onbt[:].unsqueeze(1).to_broadcast([128, 4, 64]), op=ALU.mult),
                reads=[gt_, onbt], writes=[gt_])
            P.op("dve", lambda e: e.tensor_tensor(
                out=osb[:], in0=osb[:], in1=ss[:].unsqueeze(2).to_broadcast([128, 4, 64]), op=ALU.mult),
                reads=[osb, ss], writes=[osb])
            P.op("dve", lambda e, gt_=gt_, yy=yy: e.tensor_tensor(out=yy[:], in0=osb[:], in1=gt_[:], op=ALU.mult),
                 reads=[osb, gt_], writes=[yy])
            P.dma("sp", lambda e, g=g, yy=yy: e.dma_start(out=y[:, 4 * g:4 * g + 4, :], in_=yy[:]), yy,
                  reads=[yy], is_out=True)
        P.emit()
    return nc


OFF = dict(gq=0, gk=128, gv=256, glr=512, gg=528, su=784, sg=1040, nq=1296, nkv=1808, ngl=2576, ng=2600)


def gla_inputs(proj, p, l):
    cm = gla_consts()
    maps = []
    for c in range(NCORES):
        b, h = divmod(c, 4)
        q = proj[b, :, OFF["gq"] + h * 32:OFF["gq"] + (h + 1) * 32]
        k = proj[b, :, OFF["gk"] + h * 32:OFF["gk"] + (h + 1) * 32]
        v = proj[b, :, OFF["gv"] + h * 64:OFF["gv"] + (h + 1) * 64]
        lr = proj[b, :, OFF["glr"]:OFF["glr"] + 16]
        gt_ = proj[b, :, OFF["gg"] + h * 64:OFF["gg"] + (h + 1) * 64]
        tokmaj = lambda a: np.ascontiguousarray(a.reshape(NTILE, 128, -1).transpose(1, 0, 2))
        maps.append({
            "qT": np.ascontiguousarray(q.T), "kT": np.ascontiguousarray(k.T),
            "lrT1": np.ascontiguousarray(np.concatenate([lr.T, np.ones((1, T), np.float32)], 0)),
            "ktok": tokmaj(k), "vtok": tokmaj(v), "gtok": tokmaj(gt_),
            "w2b": np.ascontiguousarray(np.concatenate(
                [p["gla_w2"][l][:, h * 32:(h + 1) * 32], p["gla_b2"][l][None, h * 32:(h + 1) * 32]], 0)),
            "onb": np.ascontiguousarray(np.broadcast_to(p["gla_onorm"][l][None, :], (128, 64))),
            "cm": cm,
        })
    return maps


def gla_gather(res):
    out = np.zeros((BATCH, T, 256), np.float32)
    for c in range(NCORES):
        b, h = divmod(c, 4)
        out[b, :, h * 64:(h + 1) * 64] = res[c]["y"].transpose(1, 0, 2).reshape(T, 64)
    return out


S5L = 64
S5N = T // S5L


def s5_consts():
    kio = np.broadcast_to(np.arange(65, dtype=np.float32)[None, :], (128, 65)).copy()
    r = np.arange(128)
    dmask = ((r[None, :] // 16) >= (r[:, None] // 16)).astype(np.float32)
    Wm = np.concatenate([np.zeros((128, 7 * 128), np.float32), dmask, np.ones((128, 7 * 128), np.float32)], 1)
    ident = np.eye(128, dtype=np.float32)
    return kio, Wm, ident


def _fact(n):
    f = 1.0
    for i in range(2, n + 1):
        f *= i
    return f


def build_s5():
    nc = bass.Bass("TRN2", target_bir_lowering=False)
    U = nc.dram_tensor("U", [2, 128, 8, 256], F32, kind="ExternalInput").ap()
    lam = nc.dram_tensor("lam", [128, 2], F32, kind="ExternalInput").ap()
    lst = nc.dram_tensor("lst", [128, 1], F32, kind="ExternalInput").ap()
    Bri = nc.dram_tensor("Bri", [128, 2, 16], F32, kind="ExternalInput").ap()
    Cri = nc.dram_tensor("Cri", [128, 2, 16], F32, kind="ExternalInput").ap()
    kio_d = nc.dram_tensor("kio", [128, 65], F32, kind="ExternalInput").ap()
    Wm_d = nc.dram_tensor("Wm", [128, 1920], F32, kind="ExternalInput").ap()
    id_d = nc.dram_tensor("ident", [128, 128], F32, kind="ExternalInput").ap()
    Y = nc.dram_tensor("Y", [2, 128, 8, 256], F32, kind="ExternalOutput").ap()
    with ExitStack() as st:
        P = Prog(nc, st)
        col = lambda name, w=1: P.sb([128, w], F32, name)

        def load(name, shape, src):
            t = P.sb(shape, F32, name)
            P.dma("sp", lambda e: e.dma_start(out=t[:], in_=src), t, writes=[t])
            return t

        lamt = load("lamt", [128, 2], lam)
        lstt = load("lstt", [128, 1], lst)
        Bt = load("Bt", [128, 2, 16], Bri)
        Ct = load("Ct", [128, 2, 16], Cri)
        kio = load("kiot", [128, 65], kio_d)
        Wm = load("Wmt", [128, 1920], Wm_d)
        ident = load("identt", [128, 128], id_d)
        Ut = []
        for gl in range(2):
            t = P.sb([128, 8, 256], F32, f"U{gl}")
            P.dma("pool", lambda e, t=t, gl=gl: e.dma_start(out=t[:], in_=U[gl]), t, writes=[t])
            Ut.append(t)

        def ts(out, in0, s1, op0, s2=None, op1=None, r=(), w=()):
            if op1 is None:
                P.op("dve", lambda e: e.tensor_scalar(out=out, in0=in0, scalar1=s1, scalar2=None, op0=op0), reads=r, writes=w)
            else:
                P.op("dve", lambda e: e.tensor_scalar(out=out, in0=in0, scalar1=s1, scalar2=s2, op0=op0, op1=op1),
                     reads=r, writes=w)

        def tt(out, a, b, op, r=(), w=()):
            P.op("dve", lambda e: e.tensor_tensor(out=out, in0=a, in1=b, op=op), reads=r, writes=w)

        def stt(out, in0, sc, in1, op0, op1, r=(), w=()):
            P.op("dve", lambda e: e.scalar_tensor_tensor(out=out, in0=in0, scalar=sc, in1=in1, op0=op0, op1=op1),
                 reads=r, writes=w)

        def horner(name, xs, coefs):
            acc = col(name)
            P.op("pool", lambda e: e.memset(acc[:], float(coefs[-1])), writes=[acc])
            for c in reversed(coefs[:-1]):
                ts(acc[:], acc[:], xs[:, 0:1], ALU.mult, float(c), ALU.add, r=[acc, xs], w=[acc])
            return acc

        y4 = col("y4")
        ts(y4[:], lstt[:], 0.25, ALU.mult, r=[lstt], w=[y4])
        dt = horner("dt", y4, [1.0 / _fact(k) for k in range(19)])
        tt(dt[:], dt[:], dt[:], ALU.mult, r=[dt], w=[dt])
        tt(dt[:], dt[:], dt[:], ALU.mult, r=[dt], w=[dt])
        lr = col("lr")
        ts(lr[:], lamt[:, 0:1], -1e-4, ALU.min, r=[lamt], w=[lr])
        li = col("li")
        ts(li[:], lamt[:, 1:2], 1.0, ALU.mult, r=[lamt], w=[li])
        xx = col("xx")
        tt(xx[:], lr[:], dt[:], ALU.mult, r=[lr, dt], w=[xx])
        negx = col("negx")
        ts(negx[:], xx[:], -1.0, ALU.mult, r=[xx], w=[negx])
        q = horner("q", xx, [1.0 / _fact(k + 1) for k in range(10)])
        em1 = col("em1")
        tt(em1[:], q[:], xx[:], ALU.mult, r=[q, xx], w=[em1])
        mag = col("mag")
        ts(mag[:], em1[:], 1.0, ALU.add, r=[em1], w=[mag])
        phi = col("phi")
        stt(phi[:], li[:], 1.0 / 32, dt[:], ALU.mult, ALU.mult, r=[li, dt], w=[phi])
        ww = col("ww")
        tt(ww[:], phi[:], phi[:], ALU.mult, r=[phi], w=[ww])
        ps_ = horner("ps", ww, [(-1.0) ** k / _fact(2 * k + 1) for k in range(8)])
        pc_ = horner("pc", ww, [(-1.0) ** (k + 1) / _fact(2 * k + 2) for k in range(8)])
        sA = col("sA")
        tt(sA[:], ps_[:], phi[:], ALU.mult, r=[ps_, phi], w=[sA])
        cA = col("cA")
        tt(cA[:], pc_[:], ww[:], ALU.mult, r=[pc_, ww], w=[cA])
        sB, cB, a1, s2 = col("sB"), col("cB"), col("a1"), col("s2")
        cur = (cA, sA)
        nxt = (cB, sB)
        for _ in range(5):
            cm_, s_ = cur
            cn, sn = nxt
            ts(a1[:], cm_[:], 2.0, ALU.add, cm_[:, 0:1], ALU.mult, r=[cm_], w=[a1])
            tt(s2[:], s_[:], s_[:], ALU.mult, r=[s_], w=[s2])
            tt(cn[:], a1[:], s2[:], ALU.subtract, r=[a1, s2], w=[cn])
            ts(sn[:], cm_[:], 1.0, ALU.add, s_[:, 0:1], ALU.mult, r=[cm_, s_], w=[sn])
            ts(sn[:], sn[:], 2.0, ALU.mult, r=[sn], w=[sn])
            cur, nxt = nxt, cur
        cm_, s_ = cur
        cc = col("cc")
        ts(cc[:], cm_[:], 1.0, ALU.add, r=[cm_], w=[cc])
        ai = col("ai")
        tt(ai[:], mag[:], s_[:], ALU.mult, r=[mag, s_], w=[ai])
        am1r = col("am1r")
        tt(am1r[:], mag[:], cm_[:], ALU.mult, r=[mag, cm_], w=[am1r])
        tt(am1r[:], am1r[:], em1[:], ALU.add, r=[am1r, em1], w=[am1r])
        den = col("den")
        tt(den[:], lr[:], lr[:], ALU.mult, r=[lr], w=[den])
        stt(den[:], li[:], li[:, 0:1], den[:], ALU.mult, ALU.add, r=[li, den], w=[den])
        P.op("dve", lambda e: e.reciprocal(out=den[:], in_=den[:]), reads=[den], writes=[den])
        u1, u2, fr, fi = col("u1"), col("u2"), col("fr"), col("fi")
        tt(u1[:], am1r[:], lr[:], ALU.mult, r=[am1r, lr], w=[u1])
        stt(u1[:], ai[:], li[:, 0:1], u1[:], ALU.mult, ALU.add, r=[ai, li, u1], w=[u1])
        tt(fr[:], u1[:], den[:], ALU.mult, r=[u1, den], w=[fr])
        tt(u2[:], am1r[:], li[:], ALU.mult, r=[am1r, li], w=[u2])
        stt(u2[:], ai[:], lr[:, 0:1], u2[:], ALU.mult, ALU.subtract, r=[ai, lr, u2], w=[u2])
        tt(fi[:], u2[:], den[:], ALU.mult, r=[u2, den], w=[fi])
        Bb = P.sb([128, 2, 16], F32, "Bb")
        v1 = col("v1", 16)
        ts(v1[:], Bt[:, 1, :], fi[:, 0:1], ALU.mult, r=[Bt, fi], w=[v1])
        stt(Bb[:, 0, :], Bt[:, 0, :], fr[:, 0:1], v1[:], ALU.mult, ALU.subtract, r=[Bt, fr, v1], w=[Bb])
        ts(v1[:], Bt[:, 0, :], fi[:, 0:1], ALU.mult, r=[Bt, fi], w=[v1])
        stt(Bb[:, 1, :], Bt[:, 1, :], fr[:, 0:1], v1[:], ALU.mult, ALU.add, r=[Bt, fr, v1], w=[Bb])
        Er, Ei = col("Er", 65), col("Ei", 65)
        t1, t2 = col("t1", 32), col("t2", 32)
        P.op("pool", lambda e: e.memset(Er[:, 0:1], 1.0), writes=[Er])
        P.op("pool", lambda e: e.memset(Ei[:, 0:1], 0.0), writes=[Ei])
        ts(Er[:, 1:2], cc[:], 1.0, ALU.mult, r=[cc], w=[Er])
        ts(Ei[:, 1:2], s_[:], 1.0, ALU.mult, r=[s_], w=[Ei])
        m = 1
        while m <= 32:
            er, ei = Er[:, m:m + 1], Ei[:, m:m + 1]
            ts(t1[:, 0:m], Ei[:, 1:m + 1], ei, ALU.mult, r=[Ei], w=[t1])
            ts(t2[:, 0:m], Ei[:, 1:m + 1], er, ALU.mult, r=[Ei, Er], w=[t2])
            stt(Ei[:, m + 1:2 * m + 1], Er[:, 1:m + 1], ei, t2[:, 0:m], ALU.mult, ALU.add, r=[Er, Ei, t2], w=[Ei])
            stt(Er[:, m + 1:2 * m + 1], Er[:, 1:m + 1], er, t1[:, 0:m], ALU.mult, ALU.subtract, r=[Er, t1], w=[Er])
            m *= 2
        magk, imagk = col("magk", 65), col("imagk", 65)
        P.op("act", lambda e: e.activation(out=magk[:], in_=kio[:], func=AF.Exp, scale=xx[:, 0:1]), reads=[kio, xx], writes=[magk])
        P.op("act", lambda e: e.activation(out=imagk[:], in_=kio[:], func=AF.Exp, scale=negx[:, 0:1]), reads=[kio, negx],
             writes=[imagk])
        Pr, Pi, Qr, Qi = col("Pr", 65), col("Pi", 65), col("Qr", 65), col("Qi", 65)
        tt(Pr[:], Er[:], magk[:], ALU.mult, r=[Er, magk], w=[Pr])
        tt(Pi[:], Ei[:], magk[:], ALU.mult, r=[Ei, magk], w=[Pi])
        tt(Qr[:], Er[:], imagk[:], ALU.mult, r=[Er, imagk], w=[Qr])
        stt(Qi[:], Ei[:], -1.0, imagk[:], ALU.mult, ALU.mult, r=[Ei, imagk], w=[Qi])
        KBr = P.sb([128, 64, 16], F32, "KBr")
        KBi = P.sb([128, 64, 16], F32, "KBi")
        QCr = P.sb([128, 64, 16], F32, "QCr")
        QCi = P.sb([128, 64, 16], F32, "QCi")
        tmp = P.sb([128, 64, 16], F32, "tmp")
        bj = lambda tl: tl[:, 0:64].unsqueeze(2).to_broadcast([128, 64, 16])
        bc = lambda ap: ap.unsqueeze(1).to_broadcast([128, 64, 16])
        tt(KBr[:], bj(Qr), bc(Bb[:, 0, :]), ALU.mult, r=[Qr, Bb], w=[KBr])
        tt(tmp[:], bj(Qi), bc(Bb[:, 1, :]), ALU.mult, r=[Qi, Bb], w=[tmp])
        tt(KBr[:], KBr[:], tmp[:], ALU.subtract, r=[KBr, tmp], w=[KBr])
        tt(KBi[:], bj(Qr), bc(Bb[:, 1, :]), ALU.mult, r=[Qr, Bb], w=[KBi])
        tt(tmp[:], bj(Qi), bc(Bb[:, 0, :]), ALU.mult, r=[Qi, Bb], w=[tmp])
        tt(KBi[:], KBi[:], tmp[:], ALU.add, r=[KBi, tmp], w=[KBi])
        tt(QCr[:], bj(Pr), bc(Ct[:, 0, :]), ALU.mult, r=[Pr, Ct], w=[QCr])
        tt(tmp[:], bj(Pi), bc(Ct[:, 1, :]), ALU.mult, r=[Pi, Ct], w=[tmp])
        tt(QCr[:], QCr[:], tmp[:], ALU.subtract, r=[QCr, tmp], w=[QCr])
        tt(QCi[:], bj(Pr), bc(Ct[:, 1, :]), ALU.mult, r=[Pr, Ct], w=[QCi])
        tt(tmp[:], bj(Pi), bc(Ct[:, 0, :]), ALU.mult, r=[Pi, Ct], w=[tmp])
        stt(QCi[:], QCi[:], -1.0, tmp[:], ALU.mult, ALU.subtract, r=[QCi, tmp], w=[QCi])
        fl = lambda tl: tl[:].rearrange("p j c -> p (j c)")
        banks = [P.ps([128, 512], F32, f"bk{i}") for i in range(8)]
        nb = 0
        TZ = [P.sb([128, 8, 1024], F32, f"TZ{gl}") for gl in range(2)]
        for gl in range(2):
            rows = slice(64 * gl, 64 * gl + 64)
            for rc in range(8):
                for ch in range(2):
                    if 4 * ch + 3 < rc:
                        continue
                    bk = banks[nb % 4]
                    nb += 1
                    P.op("pe", lambda e, bk=bk, rows=rows, rc=rc, ch=ch: e.matmul(
                        bk[:], fl(KBr)[rows, rc * 128:(rc + 1) * 128], fl(QCr)[rows, ch * 512:(ch + 1) * 512],
                        start=True, stop=False), reads=[KBr, QCr], writes=[bk])
                    P.op("pe", lambda e, bk=bk, rows=rows, rc=rc, ch=ch: e.matmul(
                        bk[:], fl(KBi)[rows, rc * 128:(rc + 1) * 128], fl(QCi)[rows, ch * 512:(ch + 1) * 512],
                        start=False, stop=True), reads=[KBi, QCi], writes=[bk])
                    w0 = (7 - rc) * 128 + ch * 512
                    P.op("dve", lambda e, bk=bk, gl=gl, rc=rc, ch=ch, w0=w0: e.tensor_tensor(
                        out=TZ[gl][:, rc, ch * 512:(ch + 1) * 512], in0=bk[:], in1=Wm[:, w0:w0 + 512], op=ALU.mult),
                        reads=[bk, Wm], writes=[TZ[gl]])
        KBT = [[P.sb([128, 8, 64], F32, f"KBT{gl}{ri}") for ri in range(2)] for gl in range(2)]
        for gl in range(2):
            rows = slice(64 * gl, 64 * gl + 64)
            for ri, src in enumerate((KBr, KBi)):
                bk = banks[4 + (2 * gl + ri) % 2]
                for kc in range(8):
                    P.op("pe", lambda e, bk=bk, rows=rows, kc=kc, src=src: e.transpose(
                        bk[:, kc * 64:(kc + 1) * 64], fl(src)[rows, kc * 128:(kc + 1) * 128], ident[rows, rows]),
                        reads=[src, ident], writes=[bk])
                P.op("act", lambda e, bk=bk, gl=gl, ri=ri: e.copy(
                    out=KBT[gl][ri][:], in_=bk[:].rearrange("p (k c) -> p k c", k=8)), reads=[bk], writes=[KBT[gl][ri]])
        bx = banks[6]
        for gl in range(2):
            rows = slice(64 * gl, 64 * gl + 64)
            for ri in range(2):
                for kc in range(8):
                    P.op("pe", lambda e, gl=gl, rows=rows, ri=ri, kc=kc: e.matmul(
                        bx[rows, ri * 256:(ri + 1) * 256], KBT[gl][ri][:, kc, :], Ut[gl][:, kc, :],
                        start=(kc == 0), stop=(kc == 7)), reads=[KBT[gl][ri], Ut[gl]], writes=[bx])
        Wr, Wi = col("Wr", 256), col("Wi", 256)
        Xs = col("Xs", 512)
        P.op("act", lambda e: e.copy(out=Xs[:], in_=bx[:]), reads=[bx], writes=[Xs])
        Ar, Ai = Pr[:, 64:65], Pi[:, 64:65]
        ts(Wr[:], Xs[:, 256:512], Ai, ALU.mult, r=[Xs, Pi], w=[Wr])
        stt(Wr[:], Xs[:, 0:256], Ar, Wr[:], ALU.mult, ALU.subtract, r=[Xs, Pr, Wr], w=[Wr])
        ts(Wi[:], Xs[:, 0:256], Ai, ALU.mult, r=[Xs, Pi], w=[Wi])
        stt(Wi[:], Xs[:, 256:512], Ar, Wi[:], ALU.mult, ALU.add, r=[Xs, Pr, Wi], w=[Wi])
        Zr, Zi = P.sb([128, 2, 128], F32, "Zr"), P.sb([128, 2, 128], F32, "Zi")
        mm1, mm2 = col("mm1", 2), col("mm2", 2)
        P.op("pool", lambda e: e.memset(Zr[:], 0.0), writes=[Zr])
        P.op("pool", lambda e: e.memset(Zi[:], 0.0), writes=[Zi])
        W3r = Wr[:].rearrange("p (b n) -> p b n", b=2)
        W3i = Wi[:].rearrange("p (b n) -> p b n", b=2)
        for n in range(S5N - 1):
            stt(mm1[:], Zi[:, :, n], Ai, W3r[:, :, n], ALU.mult, ALU.subtract, r=[Zi, Pi, Wr], w=[mm1])
            stt(mm2[:], Zr[:, :, n], Ai, W3i[:, :, n], ALU.mult, ALU.add, r=[Zr, Pi, Wi], w=[mm2])
            stt(Zr[:, :, n + 1], Zr[:, :, n], Ar, mm1[:], ALU.mult, ALU.subtract, r=[Zr, Pr, mm1], w=[Zr])
            stt(Zi[:, :, n + 1], Zi[:, :, n], Ar, mm2[:], ALU.mult, ALU.add, r=[Zi, Pr, mm2], w=[Zi])
        Yt = [P.sb([128, 256], F32, f"Yt{i}") for i in range(2)]
        ny = 0
        for gl in range(2):
            rows = slice(64 * gl, 64 * gl + 64)
            for ob in range(8):
                bk = banks[ny % 4]
                yt_ = Yt[ny % 2]
                ny += 1
                for kc in range(ob + 1):
                    P.op("pe", lambda e, bk=bk, gl=gl, kc=kc, ob=ob: e.matmul(
                        bk[:, 0:256], TZ[gl][:, kc, ob * 128:(ob + 1) * 128], Ut[gl][:, kc, :],
                        start=(kc == 0), stop=False), reads=[TZ[gl], Ut[gl]], writes=[bk])
                P.op("pe", lambda e, bk=bk, rows=rows, ob=ob: e.matmul(
                    bk[:, 0:256], fl(QCr)[rows, ob * 128:(ob + 1) * 128], Zr[rows].rearrange("p b n -> p (b n)"),
                    start=False, stop=False), reads=[QCr, Zr], writes=[bk])
                P.op("pe", lambda e, bk=bk, rows=rows, ob=ob: e.matmul(
                    bk[:, 0:256], fl(QCi)[rows, ob * 128:(ob + 1) * 128], Zi[rows].rearrange("p b n -> p (b n)"),
                    start=False, stop=True), reads=[QCi, Zi], writes=[bk])
                P.op("act", lambda e, bk=bk, yt_=yt_: e.copy(out=yt_[:], in_=bk[:, 0:256]), reads=[bk], writes=[yt_])
                P.dma("sp", lambda e, gl=gl, ob=ob, yt_=yt_: e.dma_start(out=Y[gl, :, ob, :], in_=yt_[:]), yt_,
                      reads=[yt_], is_out=True)
        P.emit()
    return nc


def s5_inputs(proj, p, l):
    kio, Wm, ident = s5_consts()
    maps = []
    for c in range(NCORES):
        Us, lam, lst, Bri, Cri = [], [], [], [], []
        for gl in range(2):
            g = 2 * c + gl
            u = proj[:, :, OFF["su"] + g * 16:OFF["su"] + (g + 1) * 16]
            u = u.reshape(BATCH, S5N, 8, 8, 16).transpose(3, 4, 2, 0, 1)
            Us.append(u.reshape(128, 8, 2 * S5N))
            lam.append(np.stack([p["s5_lam_re"][l][g], p["s5_lam_im"][l][g]], 1))
            lst.append(np.full((64, 1), p["s5_log_step"][l][g], np.float32))
            Bri.append(np.stack([p["s5_b_re"][l][g], p["s5_b_im"][l][g]], 1))
            Cri.append(np.stack([p["s5_c_re"][l][g].T, p["s5_c_im"][l][g].T], 1))
        cat = lambda xs: np.ascontiguousarray(np.concatenate(xs, 0).astype(np.float32))
        maps.append({"U": np.ascontiguousarray(np.stack(Us, 0)), "lam": cat(lam), "lst": cat(lst),
                     "Bri": cat(Bri), "Cri": cat(Cri), "kio": kio, "Wm": Wm, "ident": ident})
    return maps


def s5_gather(res):
    out = np.zeros((BATCH, T, 256), np.float32)
    for c in range(NCORES):
        Yc = res[c]["Y"]
        for gl in range(2):
            g = 2 * c + gl
            a = Yc[gl].reshape(8, 16, 8, BATCH, S5N).transpose(3, 4, 2, 0, 1)
            out[:, :, g * 16:(g + 1) * 16] = a.reshape(BATCH, T, 16)
    return out


NQB = 32
NEGB = -30000.0


def nsa_consts():
    r = np.arange(128)
    kl, ql = r[:, None], r[None, :]
    mdiag = np.where(kl <= ql, 0.0, NEGB).astype(np.float32)
    mfar = np.where(kl > ql, 0.0, NEGB).astype(np.float32)
    mall = np.full((128, 128), NEGB, np.float32)
    mzero = np.zeros((128, 128), np.float32)
    mw = [np.stack([mfar, mzero, mzero, mzero, mdiag, mall], 1),
          np.stack([mall, mfar, mzero, mzero, mzero, mdiag], 1)]
    ms = [np.stack([mdiag, mall], 1), np.stack([mzero, mdiag], 1)]
    cmpm = np.zeros((128, 16, 128), np.float32)
    for v in range(16):
        cmpm[:, v, :] = np.where(16 * (kl - 8 * v) + 31 <= ql, 0.0, NEGB)
    c = np.arange(512)[:, None]
    s = np.arange(128)[None, :]
    ovl = ((16 * c < 64 * s + 64) & (16 * c + 31 >= 64 * s) & (c < 511)).astype(np.float32)
    ovl = ovl.reshape(4, 128, 128).transpose(1, 0, 2)
    u = np.arange(255)[None, :] - 127
    curl = (r[:, None] >= 64).astype(np.int64)
    forced = (u == curl) | (u == curl - 1)
    invalid = u > curl
    W1 = np.where(forced | invalid, 0.0, 1.0)
    W2 = np.where(invalid, -1e30, np.where(forced, 1e4, 0.0))
    W12 = np.stack([W1, W2], 1).astype(np.float32)
    E = (np.arange(8192)[None, :] // 64 == r[:, None]).astype(np.float32)
    return dict(mw=mw, ms=ms, cmpm=cmpm, ovl=np.ascontiguousarray(ovl), W12=W12, E=E,
                ident=np.eye(128, dtype=np.float32), ones64=np.ones((64, 64), np.float32))


def build_nsa(bcast_rhs=True):
    nc = bass.Bass("TRN2", target_bir_lowering=False)
    din = lambda name, shape: nc.dram_tensor(name, list(shape), F32, kind="ExternalInput").ap()
    kT4 = din("kT4", [4, 64, T])
    vtok2 = din("vtok2", [2, 128, NTILE, 64])
    qTd = din("qTd", [64, 4, NQB * 128])
    gld = din("gld", [128, NQB, 12])
    gbd = din("gbd", [128, 12])
    gated = din("gated", [128, NQB, 256])
    qnd = din("qn", [64, 1])
    knd = din("kn", [64, 3])
    posd = din("posT", [64, 2, 32])
    w1d = din("w1d", [2, 64, 32, 256])
    b1d = din("b1d", [128, 2, 2])
    w2d = din("w2d", [128, 2, 2, 64])
    b2kd = din("b2k", [64, 1])
    b2vd = din("b2v", [128, 64])
    ones64d = din("ones64", [64, 64])
    Ed = din("E", [128, T])
    mwd = din("mw", [128, 6, 128])
    msd = din("ms", [128, 2, 128])
    identd = din("ident", [128, 128])
    cmpmd = din("cmpm", [128, 8, 128])
    ovld = din("ovl", [128, 4, 128])
    W12d = din("W12", [128, 2, 253])
    y = nc.dram_tensor("y", [128, NQB, 256], F32, kind="ExternalOutput").ap()
    with ExitStack() as st:
        P = Prog(nc, st)
        dq = ["sp", "pool"]
        ndq = [0]

        def load(name, shape, src, dt=F32):
            t = P.sb(shape, dt, name)
            qn_ = dq[ndq[0] % 2]
            ndq[0] += 1
            P.dma(qn_, lambda e: e.dma_start(out=t[:], in_=src), t, writes=[t])
            return t

        stg = [P.sb([128, 2048], F32, f"stg{i}") for i in range(2)]
        nst = [0]

        def load_cast(dst_ap, dst_lt, src, shape_p, ncols, eng=None):
            s = stg[nst[0] % 2]
            eng = eng or ("dve" if nst[0] % 2 == 0 else "act")
            qn_ = dq[nst[0] % 2]
            nst[0] += 1
            P.dma(qn_, lambda e: e.dma_start(out=s[0:shape_p, 0:ncols], in_=src), s, writes=[s])
            if eng == "dve":
                P.op("dve", lambda e: e.tensor_copy(out=dst_ap, in_=s[0:shape_p, 0:ncols]), reads=[s], writes=[dst_lt])
            else:
                P.op("act", lambda e: e.copy(out=dst_ap, in_=s[0:shape_p, 0:ncols]), reads=[s], writes=[dst_lt])

        banks = [P.ps([128, 512], F32, f"bk{i}") for i in range(8)]
        SB = banks[0:3]
        OC0, OC1, OS, OW, MISC = banks[3], banks[4], banks[5], banks[6], banks[7]
        qn = load("qn", [64, 1], qnd)
        kn = load("kn", [64, 3], knd)
        b1 = load("b1", [128, 2, 2], b1d)
        b2k = load("b2k", [64, 1], b2kd)
        b2v = load("b2v", [128, 64], b2vd)
        ones64 = load("ones64", [64, 64], ones64d)
        identf = load("identf", [128, 128], identd)
        W12 = load("W12", [128, 2, 253], W12d)
        gb = load("gb", [128, 12], gbd)
        gl = load("gl", [128, NQB, 12], gld)
        epst = P.sb([128, 1], F32, "eps")
        P.op("pool", lambda e: e.memset(epst[:], RMS_EPS), writes=[epst])
        qsc = P.sb([64, 1], F32, "qsc")
        P.op("dve", lambda e: e.tensor_scalar(out=qsc[:], in0=qn[:], scalar1=64 ** -0.5, scalar2=None, op0=ALU.mult),
             reads=[qn], writes=[qsc])
        identb = P.sb([128, 128], BF16, "identb")
        P.op("dve", lambda e: e.tensor_copy(out=identb[:], in_=identf[:]), reads=[identf], writes=[identb])
        mwb = P.sb([128, 6, 128], BF16, "mwb")
        load_cast(mwb[:].rearrange("p a b -> p (a b)"), mwb, mwd.rearrange("p a b -> p (a b)"), 128, 768)
        msb = P.sb([128, 2, 128], BF16, "msb")
        load_cast(msb[:].rearrange("p a b -> p (a b)"), msb, msd.rearrange("p a b -> p (a b)"), 128, 256)
        cmpmb = P.sb([128, 8, 128], BF16, "cmpmb")
        load_cast(cmpmb[:].rearrange("p a b -> p (a b)"), cmpmb, cmpmd.rearrange("p a b -> p (a b)"), 128, 1024)
        Eb = P.sb([128, T], BF16, "Eb")
        for i in range(4):
            load_cast(Eb[:, i * 2048:(i + 1) * 2048], Eb, Ed[:, i * 2048:(i + 1) * 2048], 128, 2048)
        P.op("dve", lambda e: e.tensor_tensor(out=gl[:], in0=gl[:], in1=gb[:].unsqueeze(1).to_broadcast([128, NQB, 12]),
                                              op=ALU.add), reads=[gl, gb], writes=[gl])
        P.op("act", lambda e: e.activation(out=gl[:], in_=gl[:], func=AF.Sigmoid), reads=[gl], writes=[gl])
        V1 = [P.sb([128, NTILE, 65], BF16, f"V1_{i}") for i in range(2)]
        for j in range(2):
            P.op("pool", lambda e, j=j: e.memset(V1[j][:, :, 64:65], 1.0), writes=[V1[j]])
            for i in range(2):
                load_cast(V1[j][:, i * 32:(i + 1) * 32, 0:64], V1[j],
                          vtok2[j][:, i * 32:(i + 1) * 32, :].rearrange("p a b -> p (a b)"), 128, 2048)

        rn_sq = P.sb([64, 512], F32, "rn_sq")
        rn_rt = P.sb([64, 512], F32, "rn_rt")

        def rms_fm(src_ap, src_lt, ncol, scale_ap, scale_lt, dst_ap, dst_lt, bank):
            P.op("act", lambda e: e.activation(out=rn_sq[:, 0:ncol], in_=src_ap, func=AF.Square), reads=[src_lt], writes=[rn_sq])
            P.op("pe", lambda e: e.matmul(bank[0:64, 0:ncol], ones64[:], rn_sq[:, 0:ncol], start=True, stop=True),
                 reads=[ones64, rn_sq], writes=[bank])
            P.op("act", lambda e: e.activation(out=rn_rt[:, 0:ncol], in_=bank[0:64, 0:ncol], func=AF.Sqrt, scale=1.0 / 64,
                                               bias=epst[0:64, :]), reads=[bank, epst], writes=[rn_rt])
            P.op("dve", lambda e: e.reciprocal(out=rn_rt[:, 0:ncol], in_=rn_rt[:, 0:ncol]), reads=[rn_rt], writes=[rn_rt])
            P.op("dve", lambda e: e.scalar_tensor_tensor(out=dst_ap, in0=src_ap, scalar=scale_ap, in1=rn_rt[:, 0:ncol],
                                                         op0=ALU.mult, op1=ALU.mult),
                 reads=[src_lt, scale_lt, rn_rt], writes=[dst_lt])

        KT = [P.sb([64, T], BF16, f"KT{i}") for i in range(2)]
        nrm = 0
        for j in range(2):
            for i in range(4):
                s = stg[nst[0] % 2]
                qn_ = dq[nst[0] % 2]
                nst[0] += 1
                P.dma(qn_, lambda e, s=s, j=j, i=i: e.dma_start(out=s[0:64, :], in_=kT4[2 + j][:, i * 2048:(i + 1) * 2048]),
                      s, writes=[s])
                for t in range(4):
                    rms_fm(s[0:64, t * 512:(t + 1) * 512], s, 512, kn[:, 1 + j:2 + j], kn,
                           KT[j][:, i * 2048 + t * 512:i * 2048 + (t + 1) * 512], KT[j], banks[nrm % 3])
                    nrm += 1
        kcT = P.sb([64, T + 16], BF16, "kcT")
        P.op("pool", lambda e: e.memset(kcT[:, T:T + 16], 0.0), writes=[kcT])
        w1b = P.sb([64, 32, 256], BF16, "w1b")
        posb = P.sb([64, 2, 32], BF16, "posb")
        load_cast(posb[:].rearrange("p a b -> p (a b)"), posb, posd.rearrange("p a b -> p (a b)"), 64, 64)
        w2f = load("w2f", [128, 2, 2, 64], w2d)
        w2b = P.sb([128, 2, 2, 64], BF16, "w2b")
        P.op("dve", lambda e: e.tensor_copy(out=w2b[:], in_=w2f[:]), reads=[w2f], writes=[w2b])
        hidT = P.sb([128, 2, 512], BF16, "hidT")
        P.op("pool", lambda e: e.memset(hidT[:], 0.0), writes=[hidT])
        biasv = P.sb([128, 2], F32, "biasv")
        hx = P.sb([128, 512], F32, "hx")
        hu = P.sb([128, 512], F32, "hu")
        P.op("pool", lambda e: e.memset(hx[:], 0.0), writes=[hx])
        kcmpT = P.sb([64, 512], BF16, "kcmpT")
        kraw = P.sb([64, 512], F32, "kraw")
        P.op("pool", lambda e: e.memset(kraw[:], 0.0), writes=[kraw])
        Vc1 = P.sb([128, 4, 193], BF16, "Vc1")
        P.op("pool", lambda e: e.memset(Vc1[:, :, 64:65], 1.0), writes=[Vc1])
        load_cast(Vc1[:, :, 65:193], Vc1, ovld.rearrange("p a b -> p (a b)"), 128, 512, eng="dve")
        for kv in range(2):
            for i in range(4):
                load_cast(kcT[:, i * 2048:(i + 1) * 2048], kcT, kT4[kv][:, i * 2048:(i + 1) * 2048], 64, 2048)
            for i in range(4):
                load_cast(w1b[:, i * 8:(i + 1) * 8, :].rearrange("p a b -> p (a b)"), w1b,
                          w1d[kv][:, i * 8:(i + 1) * 8, :].rearrange("p a b -> p (a b)"), 64, 2048)
            for hc in range(2):
                bk = banks[hc]
                for l in range(32):
                    P.op("pe", lambda e, bk=bk, l=l, hc=hc: e.matmul(
                        bk[:, 0:511], w1b[:, l, hc * 128:(hc + 1) * 128], kcT[:, l:l + 16 * 511:16],
                        start=(l == 0), stop=(l == 31)), reads=[w1b, kcT], writes=[bk])
                pb = MISC
                for l in range(32):
                    P.op("pe", lambda e, pb=pb, l=l, hc=hc, kv=kv: e.matmul(
                        pb[:, hc:hc + 1], w1b[:, l, hc * 128:(hc + 1) * 128], posb[:, kv, l:l + 1],
                        start=(l == 0), stop=(l == 31)), reads=[w1b, posb], writes=[pb])
                P.op("dve", lambda e, hc=hc, kv=kv, pb=pb: e.tensor_tensor(
                    out=biasv[:, hc:hc + 1], in0=pb[:, hc:hc + 1], in1=b1[:, kv, hc:hc + 1], op=ALU.add),
                    reads=[pb, b1], writes=[biasv])
                P.op("act", lambda e, bk=bk, hc=hc: e.activation(
                    out=hx[:, 0:511], in_=bk[:, 0:511], func=AF.Identity, bias=biasv[:, hc:hc + 1]),
                    reads=[bk, biasv], writes=[hx])
                P.op("dve", lambda e: e.tensor_tensor(out=hu[:], in0=hx[:], in1=hx[:], op=ALU.mult), reads=[hx], writes=[hu])
                P.op("dve", lambda e: e.tensor_scalar(out=hu[:], in0=hu[:], scalar1=0.044715, scalar2=1.0, op0=ALU.mult,
                                                      op1=ALU.add), reads=[hu], writes=[hu])
                P.op("dve", lambda e: e.tensor_tensor(out=hu[:], in0=hu[:], in1=hx[:], op=ALU.mult), reads=[hu, hx], writes=[hu])
                P.op("act", lambda e: e.activation(out=hu[:], in_=hu[:], func=AF.Tanh, scale=0.7978845608028654),
                     reads=[hu], writes=[hu])
                P.op("dve", lambda e: e.tensor_scalar(out=hu[:], in0=hu[:], scalar1=0.5, scalar2=0.5, op0=ALU.mult,
                                                      op1=ALU.add), reads=[hu], writes=[hu])
                P.op("dve", lambda e, hc=hc: e.tensor_tensor(out=hidT[:, hc, 0:511], in0=hu[:, 0:511], in1=hx[:, 0:511],
                                                            op=ALU.mult), reads=[hu, hx], writes=[hidT])
            if kv == 0:
                bk = banks[2]
                for hc in range(2):
                    P.op("pe", lambda e, bk=bk, hc=hc: e.matmul(
                        bk[0:64, 0:511], w2b[:, 0, hc, :], hidT[:, hc, 0:511], start=(hc == 0), stop=(hc == 1)),
                        reads=[w2b, hidT], writes=[bk])
                P.op("act", lambda e, bk=bk: e.activation(out=kraw[:, 0:511], in_=bk[0:64, 0:511], func=AF.Identity,
                                                          bias=b2k[:, 0:1]), reads=[bk, b2k], writes=[kraw])
                rms_fm(kraw[:, :], kraw, 512, kn[:, 0:1], kn, kcmpT[:, :], kcmpT, banks[0])
            else:
                bk = banks[2]
                for cc in range(4):
                    for hc in range(2):
                        P.op("pe", lambda e, bk=bk, hc=hc, cc=cc: e.matmul(
                            bk[:, cc * 64:(cc + 1) * 64], hidT[:, hc, cc * 128:(cc + 1) * 128], w2b[:, 1, hc, :],
                            start=(hc == 0), stop=(hc == 1)), reads=[w2b, hidT], writes=[bk])
                P.op("dve", lambda e, bk=bk: e.tensor_tensor(
                    out=Vc1[:, :, 0:64], in0=bk[:, 0:256].rearrange("p (c d) -> p c d", c=4),
                    in1=b2v[:].unsqueeze(1).to_broadcast([128, 4, 64]), op=ALU.add), reads=[bk, b2v], writes=[Vc1])
        qTb = P.sb([64, NQB, 4, 128], BF16, "qTb")
        for h in range(4):
            for i in range(2):
                s = stg[nst[0] % 2]
                qn_ = dq[nst[0] % 2]
                nst[0] += 1
                P.dma(qn_, lambda e, s=s, h=h, i=i: e.dma_start(out=s[0:64, :], in_=qTd[:, h, i * 2048:(i + 1) * 2048]),
                      s, writes=[s])
                for t in range(4):
                    blk0 = i * 16 + t * 4
                    rms_fm(s[0:64, t * 512:(t + 1) * 512], s, 512, qsc[:, 0:1], qsc,
                           qTb[:, blk0:blk0 + 4, h, :], qTb, banks[nrm % 3])
                    nrm += 1
        Pt = [P.sb([128, 512], BF16, f"Pt{i}") for i in range(3)]
        npair = [0]
        gt = [P.sb([128, 256], F32, f"gt{i}") for i in range(2)]
        yt = [P.sb([128, 256], F32, f"yt{i}") for i in range(2)]
        rec = P.sb([128, 3, 4], F32, "rec")
        imp = P.sb([128, 128], F32, "imp")
        imp2 = P.sb([128, 128], F32, "imp2")
        m8a = P.sb([128, 8], F32, "m8a")
        m8b = P.sb([128, 8], F32, "m8b")
        self_ = P.sb([128, 128], F32, "sel")
        negT = P.sb([128, 128], BF16, "negT")
        acc = P.sb([128, 4, 64], F32, "acc")
        tmp = P.sb([128, 4, 64], F32, "tmpo")

        def pair(kT_ap, kT_lt, qb, biases, V_ap, V_lt, obanks, ow, first, last):
            k = npair[0]
            npair[0] += 1
            S, pt = SB[k % 3], Pt[k % 3]
            P.op("pe", lambda e: e.matmul(S[:], kT_ap, qb, start=True, stop=(len(biases) == 0)),
                 reads=[kT_lt, qTb], writes=[S])
            for bi, (l_ap, lts, r_ap) in enumerate(biases):
                lastb = bi == len(biases) - 1
                if bcast_rhs:
                    P.op("pe", lambda e, l_ap=l_ap, r_ap=r_ap, lastb=lastb: e.matmul(
                        S[:], l_ap, r_ap.unsqueeze(1).to_broadcast([128, 4, 128]), start=False, stop=lastb),
                        reads=lts, writes=[S])
                else:
                    for h in range(4):
                        P.op("pe", lambda e, l_ap=l_ap, r_ap=r_ap, lastb=lastb, h=h: e.matmul(
                            S[:, h * 128:(h + 1) * 128], l_ap, r_ap, start=False, stop=(lastb and h == 3)),
                            reads=lts, writes=[S])
            P.op("act", lambda e: e.activation(out=pt[:], in_=S[:], func=AF.Exp), reads=[S], writes=[pt])
            for h in range(4):
                bk, c0 = obanks[h]
                st_ = first and (h == 0 or obanks[h][0] is not obanks[h - 1][0])
                P.op("pe", lambda e, bk=bk, c0=c0, h=h, st_=st_: e.matmul(
                    bk[:, c0:c0 + ow], pt[:, h * 128:(h + 1) * 128], V_ap, start=st_, stop=last),
                    reads=[pt, V_lt], writes=[bk])

        oc_b = [(OC0, 0), (OC0, 193), (OC1, 0), (OC1, 193)]
        os_b = [(OS, h * 65) for h in range(4)]
        ow_b = [(OW, h * 65) for h in range(4)]
        for m in range(NQB):
            qb = qTb[:, m, :, :].rearrange("p h q -> p (h q)")
            g_, yy = gt[m % 2], yt[m % 2]
            P.dma("sp", lambda e, m=m, g_=g_: e.dma_start(out=g_[:], in_=gated[:, m, :]), g_, writes=[g_])
            P.op("act", lambda e, g_=g_: e.activation(out=g_[:], in_=g_[:], func=AF.Silu), reads=[g_], writes=[g_])
            ccs = m // 8
            for cc in range(ccs + 1):
                biases = []
                if cc == ccs:
                    biases = [(identb[:], [identb, cmpmb], cmpmb[:, m % 8, :])]
                pair(kcmpT[:, cc * 128:(cc + 1) * 128], kcmpT, qb, biases, Vc1[:, cc, :], Vc1, oc_b, 193,
                     cc == 0, cc == ccs)
            ocv = [bk[:, c0:c0 + 193] for bk, c0 in oc_b]
            for h in range(4):
                P.op("dve", lambda e, h=h: e.tensor_scalar(out=rec[:, 0, h:h + 1], in0=ocv[h][:, 64:65], scalar1=1e-30,
                                                          scalar2=None, op0=ALU.max), reads=[oc_b[h][0]], writes=[rec])
            P.op("dve", lambda e: e.reciprocal(out=rec[:, 0, :], in_=rec[:, 0, :]), reads=[rec], writes=[rec])
            P.op("dve", lambda e: e.tensor_scalar(out=imp[:], in0=ocv[0][:, 65:193], scalar1=rec[:, 0, 0:1], scalar2=None,
                                                  op0=ALU.mult), reads=[OC0, rec], writes=[imp])
            for h in range(1, 4):
                P.op("dve", lambda e, h=h: e.scalar_tensor_tensor(
                    out=imp[:], in0=ocv[h][:, 65:193], scalar=rec[:, 0, h:h + 1], in1=imp[:], op0=ALU.mult, op1=ALU.add),
                    reads=[oc_b[h][0], rec, imp], writes=[imp])
            w0 = 125 - 4 * m
            P.op("dve", lambda e, w0=w0: e.tensor_tensor(out=imp[:], in0=imp[:], in1=W12[:, 0, w0:w0 + 128], op=ALU.mult),
                 reads=[imp, W12], writes=[imp])
            P.op("dve", lambda e, w0=w0: e.tensor_tensor(out=imp[:], in0=imp[:], in1=W12[:, 1, w0:w0 + 128], op=ALU.add),
                 reads=[imp, W12], writes=[imp])
            P.op("dve", lambda e: e.memset(imp[:, 0:1], 1e4), reads=[], writes=[imp])
            P.op("dve", lambda e: e.max(out=m8a[:], in_=imp[:]), reads=[imp], writes=[m8a])
            P.op("dve", lambda e: e.match_replace(out=imp2[:], in_to_replace=m8a[:], in_values=imp[:], imm_value=-3e38),
                 reads=[imp, m8a], writes=[imp2])
            P.op("dve", lambda e: e.max(out=m8b[:], in_=imp2[:]), reads=[imp2], writes=[m8b])
            P.op("dve", lambda e: e.tensor_scalar(out=self_[:], in0=imp[:], scalar1=m8b[:, 7:8], scalar2=None, op0=ALU.is_ge),
                 reads=[imp, m8b], writes=[self_])
            P.op("pe", lambda e: e.transpose(MISC[:, 0:128], self_[:], identf[:]), reads=[self_, identf], writes=[MISC])
            P.op("dve", lambda e: e.tensor_scalar(out=negT[:], in0=MISC[:, 0:128], scalar1=-1.0, scalar2=-NEGB, op0=ALU.add,
                                                  op1=ALU.mult), reads=[MISC], writes=[negT])
            kbs = [kb for kb in range(2 * m - 4, 2 * m + 2) if kb >= 0]
            for kb in kbs:
                o = kb - (2 * m - 4)
                biases = [(identb[:], [identb, mwb], mwb[:, o, :])]
                pair(KT[1][:, kb * 128:(kb + 1) * 128], KT[1], qb, biases, V1[1][:, kb, :], V1[1], ow_b, 65,
                     kb == kbs[0], kb == kbs[-1])
            for kb in range(2 * m + 2):
                biases = [(Eb[:, kb * 128:(kb + 1) * 128], [Eb, negT], negT[:])]
                if kb >= 2 * m:
                    biases.append((identb[:], [identb, msb], msb[:, kb - 2 * m, :]))
                pair(KT[0][:, kb * 128:(kb + 1) * 128], KT[0], qb, biases, V1[0][:, kb, :], V1[0], os_b, 65,
                     kb == 0, kb == 2 * m + 1)
            for br, (ob_, wdt) in enumerate(((oc_b, 193), (os_b, 65), (ow_b, 65))):
                if br > 0:
                    bk = ob_[0][0]
                    P.op("dve", lambda e, bk=bk, br=br: e.tensor_scalar(
                        out=rec[:, br, :], in0=bk[:, 0:260].rearrange("p (h c) -> p h c", h=4)[:, :, 64],
                        scalar1=1e-30, scalar2=None, op0=ALU.max), reads=[bk], writes=[rec])
                    P.op("dve", lambda e, br=br: e.reciprocal(out=rec[:, br, :], in_=rec[:, br, :]), reads=[rec], writes=[rec])
                P.op("dve", lambda e, br=br, m=m: e.tensor_tensor(
                    out=rec[:, br, :], in0=rec[:, br, :], in1=gl[:, m, :].rearrange("p (h b) -> p h b", h=4)[:, :, br],
                    op=ALU.mult), reads=[rec, gl], writes=[rec])
            for br, ob_ in enumerate((oc_b, os_b, ow_b)):
                dst = acc if br == 0 else tmp
                if br == 0:
                    for hp in range(2):
                        bk = ob_[2 * hp][0]
                        P.op("dve", lambda e, bk=bk, hp=hp, dst=dst: e.tensor_tensor(
                            out=dst[:, 2 * hp:2 * hp + 2, :],
                            in0=bk[:, 0:386].rearrange("p (h c) -> p h c", h=2)[:, :, 0:64],
                            in1=rec[:, 0, 2 * hp:2 * hp + 2].unsqueeze(2).to_broadcast([128, 2, 64]), op=ALU.mult),
                            reads=[bk, rec], writes=[dst])
                else:
                    bk = ob_[0][0]
                    P.op("dve", lambda e, bk=bk, br=br, dst=dst: e.tensor_tensor(
                        out=dst[:], in0=bk[:, 0:260].rearrange("p (h c) -> p h c", h=4)[:, :, 0:64],
                        in1=rec[:, br, :].unsqueeze(2).to_broadcast([128, 4, 64]), op=ALU.mult),
                        reads=[bk, rec], writes=[dst])
                    P.op("dve", lambda e: e.tensor_tensor(out=acc[:], in0=acc[:], in1=tmp[:], op=ALU.add),
                         reads=[acc, tmp], writes=[acc])
            P.op("dve", lambda e, g_=g_, yy=yy: e.tensor_tensor(
                out=yy[:], in0=acc[:].rearrange("p h d -> p (h d)"), in1=g_[:], op=ALU.mult),
                reads=[acc, g_], writes=[yy])
            P.dma("sp", lambda e, m=m, yy=yy: e.dma_start(out=y[:, m, :], in_=yy[:]), yy, reads=[yy], is_out=True)
        P.emit()
    return nc


def nsa_inputs(proj, p, l):
    C = nsa_consts()
    maps = []
    nkv = proj[:, :, OFF["nkv"]:OFF["nkv"] + 768].reshape(BATCH, T, 3, 2, 2, 64)
    for c in range(NCORES):
        b, rem = divmod(c, 4)
        g, par = divmod(rem, 2)
        blks = np.arange(NQB) * 2 + par
        tok = (blks[:, None] * 128 + np.arange(128)[None, :]).reshape(-1)
        kT4 = np.stack([nkv[b, :, 0, 0, g].T, nkv[b, :, 0, 1, g].T, nkv[b, :, 1, 0, g].T, nkv[b, :, 2, 0, g].T], 0)
        tokmaj = lambda a: a.reshape(NTILE, 128, -1).transpose(1, 0, 2)
        vtok2 = np.stack([tokmaj(nkv[b, :, 1, 1, g]), tokmaj(nkv[b, :, 2, 1, g])], 0)
        q = proj[b, tok, OFF["nq"] + g * 256:OFF["nq"] + (g + 1) * 256].reshape(-1, 4, 64)
        qTd = q.transpose(2, 1, 0)
        glg = proj[b, tok, OFF["ngl"] + g * 12:OFF["ngl"] + (g + 1) * 12].reshape(NQB, 128, 12).transpose(1, 0, 2)
        gate = proj[b, tok, OFF["ng"] + g * 256:OFF["ng"] + (g + 1) * 256].reshape(NQB, 128, 256).transpose(1, 0, 2)
        cmpm = C["cmpm"][:, par::2, :]
        W12 = C["W12"][:, :, (2 - 2 * par):(2 - 2 * par) + 253]
        f = lambda a: np.ascontiguousarray(a, dtype=np.float32)
        maps.append({
            "kT4": f(kT4), "vtok2": f(vtok2), "qTd": f(qTd), "gld": f(glg),
            "gbd": f(np.broadcast_to(p["nsa_gate_b"][l][None, g * 12:(g + 1) * 12], (128, 12))),
            "gated": f(gate), "qn": f(p["nsa_qn"][l][:, None]), "kn": f(p["nsa_kn"][l].T),
            "posT": f(p["nsa_cmp_pos"][l].transpose(2, 0, 1)),
            "w1d": f(p["nsa_cmp_w1"][l].reshape(2, 32, 64, 256).transpose(0, 2, 1, 3)),
            "b1d": f(p["nsa_cmp_b1"][l].reshape(2, 2, 128).transpose(2, 0, 1)),
            "w2d": f(p["nsa_cmp_w2"][l].reshape(2, 2, 128, 64).transpose(2, 0, 1, 3)),
            "b2k": f(p["nsa_cmp_b2"][l][0][:, None]),
            "b2v": f(np.broadcast_to(p["nsa_cmp_b2"][l][1][None, :], (128, 64))),
            "ones64": C["ones64"], "E": C["E"], "mw": f(C["mw"][par]), "ms": f(C["ms"][par]), "ident": C["ident"],
            "cmpm": f(cmpm), "ovl": C["ovl"], "W12": f(W12),
        })
    return maps


def nsa_gather(res):
    out = np.zeros((BATCH, T, 512), np.float32)
    for c in range(NCORES):
        b, rem = divmod(c, 4)
        g, par = divmod(rem, 2)
        yc = res[c]["y"].transpose(1, 0, 2)
        o = out[b].reshape(NTILE, 128, 512)
        o[par::2, :, g * 256:(g + 1) * 256] = yc
    return out


_CACHE = {}


def _prog(name, fn):
    if name not in _CACHE:
        _CACHE[name] = fn()
    return _CACHE[name]


def _x_to_T(xc):
    return np.ascontiguousarray(xc.T.reshape(KC, 128, -1).transpose(1, 0, 2))


def _fm(a, nchunk):
    af = a.reshape(BATCH * T, nchunk, 128)
    return [np.ascontiguousarray(af[c * NT:(c + 1) * NT].transpose(2, 1, 0)) for c in range(NCORES)]


def _win_layout(w):
    wp = np.zeros((D_MODEL, D_IN_PAD), np.float32)
    wp[:, :D_IN] = w
    return np.ascontiguousarray(wp.reshape(KC, 128, NFC, 128).transpose(2, 1, 0, 3))


def kernel(**p):
    p = {k: np.asarray(v, dtype=np.float32) for k, v in p.items()}
    x = p["x"]
    xT = [_x_to_T(x.reshape(-1, D_MODEL)[c * NT:(c + 1) * NT]) for c in range(NCORES)]
    proj = None
    mixers = None
    for l in range(DEPTH + 1):
        do_out, do_in = l > 0, l < DEPTH
        maps = [{"xT": xT[c]} for c in range(NCORES)]
        if do_out:
            lo = l - 1
            y_gla, y_ssm, y_nsa = mixers
            mix = np.concatenate([y_gla, np.zeros_like(y_ssm), y_nsa], -1)
            mixT = _fm(mix, 8)
            yssmT = _fm(y_ssm, 2)
            uT = _fm(proj[:, :, OFF["su"]:OFF["su"] + 256], 2)
            sgT = _fm(proj[:, :, OFF["sg"]:OFF["sg"] + 256], 2)
            wout = np.ascontiguousarray(p["w_out"][lo].reshape(KC, 128, D_MODEL).transpose(1, 0, 2))
            dsk = np.ascontiguousarray(p["s5_d"][lo].reshape(2, 128).T)
            gluw = np.ascontiguousarray(p["s5_glu_w"][lo].reshape(2, 128, 512).transpose(1, 0, 2))
            glub = np.ascontiguousarray(p["s5_glu_b"][lo].reshape(4, 128).T)
            for c in range(NCORES):
                maps[c].update({"mixT": mixT[c], "wout": wout, "yssmT": yssmT[c], "uT": uT[c], "sgT": sgT[c],
                                "dsk": dsk, "gluw": gluw, "glub": glub})
        if do_in:
            win = _win_layout(p["w_in"][l])
            gin = np.ascontiguousarray(p["norm_g"][l].reshape(KC, 128).T)
            for c in range(NCORES):
                maps[c].update({"win": win, "gin": gin})
        res = _run(_prog(f"op{int(do_out)}{int(do_in)}", lambda: build_op(do_out, do_in)), maps)
        if do_out:
            xT = [res[c]["xoT"] for c in range(NCORES)]
        if not do_in:
            break
        proj = np.concatenate([res[c]["projT"].reshape(D_IN_PAD, NT)[:D_IN].T for c in range(NCORES)], 0)
        proj = proj.reshape(BATCH, T, D_IN)
        y_gla = gla_gather(_run(_prog("gla", build_gla), gla_inputs(proj, p, l)))
        y_ssm = s5_gather(_run(_prog("s5", build_s5), s5_inputs(proj, p, l)))
        y_nsa = nsa_gather(_run(_prog("nsa", build_nsa), nsa_inputs(proj, p, l)))
        mixers = (y_gla, y_ssm, y_nsa)
    out = np.concatenate([xT[c].transpose(2, 1, 0).reshape(NT, D_MODEL) for c in range(NCORES)], 0)
    return out.reshape(BATCH, T, D_MODEL).astype(np.float32)
```

```python
from contextlib import ExitStack
import numpy as np
import concourse.bass as bass
import concourse.mybir as mybir
from concourse.bass_utils import run_bass_kernel_spmd

F32 = mybir.dt.float32
BF16 = mybir.dt.bfloat16
AF = mybir.ActivationFunctionType
ALU = mybir.AluOpType
AX = mybir.AxisListType

NCORES = 8
D_MODEL = 1024
BATCH = 2
SEQ = 8192
DEPTH = 4
D_IN = 3112
D_IN_PAD = 3200
RMS_EPS = 1e-6


class LT:
    def __init__(self, ap, name=""):
        self.ap = ap
        self.name = name
        self.w = None
        self.r = []
        self.dsem = None
        self.dcnt = 0

    def __getitem__(self, idx):
        return self.ap[idx]


class Prog:
    ENGS = ("pe", "dve", "act", "pool", "sp")

    def __init__(self, nc, stack):
        self.nc = nc
        self.stack = stack
        self.q = {e: [] for e in self.ENGS}
        self.sem = {e: stack.enter_context(nc.semaphore("s_" + e)) for e in self.ENGS}
        self.cnt = {e: 0 for e in self.ENGS}
        self.seen = {e: {} for e in self.ENGS}
        self.out_events = []
        self.nsem = 0
        self.ntile = 0

    def sb(self, shape, dt, name=None):
        self.ntile += 1
        name = "sb_" + (name or f"t{self.ntile}")
        t = self.stack.enter_context(self.nc.sbuf_tensor(name, list(shape), dt))
        return LT(t, name)

    def ps(self, shape, dt=F32, name=None):
        self.ntile += 1
        name = "ps_" + (name or f"p{self.ntile}")
        t = self.stack.enter_context(self.nc.psum_tensor(name, list(shape), dt))
        return LT(t, name)

    def sub(self, lt, idx, name=""):
        return LT(lt.ap[idx], name or lt.name)

    def _dsem(self, t):
        if t.dsem is None:
            self.nsem += 1
            t.dsem = self.stack.enter_context(self.nc.semaphore(f"d{self.nsem}"))
        return t.dsem

    def _deps(self, eng, reads, writes):
        evs = []
        for t in reads:
            if t.w is not None:
                evs.append(t.w)
        for t in writes:
            if t.w is not None:
                evs.append(t.w)
            evs.extend(t.r)
        agg = {}
        for sem, val, tile, src in evs:
            if tile is not None:
                val = max(val, tile.dcnt * 16)
            if src == "pe" and eng == "pe":
                continue
            k = id(sem)
            if k not in agg or agg[k][1] < val:
                agg[k] = (sem, val)
        waits = []
        seen = self.seen[eng]
        for k, (sem, val) in agg.items():
            if seen.get(k, 0) >= val:
                continue
            seen[k] = val
            waits.append((sem, val))
        return waits

    def op(self, eng, fn, reads=(), writes=()):
        waits = self._deps(eng, reads, writes)
        self.cnt[eng] += 1
        ev = (self.sem[eng], self.cnt[eng], None, eng)
        self.q[eng].append((waits, fn, (self.sem[eng], 1)))
        for t in reads:
            t.r.append(ev)
        for t in writes:
            t.w = ev
            t.r = []
        return ev

    def dma(self, queue, fn, sbt, reads=(), writes=(), is_out=False):
        waits = self._deps(queue, reads, writes)
        sem = self._dsem(sbt)
        sbt.dcnt += 1
        ev = (sem, sbt.dcnt * 16, sbt, "dma")
        self.q[queue].append((waits, fn, (sem, 16)))
        for t in reads:
            t.r.append(ev)
        for t in writes:
            t.w = ev
            t.r = []
        if is_out:
            self.out_events.append(ev)
        return ev

    def emit(self):
        nc = self.nc
        fin = []
        seen = {}
        for sem, val, tile, _ in self.out_events:
            v = tile.dcnt * 16
            seen[id(sem)] = (sem, v)
        fin = list(seen.values())
        engmap = {"pe": "tensor", "dve": "vector", "act": "scalar", "pool": "gpsimd", "sp": "sync"}
        with nc.Block() as block:
            for e in self.ENGS:
                items = self.q[e]
                extra = fin if e == "sp" else []

                def body(engine, items=items, extra=extra):
                    for waits, fn, inc in items:
                        for sem, val in waits:
                            engine.wait_ge(sem, val)
                        ins = fn(engine)
                        ins.then_inc(inc[0], inc[1])
                    for sem, val in extra:
                        engine.wait_ge(sem, val)

                if items or extra:
                    getattr(block, engmap[e])(body)


def _run(nc, in_maps):
    res = run_bass_kernel_spmd(nc, in_maps, core_ids=list(range(NCORES)))
    return res.results


NT = 2048
KC = 8
NFC = D_IN_PAD // 128
NF32 = 2


def build_op(do_out, do_in):
    nc = bass.Bass("TRN2", target_bir_lowering=False)
    din = lambda name, shape: nc.dram_tensor(name, list(shape), F32, kind="ExternalInput").ap()
    xT = din("xT", [128, KC, NT])
    if do_out:
        mixT = din("mixT", [128, KC, NT])
        wout = din("wout", [128, KC, D_MODEL])
        yssmT = din("yssmT", [128, 2, NT])
        uT = din("uT", [128, 2, NT])
        sgT = din("sgT", [128, 2, NT])
        dskd = din("dsk", [128, 2])
        gluwd = din("gluw", [128, 2, 512])
        glubd = din("glub", [128, 4])
        xoT = nc.dram_tensor("xoT", [128, KC, NT], F32, kind="ExternalOutput").ap()
    if do_in:
        win = din("win", [NFC, 128, KC, 128])
        gin = din("gin", [128, KC])
        projT = nc.dram_tensor("projT", [NFC, 128, NT], F32, kind="ExternalOutput").ap()
    with ExitStack() as st:
        P = Prog(nc, st)
        xs = [P.sb([128, NT], F32, f"x{k}") for k in range(KC)]
        actb = [P.sb([128, NT], BF16, f"ab{k}") for k in range(KC)]
        Fb = [P.sb([128, NT], F32, f"F{i}") for i in range(4)]
        wstg = [P.sb([128, D_MODEL], F32, f"wstg{i}") for i in range(2)]
        banks = [P.ps([128, 512], F32, f"bk{i}") for i in range(8)]
        for k in range(KC):
            P.dma("sp", lambda e, k=k: e.dma_start(out=xs[k][:], in_=xT[:, k, :]), xs[k], writes=[xs[k]])
        nb = 0
        if do_out:
            wob = [P.sb([128, D_MODEL], BF16, f"wob{k}") for k in range(KC)]
            ge = [P.sb([128, NT], BF16, f"ge{k}") for k in range(2)]
            dsk = P.sb([128, 2], F32, "dsk")
            glub = P.sb([128, 4], F32, "glub")
            gluwf = P.sb([128, 2, 512], F32, "gluwf")
            gluwb = P.sb([128, 2, 512], BF16, "gluwb")
            P.dma("sp", lambda e: e.dma_start(out=dsk[:], in_=dskd), dsk, writes=[dsk])
            P.dma("sp", lambda e: e.dma_start(out=glub[:], in_=glubd), glub, writes=[glub])
            P.dma("sp", lambda e: e.dma_start(out=gluwf[:], in_=gluwd), gluwf, writes=[gluwf])
            P.op("act", lambda e: e.copy(out=gluwb[:], in_=gluwf[:]), reads=[gluwf], writes=[gluwb])
            for k in range(KC):
                w = wstg[k % 2]
                P.dma("sp", lambda e, k=k, w=w: e.dma_start(out=w[:], in_=wout[:, k, :]), w, writes=[w])
                P.op("act", lambda e, k=k, w=w: e.copy(out=wob[k][:], in_=w[:]), reads=[w], writes=[wob[k]])
            for i, k in enumerate((0, 1, 4, 5, 6, 7)):
                s_ = Fb[i % 2]
                P.dma("pool", lambda e, k=k, s_=s_: e.dma_start(out=s_[:], in_=mixT[:, k, :]), s_, writes=[s_])
                P.op("dve", lambda e, k=k, s_=s_: e.tensor_copy(out=actb[k][:], in_=s_[:]), reads=[s_], writes=[actb[k]])
            F0, F1, F2, F3 = Fb
            for kc in range(2):
                P.dma("pool", lambda e, kc=kc: e.dma_start(out=F0[:], in_=yssmT[:, kc, :]), F0, writes=[F0])
                P.dma("sp", lambda e, kc=kc: e.dma_start(out=F1[:], in_=uT[:, kc, :]), F1, writes=[F1])
                P.op("dve", lambda e, kc=kc: e.scalar_tensor_tensor(
                    out=F0[:], in0=F1[:], scalar=dsk[:, kc:kc + 1], in1=F0[:], op0=ALU.mult, op1=ALU.add),
                    reads=[F0, F1, dsk], writes=[F0])
                P.op("act", lambda e: e.activation(out=F2[:], in_=F0[:], func=AF.Square), reads=[F0], writes=[F2])
                P.op("dve", lambda e: e.tensor_scalar(out=F2[:], in0=F2[:], scalar1=0.044715, scalar2=1.0, op0=ALU.mult,
                                                      op1=ALU.add), reads=[F2], writes=[F2])
                P.op("dve", lambda e: e.tensor_tensor(out=F2[:], in0=F2[:], in1=F0[:], op=ALU.mult), reads=[F2, F0], writes=[F2])
                P.op("act", lambda e: e.activation(out=F2[:], in_=F2[:], func=AF.Tanh, scale=0.7978845608028654),
                     reads=[F2], writes=[F2])
                P.op("dve", lambda e: e.tensor_scalar(out=F2[:], in0=F2[:], scalar1=0.5, scalar2=0.5, op0=ALU.mult,
                                                      op1=ALU.add), reads=[F2], writes=[F2])
                P.op("dve", lambda e, kc=kc: e.tensor_tensor(out=ge[kc][:], in0=F2[:], in1=F0[:], op=ALU.mult),
                     reads=[F2, F0], writes=[ge[kc]])
            for fcp in range(2):
                P.dma("pool", lambda e, fcp=fcp: e.dma_start(out=F1[:], in_=sgT[:, fcp, :]), F1, writes=[F1])
                P.op("act", lambda e: e.activation(out=F1[:], in_=F1[:], func=AF.Silu), reads=[F1], writes=[F1])
                for tt in range(4):
                    tsl = slice(tt * 512, (tt + 1) * 512)
                    bA, bB = banks[nb % 8], banks[(nb + 1) % 8]
                    nb += 2
                    for kc in range(2):
                        P.op("pe", lambda e, kc=kc, fcp=fcp, tsl=tsl, bA=bA: e.matmul(
                            bA[:], gluwb[:, kc, fcp * 128:(fcp + 1) * 128], ge[kc][:, tsl], start=(kc == 0), stop=(kc == 1)),
                            reads=[gluwb, ge[kc]], writes=[bA])
                    for kc in range(2):
                        P.op("pe", lambda e, kc=kc, fcp=fcp, tsl=tsl, bB=bB: e.matmul(
                            bB[:], gluwb[:, kc, (fcp + 2) * 128:(fcp + 3) * 128], ge[kc][:, tsl], start=(kc == 0),
                            stop=(kc == 1)), reads=[gluwb, ge[kc]], writes=[bB])
                    P.op("act", lambda e, fcp=fcp, tsl=tsl, bB=bB: e.activation(
                        out=F3[:, tsl], in_=bB[:], func=AF.Sigmoid, bias=glub[:, fcp + 2:fcp + 3]),
                        reads=[bB, glub], writes=[F3])
                    P.op("dve", lambda e, fcp=fcp, tsl=tsl, bA=bA: e.scalar_tensor_tensor(
                        out=F3[:, tsl], in0=bA[:], scalar=glub[:, fcp:fcp + 1], in1=F3[:, tsl], op0=ALU.add, op1=ALU.mult),
                        reads=[bA, glub, F3], writes=[F3])
                P.op("dve", lambda e, fcp=fcp: e.tensor_tensor(out=actb[2 + fcp][:], in0=F3[:], in1=F1[:], op=ALU.mult),
                     reads=[F3, F1], writes=[actb[2 + fcp]])
            for fo in range(KC):
                for tt in range(NT // 512):
                    bk = banks[nb % 8]
                    nb += 1
                    for k in range(KC):
                        P.op("pe", lambda e, k=k, fo=fo, tt=tt, bk=bk: e.matmul(
                            bk[:], wob[k][:, fo * 128:(fo + 1) * 128], actb[k][:, tt * 512:(tt + 1) * 512],
                            start=(k == 0), stop=(k == KC - 1)),
                            reads=[wob[k], actb[k]], writes=[bk])
                    P.op("dve", lambda e, fo=fo, tt=tt, bk=bk: e.tensor_tensor(
                        out=xs[fo][:, tt * 512:(tt + 1) * 512], in0=bk[:], in1=xs[fo][:, tt * 512:(tt + 1) * 512],
                        op=ALU.add), reads=[bk, xs[fo]], writes=[xs[fo]])
                P.dma("sp", lambda e, fo=fo: e.dma_start(out=xoT[:, fo, :], in_=xs[fo][:]), xs[fo],
                      reads=[xs[fo]], is_out=True)
        if do_in:
            gt = P.sb([128, KC], F32, "gt")
            P.dma("sp", lambda e: e.dma_start(out=gt[:], in_=gin), gt, writes=[gt])
            ones = P.sb([128, 128], F32, "ones")
            P.op("pool", lambda e: e.memset(ones[:], 1.0), writes=[ones])
            rstd = P.sb([128, NT], F32, "rstd")
            sq = Fb[0:2]
            sbk = banks[0:4]
            for k in range(KC):
                s = sq[k % 2]
                P.op("act", lambda e, k=k, s=s: e.activation(out=s[:], in_=xs[k][:], func=AF.Square),
                     reads=[xs[k]], writes=[s])
                for tt in range(4):
                    P.op("pe", lambda e, k=k, s=s, tt=tt: e.matmul(
                        sbk[tt][:], ones[:], s[:, tt * 512:(tt + 1) * 512], start=(k == 0), stop=(k == KC - 1)),
                        reads=[ones, s], writes=[sbk[tt]])
                P.op("dve", lambda e, k=k: e.tensor_scalar(
                    out=actb[k][:], in0=xs[k][:], scalar1=gt[:, k:k + 1], scalar2=None, op0=ALU.mult),
                    reads=[xs[k], gt], writes=[actb[k]])
            epst = P.sb([128, 1], F32, "eps")
            P.op("pool", lambda e: e.memset(epst[:], RMS_EPS), writes=[epst])
            for tt in range(4):
                P.op("act", lambda e, tt=tt: e.activation(
                    out=rstd[:, tt * 512:(tt + 1) * 512], in_=sbk[tt][:], func=AF.Sqrt,
                    scale=1.0 / D_MODEL, bias=epst[:]), reads=[sbk[tt], epst], writes=[rstd])
            P.op("dve", lambda e: e.reciprocal(out=rstd[:], in_=rstd[:]), reads=[rstd], writes=[rstd])
            wb = [P.sb([128, KC, 128], BF16, f"wb{i}") for i in range(2)]
            ot = Fb[2:4]
            nb = 4
            for fc in range(NFC):
                w = wstg[fc % 2]
                wbb = wb[fc % 2]
                o = ot[fc % 2]
                P.dma("pool", lambda e, fc=fc, w=w: e.dma_start(
                    out=w[:], in_=win[fc].rearrange("p k f -> p (k f)")), w, writes=[w])
                f32c = fc < NF32
                w3 = w[:].rearrange("p (k f) -> p k f", k=KC)
                if f32c:
                    P.op("dve", lambda e, w3=w3: e.tensor_tensor(
                        out=w3, in0=w3, in1=gt[:].unsqueeze(2).to_broadcast([128, KC, 128]), op=ALU.mult),
                        reads=[w, gt], writes=[w])
                else:
                    P.op("act", lambda e, w=w, wbb=wbb: e.copy(out=wbb[:].rearrange("p k f -> p (k f)"), in_=w[:]),
                         reads=[w], writes=[wbb])
                for tt in range(4):
                    bk = banks[4 + nb % 4]
                    nb += 1
                    for k in range(KC):
                        if f32c:
                            P.op("pe", lambda e, k=k, tt=tt, bk=bk, w3=w3: e.matmul(
                                bk[:], w3[:, k, :], xs[k][:, tt * 512:(tt + 1) * 512],
                                start=(k == 0), stop=(k == KC - 1)), reads=[w, xs[k]], writes=[bk])
                        else:
                            P.op("pe", lambda e, k=k, tt=tt, bk=bk, wbb=wbb: e.matmul(
                                bk[:], wbb[:, k, :], actb[k][:, tt * 512:(tt + 1) * 512],
                                start=(k == 0), stop=(k == KC - 1)), reads=[wbb, actb[k]], writes=[bk])
                    P.op("dve", lambda e, tt=tt, bk=bk, o=o: e.tensor_tensor(
                        out=o[:, tt * 512:(tt + 1) * 512], in0=bk[:], in1=rstd[:, tt * 512:(tt + 1) * 512],
                        op=ALU.mult), reads=[bk, rstd], writes=[o])
                P.dma("sp", lambda e, fc=fc, o=o: e.dma_start(out=projT[fc], in_=o[:]), o, reads=[o], is_out=True)
        P.emit()
    return nc


T = SEQ
NTILE = T // 128
NCHUNK = T // 128
GLA_G = 4


def gla_consts():
    j = np.arange(128)[:, None]
    i = np.arange(128)[None, :]
    cm = np.zeros((128, 3, 128), np.float32)
    cm[:, 0, :] = (j <= i)
    cm[:, 1, :] = (j > i)
    cm[:, 2, 0:4] = 1.0
    return cm


def build_gla():
    nc = bass.Bass("TRN2", target_bir_lowering=False)
    qT = nc.dram_tensor("qT", [32, T], F32, kind="ExternalInput").ap()
    kT = nc.dram_tensor("kT", [32, T], F32, kind="ExternalInput").ap()
    lrT1 = nc.dram_tensor("lrT1", [17, T], F32, kind="ExternalInput").ap()
    ktok = nc.dram_tensor("ktok", [128, NTILE, 32], F32, kind="ExternalInput").ap()
    vtok = nc.dram_tensor("vtok", [128, NTILE, 64], F32, kind="ExternalInput").ap()
    gtok = nc.dram_tensor("gtok", [128, NTILE, 64], F32, kind="ExternalInput").ap()
    w2b = nc.dram_tensor("w2b", [17, 32], F32, kind="ExternalInput").ap()
    onb = nc.dram_tensor("onb", [128, 64], F32, kind="ExternalInput").ap()
    cm = nc.dram_tensor("cm", [128, 3, 128], F32, kind="ExternalInput").ap()
    y = nc.dram_tensor("y", [128, NTILE, 64], F32, kind="ExternalOutput").ap()
    NG = NTILE // GLA_G
    with ExitStack() as st:
        P = Prog(nc, st)
        cmt = P.sb([128, 3, 128], F32, "cmt")
        w2t = P.sb([17, 32], F32, "w2t")
        onbt = P.sb([128, 64], F32, "onbt")
        ktokt = P.sb([128, NTILE, 32], F32, "ktokt")
        vb = P.sb([128, NTILE, 64], F32, "vb")
        qeT = P.sb([32, T], F32, "qeT")
        scm = P.sb([128, NTILE, 128], F32, "scm")
        kv_all = P.sb([32, NCHUNK, 64], F32, "kv_all")
        S_bf = P.sb([32, NCHUNK + 1, 64], F32, "S_bf")
        dec_all = P.sb([32, NCHUNK], F32, "dec_all")
        epst = P.sb([128, 1], F32, "eps")
        qTg = [P.sb([32, 512], F32, f"qTg{i}") for i in range(2)]
        kTg = [P.sb([32, 512], F32, f"kTg{i}") for i in range(2)]
        lrg = [P.sb([17, 512], F32, f"lrg{i}") for i in range(2)]
        e1 = P.sb([128, 128], F32, "e1")
        lt = P.sb([128, 128], F32, "lt")
        eqTt = P.sb([32, 512], F32, "eqTt")
        ekTt = P.sb([32, 512], F32, "ekTt")
        ekd = P.sb([128, 128], F32, "ekd")
        keT = P.sb([32, 512], F32, "keT")
        kd = P.sb([128, 4, 32], F32, "kd")
        osb = P.sb([128, 4, 64], F32, "osb")
        sq = P.sb([128, 4, 64], F32, "sq")
        ss = P.sb([128, 4], F32, "ss")
        gg = [P.sb([128, 4, 64], F32, f"gg{i}") for i in range(2)]
        yt = [P.sb([128, 4, 64], F32, f"yt{i}") for i in range(2)]
        bz = P.ps([128, 512], F32, "bz")
        zps = P.sub(bz, (slice(None), slice(0, 128)), "zps")
        sups = P.sub(bz, (slice(None), slice(128, 256)), "sups")
        bT = P.ps([32, 512], F32, "bT")
        bL = P.ps([32, 512], F32, "bL")
        bs = P.ps([128, 512], F32, "bs")
        bkv = P.ps([32, 512], F32, "bkv")
        bo = [P.ps([128, 512], F32, f"bo{i}") for i in range(2)]

        P.dma("sp", lambda e: e.dma_start(out=cmt[:], in_=cm), cmt, writes=[cmt])
        P.dma("sp", lambda e: e.dma_start(out=w2t[:], in_=w2b), w2t, writes=[w2t])
        P.dma("sp", lambda e: e.dma_start(out=onbt[:], in_=onb), onbt, writes=[onbt])
        P.dma("sp", lambda e: e.dma_start(out=ktokt[:], in_=ktok), ktokt, writes=[ktokt])
        P.op("pool", lambda e: e.memset(epst[:], RMS_EPS), writes=[epst])
        for i in range(4):
            P.dma("pool", lambda e, i=i: e.dma_start(out=vb[:, i * 16:(i + 1) * 16, :], in_=vtok[:, i * 16:(i + 1) * 16, :]),
                  vb, writes=[vb])
        LTm = cmt[:, 0, :]
        UTm = cmt[:, 1, :]
        BDs = cmt[:, 2, 0:1]
        for g in range(NG):
            qg, kg, lg = qTg[g % 2], kTg[g % 2], lrg[g % 2]
            tok = slice(g * 512, (g + 1) * 512)
            P.dma("sp", lambda e, qg=qg, tok=tok: e.dma_start(out=qg[:], in_=qT[:, tok]), qg, writes=[qg])
            P.dma("sp", lambda e, kg=kg, tok=tok: e.dma_start(out=kg[:], in_=kT[:, tok]), kg, writes=[kg])
            P.dma("sp", lambda e, lg=lg, tok=tok: e.dma_start(out=lg[:], in_=lrT1[:, tok]), lg, writes=[lg])
            for t in range(4):
                P.op("pe", lambda e, t=t, lg=lg: e.matmul(zps[:, t * 32:(t + 1) * 32], lg[:, t * 128:(t + 1) * 128],
                                                        w2t[:], start=True, stop=True), reads=[lg, w2t], writes=[zps])
            P.op("act", lambda e: e.activation(out=e1[:], in_=zps[:], func=AF.Exp, scale=-1.0), reads=[zps], writes=[e1])
            P.op("act", lambda e: e.activation(out=lt[:], in_=e1[:], func=AF.Ln, bias=1.0), reads=[e1], writes=[lt])
            for t in range(4):
                lsl = lt[:, t * 32:(t + 1) * 32]
                P.op("pe", lambda e, t=t, lsl=lsl: e.matmul(sups[:, t * 32:(t + 1) * 32], UTm, lsl, start=True, stop=True),
                     reads=[cmt, lt], writes=[sups])
                P.op("pe", lambda e, t=t, lsl=lsl: e.matmul(bT[:, t * 128:(t + 1) * 128], lsl, LTm, start=True, stop=True),
                     reads=[cmt, lt], writes=[bT])
                P.op("pe", lambda e, t=t, lsl=lsl: e.matmul(bL[:, t:t + 1], lsl, BDs, start=True, stop=True),
                     reads=[cmt, lt], writes=[bL])
            P.op("act", lambda e: e.activation(out=eqTt[:], in_=bT[:], func=AF.Exp, scale=-1.0 / 16), reads=[bT], writes=[eqTt])
            P.op("act", lambda e: e.activation(out=ekTt[:], in_=bT[:], func=AF.Exp, scale=1.0 / 16), reads=[bT], writes=[ekTt])
            P.op("act", lambda e: e.activation(out=ekd[:], in_=sups[:], func=AF.Exp, scale=-1.0 / 16), reads=[sups], writes=[ekd])
            P.op("act", lambda e, g=g: e.activation(out=dec_all[:, 4 * g:4 * g + 4], in_=bL[:, 0:4], func=AF.Exp,
                                                   scale=-1.0 / 16), reads=[bL], writes=[dec_all])
            P.op("dve", lambda e, qg=qg, tok=tok: e.scalar_tensor_tensor(
                out=qeT[:, tok], in0=qg[:], scalar=32 ** -0.5, in1=eqTt[:], op0=ALU.mult, op1=ALU.mult),
                reads=[qg, eqTt], writes=[qeT])
            P.op("dve", lambda e, kg=kg: e.tensor_tensor(out=keT[:], in0=kg[:], in1=ekTt[:], op=ALU.mult),
                 reads=[kg, ekTt], writes=[keT])
            P.op("dve", lambda e, g=g: e.tensor_tensor(
                out=kd[:], in0=ktokt[:, 4 * g:4 * g + 4, :], in1=ekd[:].rearrange("p (t d) -> p t d", t=4), op=ALU.mult),
                reads=[ktokt, ekd], writes=[kd])
            for t in range(4):
                P.op("pe", lambda e, t=t, g=g: e.matmul(
                    bs[:, t * 128:(t + 1) * 128], keT[:, t * 128:(t + 1) * 128],
                    qeT[:, (4 * g + t) * 128:(4 * g + t + 1) * 128], start=True, stop=True),
                    reads=[keT, qeT], writes=[bs])
            for t in range(4):
                P.op("pe", lambda e, t=t, g=g: e.matmul(
                    bkv[:, t * 64:(t + 1) * 64], kd[:, t, :], vb[:, 4 * g + t, :], start=True, stop=True),
                    reads=[kd, vb], writes=[bkv])
            P.op("dve", lambda e, g=g: e.tensor_tensor(
                out=scm[:, 4 * g:4 * g + 4, :], in0=bs[:].rearrange("p (t i) -> p t i", t=4),
                in1=cmt[:, 0:1, :].to_broadcast([128, 4, 128]), op=ALU.mult), reads=[bs, cmt], writes=[scm])
            P.op("act", lambda e, g=g: e.copy(out=kv_all[:, 4 * g:4 * g + 4, :],
                                             in_=bkv[:, 0:256].rearrange("p (c e) -> p c e", c=4)),
                 reads=[bkv], writes=[kv_all])
        P.op("pool", lambda e: e.memset(S_bf[:, 0, :], 0.0), writes=[S_bf])
        for ee in range(64):
            P.op("dve", lambda e, ee=ee: e.tensor_tensor_scan(
                out=S_bf[:, 1:NCHUNK + 1, ee], data0=dec_all[:], data1=kv_all[:, :, ee], initial=0.0,
                op0=ALU.mult, op1=ALU.add), reads=[dec_all, kv_all], writes=[S_bf])
        for g in range(NG):
            b = bo[g % 2]
            gt_, yy = gg[g % 2], yt[g % 2]
            P.dma("sp", lambda e, g=g, gt_=gt_: e.dma_start(out=gt_[:], in_=gtok[:, 4 * g:4 * g + 4, :]), gt_, writes=[gt_])
            for t in range(4):
                P.op("pe", lambda e, t=t, g=g, b=b: e.matmul(
                    b[:, t * 64:(t + 1) * 64], scm[:, 4 * g + t, :], vb[:, 4 * g + t, :], start=True, stop=False),
                    reads=[scm, vb], writes=[b])
                P.op("pe", lambda e, t=t, g=g, b=b: e.matmul(
                    b[:, t * 64:(t + 1) * 64], qeT[:, (4 * g + t) * 128:(4 * g + t + 1) * 128],
                    S_bf[:, 4 * g + t, :], start=False, stop=True), reads=[qeT, S_bf], writes=[b])
            P.op("act", lambda e, b=b: e.copy(out=osb[:], in_=b[:, 0:256].rearrange("p (t e) -> p t e", t=4)),
                 reads=[b], writes=[osb])
            P.op("dve", lambda e: e.tensor_tensor(out=sq[:], in0=osb[:], in1=osb[:], op=ALU.mult), reads=[osb], writes=[sq])
            P.op("dve", lambda e: e.tensor_reduce(out=ss[:], in_=sq[:], axis=AX.X, op=ALU.add), reads=[sq], writes=[ss])
            P.op("act", lambda e: e.activation(out=ss[:], in_=ss[:], func=AF.Sqrt, scale=1.0 / 64, bias=epst[:]),
                 reads=[ss, epst], writes=[ss])
            P.op("dve", lambda e: e.reciprocal(out=ss[:], in_=ss[:]), reads=[ss], writes=[ss])
            P.op("act", lambda e, gt_=gt_: e.activation(out=gt_[:], in_=gt_[:], func=AF.Silu), reads=[gt_], writes=[gt_])
            P.op("dve", lambda e, gt_=gt_: e.tensor_tensor(
                out=gt_[:], in0=gt_[:], in1=onbt[:].unsqueeze(1).to_broadcast([128, 4, 64]), op=ALU.mult),
                reads=[gt_, onbt], writes=[gt_])
            P.op("dve", lambda e: e.tensor_tensor(
                out=osb[:], in0=osb[:], in1=ss[:].unsqueeze(2).to_broadcast([128, 4, 64]), op=ALU.mult),
                reads=[osb, ss], writes=[osb])
            P.op("dve", lambda e, gt_=gt_, yy=yy: e.tensor_tensor(out=yy[:], in0=osb[:], in1=gt_[:], op=ALU.mult),
                 reads=[osb, gt_], writes=[yy])
            P.dma("sp", lambda e, g=g, yy=yy: e.dma_start(out=y[:, 4 * g:4 * g + 4, :], in_=yy[:]), yy,
                  reads=[yy], is_out=True)
        P.emit()
    return nc


OFF = dict(gq=0, gk=128, gv=256, glr=512, gg=528, su=784, sg=1040, nq=1296, nkv=1808, ngl=2576, ng=2600)


def gla_inputs(proj, p, l):
    cm = gla_consts()
    maps = []
    for c in range(NCORES):
        b, h = divmod(c, 4)
        q = proj[b, :, OFF["gq"] + h * 32:OFF["gq"] + (h + 1) * 32]
        k = proj[b, :, OFF["gk"] + h * 32:OFF["gk"] + (h + 1) * 32]
        v = proj[b, :, OFF["gv"] + h * 64:OFF["gv"] + (h + 1) * 64]
        lr = proj[b, :, OFF["glr"]:OFF["glr"] + 16]
        gt_ = proj[b, :, OFF["gg"] + h * 64:OFF["gg"] + (h + 1) * 64]
        tokmaj = lambda a: np.ascontiguousarray(a.reshape(NTILE, 128, -1).transpose(1, 0, 2))
        maps.append({
            "qT": np.ascontiguousarray(q.T), "kT": np.ascontiguousarray(k.T),
            "lrT1": np.ascontiguousarray(np.concatenate([lr.T, np.ones((1, T), np.float32)], 0)),
            "ktok": tokmaj(k), "vtok": tokmaj(v), "gtok": tokmaj(gt_),
            "w2b": np.ascontiguousarray(np.concatenate(
                [p["gla_w2"][l][:, h * 32:(h + 1) * 32], p["gla_b2"][l][None, h * 32:(h + 1) * 32]], 0)),
            "onb": np.ascontiguousarray(np.broadcast_to(p["gla_onorm"][l][None, :], (128, 64))),
            "cm": cm,
        })
    return maps


def gla_gather(res):
    out = np.zeros((BATCH, T, 256), np.float32)
    for c in range(NCORES):
        b, h = divmod(c, 4)
        out[b, :, h * 64:(h + 1) * 64] = res[c]["y"].transpose(1, 0, 2).reshape(T, 64)
    return out


S5L = 64
S5N = T // S5L


def s5_consts():
    kio = np.broadcast_to(np.arange(65, dtype=np.float32)[None, :], (128, 65)).copy()
    r = np.arange(128)
    dmask = ((r[None, :] // 16) >= (r[:, None] // 16)).astype(np.float32)
    Wm = np.concatenate([np.zeros((128, 7 * 128), np.float32), dmask, np.ones((128, 7 * 128), np.float32)], 1)
    ident = np.eye(128, dtype=np.float32)
    return kio, Wm, ident


def _fact(n):
    f = 1.0
    for i in range(2, n + 1):
        f *= i
    return f


def build_s5():
    nc = bass.Bass("TRN2", target_bir_lowering=False)
    U = nc.dram_tensor("U", [2, 128, 8, 256], F32, kind="ExternalInput").ap()
    lam = nc.dram_tensor("lam", [128, 2], F32, kind="ExternalInput").ap()
    lst = nc.dram_tensor("lst", [128, 1], F32, kind="ExternalInput").ap()
    Bri = nc.dram_tensor("Bri", [128, 2, 16], F32, kind="ExternalInput").ap()
    Cri = nc.dram_tensor("Cri", [128, 2, 16], F32, kind="ExternalInput").ap()
    kio_d = nc.dram_tensor("kio", [128, 65], F32, kind="ExternalInput").ap()
    Wm_d = nc.dram_tensor("Wm", [128, 1920], F32, kind="ExternalInput").ap()
    id_d = nc.dram_tensor("ident", [128, 128], F32, kind="ExternalInput").ap()
    Y = nc.dram_tensor("Y", [2, 128, 8, 256], F32, kind="ExternalOutput").ap()
    with ExitStack() as st:
        P = Prog(nc, st)
        col = lambda name, w=1: P.sb([128, w], F32, name)

        def load(name, shape, src):
            t = P.sb(shape, F32, name)
            P.dma("sp", lambda e: e.dma_start(out=t[:], in_=src), t, writes=[t])
            return t

        lamt = load("lamt", [128, 2], lam)
        lstt = load("lstt", [128, 1], lst)
        Bt = load("Bt", [128, 2, 16], Bri)
        Ct = load("Ct", [128, 2, 16], Cri)
        kio = load("kiot", [128, 65], kio_d)
        Wm = load("Wmt", [128, 1920], Wm_d)
        ident = load("identt", [128, 128], id_d)
        Ut = []
        for gl in range(2):
            t = P.sb([128, 8, 256], F32, f"U{gl}")
            P.dma("pool", lambda e, t=t, gl=gl: e.dma_start(out=t[:], in_=U[gl]), t, writes=[t])
            Ut.append(t)

        def ts(out, in0, s1, op0, s2=None, op1=None, r=(), w=()):
            if op1 is None:
                P.op("dve", lambda e: e.tensor_scalar(out=out, in0=in0, scalar1=s1, scalar2=None, op0=op0), reads=r, writes=w)
            else:
                P.op("dve", lambda e: e.tensor_scalar(out=out, in0=in0, scalar1=s1, scalar2=s2, op0=op0, op1=op1),
                     reads=r, writes=w)

        def tt(out, a, b, op, r=(), w=()):
            P.op("dve", lambda e: e.tensor_tensor(out=out, in0=a, in1=b, op=op), reads=r, writes=w)

        def stt(out, in0, sc, in1, op0, op1, r=(), w=()):
            P.op("dve", lambda e: e.scalar_tensor_tensor(out=out, in0=in0, scalar=sc, in1=in1, op0=op0, op1=op1),
                 reads=r, writes=w)

        def horner(name, xs, coefs):
            acc = col(name)
            P.op("pool", lambda e: e.memset(acc[:], float(coefs[-1])), writes=[acc])
            for c in reversed(coefs[:-1]):
                ts(acc[:], acc[:], xs[:, 0:1], ALU.mult, float(c), ALU.add, r=[acc, xs], w=[acc])
            return acc

        y4 = col("y4")
        ts(y4[:], lstt[:], 0.25, ALU.mult, r=[lstt], w=[y4])
        dt = horner("dt", y4, [1.0 / _fact(k) for k in range(19)])
        tt(dt[:], dt[:], dt[:], ALU.mult, r=[dt], w=[dt])
        tt(dt[:], dt[:], dt[:], ALU.mult, r=[dt], w=[dt])
        lr = col("lr")
        ts(lr[:], lamt[:, 0:1], -1e-4, ALU.min, r=[lamt], w=[lr])
        li = col("li")
        ts(li[:], lamt[:, 1:2], 1.0, ALU.mult, r=[lamt], w=[li])
        xx = col("xx")
        tt(xx[:], lr[:], dt[:], ALU.mult, r=[lr, dt], w=[xx])
        negx = col("negx")
        ts(negx[:], xx[:], -1.0, ALU.mult, r=[xx], w=[negx])
        q = horner("q", xx, [1.0 / _fact(k + 1) for k in range(10)])
        em1 = col("em1")
        tt(em1[:], q[:], xx[:], ALU.mult, r=[q, xx], w=[em1])
        mag = col("mag")
        ts(mag[:], em1[:], 1.0, ALU.add, r=[em1], w=[mag])
        phi = col("phi")
        stt(phi[:], li[:], 1.0 / 32, dt[:], ALU.mult, ALU.mult, r=[li, dt], w=[phi])
        ww = col("ww")
        tt(ww[:], phi[:], phi[:], ALU.mult, r=[phi], w=[ww])
        ps_ = horner("ps", ww, [(-1.0) ** k / _fact(2 * k + 1) for k in range(8)])
        pc_ = horner("pc", ww, [(-1.0) ** (k + 1) / _fact(2 * k + 2) for k in range(8)])
        sA = col("sA")
        tt(sA[:], ps_[:], phi[:], ALU.mult, r=[ps_, phi], w=[sA])
        cA = col("cA")
        tt(cA[:], pc_[:], ww[:], ALU.mult, r=[pc_, ww], w=[cA])
        sB, cB, a1, s2 = col("sB"), col("cB"), col("a1"), col("s2")
        cur = (cA, sA)
        nxt = (cB, sB)
        for _ in range(5):
            cm_, s_ = cur
            cn, sn = nxt
            ts(a1[:], cm_[:], 2.0, ALU.add, cm_[:, 0:1], ALU.mult, r=[cm_], w=[a1])
            tt(s2[:], s_[:], s_[:], ALU.mult, r=[s_], w=[s2])
            tt(cn[:], a1[:], s2[:], ALU.subtract, r=[a1, s2], w=[cn])
            ts(sn[:], cm_[:], 1.0, ALU.add, s_[:, 0:1], ALU.mult, r=[cm_, s_], w=[sn])
            ts(sn[:], sn[:], 2.0, ALU.mult, r=[sn], w=[sn])
            cur, nxt = nxt, cur
        cm_, s_ = cur
        cc = col("cc")
        ts(cc[:], cm_[:], 1.0, ALU.add, r=[cm_], w=[cc])
        ai = col("ai")
        tt(ai[:], mag[:], s_[:], ALU.mult, r=[mag, s_], w=[ai])
        am1r = col("am1r")
        tt(am1r[:], mag[:], cm_[:], ALU.mult, r=[mag, cm_], w=[am1r])
        tt(am1r[:], am1r[:], em1[:], ALU.add, r=[am1r, em1], w=[am1r])
        den = col("den")
        tt(den[:], lr[:], lr[:], ALU.mult, r=[lr], w=[den])
        stt(den[:], li[:], li[:, 0:1], den[:], ALU.mult, ALU.add, r=[li, den], w=[den])
        P.op("dve", lambda e: e.reciprocal(out=den[:], in_=den[:]), reads=[den], writes=[den])
        u1, u2, fr, fi = col("u1"), col("u2"), col("fr"), col("fi")
        tt(u1[:], am1r[:], lr[:], ALU.mult, r=[am1r, lr], w=[u1])
        stt(u1[:], ai[:], li[:, 0:1], u1[:], ALU.mult, ALU.add, r=[ai, li, u1], w=[u1])
        tt(fr[:], u1[:], den[:], ALU.mult, r=[u1, den], w=[fr])
        tt(u2[:], am1r[:], li[:], ALU.mult, r=[am1r, li], w=[u2])
        stt(u2[:], ai[:], lr[:, 0:1], u2[:], ALU.mult, ALU.subtract, r=[ai, lr, u2], w=[u2])
        tt(fi[:], u2[:], den[:], ALU.mult, r=[u2, den], w=[fi])
        Bb = P.sb([128, 2, 16], F32, "Bb")
        v1 = col("v1", 16)
        ts(v1[:], Bt[:, 1, :], fi[:, 0:1], ALU.mult, r=[Bt, fi], w=[v1])
        stt(Bb[:, 0, :], Bt[:, 0, :], fr[:, 0:1], v1[:], ALU.mult, ALU.subtract, r=[Bt, fr, v1], w=[Bb])
        ts(v1[:], Bt[:, 0, :], fi[:, 0:1], ALU.mult, r=[Bt, fi], w=[v1])
        stt(Bb[:, 1, :], Bt[:, 1, :], fr[:, 0:1], v1[:], ALU.mult, ALU.add, r=[Bt, fr, v1], w=[Bb])
        Er, Ei = col("Er", 65), col("Ei", 65)
        t1, t2 = col("t1", 32), col("t2", 32)
        P.op("pool", lambda e: e.memset(Er[:, 0:1], 1.0), writes=[Er])
        P.op("pool", lambda e: e.memset(Ei[:, 0:1], 0.0), writes=[Ei])
        ts(Er[:, 1:2], cc[:], 1.0, ALU.mult, r=[cc], w=[Er])
        ts(Ei[:, 1:2], s_[:], 1.0, ALU.mult, r=[s_], w=[Ei])
        m = 1
        while m <= 32:
            er, ei = Er[:, m:m + 1], Ei[:, m:m + 1]
            ts(t1[:, 0:m], Ei[:, 1:m + 1], ei, ALU.mult, r=[Ei], w=[t1])
            ts(t2[:, 0:m], Ei[:, 1:m + 1], er, ALU.mult, r=[Ei, Er], w=[t2])
            stt(Ei[:, m + 1:2 * m + 1], Er[:, 1:m + 1], ei, t2[:, 0:m], ALU.mult, ALU.add, r=[Er, Ei, t2], w=[Ei])
            stt(Er[:, m + 1:2 * m + 1], Er[:, 1:m + 1], er, t1[:, 0:m], ALU.mult, ALU.subtract, r=[Er, t1], w=[Er])
            m *= 2
        magk, imagk = col("magk", 65), col("imagk", 65)
        P.op("act", lambda e: e.activation(out=magk[:], in_=kio[:], func=AF.Exp, scale=xx[:, 0:1]), reads=[kio, xx], writes=[magk])
        P.op("act", lambda e: e.activation(out=imagk[:], in_=kio[:], func=AF.Exp, scale=negx[:, 0:1]), reads=[kio, negx],
             writes=[imagk])
        Pr, Pi, Qr, Qi = col("Pr", 65), col("Pi", 65), col("Qr", 65), col("Qi", 65)
        tt(Pr[:], Er[:], magk[:], ALU.mult, r=[Er, magk], w=[Pr])
        tt(Pi[:], Ei[:], magk[:], ALU.mult, r=[Ei, magk], w=[Pi])
        tt(Qr[:], Er[:], imagk[:], ALU.mult, r=[Er, imagk], w=[Qr])
        stt(Qi[:], Ei[:], -1.0, imagk[:], ALU.mult, ALU.mult, r=[Ei, imagk], w=[Qi])
        KBr = P.sb([128, 64, 16], F32, "KBr")
        KBi = P.sb([128, 64, 16], F32, "KBi")
        QCr = P.sb([128, 64, 16], F32, "QCr")
        QCi = P.sb([128, 64, 16], F32, "QCi")
        tmp = P.sb([128, 64, 16], F32, "tmp")
        bj = lambda tl: tl[:, 0:64].unsqueeze(2).to_broadcast([128, 64, 16])
        bc = lambda ap: ap.unsqueeze(1).to_broadcast([128, 64, 16])
        tt(KBr[:], bj(Qr), bc(Bb[:, 0, :]), ALU.mult, r=[Qr, Bb], w=[KBr])
        tt(tmp[:], bj(Qi), bc(Bb[:, 1, :]), ALU.mult, r=[Qi, Bb], w=[tmp])
        tt(KBr[:], KBr[:], tmp[:], ALU.subtract, r=[KBr, tmp], w=[KBr])
        tt(KBi[:], bj(Qr), bc(Bb[:, 1, :]), ALU.mult, r=[Qr, Bb], w=[KBi])
        tt(tmp[:], bj(Qi), bc(Bb[:, 0, :]), ALU.mult, r=[Qi, Bb], w=[tmp])
        tt(KBi[:], KBi[:], tmp[:], ALU.add, r=[KBi, tmp], w=[KBi])
        tt(QCr[:], bj(Pr), bc(Ct[:, 0, :]), ALU.mult, r=[Pr, Ct], w=[QCr])
        tt(tmp[:], bj(Pi), bc(Ct[:, 1, :]), ALU.mult, r=[Pi, Ct], w=[tmp])
        tt(QCr[:], QCr[:], tmp[:], ALU.subtract, r=[QCr, tmp], w=[QCr])
        tt(QCi[:], bj(Pr), bc(Ct[:, 1, :]), ALU.mult, r=[Pr, Ct], w=[QCi])
        tt(tmp[:], bj(Pi), bc(Ct[:, 0, :]), ALU.mult, r=[Pi, Ct], w=[tmp])
        stt(QCi[:], QCi[:], -1.0, tmp[:], ALU.mult, ALU.subtract, r=[QCi, tmp], w=[QCi])
        fl = lambda tl: tl[:].rearrange("p j c -> p (j c)")
        banks = [P.ps([128, 512], F32, f"bk{i}") for i in range(8)]
        nb = 0
        TZ = [P.sb([128, 8, 1024], F32, f"TZ{gl}") for gl in range(2)]
        for gl in range(2):
            rows = slice(64 * gl, 64 * gl + 64)
            for rc in range(8):
                for ch in range(2):
                    if 4 * ch + 3 < rc:
                        continue
                    bk = banks[nb % 4]
                    nb += 1
                    P.op("pe", lambda e, bk=bk, rows=rows, rc=rc, ch=ch: e.matmul(
                        bk[:], fl(KBr)[rows, rc * 128:(rc + 1) * 128], fl(QCr)[rows, ch * 512:(ch + 1) * 512],
                        start=True, stop=False), reads=[KBr, QCr], writes=[bk])
                    P.op("pe", lambda e, bk=bk, rows=rows, rc=rc, ch=ch: e.matmul(
                        bk[:], fl(KBi)[rows, rc * 128:(rc + 1) * 128], fl(QCi)[rows, ch * 512:(ch + 1) * 512],
                        start=False, stop=True), reads=[KBi, QCi], writes=[bk])
                    w0 = (7 - rc) * 128 + ch * 512
                    P.op("dve", lambda e, bk=bk, gl=gl, rc=rc, ch=ch, w0=w0: e.tensor_tensor(
                        out=TZ[gl][:, rc, ch * 512:(ch + 1) * 512], in0=bk[:], in1=Wm[:, w0:w0 + 512], op=ALU.mult),
                        reads=[bk, Wm], writes=[TZ[gl]])
        KBT = [[P.sb([128, 8, 64], F32, f"KBT{gl}{ri}") for ri in range(2)] for gl in range(2)]
        for gl in range(2):
            rows = slice(64 * gl, 64 * gl + 64)
            for ri, src in enumerate((KBr, KBi)):
                bk = banks[4 + (2 * gl + ri) % 2]
                for kc in range(8):
                    P.op("pe", lambda e, bk=bk, rows=rows, kc=kc, src=src: e.transpose(
                        bk[:, kc * 64:(kc + 1) * 64], fl(src)[rows, kc * 128:(kc + 1) * 128], ident[rows, rows]),
                        reads=[src, ident], writes=[bk])
                P.op("act", lambda e, bk=bk, gl=gl, ri=ri: e.copy(
                    out=KBT[gl][ri][:], in_=bk[:].rearrange("p (k c) -> p k c", k=8)), reads=[bk], writes=[KBT[gl][ri]])
        bx = banks[6]
        for gl in range(2):
            rows = slice(64 * gl, 64 * gl + 64)
            for ri in range(2):
                for kc in range(8):
                    P.op("pe", lambda e, gl=gl, rows=rows, ri=ri, kc=kc: e.matmul(
                        bx[rows, ri * 256:(ri + 1) * 256], KBT[gl][ri][:, kc, :], Ut[gl][:, kc, :],
                        start=(kc == 0), stop=(kc == 7)), reads=[KBT[gl][ri], Ut[gl]], writes=[bx])
        Wr, Wi = col("Wr", 256), col("Wi", 256)
        Xs = col("Xs", 512)
        P.op("act", lambda e: e.copy(out=Xs[:], in_=bx[:]), reads=[bx], writes=[Xs])
        Ar, Ai = Pr[:, 64:65], Pi[:, 64:65]
        ts(Wr[:], Xs[:, 256:512], Ai, ALU.mult, r=[Xs, Pi], w=[Wr])
        stt(Wr[:], Xs[:, 0:256], Ar, Wr[:], ALU.mult, ALU.subtract, r=[Xs, Pr, Wr], w=[Wr])
        ts(Wi[:], Xs[:, 0:256], Ai, ALU.mult, r=[Xs, Pi], w=[Wi])
        stt(Wi[:], Xs[:, 256:512], Ar, Wi[:], ALU.mult, ALU.add, r=[Xs, Pr, Wi], w=[Wi])
        Zr, Zi = P.sb([128, 2, 128], F32, "Zr"), P.sb([128, 2, 128], F32, "Zi")
        P.op("pool", lambda e: e.memset(Zr[:], 0.0), writes=[Zr])
        P.op("pool", lambda e: e.memset(Zi[:], 0.0), writes=[Zi])
        Ya = (P.sb([128, 2, 128], F32, "Yar"), P.sb([128, 2, 128], F32, "Yai"))
        Yb = (P.sb([128, 2, 128], F32, "Ybr"), P.sb([128, 2, 128], F32, "Ybi"))
        sc1 = P.sb([128, 2, 128], F32, "sc1")
        sc2 = P.sb([128, 2, 128], F32, "sc2")
        Mp = P.sb([128, 7, 2], F32, "Mp")
        ma, mb_ = col("ma"), col("mb")
        ts(Mp[:, 0, 0:1], Ar, 1.0, ALU.mult, r=[Pr], w=[Mp])
        ts(Mp[:, 0, 1:2], Ai, 1.0, ALU.mult, r=[Pi], w=[Mp])
        for j in range(6):
            mr_, mi_ = Mp[:, j, 0:1], Mp[:, j, 1:2]
            tt(ma[:], mr_, mr_, ALU.mult, r=[Mp], w=[ma])
            tt(mb_[:], mi_, mi_, ALU.mult, r=[Mp], w=[mb_])
            tt(Mp[:, j + 1, 0:1], ma[:], mb_[:], ALU.subtract, r=[ma, mb_], w=[Mp])
            stt(Mp[:, j + 1, 1:2], mr_, 2.0, mi_, ALU.mult, ALU.mult, r=[Mp], w=[Mp])
        W3r = Wr[:].rearrange("p (b n) -> p b n", b=2)
        W3i = Wi[:].rearrange("p (b n) -> p b n", b=2)
        src = None
        N_ = S5N
        for j in range(7):
            sft = 1 << j
            dst = Ya if j % 2 == 0 else Yb
            if src is None:
                sr, si, srl, sil = W3r, W3i, [Wr], [Wi]
            else:
                sr, si, srl, sil = src[0][:], src[1][:], [src[0]], [src[1]]
            mr_, mi_ = Mp[:, j, 0:1], Mp[:, j, 1:2]
            lo, hi = slice(0, N_ - sft), slice(sft, N_)
            ts(sc1[:, :, lo], si[:, :, lo], mi_, ALU.mult, r=sil + [Mp], w=[sc1])
            stt(sc1[:, :, lo], sr[:, :, lo], mr_, sc1[:, :, lo], ALU.mult, ALU.subtract, r=srl + [Mp, sc1], w=[sc1])
            ts(sc2[:, :, lo], sr[:, :, lo], mi_, ALU.mult, r=srl + [Mp], w=[sc2])
            stt(sc2[:, :, lo], si[:, :, lo], mr_, sc2[:, :, lo], ALU.mult, ALU.add, r=sil + [Mp, sc2], w=[sc2])
            tt(dst[0][:, :, hi], sr[:, :, hi], sc1[:, :, lo], ALU.add, r=srl + [sc1], w=[dst[0]])
            tt(dst[1][:, :, hi], si[:, :, hi], sc2[:, :, lo], ALU.add, r=sil + [sc2], w=[dst[1]])
            P.op("act", lambda e, dst=dst, sr=sr, sft=sft: e.copy(out=dst[0][:, :, 0:sft], in_=sr[:, :, 0:sft]),
                 reads=srl, writes=[dst[0]])
            P.op("act", lambda e, dst=dst, si=si, sft=sft: e.copy(out=dst[1][:, :, 0:sft], in_=si[:, :, 0:sft]),
                 reads=sil, writes=[dst[1]])
            src = dst
        P.op("act", lambda e: e.copy(out=Zr[:, :, 1:N_], in_=src[0][:, :, 0:N_ - 1]), reads=[src[0]], writes=[Zr])
        P.op("dve", lambda e: e.tensor_copy(out=Zi[:, :, 1:N_], in_=src[1][:, :, 0:N_ - 1]), reads=[src[1]], writes=[Zi])
        Yt = [P.sb([128, 256], F32, f"Yt{i}") for i in range(2)]
        ny = 0
        for gl in range(2):
            rows = slice(64 * gl, 64 * gl + 64)
            for ob in range(8):
                bk = banks[ny % 4]
                yt_ = Yt[ny % 2]
                ny += 1
                for kc in range(ob + 1):
                    P.op("pe", lambda e, bk=bk, gl=gl, kc=kc, ob=ob: e.matmul(
                        bk[:, 0:256], TZ[gl][:, kc, ob * 128:(ob + 1) * 128], Ut[gl][:, kc, :],
                        start=(kc == 0), stop=False), reads=[TZ[gl], Ut[gl]], writes=[bk])
                P.op("pe", lambda e, bk=bk, rows=rows, ob=ob: e.matmul(
                    bk[:, 0:256], fl(QCr)[rows, ob * 128:(ob + 1) * 128], Zr[rows].rearrange("p b n -> p (b n)"),
                    start=False, stop=False), reads=[QCr, Zr], writes=[bk])
                P.op("pe", lambda e, bk=bk, rows=rows, ob=ob: e.matmul(
                    bk[:, 0:256], fl(QCi)[rows, ob * 128:(ob + 1) * 128], Zi[rows].rearrange("p b n -> p (b n)"),
                    start=False, stop=True), reads=[QCi, Zi], writes=[bk])
                P.op("act", lambda e, bk=bk, yt_=yt_: e.copy(out=yt_[:], in_=bk[:, 0:256]), reads=[bk], writes=[yt_])
                P.dma("sp", lambda e, gl=gl, ob=ob, yt_=yt_: e.dma_start(out=Y[gl, :, ob, :], in_=yt_[:]), yt_,
                      reads=[yt_], is_out=True)
        P.emit()
    return nc


def s5_inputs(proj, p, l):
    kio, Wm, ident = s5_consts()
    maps = []
    for c in range(NCORES):
        Us, lam, lst, Bri, Cri = [], [], [], [], []
        for gl in range(2):
            g = 2 * c + gl
            u = proj[:, :, OFF["su"] + g * 16:OFF["su"] + (g + 1) * 16]
            u = u.reshape(BATCH, S5N, 8, 8, 16).transpose(3, 4, 2, 0, 1)
            Us.append(u.reshape(128, 8, 2 * S5N))
            lam.append(np.stack([p["s5_lam_re"][l][g], p["s5_lam_im"][l][g]], 1))
            lst.append(np.full((64, 1), p["s5_log_step"][l][g], np.float32))
            Bri.append(np.stack([p["s5_b_re"][l][g], p["s5_b_im"][l][g]], 1))
            Cri.append(np.stack([p["s5_c_re"][l][g].T, p["s5_c_im"][l][g].T], 1))
        cat = lambda xs: np.ascontiguousarray(np.concatenate(xs, 0).astype(np.float32))
        maps.append({"U": np.ascontiguousarray(np.stack(Us, 0)), "lam": cat(lam), "lst": cat(lst),
                     "Bri": cat(Bri), "Cri": cat(Cri), "kio": kio, "Wm": Wm, "ident": ident})
    return maps


def s5_gather(res):
    out = np.zeros((BATCH, T, 256), np.float32)
    for c in range(NCORES):
        Yc = res[c]["Y"]
        for gl in range(2):
            g = 2 * c + gl
            a = Yc[gl].reshape(8, 16, 8, BATCH, S5N).transpose(3, 4, 2, 0, 1)
            out[:, :, g * 16:(g + 1) * 16] = a.reshape(BATCH, T, 16)
    return out


NQB = 32
NEGB = -30000.0


def nsa_consts():
    r = np.arange(128)
    kl, ql = r[:, None], r[None, :]
    mdiag = np.where(kl <= ql, 0.0, NEGB).astype(np.float32)
    mfar = np.where(kl > ql, 0.0, NEGB).astype(np.float32)
    mall = np.full((128, 128), NEGB, np.float32)
    mzero = np.zeros((128, 128), np.float32)
    mw = [np.stack([mfar, mzero, mzero, mzero, mdiag, mall], 1),
          np.stack([mall, mfar, mzero, mzero, mzero, mdiag], 1)]
    ms = [np.stack([mdiag, mall], 1), np.stack([mzero, mdiag], 1)]
    cmpm = np.zeros((128, 16, 128), np.float32)
    for v in range(16):
        cmpm[:, v, :] = np.where(16 * (kl - 8 * v) + 31 <= ql, 0.0, NEGB)
    c = np.arange(512)[:, None]
    s = np.arange(128)[None, :]
    ovl = ((16 * c < 64 * s + 64) & (16 * c + 31 >= 64 * s) & (c < 511)).astype(np.float32)
    ovl = ovl.reshape(4, 128, 128).transpose(1, 0, 2)
    u = np.arange(255)[None, :] - 127
    curl = (r[:, None] >= 64).astype(np.int64)
    forced = (u == curl) | (u == curl - 1)
    invalid = u > curl
    W1 = np.where(forced | invalid, 0.0, 1.0)
    W2 = np.where(invalid, -1e30, np.where(forced, 1e4, 0.0))
    W12 = np.stack([W1, W2], 1).astype(np.float32)
    E = (np.arange(8192)[None, :] // 64 == r[:, None]).astype(np.float32)
    return dict(mw=mw, ms=ms, cmpm=cmpm, ovl=np.ascontiguousarray(ovl), W12=W12, E=E,
                ident=np.eye(128, dtype=np.float32), ones64=np.ones((64, 64), np.float32))


def build_nsa(bcast_rhs=True, nblk=NQB):
    nc = bass.Bass("TRN2", target_bir_lowering=False)
    din = lambda name, shape: nc.dram_tensor(name, list(shape), F32, kind="ExternalInput").ap()
    kT4 = din("kT4", [4, 64, T])
    vtok2 = din("vtok2", [2, 128, NTILE, 64])
    qTd = din("qTd", [64, 4, NQB * 128])
    gld = din("gld", [128, NQB, 12])
    gbd = din("gbd", [128, 12])
    gated = din("gated", [128, NQB, 256])
    qnd = din("qn", [64, 1])
    knd = din("kn", [64, 3])
    posd = din("posT", [64, 2, 32])
    w1d = din("w1d", [2, 64, 32, 256])
    b1d = din("b1d", [128, 2, 2])
    w2d = din("w2d", [128, 2, 2, 64])
    b2kd = din("b2k", [64, 1])
    b2vd = din("b2v", [128, 64])
    ones64d = din("ones64", [64, 64])
    Ed = din("E", [128, T])
    mwd = din("mw", [128, 6, 128])
    msd = din("ms", [128, 2, 128])
    identd = din("ident", [128, 128])
    cmpmd = din("cmpm", [128, 8, 128])
    ovld = din("ovl", [128, 4, 128])
    W12d = din("W12", [128, 2, 253])
    y = nc.dram_tensor("y", [128, NQB, 256], F32, kind="ExternalOutput").ap()
    with ExitStack() as st:
        P = Prog(nc, st)
        dq = ["sp", "pool"]
        ndq = [0]

        def load(name, shape, src, dt=F32):
            t = P.sb(shape, dt, name)
            qn_ = dq[ndq[0] % 2]
            ndq[0] += 1
            P.dma(qn_, lambda e: e.dma_start(out=t[:], in_=src), t, writes=[t])
            return t

        stg = [P.sb([128, 2048], F32, f"stg{i}") for i in range(2)]
        nst = [0]

        def load_cast(dst_ap, dst_lt, src, shape_p, ncols, eng=None):
            s = stg[nst[0] % 2]
            eng = eng or ("dve" if nst[0] % 2 == 0 else "act")
            qn_ = dq[nst[0] % 2]
            nst[0] += 1
            P.dma(qn_, lambda e: e.dma_start(out=s[0:shape_p, 0:ncols], in_=src), s, writes=[s])
            if eng == "dve":
                P.op("dve", lambda e: e.tensor_copy(out=dst_ap, in_=s[0:shape_p, 0:ncols]), reads=[s], writes=[dst_lt])
            else:
                P.op("act", lambda e: e.copy(out=dst_ap, in_=s[0:shape_p, 0:ncols]), reads=[s], writes=[dst_lt])

        banks = [P.ps([128, 512], F32, f"bk{i}") for i in range(8)]
        SB = banks[0:3]
        OC0, OC1, OS, OW, MISC = banks[3], banks[4], banks[5], banks[6], banks[7]
        MSLOT = [OC0, OC1, MISC]
        qn = load("qn", [64, 1], qnd)
        kn = load("kn", [64, 3], knd)
        b1 = load("b1", [128, 2, 2], b1d)
        b2k = load("b2k", [64, 1], b2kd)
        b2v = load("b2v", [128, 64], b2vd)
        ones64 = load("ones64", [64, 64], ones64d)
        identf = load("identf", [128, 128], identd)
        W12 = load("W12", [128, 2, 253], W12d)
        gb = load("gb", [128, 12], gbd)
        gl = load("gl", [128, NQB, 12], gld)
        epst = P.sb([128, 1], F32, "eps")
        P.op("pool", lambda e: e.memset(epst[:], RMS_EPS), writes=[epst])
        qsc = P.sb([64, 1], F32, "qsc")
        P.op("dve", lambda e: e.tensor_scalar(out=qsc[:], in0=qn[:], scalar1=64 ** -0.5, scalar2=None, op0=ALU.mult),
             reads=[qn], writes=[qsc])
        identb = P.sb([128, 128], BF16, "identb")
        P.op("dve", lambda e: e.tensor_copy(out=identb[:], in_=identf[:]), reads=[identf], writes=[identb])
        mwb = P.sb([128, 6, 128], BF16, "mwb")
        load_cast(mwb[:].rearrange("p a b -> p (a b)"), mwb, mwd.rearrange("p a b -> p (a b)"), 128, 768)
        msb = P.sb([128, 2, 128], BF16, "msb")
        load_cast(msb[:].rearrange("p a b -> p (a b)"), msb, msd.rearrange("p a b -> p (a b)"), 128, 256)
        cmpmb = P.sb([128, 8, 128], BF16, "cmpmb")
        load_cast(cmpmb[:].rearrange("p a b -> p (a b)"), cmpmb, cmpmd.rearrange("p a b -> p (a b)"), 128, 1024)
        Eb = P.sb([128, T], BF16, "Eb")
        for i in range(4):
            load_cast(Eb[:, i * 2048:(i + 1) * 2048], Eb, Ed[:, i * 2048:(i + 1) * 2048], 128, 2048)
        P.op("dve", lambda e: e.tensor_tensor(out=gl[:], in0=gl[:], in1=gb[:].unsqueeze(1).to_broadcast([128, NQB, 12]),
                                              op=ALU.add), reads=[gl, gb], writes=[gl])
        P.op("act", lambda e: e.activation(out=gl[:], in_=gl[:], func=AF.Sigmoid), reads=[gl], writes=[gl])
        V1 = [P.sb([128, NTILE, 65], BF16, f"V1_{i}") for i in range(2)]
        for j in range(2):
            P.op("pool", lambda e, j=j: e.memset(V1[j][:, :, 64:65], 1.0), writes=[V1[j]])
            for i in range(2):
                load_cast(V1[j][:, i * 32:(i + 1) * 32, 0:64], V1[j],
                          vtok2[j][:, i * 32:(i + 1) * 32, :].rearrange("p a b -> p (a b)"), 128, 2048)

        rn_sq = [P.sb([64, 512], F32, f"rn_sq{i}") for i in range(4)]
        rn_rt = [P.sb([64, 512], F32, f"rn_rt{i}") for i in range(4)]

        def rms_batch(jobs):
            for i, (src_ap, src_lt, _, _, _, _) in enumerate(jobs):
                P.op("act", lambda e, i=i, src_ap=src_ap: e.activation(out=rn_sq[i][:], in_=src_ap, func=AF.Square),
                     reads=[src_lt], writes=[rn_sq[i]])
            for i in range(len(jobs)):
                P.op("pe", lambda e, i=i: e.matmul(banks[i][0:64, :], ones64[:], rn_sq[i][:], start=True, stop=True),
                     reads=[ones64, rn_sq[i]], writes=[banks[i]])
            for i in range(len(jobs)):
                P.op("act", lambda e, i=i: e.activation(out=rn_rt[i][:], in_=banks[i][0:64, :], func=AF.Sqrt, scale=1.0 / 64,
                                                        bias=epst[0:64, :]), reads=[banks[i], epst], writes=[rn_rt[i]])
            for i in range(len(jobs)):
                P.op("dve", lambda e, i=i: e.reciprocal(out=rn_rt[i][:], in_=rn_rt[i][:]), reads=[rn_rt[i]], writes=[rn_rt[i]])
            for i, (src_ap, src_lt, scale_ap, scale_lt, dst_ap, dst_lt) in enumerate(jobs):
                P.op("dve", lambda e, i=i, src_ap=src_ap, scale_ap=scale_ap, dst_ap=dst_ap: e.scalar_tensor_tensor(
                    out=dst_ap, in0=src_ap, scalar=scale_ap, in1=rn_rt[i][:], op0=ALU.mult, op1=ALU.mult),
                    reads=[src_lt, scale_lt, rn_rt[i]], writes=[dst_lt])

        def rms_fm(src_ap, src_lt, ncol, scale_ap, scale_lt, dst_ap, dst_lt, bank):
            assert ncol == 512
            rms_batch([(src_ap, src_lt, scale_ap, scale_lt, dst_ap, dst_lt)])

        KT = [P.sb([64, T], BF16, f"KT{i}") for i in range(2)]
        nrm = 0
        for j in range(2):
            for i in range(4):
                s = stg[nst[0] % 2]
                qn_ = dq[nst[0] % 2]
                nst[0] += 1
                P.dma(qn_, lambda e, s=s, j=j, i=i: e.dma_start(out=s[0:64, :], in_=kT4[2 + j][:, i * 2048:(i + 1) * 2048]),
                      s, writes=[s])
                rms_batch([(s[0:64, t * 512:(t + 1) * 512], s, kn[:, 1 + j:2 + j], kn,
                            KT[j][:, i * 2048 + t * 512:i * 2048 + (t + 1) * 512], KT[j]) for t in range(4)])
        kcT = P.sb([64, T + 16], BF16, "kcT")
        P.op("pool", lambda e: e.memset(kcT[:, T:T + 16], 0.0), writes=[kcT])
        w1b = P.sb([64, 32, 256], BF16, "w1b")
        posb = P.sb([64, 2, 32], BF16, "posb")
        load_cast(posb[:].rearrange("p a b -> p (a b)"), posb, posd.rearrange("p a b -> p (a b)"), 64, 64)
        w2f = load("w2f", [128, 2, 2, 64], w2d)
        w2b = P.sb([128, 2, 2, 64], BF16, "w2b")
        P.op("dve", lambda e: e.tensor_copy(out=w2b[:], in_=w2f[:]), reads=[w2f], writes=[w2b])
        hidT = P.sb([128, 2, 512], BF16, "hidT")
        P.op("pool", lambda e: e.memset(hidT[:], 0.0), writes=[hidT])
        biasv = P.sb([128, 2], F32, "biasv")
        hx = P.sb([128, 512], F32, "hx")
        hu = P.sb([128, 512], F32, "hu")
        P.op("pool", lambda e: e.memset(hx[:], 0.0), writes=[hx])
        kcmpT = P.sb([64, 512], BF16, "kcmpT")
        kraw = P.sb([64, 512], F32, "kraw")
        P.op("pool", lambda e: e.memset(kraw[:], 0.0), writes=[kraw])
        Vc1 = P.sb([128, 4, 193], BF16, "Vc1")
        P.op("pool", lambda e: e.memset(Vc1[:, :, 64:65], 1.0), writes=[Vc1])
        load_cast(Vc1[:, :, 65:193], Vc1, ovld.rearrange("p a b -> p (a b)"), 128, 512, eng="dve")
        for kv in range(2):
            for i in range(4):
                load_cast(kcT[:, i * 2048:(i + 1) * 2048], kcT, kT4[kv][:, i * 2048:(i + 1) * 2048], 64, 2048)
            for i in range(4):
                load_cast(w1b[:, i * 8:(i + 1) * 8, :].rearrange("p a b -> p (a b)"), w1b,
                          w1d[kv][:, i * 8:(i + 1) * 8, :].rearrange("p a b -> p (a b)"), 64, 2048)
            for hc in range(2):
                bk = banks[hc]
                for l in range(32):
                    P.op("pe", lambda e, bk=bk, l=l, hc=hc: e.matmul(
                        bk[:, 0:511], w1b[:, l, hc * 128:(hc + 1) * 128], kcT[:, l:l + 16 * 511:16],
                        start=(l == 0), stop=(l == 31)), reads=[w1b, kcT], writes=[bk])
                pb = MISC
                for l in range(32):
                    P.op("pe", lambda e, pb=pb, l=l, hc=hc, kv=kv: e.matmul(
                        pb[:, hc:hc + 1], w1b[:, l, hc * 128:(hc + 1) * 128], posb[:, kv, l:l + 1],
                        start=(l == 0), stop=(l == 31)), reads=[w1b, posb], writes=[pb])
                P.op("dve", lambda e, hc=hc, kv=kv, pb=pb: e.tensor_tensor(
                    out=biasv[:, hc:hc + 1], in0=pb[:, hc:hc + 1], in1=b1[:, kv, hc:hc + 1], op=ALU.add),
                    reads=[pb, b1], writes=[biasv])
                P.op("act", lambda e, bk=bk, hc=hc: e.activation(
                    out=hx[:, 0:511], in_=bk[:, 0:511], func=AF.Identity, bias=biasv[:, hc:hc + 1]),
                    reads=[bk, biasv], writes=[hx])
                P.op("dve", lambda e: e.tensor_tensor(out=hu[:], in0=hx[:], in1=hx[:], op=ALU.mult), reads=[hx], writes=[hu])
                P.op("dve", lambda e: e.tensor_scalar(out=hu[:], in0=hu[:], scalar1=0.044715, scalar2=1.0, op0=ALU.mult,
                                                      op1=ALU.add), reads=[hu], writes=[hu])
                P.op("dve", lambda e: e.tensor_tensor(out=hu[:], in0=hu[:], in1=hx[:], op=ALU.mult), reads=[hu, hx], writes=[hu])
                P.op("act", lambda e: e.activation(out=hu[:], in_=hu[:], func=AF.Tanh, scale=0.7978845608028654),
                     reads=[hu], writes=[hu])
                P.op("dve", lambda e: e.tensor_scalar(out=hu[:], in0=hu[:], scalar1=0.5, scalar2=0.5, op0=ALU.mult,
                                                      op1=ALU.add), reads=[hu], writes=[hu])
                P.op("dve", lambda e, hc=hc: e.tensor_tensor(out=hidT[:, hc, 0:511], in0=hu[:, 0:511], in1=hx[:, 0:511],
                                                            op=ALU.mult), reads=[hu, hx], writes=[hidT])
            if kv == 0:
                bk = banks[2]
                for hc in range(2):
                    P.op("pe", lambda e, bk=bk, hc=hc: e.matmul(
                        bk[0:64, 0:511], w2b[:, 0, hc, :], hidT[:, hc, 0:511], start=(hc == 0), stop=(hc == 1)),
                        reads=[w2b, hidT], writes=[bk])
                P.op("act", lambda e, bk=bk: e.activation(out=kraw[:, 0:511], in_=bk[0:64, 0:511], func=AF.Identity,
                                                          bias=b2k[:, 0:1]), reads=[bk, b2k], writes=[kraw])
                rms_fm(kraw[:, :], kraw, 512, kn[:, 0:1], kn, kcmpT[:, :], kcmpT, banks[0])
            else:
                bk = banks[2]
                for cc in range(4):
                    for hc in range(2):
                        P.op("pe", lambda e, bk=bk, hc=hc, cc=cc: e.matmul(
                            bk[:, cc * 64:(cc + 1) * 64], hidT[:, hc, cc * 128:(cc + 1) * 128], w2b[:, 1, hc, :],
                            start=(hc == 0), stop=(hc == 1)), reads=[w2b, hidT], writes=[bk])
                P.op("dve", lambda e, bk=bk: e.tensor_tensor(
                    out=Vc1[:, :, 0:64], in0=bk[:, 0:256].rearrange("p (c d) -> p c d", c=4),
                    in1=b2v[:].unsqueeze(1).to_broadcast([128, 4, 64]), op=ALU.add), reads=[bk, b2v], writes=[Vc1])
        qTb = P.sb([64, NQB, 4, 128], BF16, "qTb")
        for h in range(4):
            for i in range(2):
                s = stg[nst[0] % 2]
                qn_ = dq[nst[0] % 2]
                nst[0] += 1
                P.dma(qn_, lambda e, s=s, h=h, i=i: e.dma_start(out=s[0:64, :], in_=qTd[:, h, i * 2048:(i + 1) * 2048]),
                      s, writes=[s])
                rms_batch([(s[0:64, t * 512:(t + 1) * 512], s, qsc[:, 0:1], qsc,
                            qTb[:, i * 16 + t * 4:i * 16 + t * 4 + 4, h, :], qTb) for t in range(4)])
        Pt = [P.sb([128, 512], BF16, f"Pt{i}") for i in range(3)]
        npair = [0]
        gt = [P.sb([128, 256], F32, f"gt{i}") for i in range(2)]
        yt = [P.sb([128, 256], F32, f"yt{i}") for i in range(2)]
        rec = P.sb([128, 3, 4], F32, "rec")
        imp = P.sb([128, 128], F32, "imp")
        imp2 = P.sb([128, 128], F32, "imp2")
        m8a = P.sb([128, 8], F32, "m8a")
        m8b = P.sb([128, 8], F32, "m8b")
        self_ = P.sb([128, 128], F32, "sel")
        negT = P.sb([128, 128], BF16, "negT")
        acc = P.sb([128, 4, 64], F32, "acc")
        tmp = P.sb([128, 4, 64], F32, "tmpo")

        items = []

        def pair(kT_ap, kT_lt, qb, biases, V_ap, V_lt, obanks, ow, first, last, pre=(), post=(), mmask=None):
            def front(k):
                S, pt = SB[k % 3], Pt[k % 3]
                if mmask is not None:
                    ms_ = MSLOT[k % 3]
                    P.op("pe", lambda e: e.matmul(ms_[:, 0:128], mmask[0], mmask[2], start=True, stop=True),
                         reads=mmask[1], writes=[ms_])
                P.op("pe", lambda e: e.matmul(S[:], kT_ap, qb, start=True, stop=(len(biases) == 0)),
                     reads=[kT_lt, qTb], writes=[S])
                for bi, (l_ap, lts, r_ap) in enumerate(biases):
                    lastb = bi == len(biases) - 1
                    P.op("pe", lambda e, l_ap=l_ap, r_ap=r_ap, lastb=lastb: e.matmul(
                        S[:], l_ap, r_ap.unsqueeze(1).to_broadcast([128, 4, 128]), start=False, stop=lastb),
                        reads=lts, writes=[S])
                P.op("act", lambda e: e.activation(out=pt[:], in_=S[:], func=AF.Exp), reads=[S], writes=[pt])
                if mmask is not None:
                    P.op("dve", lambda e: e.tensor_tensor(
                        out=pt[:].rearrange("p (h q) -> p h q", h=4), in0=pt[:].rearrange("p (h q) -> p h q", h=4),
                        in1=ms_[:, 0:128].unsqueeze(1).to_broadcast([128, 4, 128]), op=ALU.mult), reads=[pt, ms_], writes=[pt])

            def back(k):
                pt = Pt[k % 3]
                for h in range(4):
                    bk, c0 = obanks[h]
                    st_ = first and (h == 0 or obanks[h][0] is not obanks[h - 1][0])
                    P.op("pe", lambda e, bk=bk, c0=c0, h=h, st_=st_: e.matmul(
                        bk[:, c0:c0 + ow], pt[:, h * 128:(h + 1) * 128], V_ap, start=st_, stop=last),
                        reads=[pt, V_lt], writes=[bk])

            items.append(dict(front=front, back=back, pre=list(pre), post=list(post)))

        oc_b = [(OC0, 0), (OC0, 193), (OC1, 0), (OC1, 193)]
        os_b = [(OS, h * 65) for h in range(4)]
        ow_b = [(OW, h * 65) for h in range(4)]
        ocv = [bk[:, c0:c0 + 193] for bk, c0 in oc_b]

        def gate_load(m):
            g_ = gt[m % 2]
            P.dma("sp", lambda e: e.dma_start(out=g_[:], in_=gated[:, m, :]), g_, writes=[g_])
            P.op("act", lambda e: e.activation(out=g_[:], in_=g_[:], func=AF.Silu), reads=[g_], writes=[g_])

        def topk_chain(m):
            for h in range(4):
                P.op("dve", lambda e, h=h: e.tensor_scalar(out=rec[:, 0, h:h + 1], in0=ocv[h][:, 64:65], scalar1=1e-30,
                                                          scalar2=None, op0=ALU.max), reads=[oc_b[h][0]], writes=[rec])
            P.op("dve", lambda e: e.reciprocal(out=rec[:, 0, :], in_=rec[:, 0, :]), reads=[rec], writes=[rec])
            P.op("dve", lambda e: e.tensor_scalar(out=imp[:], in0=ocv[0][:, 65:193], scalar1=rec[:, 0, 0:1], scalar2=None,
                                                  op0=ALU.mult), reads=[OC0, rec], writes=[imp])
            for h in range(1, 4):
                P.op("dve", lambda e, h=h: e.scalar_tensor_tensor(
                    out=imp[:], in0=ocv[h][:, 65:193], scalar=rec[:, 0, h:h + 1], in1=imp[:], op0=ALU.mult, op1=ALU.add),
                    reads=[oc_b[h][0], rec, imp], writes=[imp])
            P.op("dve", lambda e: e.tensor_tensor(
                out=rec[:, 0, :], in0=rec[:, 0, :], in1=gl[:, m, :].rearrange("p (h b) -> p h b", h=4)[:, :, 0],
                op=ALU.mult), reads=[rec, gl], writes=[rec])
            for hp in range(2):
                bk = oc_b[2 * hp][0]
                P.op("dve", lambda e, bk=bk, hp=hp: e.tensor_tensor(
                    out=acc[:, 2 * hp:2 * hp + 2, :],
                    in0=bk[:, 0:386].rearrange("p (h c) -> p h c", h=2)[:, :, 0:64],
                    in1=rec[:, 0, 2 * hp:2 * hp + 2].unsqueeze(2).to_broadcast([128, 2, 64]), op=ALU.mult),
                    reads=[bk, rec], writes=[acc])
            w0 = 125 - 4 * m
            P.op("dve", lambda e: e.tensor_tensor(out=imp[:], in0=imp[:], in1=W12[:, 0, w0:w0 + 128], op=ALU.mult),
                 reads=[imp, W12], writes=[imp])
            P.op("dve", lambda e: e.tensor_tensor(out=imp[:], in0=imp[:], in1=W12[:, 1, w0:w0 + 128], op=ALU.add),
                 reads=[imp, W12], writes=[imp])
            P.op("dve", lambda e: e.memset(imp[:, 0:1], 1e4), reads=[], writes=[imp])
            P.op("dve", lambda e: e.max(out=m8a[:], in_=imp[:]), reads=[imp], writes=[m8a])
            P.op("dve", lambda e: e.match_replace(out=imp2[:], in_to_replace=m8a[:], in_values=imp[:], imm_value=-3e38),
                 reads=[imp, m8a], writes=[imp2])
            P.op("dve", lambda e: e.max(out=m8b[:], in_=imp2[:]), reads=[imp2], writes=[m8b])
            P.op("dve", lambda e: e.tensor_scalar(out=self_[:], in0=imp[:], scalar1=m8b[:, 7:8], scalar2=None, op0=ALU.is_ge),
                 reads=[imp, m8b], writes=[self_])

        def sel_mask(m):
            P.op("pe", lambda e: e.transpose(MISC[:, 0:128], self_[:], identf[:]), reads=[self_, identf], writes=[MISC])
            P.op("act", lambda e: e.copy(out=negT[:], in_=MISC[:, 0:128]), reads=[MISC], writes=[negT])

        def combine(m):
            g_, yy = gt[m % 2], yt[m % 2]
            for br, ob_ in ((1, os_b), (2, ow_b)):
                bk = ob_[0][0]
                P.op("dve", lambda e, bk=bk, br=br: e.tensor_scalar(
                    out=rec[:, br, :], in0=bk[:, 0:260].rearrange("p (h c) -> p h c", h=4)[:, :, 64],
                    scalar1=1e-30, scalar2=None, op0=ALU.max), reads=[bk], writes=[rec])
                P.op("dve", lambda e, br=br: e.reciprocal(out=rec[:, br, :], in_=rec[:, br, :]), reads=[rec], writes=[rec])
                P.op("dve", lambda e, br=br: e.tensor_tensor(
                    out=rec[:, br, :], in0=rec[:, br, :], in1=gl[:, m, :].rearrange("p (h b) -> p h b", h=4)[:, :, br],
                    op=ALU.mult), reads=[rec, gl], writes=[rec])
                P.op("dve", lambda e, bk=bk, br=br: e.tensor_tensor(
                    out=tmp[:], in0=bk[:, 0:260].rearrange("p (h c) -> p h c", h=4)[:, :, 0:64],
                    in1=rec[:, br, :].unsqueeze(2).to_broadcast([128, 4, 64]), op=ALU.mult),
                    reads=[bk, rec], writes=[tmp])
                P.op("dve", lambda e: e.tensor_tensor(out=acc[:], in0=acc[:], in1=tmp[:], op=ALU.add),
                     reads=[acc, tmp], writes=[acc])
            P.op("dve", lambda e: e.tensor_tensor(
                out=yy[:], in0=acc[:].rearrange("p h d -> p (h d)"), in1=g_[:], op=ALU.mult),
                reads=[acc, g_], writes=[yy])
            P.dma("sp", lambda e: e.dma_start(out=y[:, m, :], in_=yy[:]), yy, reads=[yy], is_out=True)

        for m in range(nblk):
            qb = qTb[:, m, :, :].rearrange("p h q -> p (h q)")
            ccs = m // 8
            for cc in range(ccs + 1):
                biases = []
                if cc == ccs:
                    biases = [(identb[:], [identb, cmpmb], cmpmb[:, m % 8, :])]
                pair(kcmpT[:, cc * 128:(cc + 1) * 128], kcmpT, qb, biases, Vc1[:, cc, :], Vc1, oc_b, 193,
                     cc == 0, cc == ccs, pre=[lambda m=m: gate_load(m)] if cc == 0 else (),
                     post=[lambda m=m: topk_chain(m)] if cc == ccs else ())
            kbs = [kb for kb in range(2 * m - 4, 2 * m + 2) if kb >= 0]
            for kb in kbs:
                o = kb - (2 * m - 4)
                biases = [] if o in (2, 3) else [(identb[:], [identb, mwb], mwb[:, o, :])]
                pair(KT[1][:, kb * 128:(kb + 1) * 128], KT[1], qb, biases, V1[1][:, kb, :], V1[1], ow_b, 65,
                     kb == kbs[0], kb == kbs[-1])
            for kb in range(2 * m + 2):
                biases = []
                if kb >= 2 * m:
                    biases.append((identb[:], [identb, msb], msb[:, kb - 2 * m, :]))
                pair(KT[0][:, kb * 128:(kb + 1) * 128], KT[0], qb, biases, V1[0][:, kb, :], V1[0], os_b, 65,
                     kb == 0, kb == 2 * m + 1, pre=[lambda m=m: sel_mask(m)] if kb == 0 else (),
                     post=[lambda m=m: combine(m)] if kb == 2 * m + 1 else (),
                     mmask=(Eb[:, kb * 128:(kb + 1) * 128], [Eb, negT], negT[:]))
        DEPTH_PIPE = 2
        n_it = len(items)
        for idx in range(n_it + DEPTH_PIPE):
            if idx < n_it:
                for f in items[idx]["pre"]:
                    f()
                items[idx]["front"](idx)
            j = idx - DEPTH_PIPE
            if j >= 0:
                items[j]["back"](j)
                for f in items[j]["post"]:
                    f()
        P.emit()
    return nc


def nsa_inputs(proj, p, l):
    C = nsa_consts()
    maps = []
    nkv = proj[:, :, OFF["nkv"]:OFF["nkv"] + 768].reshape(BATCH, T, 3, 2, 2, 64)
    for c in range(NCORES):
        b, rem = divmod(c, 4)
        g, par = divmod(rem, 2)
        blks = np.arange(NQB) * 2 + par
        tok = (blks[:, None] * 128 + np.arange(128)[None, :]).reshape(-1)
        kT4 = np.stack([nkv[b, :, 0, 0, g].T, nkv[b, :, 0, 1, g].T, nkv[b, :, 1, 0, g].T, nkv[b, :, 2, 0, g].T], 0)
        tokmaj = lambda a: a.reshape(NTILE, 128, -1).transpose(1, 0, 2)
        vtok2 = np.stack([tokmaj(nkv[b, :, 1, 1, g]), tokmaj(nkv[b, :, 2, 1, g])], 0)
        q = proj[b, tok, OFF["nq"] + g * 256:OFF["nq"] + (g + 1) * 256].reshape(-1, 4, 64)
        qTd = q.transpose(2, 1, 0)
        glg = proj[b, tok, OFF["ngl"] + g * 12:OFF["ngl"] + (g + 1) * 12].reshape(NQB, 128, 12).transpose(1, 0, 2)
        gate = proj[b, tok, OFF["ng"] + g * 256:OFF["ng"] + (g + 1) * 256].reshape(NQB, 128, 256).transpose(1, 0, 2)
        cmpm = C["cmpm"][:, par::2, :]
        W12 = C["W12"][:, :, (2 - 2 * par):(2 - 2 * par) + 253]
        f = lambda a: np.ascontiguousarray(a, dtype=np.float32)
        maps.append({
            "kT4": f(kT4), "vtok2": f(vtok2), "qTd": f(qTd), "gld": f(glg),
            "gbd": f(np.broadcast_to(p["nsa_gate_b"][l][None, g * 12:(g + 1) * 12], (128, 12))),
            "gated": f(gate), "qn": f(p["nsa_qn"][l][:, None]), "kn": f(p["nsa_kn"][l].T),
            "posT": f(p["nsa_cmp_pos"][l].transpose(2, 0, 1)),
            "w1d": f(p["nsa_cmp_w1"][l].reshape(2, 32, 64, 256).transpose(0, 2, 1, 3)),
            "b1d": f(p["nsa_cmp_b1"][l].reshape(2, 2, 128).transpose(2, 0, 1)),
            "w2d": f(p["nsa_cmp_w2"][l].reshape(2, 2, 128, 64).transpose(2, 0, 1, 3)),
            "b2k": f(p["nsa_cmp_b2"][l][0][:, None]),
            "b2v": f(np.broadcast_to(p["nsa_cmp_b2"][l][1][None, :], (128, 64))),
            "ones64": C["ones64"], "E": C["E"], "mw": f(C["mw"][par]), "ms": f(C["ms"][par]), "ident": C["ident"],
            "cmpm": f(cmpm), "ovl": C["ovl"], "W12": f(W12),
        })
    return maps


def nsa_gather(res):
    out = np.zeros((BATCH, T, 512), np.float32)
    for c in range(NCORES):
        b, rem = divmod(c, 4)
        g, par = divmod(rem, 2)
        yc = res[c]["y"].transpose(1, 0, 2)
        o = out[b].reshape(NTILE, 128, 512)
        o[par::2, :, g * 256:(g + 1) * 256] = yc
    return out


_CACHE = {}


def _prog(name, fn):
    if name not in _CACHE:
        _CACHE[name] = fn()
    return _CACHE[name]


def _x_to_T(xc):
    return np.ascontiguousarray(xc.T.reshape(KC, 128, -1).transpose(1, 0, 2))


def _fm(a, nchunk):
    af = a.reshape(BATCH * T, nchunk, 128)
    return [np.ascontiguousarray(af[c * NT:(c + 1) * NT].transpose(2, 1, 0)) for c in range(NCORES)]


def _win_layout(w):
    wp = np.zeros((D_MODEL, D_IN_PAD), np.float32)
    wp[:, :D_IN] = w
    return np.ascontiguousarray(wp.reshape(KC, 128, NFC, 128).transpose(2, 1, 0, 3))


def kernel(**p):
    p = {k: np.asarray(v, dtype=np.float32) for k, v in p.items()}
    x = p["x"]
    xT = [_x_to_T(x.reshape(-1, D_MODEL)[c * NT:(c + 1) * NT]) for c in range(NCORES)]
    proj = None
    mixers = None
    for l in range(DEPTH + 1):
        do_out, do_in = l > 0, l < DEPTH
        maps = [{"xT": xT[c]} for c in range(NCORES)]
        if do_out:
            lo = l - 1
            y_gla, y_ssm, y_nsa = mixers
            mix = np.concatenate([y_gla, np.zeros_like(y_ssm), y_nsa], -1)
            mixT = _fm(mix, 8)
            yssmT = _fm(y_ssm, 2)
            uT = _fm(proj[:, :, OFF["su"]:OFF["su"] + 256], 2)
            sgT = _fm(proj[:, :, OFF["sg"]:OFF["sg"] + 256], 2)
            wout = np.ascontiguousarray(p["w_out"][lo].reshape(KC, 128, D_MODEL).transpose(1, 0, 2))
            dsk = np.ascontiguousarray(p["s5_d"][lo].reshape(2, 128).T)
            gluw = np.ascontiguousarray(p["s5_glu_w"][lo].reshape(2, 128, 512).transpose(1, 0, 2))
            glub = np.ascontiguousarray(p["s5_glu_b"][lo].reshape(4, 128).T)
            for c in range(NCORES):
                maps[c].update({"mixT": mixT[c], "wout": wout, "yssmT": yssmT[c], "uT": uT[c], "sgT": sgT[c],
                                "dsk": dsk, "gluw": gluw, "glub": glub})
        if do_in:
            win = _win_layout(p["w_in"][l])
            gin = np.ascontiguousarray(p["norm_g"][l].reshape(KC, 128).T)
            for c in range(NCORES):
                maps[c].update({"win": win, "gin": gin})
        res = _run(_prog(f"op{int(do_out)}{int(do_in)}", lambda: build_op(do_out, do_in)), maps)
        if do_out:
            xT = [res[c]["xoT"] for c in range(NCORES)]
        if not do_in:
            break
        proj = np.concatenate([res[c]["projT"].reshape(D_IN_PAD, NT)[:D_IN].T for c in range(NCORES)], 0)
        proj = proj.reshape(BATCH, T, D_IN)
        y_gla = gla_gather(_run(_prog("gla", build_gla), gla_inputs(proj, p, l)))
        y_ssm = s5_gather(_run(_prog("s5", build_s5), s5_inputs(proj, p, l)))
        y_nsa = nsa_gather(_run(_prog("nsa", build_nsa), nsa_inputs(proj, p, l)))
        mixers = (y_gla, y_ssm, y_nsa)
    out = np.concatenate([xT[c].transpose(2, 1, 0).reshape(NT, D_MODEL) for c in range(NCORES)], 0)
    return out.reshape(BATCH, T, D_MODEL).astype(np.float32)
```

```python
from contextlib import ExitStack
import numpy as np
import concourse.bass as bass
import concourse.mybir as mybir
from concourse.bass_utils import run_bass_kernel_spmd

F32 = mybir.dt.float32
BF16 = mybir.dt.bfloat16
AF = mybir.ActivationFunctionType
ALU = mybir.AluOpType
AX = mybir.AxisListType

NCORES = 8
D_MODEL = 1024
BATCH = 2
SEQ = 8192
DEPTH = 4
D_IN = 3112
D_IN_PAD = 3200
RMS_EPS = 1e-6


class LT:
    def __init__(self, ap, name=""):
        self.ap = ap
        self.name = name
        self.w = None
        self.r = []
        self.dsem = None
        self.dcnt = 0

    def __getitem__(self, idx):
        return self.ap[idx]


class Prog:
    ENGS = ("pe", "dve", "act", "pool", "sp")

    def __init__(self, nc, stack):
        self.nc = nc
        self.stack = stack
        self.q = {e: [] for e in self.ENGS}
        self.sem = {e: stack.enter_context(nc.semaphore("s_" + e)) for e in self.ENGS}
        self.cnt = {e: 0 for e in self.ENGS}
        self.seen = {e: {} for e in self.ENGS}
        self.out_events = []
        self.nsem = 0
        self.ntile = 0

    def sb(self, shape, dt, name=None):
        self.ntile += 1
        name = "sb_" + (name or f"t{self.ntile}")
        t = self.stack.enter_context(self.nc.sbuf_tensor(name, list(shape), dt))
        return LT(t, name)

    def ps(self, shape, dt=F32, name=None):
        self.ntile += 1
        name = "ps_" + (name or f"p{self.ntile}")
        t = self.stack.enter_context(self.nc.psum_tensor(name, list(shape), dt))
        return LT(t, name)

    def sub(self, lt, idx, name=""):
        return LT(lt.ap[idx], name or lt.name)

    def _dsem(self, t):
        if t.dsem is None:
            self.nsem += 1
            t.dsem = self.stack.enter_context(self.nc.semaphore(f"d{self.nsem}"))
        return t.dsem

    def _deps(self, eng, reads, writes):
        evs = []
        for t in reads:
            if t.w is not None:
                evs.append(t.w)
        for t in writes:
            if t.w is not None:
                evs.append(t.w)
            evs.extend(t.r)
        agg = {}
        for sem, val, tile, src in evs:
            if tile is not None:
                val = max(val, tile.dcnt * 16)
            if src == "pe" and eng == "pe":
                continue
            k = id(sem)
            if k not in agg or agg[k][1] < val:
                agg[k] = (sem, val)
        waits = []
        seen = self.seen[eng]
        for k, (sem, val) in agg.items():
            if seen.get(k, 0) >= val:
                continue
            seen[k] = val
            waits.append((sem, val))
        return waits

    def op(self, eng, fn, reads=(), writes=()):
        waits = self._deps(eng, reads, writes)
        self.cnt[eng] += 1
        ev = (self.sem[eng], self.cnt[eng], None, eng)
        self.q[eng].append((waits, fn, (self.sem[eng], 1)))
        for t in reads:
            t.r.append(ev)
        for t in writes:
            t.w = ev
            t.r = []
        return ev

    def dma(self, queue, fn, sbt, reads=(), writes=(), is_out=False):
        waits = self._deps(queue, reads, writes)
        sem = self._dsem(sbt)
        sbt.dcnt += 1
        ev = (sem, sbt.dcnt * 16, sbt, "dma")
        self.q[queue].append((waits, fn, (sem, 16)))
        for t in reads:
            t.r.append(ev)
        for t in writes:
            t.w = ev
            t.r = []
        if is_out:
            self.out_events.append(ev)
        return ev

    def emit(self):
        nc = self.nc
        fin = []
        seen = {}
        for sem, val, tile, _ in self.out_events:
            v = tile.dcnt * 16
            seen[id(sem)] = (sem, v)
        fin = list(seen.values())
        engmap = {"pe": "tensor", "dve": "vector", "act": "scalar", "pool": "gpsimd", "sp": "sync"}
        with nc.Block() as block:
            for e in self.ENGS:
                items = self.q[e]
                extra = fin if e == "sp" else []

                def body(engine, items=items, extra=extra):
                    for waits, fn, inc in items:
                        for sem, val in waits:
                            engine.wait_ge(sem, val)
                        ins = fn(engine)
                        ins.then_inc(inc[0], inc[1])
                    for sem, val in extra:
                        engine.wait_ge(sem, val)

                if items or extra:
                    getattr(block, engmap[e])(body)


def _run(nc, in_maps):
    res = run_bass_kernel_spmd(nc, in_maps, core_ids=list(range(NCORES)))
    return res.results


NT = 2048
KC = 8
NFC = D_IN_PAD // 128
NF32 = 2


def build_op(do_out, do_in):
    nc = bass.Bass("TRN2", target_bir_lowering=False)
    din = lambda name, shape: nc.dram_tensor(name, list(shape), F32, kind="ExternalInput").ap()
    xT = din("xT", [128, KC, NT])
    if do_out:
        mixT = din("mixT", [128, KC, NT])
        wout = din("wout", [128, KC, D_MODEL])
        yssmT = din("yssmT", [128, 2, NT])
        uT = din("uT", [128, 2, NT])
        sgT = din("sgT", [128, 2, NT])
        dskd = din("dsk", [128, 2])
        gluwd = din("gluw", [128, 2, 512])
        glubd = din("glub", [128, 4])
        xoT = nc.dram_tensor("xoT", [128, KC, NT], F32, kind="ExternalOutput").ap()
    if do_in:
        win = din("win", [NFC, 128, KC, 128])
        gin = din("gin", [128, KC])
        projT = nc.dram_tensor("projT", [NFC, 128, NT], F32, kind="ExternalOutput").ap()
    with ExitStack() as st:
        P = Prog(nc, st)
        xs = [P.sb([128, NT], F32, f"x{k}") for k in range(KC)]
        actb = [P.sb([128, NT], BF16, f"ab{k}") for k in range(KC)]
        Fb = [P.sb([128, NT], F32, f"F{i}") for i in range(4)]
        wstg = [P.sb([128, D_MODEL], F32, f"wstg{i}") for i in range(2)]
        banks = [P.ps([128, 512], F32, f"bk{i}") for i in range(8)]
        for k in range(KC):
            P.dma("sp", lambda e, k=k: e.dma_start(out=xs[k][:], in_=xT[:, k, :]), xs[k], writes=[xs[k]])
        nb = 0
        if do_out:
            wob = [P.sb([128, D_MODEL], BF16, f"wob{k}") for k in range(KC)]
            ge = [P.sb([128, NT], BF16, f"ge{k}") for k in range(2)]
            dsk = P.sb([128, 2], F32, "dsk")
            glub = P.sb([128, 4], F32, "glub")
            gluwf = P.sb([128, 2, 512], F32, "gluwf")
            gluwb = P.sb([128, 2, 512], BF16, "gluwb")
            P.dma("sp", lambda e: e.dma_start(out=dsk[:], in_=dskd), dsk, writes=[dsk])
            P.dma("sp", lambda e: e.dma_start(out=glub[:], in_=glubd), glub, writes=[glub])
            P.dma("sp", lambda e: e.dma_start(out=gluwf[:], in_=gluwd), gluwf, writes=[gluwf])
            P.op("act", lambda e: e.copy(out=gluwb[:], in_=gluwf[:]), reads=[gluwf], writes=[gluwb])
            for k in range(KC):
                w = wstg[k % 2]
                P.dma("sp", lambda e, k=k, w=w: e.dma_start(out=w[:], in_=wout[:, k, :]), w, writes=[w])
                P.op("act", lambda e, k=k, w=w: e.copy(out=wob[k][:], in_=w[:]), reads=[w], writes=[wob[k]])
            for i, k in enumerate((0, 1, 4, 5, 6, 7)):
                s_ = Fb[i % 2]
                P.dma("pool", lambda e, k=k, s_=s_: e.dma_start(out=s_[:], in_=mixT[:, k, :]), s_, writes=[s_])
                P.op("dve", lambda e, k=k, s_=s_: e.tensor_copy(out=actb[k][:], in_=s_[:]), reads=[s_], writes=[actb[k]])
            F0, F1, F2, F3 = Fb
            for kc in range(2):
                P.dma("pool", lambda e, kc=kc: e.dma_start(out=F0[:], in_=yssmT[:, kc, :]), F0, writes=[F0])
                P.dma("sp", lambda e, kc=kc: e.dma_start(out=F1[:], in_=uT[:, kc, :]), F1, writes=[F1])
                P.op("dve", lambda e, kc=kc: e.scalar_tensor_tensor(
                    out=F0[:], in0=F1[:], scalar=dsk[:, kc:kc + 1], in1=F0[:], op0=ALU.mult, op1=ALU.add),
                    reads=[F0, F1, dsk], writes=[F0])
                P.op("act", lambda e: e.activation(out=F2[:], in_=F0[:], func=AF.Square), reads=[F0], writes=[F2])
                P.op("dve", lambda e: e.tensor_scalar(out=F2[:], in0=F2[:], scalar1=0.044715, scalar2=1.0, op0=ALU.mult,
                                                      op1=ALU.add), reads=[F2], writes=[F2])
                P.op("dve", lambda e: e.tensor_tensor(out=F2[:], in0=F2[:], in1=F0[:], op=ALU.mult), reads=[F2, F0], writes=[F2])
                P.op("act", lambda e: e.activation(out=F2[:], in_=F2[:], func=AF.Tanh, scale=0.7978845608028654),
                     reads=[F2], writes=[F2])
                P.op("dve", lambda e: e.tensor_scalar(out=F2[:], in0=F2[:], scalar1=0.5, scalar2=0.5, op0=ALU.mult,
                                                      op1=ALU.add), reads=[F2], writes=[F2])
                P.op("dve", lambda e, kc=kc: e.tensor_tensor(out=ge[kc][:], in0=F2[:], in1=F0[:], op=ALU.mult),
                     reads=[F2, F0], writes=[ge[kc]])
            for fcp in range(2):
                P.dma("pool", lambda e, fcp=fcp: e.dma_start(out=F1[:], in_=sgT[:, fcp, :]), F1, writes=[F1])
                P.op("act", lambda e: e.activation(out=F1[:], in_=F1[:], func=AF.Silu), reads=[F1], writes=[F1])
                for tt in range(4):
                    tsl = slice(tt * 512, (tt + 1) * 512)
                    bA, bB = banks[nb % 8], banks[(nb + 1) % 8]
                    nb += 2
                    for kc in range(2):
                        P.op("pe", lambda e, kc=kc, fcp=fcp, tsl=tsl, bA=bA: e.matmul(
                            bA[:], gluwb[:, kc, fcp * 128:(fcp + 1) * 128], ge[kc][:, tsl], start=(kc == 0), stop=(kc == 1)),
                            reads=[gluwb, ge[kc]], writes=[bA])
                    for kc in range(2):
                        P.op("pe", lambda e, kc=kc, fcp=fcp, tsl=tsl, bB=bB: e.matmul(
                            bB[:], gluwb[:, kc, (fcp + 2) * 128:(fcp + 3) * 128], ge[kc][:, tsl], start=(kc == 0),
                            stop=(kc == 1)), reads=[gluwb, ge[kc]], writes=[bB])
                    P.op("act", lambda e, fcp=fcp, tsl=tsl, bB=bB: e.activation(
                        out=F3[:, tsl], in_=bB[:], func=AF.Sigmoid, bias=glub[:, fcp + 2:fcp + 3]),
                        reads=[bB, glub], writes=[F3])
                    P.op("dve", lambda e, fcp=fcp, tsl=tsl, bA=bA: e.scalar_tensor_tensor(
                        out=F3[:, tsl], in0=bA[:], scalar=glub[:, fcp:fcp + 1], in1=F3[:, tsl], op0=ALU.add, op1=ALU.mult),
                        reads=[bA, glub, F3], writes=[F3])
                P.op("dve", lambda e, fcp=fcp: e.tensor_tensor(out=actb[2 + fcp][:], in0=F3[:], in1=F1[:], op=ALU.mult),
                     reads=[F3, F1], writes=[actb[2 + fcp]])
            for fo in range(KC):
                for tt in range(NT // 512):
                    bk = banks[nb % 8]
                    nb += 1
                    for k in range(KC):
                        P.op("pe", lambda e, k=k, fo=fo, tt=tt, bk=bk: e.matmul(
                            bk[:], wob[k][:, fo * 128:(fo + 1) * 128], actb[k][:, tt * 512:(tt + 1) * 512],
                            start=(k == 0), stop=(k == KC - 1)),
                            reads=[wob[k], actb[k]], writes=[bk])
                    P.op("dve", lambda e, fo=fo, tt=tt, bk=bk: e.tensor_tensor(
                        out=xs[fo][:, tt * 512:(tt + 1) * 512], in0=bk[:], in1=xs[fo][:, tt * 512:(tt + 1) * 512],
                        op=ALU.add), reads=[bk, xs[fo]], writes=[xs[fo]])
                P.dma("sp", lambda e, fo=fo: e.dma_start(out=xoT[:, fo, :], in_=xs[fo][:]), xs[fo],
                      reads=[xs[fo]], is_out=True)
        if do_in:
            gt = P.sb([128, KC], F32, "gt")
            P.dma("sp", lambda e: e.dma_start(out=gt[:], in_=gin), gt, writes=[gt])
            ones = P.sb([128, 128], F32, "ones")
            P.op("pool", lambda e: e.memset(ones[:], 1.0), writes=[ones])
            rstd = P.sb([128, NT], F32, "rstd")
            sq = Fb[0:2]
            sbk = banks[0:4]
            for k in range(KC):
                s = sq[k % 2]
                P.op("act", lambda e, k=k, s=s: e.activation(out=s[:], in_=xs[k][:], func=AF.Square),
                     reads=[xs[k]], writes=[s])
                for tt in range(4):
                    P.op("pe", lambda e, k=k, s=s, tt=tt: e.matmul(
                        sbk[tt][:], ones[:], s[:, tt * 512:(tt + 1) * 512], start=(k == 0), stop=(k == KC - 1)),
                        reads=[ones, s], writes=[sbk[tt]])
                P.op("dve", lambda e, k=k: e.tensor_scalar(
                    out=actb[k][:], in0=xs[k][:], scalar1=gt[:, k:k + 1], scalar2=None, op0=ALU.mult),
                    reads=[xs[k], gt], writes=[actb[k]])
            epst = P.sb([128, 1], F32, "eps")
            P.op("pool", lambda e: e.memset(epst[:], RMS_EPS), writes=[epst])
            for tt in range(4):
                P.op("act", lambda e, tt=tt: e.activation(
                    out=rstd[:, tt * 512:(tt + 1) * 512], in_=sbk[tt][:], func=AF.Sqrt,
                    scale=1.0 / D_MODEL, bias=epst[:]), reads=[sbk[tt], epst], writes=[rstd])
            P.op("dve", lambda e: e.reciprocal(out=rstd[:], in_=rstd[:]), reads=[rstd], writes=[rstd])
            wb = [P.sb([128, KC, 128], BF16, f"wb{i}") for i in range(2)]
            ot = Fb[2:4]
            nb = 4
            for fc in range(NFC):
                w = wstg[fc % 2]
                wbb = wb[fc % 2]
                o = ot[fc % 2]
                P.dma("pool", lambda e, fc=fc, w=w: e.dma_start(
                    out=w[:], in_=win[fc].rearrange("p k f -> p (k f)")), w, writes=[w])
                f32c = fc < NF32
                w3 = w[:].rearrange("p (k f) -> p k f", k=KC)
                if f32c:
                    P.op("dve", lambda e, w3=w3: e.tensor_tensor(
                        out=w3, in0=w3, in1=gt[:].unsqueeze(2).to_broadcast([128, KC, 128]), op=ALU.mult),
                        reads=[w, gt], writes=[w])
                else:
                    P.op("act", lambda e, w=w, wbb=wbb: e.copy(out=wbb[:].rearrange("p k f -> p (k f)"), in_=w[:]),
                         reads=[w], writes=[wbb])
                for tt in range(4):
                    bk = banks[4 + nb % 4]
                    nb += 1
                    for k in range(KC):
                        if f32c:
                            P.op("pe", lambda e, k=k, tt=tt, bk=bk, w3=w3: e.matmul(
                                bk[:], w3[:, k, :], xs[k][:, tt * 512:(tt + 1) * 512],
                                start=(k == 0), stop=(k == KC - 1)), reads=[w, xs[k]], writes=[bk])
                        else:
                            P.op("pe", lambda e, k=k, tt=tt, bk=bk, wbb=wbb: e.matmul(
                                bk[:], wbb[:, k, :], actb[k][:, tt * 512:(tt + 1) * 512],
                                start=(k == 0), stop=(k == KC - 1)), reads=[wbb, actb[k]], writes=[bk])
                    P.op("dve", lambda e, tt=tt, bk=bk, o=o: e.tensor_tensor(
                        out=o[:, tt * 512:(tt + 1) * 512], in0=bk[:], in1=rstd[:, tt * 512:(tt + 1) * 512],
                        op=ALU.mult), reads=[bk, rstd], writes=[o])
                P.dma("sp", lambda e, fc=fc, o=o: e.dma_start(out=projT[fc], in_=o[:]), o, reads=[o], is_out=True)
        P.emit()
    return nc


T = SEQ
NTILE = T // 128
NCHUNK = T // 128
GLA_G = 4


def gla_consts():
    j = np.arange(128)[:, None]
    i = np.arange(128)[None, :]
    cm = np.zeros((128, 3, 128), np.float32)
    cm[:, 0, :] = (j <= i)
    cm[:, 1, :] = (j > i)
    cm[:, 2, 0:4] = 1.0
    return cm


def build_gla():
    nc = bass.Bass("TRN2", target_bir_lowering=False)
    qT = nc.dram_tensor("qT", [32, T], F32, kind="ExternalInput").ap()
    kT = nc.dram_tensor("kT", [32, T], F32, kind="ExternalInput").ap()
    lrT1 = nc.dram_tensor("lrT1", [17, T], F32, kind="ExternalInput").ap()
    ktok = nc.dram_tensor("ktok", [128, NTILE, 32], F32, kind="ExternalInput").ap()
    vtok = nc.dram_tensor("vtok", [128, NTILE, 64], F32, kind="ExternalInput").ap()
    gtok = nc.dram_tensor("gtok", [128, NTILE, 64], F32, kind="ExternalInput").ap()
    w2b = nc.dram_tensor("w2b", [17, 32], F32, kind="ExternalInput").ap()
    onb = nc.dram_tensor("onb", [128, 64], F32, kind="ExternalInput").ap()
    cm = nc.dram_tensor("cm", [128, 3, 128], F32, kind="ExternalInput").ap()
    y = nc.dram_tensor("y", [128, NTILE, 64], F32, kind="ExternalOutput").ap()
    NG = NTILE // GLA_G
    with ExitStack() as st:
        P = Prog(nc, st)
        cmt = P.sb([128, 3, 128], F32, "cmt")
        w2t = P.sb([17, 32], F32, "w2t")
        onbt = P.sb([128, 64], F32, "onbt")
        ktokt = P.sb([128, NTILE, 32], F32, "ktokt")
        vb = P.sb([128, NTILE, 64], F32, "vb")
        qeT = P.sb([32, T], F32, "qeT")
        scm = P.sb([128, NTILE, 128], F32, "scm")
        kv_all = P.sb([32, NCHUNK, 64], F32, "kv_all")
        S_bf = P.sb([32, NCHUNK + 1, 64], F32, "S_bf")
        dec_all = P.sb([32, NCHUNK], F32, "dec_all")
        epst = P.sb([128, 1], F32, "eps")
        qTg = [P.sb([32, 512], F32, f"qTg{i}") for i in range(2)]
        kTg = [P.sb([32, 512], F32, f"kTg{i}") for i in range(2)]
        lrg = [P.sb([17, 512], F32, f"lrg{i}") for i in range(2)]
        e1 = P.sb([128, 128], F32, "e1")
        lt = P.sb([128, 128], F32, "lt")
        eqTt = P.sb([32, 512], F32, "eqTt")
        ekTt = P.sb([32, 512], F32, "ekTt")
        ekd = P.sb([128, 128], F32, "ekd")
        keT = P.sb([32, 512], F32, "keT")
        kd = P.sb([128, 4, 32], F32, "kd")
        osb = P.sb([128, 4, 64], F32, "osb")
        sq = P.sb([128, 4, 64], F32, "sq")
        ss = P.sb([128, 4], F32, "ss")
        gg = [P.sb([128, 4, 64], F32, f"gg{i}") for i in range(2)]
        yt = [P.sb([128, 4, 64], F32, f"yt{i}") for i in range(2)]
        bz = P.ps([128, 512], F32, "bz")
        zps = P.sub(bz, (slice(None), slice(0, 128)), "zps")
        sups = P.sub(bz, (slice(None), slice(128, 256)), "sups")
        bT = P.ps([32, 512], F32, "bT")
        bL = P.ps([32, 512], F32, "bL")
        bs = P.ps([128, 512], F32, "bs")
        bkv = P.ps([32, 512], F32, "bkv")
        bo = [P.ps([128, 512], F32, f"bo{i}") for i in range(2)]

        P.dma("sp", lambda e: e.dma_start(out=cmt[:], in_=cm), cmt, writes=[cmt])
        P.dma("sp", lambda e: e.dma_start(out=w2t[:], in_=w2b), w2t, writes=[w2t])
        P.dma("sp", lambda e: e.dma_start(out=onbt[:], in_=onb), onbt, writes=[onbt])
        P.dma("sp", lambda e: e.dma_start(out=ktokt[:], in_=ktok), ktokt, writes=[ktokt])
        P.op("pool", lambda e: e.memset(epst[:], RMS_EPS), writes=[epst])
        for i in range(4):
            P.dma("pool", lambda e, i=i: e.dma_start(out=vb[:, i * 16:(i + 1) * 16, :], in_=vtok[:, i * 16:(i + 1) * 16, :]),
                  vb, writes=[vb])
        LTm = cmt[:, 0, :]
        UTm = cmt[:, 1, :]
        BDs = cmt[:, 2, 0:1]
        for g in range(NG):
            qg, kg, lg = qTg[g % 2], kTg[g % 2], lrg[g % 2]
            tok = slice(g * 512, (g + 1) * 512)
            P.dma("sp", lambda e, qg=qg, tok=tok: e.dma_start(out=qg[:], in_=qT[:, tok]), qg, writes=[qg])
            P.dma("sp", lambda e, kg=kg, tok=tok: e.dma_start(out=kg[:], in_=kT[:, tok]), kg, writes=[kg])
            P.dma("sp", lambda e, lg=lg, tok=tok: e.dma_start(out=lg[:], in_=lrT1[:, tok]), lg, writes=[lg])
            for t in range(4):
                P.op("pe", lambda e, t=t, lg=lg: e.matmul(zps[:, t * 32:(t + 1) * 32], lg[:, t * 128:(t + 1) * 128],
                                                        w2t[:], start=True, stop=True), reads=[lg, w2t], writes=[zps])
            P.op("act", lambda e: e.activation(out=e1[:], in_=zps[:], func=AF.Exp, scale=-1.0), reads=[zps], writes=[e1])
            P.op("act", lambda e: e.activation(out=lt[:], in_=e1[:], func=AF.Ln, bias=1.0), reads=[e1], writes=[lt])
            for t in range(4):
                lsl = lt[:, t * 32:(t + 1) * 32]
                P.op("pe", lambda e, t=t, lsl=lsl: e.matmul(sups[:, t * 32:(t + 1) * 32], UTm, lsl, start=True, stop=True),
                     reads=[cmt, lt], writes=[sups])
                P.op("pe", lambda e, t=t, lsl=lsl: e.matmul(bT[:, t * 128:(t + 1) * 128], lsl, LTm, start=True, stop=True),
                     reads=[cmt, lt], writes=[bT])
                P.op("pe", lambda e, t=t, lsl=lsl: e.matmul(bL[:, t:t + 1], lsl, BDs, start=True, stop=True),
                     reads=[cmt, lt], writes=[bL])
            P.op("act", lambda e: e.activation(out=eqTt[:], in_=bT[:], func=AF.Exp, scale=-1.0 / 16), reads=[bT], writes=[eqTt])
            P.op("act", lambda e: e.activation(out=ekTt[:], in_=bT[:], func=AF.Exp, scale=1.0 / 16), reads=[bT], writes=[ekTt])
            P.op("act", lambda e: e.activation(out=ekd[:], in_=sups[:], func=AF.Exp, scale=-1.0 / 16), reads=[sups], writes=[ekd])
            P.op("act", lambda e, g=g: e.activation(out=dec_all[:, 4 * g:4 * g + 4], in_=bL[:, 0:4], func=AF.Exp,
                                                   scale=-1.0 / 16), reads=[bL], writes=[dec_all])
            P.op("dve", lambda e, qg=qg, tok=tok: e.scalar_tensor_tensor(
                out=qeT[:, tok], in0=qg[:], scalar=32 ** -0.5, in1=eqTt[:], op0=ALU.mult, op1=ALU.mult),
                reads=[qg, eqTt], writes=[qeT])
            P.op("dve", lambda e, kg=kg: e.tensor_tensor(out=keT[:], in0=kg[:], in1=ekTt[:], op=ALU.mult),
                 reads=[kg, ekTt], writes=[keT])
            P.op("dve", lambda e, g=g: e.tensor_tensor(
                out=kd[:], in0=ktokt[:, 4 * g:4 * g + 4, :], in1=ekd[:].rearrange("p (t d) -> p t d", t=4), op=ALU.mult),
                reads=[ktokt, ekd], writes=[kd])
            for t in range(4):
                P.op("pe", lambda e, t=t, g=g: e.matmul(
                    bs[:, t * 128:(t + 1) * 128], keT[:, t * 128:(t + 1) * 128],
                    qeT[:, (4 * g + t) * 128:(4 * g + t + 1) * 128], start=True, stop=True),
                    reads=[keT, qeT], writes=[bs])
            for t in range(4):
                P.op("pe", lambda e, t=t, g=g: e.matmul(
                    bkv[:, t * 64:(t + 1) * 64], kd[:, t, :], vb[:, 4 * g + t, :], start=True, stop=True),
                    reads=[kd, vb], writes=[bkv])
            P.op("dve", lambda e, g=g: e.tensor_tensor(
                out=scm[:, 4 * g:4 * g + 4, :], in0=bs[:].rearrange("p (t i) -> p t i", t=4),
                in1=cmt[:, 0:1, :].to_broadcast([128, 4, 128]), op=ALU.mult), reads=[bs, cmt], writes=[scm])
            P.op("act", lambda e, g=g: e.copy(out=kv_all[:, 4 * g:4 * g + 4, :],
                                             in_=bkv[:, 0:256].rearrange("p (c e) -> p c e", c=4)),
                 reads=[bkv], writes=[kv_all])
        P.op("pool", lambda e: e.memset(S_bf[:, 0, :], 0.0), writes=[S_bf])
        for ee in range(64):
            P.op("dve", lambda e, ee=ee: e.tensor_tensor_scan(
                out=S_bf[:, 1:NCHUNK + 1, ee], data0=dec_all[:], data1=kv_all[:, :, ee], initial=0.0,
                op0=ALU.mult, op1=ALU.add), reads=[dec_all, kv_all], writes=[S_bf])
        for g in range(NG):
            b = bo[g % 2]
            gt_, yy = gg[g % 2], yt[g % 2]
            P.dma("sp", lambda e, g=g, gt_=gt_: e.dma_start(out=gt_[:], in_=gtok[:, 4 * g:4 * g + 4, :]), gt_, writes=[gt_])
            for t in range(4):
                P.op("pe", lambda e, t=t, g=g, b=b: e.matmul(
                    b[:, t * 64:(t + 1) * 64], scm[:, 4 * g + t, :], vb[:, 4 * g + t, :], start=True, stop=False),
                    reads=[scm, vb], writes=[b])
                P.op("pe", lambda e, t=t, g=g, b=b: e.matmul(
                    b[:, t * 64:(t + 1) * 64], qeT[:, (4 * g + t) * 128:(4 * g + t + 1) * 128],
                    S_bf[:, 4 * g + t, :], start=False, stop=True), reads=[qeT, S_bf], writes=[b])
            P.op("act", lambda e, b=b: e.copy(out=osb[:], in_=b[:, 0:256].rearrange("p (t e) -> p t e", t=4)),
                 reads=[b], writes=[osb])
            P.op("dve", lambda e: e.tensor_tensor(out=sq[:], in0=osb[:], in1=osb[:], op=ALU.mult), reads=[osb], writes=[sq])
            P.op("dve", lambda e: e.tensor_reduce(out=ss[:], in_=sq[:], axis=AX.X, op=ALU.add), reads=[sq], writes=[ss])
            P.op("act", lambda e: e.activation(out=ss[:], in_=ss[:], func=AF.Sqrt, scale=1.0 / 64, bias=epst[:]),
                 reads=[ss, epst], writes=[ss])
            P.op("dve", lambda e: e.reciprocal(out=ss[:], in_=ss[:]), reads=[ss], writes=[ss])
            P.op("act", lambda e, gt_=gt_: e.activation(out=gt_[:], in_=gt_[:], func=AF.Silu), reads=[gt_], writes=[gt_])
            P.op("dve", lambda e, gt_=gt_: e.tensor_tensor(
                out=gt_[:], in0=gt_[:], in1=onbt[:].unsqueeze(1).to_broadcast([128, 4, 64]), op=ALU.mult),
                reads=[gt_, onbt], writes=[gt_])
            P.op("dve", lambda e: e.tensor_tensor(
                out=osb[:], in0=osb[:], in1=ss[:].unsqueeze(2).to_broadcast([128, 4, 64]), op=ALU.mult),
                reads=[osb, ss], writes=[osb])
            P.op("dve", lambda e, gt_=gt_, yy=yy: e.tensor_tensor(out=yy[:], in0=osb[:], in1=gt_[:], op=ALU.mult),
                 reads=[osb, gt_], writes=[yy])
            P.dma("sp", lambda e, g=g, yy=yy: e.dma_start(out=y[:, 4 * g:4 * g + 4, :], in_=yy[:]), yy,
                  reads=[yy], is_out=True)
        P.emit()
    return nc


OFF = dict(gq=0, gk=128, gv=256, glr=512, gg=528, su=784, sg=1040, nq=1296, nkv=1808, ngl=2576, ng=2600)


def gla_inputs(proj, p, l):
    cm = gla_consts()
    maps = []
    for c in range(NCORES):
        b, h = divmod(c, 4)
        q = proj[b, :, OFF["gq"] + h * 32:OFF["gq"] + (h + 1) * 32]
        k = proj[b, :, OFF["gk"] + h * 32:OFF["gk"] + (h + 1) * 32]
        v = proj[b, :, OFF["gv"] + h * 64:OFF["gv"] + (h + 1) * 64]
        lr = proj[b, :, OFF["glr"]:OFF["glr"] + 16]
        gt_ = proj[b, :, OFF["gg"] + h * 64:OFF["gg"] + (h + 1) * 64]
        tokmaj = lambda a: np.ascontiguousarray(a.reshape(NTILE, 128, -1).transpose(1, 0, 2))
        maps.append({
            "qT": np.ascontiguousarray(q.T), "kT": np.ascontiguousarray(k.T),
            "lrT1": np.ascontiguousarray(np.concatenate([lr.T, np.ones((1, T), np.float32)], 0)),
            "ktok": tokmaj(k), "vtok": tokmaj(v), "gtok": tokmaj(gt_),
            "w2b": np.ascontiguousarray(np.concatenate(
                [p["gla_w2"][l][:, h * 32:(h + 1) * 32], p["gla_b2"][l][None, h * 32:(h + 1) * 32]], 0)),
            "onb": np.ascontiguousarray(np.broadcast_to(p["gla_onorm"][l][None, :], (128, 64))),
            "cm": cm,
        })
    return maps


def gla_gather(res):
    out = np.zeros((BATCH, T, 256), np.float32)
    for c in range(NCORES):
        b, h = divmod(c, 4)
        out[b, :, h * 64:(h + 1) * 64] = res[c]["y"].transpose(1, 0, 2).reshape(T, 64)
    return out


S5L = 64
S5N = T // S5L


def s5_consts():
    kio = np.broadcast_to(np.arange(65, dtype=np.float32)[None, :], (128, 65)).copy()
    r = np.arange(128)
    dmask = ((r[None, :] // 16) >= (r[:, None] // 16)).astype(np.float32)
    Wm = np.concatenate([np.zeros((128, 7 * 128), np.float32), dmask, np.ones((128, 7 * 128), np.float32)], 1)
    ident = np.eye(128, dtype=np.float32)
    return kio, Wm, ident


def _fact(n):
    f = 1.0
    for i in range(2, n + 1):
        f *= i
    return f


def build_s5():
    nc = bass.Bass("TRN2", target_bir_lowering=False)
    U = nc.dram_tensor("U", [2, 128, 8, 256], F32, kind="ExternalInput").ap()
    lam = nc.dram_tensor("lam", [128, 2], F32, kind="ExternalInput").ap()
    lst = nc.dram_tensor("lst", [128, 1], F32, kind="ExternalInput").ap()
    Bri = nc.dram_tensor("Bri", [128, 2, 16], F32, kind="ExternalInput").ap()
    Cri = nc.dram_tensor("Cri", [128, 2, 16], F32, kind="ExternalInput").ap()
    kio_d = nc.dram_tensor("kio", [128, 65], F32, kind="ExternalInput").ap()
    Wm_d = nc.dram_tensor("Wm", [128, 1920], F32, kind="ExternalInput").ap()
    id_d = nc.dram_tensor("ident", [128, 128], F32, kind="ExternalInput").ap()
    Y = nc.dram_tensor("Y", [2, 128, 8, 256], F32, kind="ExternalOutput").ap()
    with ExitStack() as st:
        P = Prog(nc, st)
        col = lambda name, w=1: P.sb([128, w], F32, name)

        def load(name, shape, src):
            t = P.sb(shape, F32, name)
            P.dma("sp", lambda e: e.dma_start(out=t[:], in_=src), t, writes=[t])
            return t

        lamt = load("lamt", [128, 2], lam)
        lstt = load("lstt", [128, 1], lst)
        Bt = load("Bt", [128, 2, 16], Bri)
        Ct = load("Ct", [128, 2, 16], Cri)
        kio = load("kiot", [128, 65], kio_d)
        Wm = load("Wmt", [128, 1920], Wm_d)
        ident = load("identt", [128, 128], id_d)
        Ut = []
        for gl in range(2):
            t = P.sb([128, 8, 256], F32, f"U{gl}")
            P.dma("pool", lambda e, t=t, gl=gl: e.dma_start(out=t[:], in_=U[gl]), t, writes=[t])
            Ut.append(t)

        def ts(out, in0, s1, op0, s2=None, op1=None, r=(), w=()):
            if op1 is None:
                P.op("dve", lambda e: e.tensor_scalar(out=out, in0=in0, scalar1=s1, scalar2=None, op0=op0), reads=r, writes=w)
            else:
                P.op("dve", lambda e: e.tensor_scalar(out=out, in0=in0, scalar1=s1, scalar2=s2, op0=op0, op1=op1),
                     reads=r, writes=w)

        def tt(out, a, b, op, r=(), w=()):
            P.op("dve", lambda e: e.tensor_tensor(out=out, in0=a, in1=b, op=op), reads=r, writes=w)

        def stt(out, in0, sc, in1, op0, op1, r=(), w=()):
            P.op("dve", lambda e: e.scalar_tensor_tensor(out=out, in0=in0, scalar=sc, in1=in1, op0=op0, op1=op1),
                 reads=r, writes=w)

        def horner(name, xs, coefs):
            acc = col(name)
            P.op("pool", lambda e: e.memset(acc[:], float(coefs[-1])), writes=[acc])
            for c in reversed(coefs[:-1]):
                ts(acc[:], acc[:], xs[:, 0:1], ALU.mult, float(c), ALU.add, r=[acc, xs], w=[acc])
            return acc

        y4 = col("y4")
        ts(y4[:], lstt[:], 0.25, ALU.mult, r=[lstt], w=[y4])
        dt = horner("dt", y4, [1.0 / _fact(k) for k in range(19)])
        tt(dt[:], dt[:], dt[:], ALU.mult, r=[dt], w=[dt])
        tt(dt[:], dt[:], dt[:], ALU.mult, r=[dt], w=[dt])
        lr = col("lr")
        ts(lr[:], lamt[:, 0:1], -1e-4, ALU.min, r=[lamt], w=[lr])
        li = col("li")
        ts(li[:], lamt[:, 1:2], 1.0, ALU.mult, r=[lamt], w=[li])
        xx = col("xx")
        tt(xx[:], lr[:], dt[:], ALU.mult, r=[lr, dt], w=[xx])
        negx = col("negx")
        ts(negx[:], xx[:], -1.0, ALU.mult, r=[xx], w=[negx])
        q = horner("q", xx, [1.0 / _fact(k + 1) for k in range(10)])
        em1 = col("em1")
        tt(em1[:], q[:], xx[:], ALU.mult, r=[q, xx], w=[em1])
        mag = col("mag")
        ts(mag[:], em1[:], 1.0, ALU.add, r=[em1], w=[mag])
        phi = col("phi")
        stt(phi[:], li[:], 1.0 / 32, dt[:], ALU.mult, ALU.mult, r=[li, dt], w=[phi])
        ww = col("ww")
        tt(ww[:], phi[:], phi[:], ALU.mult, r=[phi], w=[ww])
        ps_ = horner("ps", ww, [(-1.0) ** k / _fact(2 * k + 1) for k in range(8)])
        pc_ = horner("pc", ww, [(-1.0) ** (k + 1) / _fact(2 * k + 2) for k in range(8)])
        sA = col("sA")
        tt(sA[:], ps_[:], phi[:], ALU.mult, r=[ps_, phi], w=[sA])
        cA = col("cA")
        tt(cA[:], pc_[:], ww[:], ALU.mult, r=[pc_, ww], w=[cA])
        sB, cB, a1, s2 = col("sB"), col("cB"), col("a1"), col("s2")
        cur = (cA, sA)
        nxt = (cB, sB)
        for _ in range(5):
            cm_, s_ = cur
            cn, sn = nxt
            ts(a1[:], cm_[:], 2.0, ALU.add, cm_[:, 0:1], ALU.mult, r=[cm_], w=[a1])
            tt(s2[:], s_[:], s_[:], ALU.mult, r=[s_], w=[s2])
            tt(cn[:], a1[:], s2[:], ALU.subtract, r=[a1, s2], w=[cn])
            ts(sn[:], cm_[:], 1.0, ALU.add, s_[:, 0:1], ALU.mult, r=[cm_, s_], w=[sn])
            ts(sn[:], sn[:], 2.0, ALU.mult, r=[sn], w=[sn])
            cur, nxt = nxt, cur
        cm_, s_ = cur
        cc = col("cc")
        ts(cc[:], cm_[:], 1.0, ALU.add, r=[cm_], w=[cc])
        ai = col("ai")
        tt(ai[:], mag[:], s_[:], ALU.mult, r=[mag, s_], w=[ai])
        am1r = col("am1r")
        tt(am1r[:], mag[:], cm_[:], ALU.mult, r=[mag, cm_], w=[am1r])
        tt(am1r[:], am1r[:], em1[:], ALU.add, r=[am1r, em1], w=[am1r])
        den = col("den")
        tt(den[:], lr[:], lr[:], ALU.mult, r=[lr], w=[den])
        stt(den[:], li[:], li[:, 0:1], den[:], ALU.mult, ALU.add, r=[li, den], w=[den])
        P.op("dve", lambda e: e.reciprocal(out=den[:], in_=den[:]), reads=[den], writes=[den])
        u1, u2, fr, fi = col("u1"), col("u2"), col("fr"), col("fi")
        tt(u1[:], am1r[:], lr[:], ALU.mult, r=[am1r, lr], w=[u1])
        stt(u1[:], ai[:], li[:, 0:1], u1[:], ALU.mult, ALU.add, r=[ai, li, u1], w=[u1])
        tt(fr[:], u1[:], den[:], ALU.mult, r=[u1, den], w=[fr])
        tt(u2[:], am1r[:], li[:], ALU.mult, r=[am1r, li], w=[u2])
        stt(u2[:], ai[:], lr[:, 0:1], u2[:], ALU.mult, ALU.subtract, r=[ai, lr, u2], w=[u2])
        tt(fi[:], u2[:], den[:], ALU.mult, r=[u2, den], w=[fi])
        Bb = P.sb([128, 2, 16], F32, "Bb")
        v1 = col("v1", 16)
        ts(v1[:], Bt[:, 1, :], fi[:, 0:1], ALU.mult, r=[Bt, fi], w=[v1])
        stt(Bb[:, 0, :], Bt[:, 0, :], fr[:, 0:1], v1[:], ALU.mult, ALU.subtract, r=[Bt, fr, v1], w=[Bb])
        ts(v1[:], Bt[:, 0, :], fi[:, 0:1], ALU.mult, r=[Bt, fi], w=[v1])
        stt(Bb[:, 1, :], Bt[:, 1, :], fr[:, 0:1], v1[:], ALU.mult, ALU.add, r=[Bt, fr, v1], w=[Bb])
        Er, Ei = col("Er", 65), col("Ei", 65)
        t1, t2 = col("t1", 32), col("t2", 32)
        P.op("pool", lambda e: e.memset(Er[:, 0:1], 1.0), writes=[Er])
        P.op("pool", lambda e: e.memset(Ei[:, 0:1], 0.0), writes=[Ei])
        ts(Er[:, 1:2], cc[:], 1.0, ALU.mult, r=[cc], w=[Er])
        ts(Ei[:, 1:2], s_[:], 1.0, ALU.mult, r=[s_], w=[Ei])
        m = 1
        while m <= 32:
            er, ei = Er[:, m:m + 1], Ei[:, m:m + 1]
            ts(t1[:, 0:m], Ei[:, 1:m + 1], ei, ALU.mult, r=[Ei], w=[t1])
            ts(t2[:, 0:m], Ei[:, 1:m + 1], er, ALU.mult, r=[Ei, Er], w=[t2])
            stt(Ei[:, m + 1:2 * m + 1], Er[:, 1:m + 1], ei, t2[:, 0:m], ALU.mult, ALU.add, r=[Er, Ei, t2], w=[Ei])
            stt(Er[:, m + 1:2 * m + 1], Er[:, 1:m + 1], er, t1[:, 0:m], ALU.mult, ALU.subtract, r=[Er, t1], w=[Er])
            m *= 2
        magk, imagk = col("magk", 65), col("imagk", 65)
        P.op("act", lambda e: e.activation(out=magk[:], in_=kio[:], func=AF.Exp, scale=xx[:, 0:1]), reads=[kio, xx], writes=[magk])
        P.op("act", lambda e: e.activation(out=imagk[:], in_=kio[:], func=AF.Exp, scale=negx[:, 0:1]), reads=[kio, negx],
             writes=[imagk])
        Pr, Pi, Qr, Qi = col("Pr", 65), col("Pi", 65), col("Qr", 65), col("Qi", 65)
        tt(Pr[:], Er[:], magk[:], ALU.mult, r=[Er, magk], w=[Pr])
        tt(Pi[:], Ei[:], magk[:], ALU.mult, r=[Ei, magk], w=[Pi])
        tt(Qr[:], Er[:], imagk[:], ALU.mult, r=[Er, imagk], w=[Qr])
        stt(Qi[:], Ei[:], -1.0, imagk[:], ALU.mult, ALU.mult, r=[Ei, imagk], w=[Qi])
        KBr = P.sb([128, 64, 16], F32, "KBr")
        KBi = P.sb([128, 64, 16], F32, "KBi")
        QCr = P.sb([128, 64, 16], F32, "QCr")
        QCi = P.sb([128, 64, 16], F32, "QCi")
        tmp = P.sb([128, 64, 16], F32, "tmp")
        bj = lambda tl: tl[:, 0:64].unsqueeze(2).to_broadcast([128, 64, 16])
        bc = lambda ap: ap.unsqueeze(1).to_broadcast([128, 64, 16])
        tt(KBr[:], bj(Qr), bc(Bb[:, 0, :]), ALU.mult, r=[Qr, Bb], w=[KBr])
        tt(tmp[:], bj(Qi), bc(Bb[:, 1, :]), ALU.mult, r=[Qi, Bb], w=[tmp])
        tt(KBr[:], KBr[:], tmp[:], ALU.subtract, r=[KBr, tmp], w=[KBr])
        tt(KBi[:], bj(Qr), bc(Bb[:, 1, :]), ALU.mult, r=[Qr, Bb], w=[KBi])
        tt(tmp[:], bj(Qi), bc(Bb[:, 0, :]), ALU.mult, r=[Qi, Bb], w=[tmp])
        tt(KBi[:], KBi[:], tmp[:], ALU.add, r=[KBi, tmp], w=[KBi])
        tt(QCr[:], bj(Pr), bc(Ct[:, 0, :]), ALU.mult, r=[Pr, Ct], w=[QCr])
        tt(tmp[:], bj(Pi), bc(Ct[:, 1, :]), ALU.mult, r=[Pi, Ct], w=[tmp])
        tt(QCr[:], QCr[:], tmp[:], ALU.subtract, r=[QCr, tmp], w=[QCr])
        tt(QCi[:], bj(Pr), bc(Ct[:, 1, :]), ALU.mult, r=[Pr, Ct], w=[QCi])
        tt(tmp[:], bj(Pi), bc(Ct[:, 0, :]), ALU.mult, r=[Pi, Ct], w=[tmp])
        stt(QCi[:], QCi[:], -1.0, tmp[:], ALU.mult, ALU.subtract, r=[QCi, tmp], w=[QCi])
        fl = lambda tl: tl[:].rearrange("p j c -> p (j c)")
        banks = [P.ps([128, 512], F32, f"bk{i}") for i in range(8)]
        nb = 0
        TZ = [P.sb([128, 8, 1024], F32, f"TZ{gl}") for gl in range(2)]
        for gl in range(2):
            rows = slice(64 * gl, 64 * gl + 64)
            for rc in range(8):
                for ch in range(2):
                    if 4 * ch + 3 < rc:
                        continue
                    bk = banks[nb % 4]
                    nb += 1
                    P.op("pe", lambda e, bk=bk, rows=rows, rc=rc, ch=ch: e.matmul(
                        bk[:], fl(KBr)[rows, rc * 128:(rc + 1) * 128], fl(QCr)[rows, ch * 512:(ch + 1) * 512],
                        start=True, stop=False), reads=[KBr, QCr], writes=[bk])
                    P.op("pe", lambda e, bk=bk, rows=rows, rc=rc, ch=ch: e.matmul(
                        bk[:], fl(KBi)[rows, rc * 128:(rc + 1) * 128], fl(QCi)[rows, ch * 512:(ch + 1) * 512],
                        start=False, stop=True), reads=[KBi, QCi], writes=[bk])
                    w0 = (7 - rc) * 128 + ch * 512
                    P.op("dve", lambda e, bk=bk, gl=gl, rc=rc, ch=ch, w0=w0: e.tensor_tensor(
                        out=TZ[gl][:, rc, ch * 512:(ch + 1) * 512], in0=bk[:], in1=Wm[:, w0:w0 + 512], op=ALU.mult),
                        reads=[bk, Wm], writes=[TZ[gl]])
        KBT = [[P.sb([128, 8, 64], F32, f"KBT{gl}{ri}") for ri in range(2)] for gl in range(2)]
        for gl in range(2):
            rows = slice(64 * gl, 64 * gl + 64)
            for ri, src in enumerate((KBr, KBi)):
                bk = banks[4 + (2 * gl + ri) % 2]
                for kc in range(8):
                    P.op("pe", lambda e, bk=bk, rows=rows, kc=kc, src=src: e.transpose(
                        bk[:, kc * 64:(kc + 1) * 64], fl(src)[rows, kc * 128:(kc + 1) * 128], ident[rows, rows]),
                        reads=[src, ident], writes=[bk])
                P.op("act", lambda e, bk=bk, gl=gl, ri=ri: e.copy(
                    out=KBT[gl][ri][:], in_=bk[:].rearrange("p (k c) -> p k c", k=8)), reads=[bk], writes=[KBT[gl][ri]])
        bx = banks[6]
        for gl in range(2):
            rows = slice(64 * gl, 64 * gl + 64)
            for ri in range(2):
                for kc in range(8):
                    P.op("pe", lambda e, gl=gl, rows=rows, ri=ri, kc=kc: e.matmul(
                        bx[rows, ri * 256:(ri + 1) * 256], KBT[gl][ri][:, kc, :], Ut[gl][:, kc, :],
                        start=(kc == 0), stop=(kc == 7)), reads=[KBT[gl][ri], Ut[gl]], writes=[bx])
        Wr, Wi = col("Wr", 256), col("Wi", 256)
        Xs = col("Xs", 512)
        P.op("act", lambda e: e.copy(out=Xs[:], in_=bx[:]), reads=[bx], writes=[Xs])
        Ar, Ai = Pr[:, 64:65], Pi[:, 64:65]
        ts(Wr[:], Xs[:, 256:512], Ai, ALU.mult, r=[Xs, Pi], w=[Wr])
        stt(Wr[:], Xs[:, 0:256], Ar, Wr[:], ALU.mult, ALU.subtract, r=[Xs, Pr, Wr], w=[Wr])
        ts(Wi[:], Xs[:, 0:256], Ai, ALU.mult, r=[Xs, Pi], w=[Wi])
        stt(Wi[:], Xs[:, 256:512], Ar, Wi[:], ALU.mult, ALU.add, r=[Xs, Pr, Wi], w=[Wi])
        Zr, Zi = P.sb([128, 2, 128], F32, "Zr"), P.sb([128, 2, 128], F32, "Zi")
        P.op("pool", lambda e: e.memset(Zr[:], 0.0), writes=[Zr])
        P.op("pool", lambda e: e.memset(Zi[:], 0.0), writes=[Zi])
        Ya = (P.sb([128, 2, 128], F32, "Yar"), P.sb([128, 2, 128], F32, "Yai"))
        Yb = (P.sb([128, 2, 128], F32, "Ybr"), P.sb([128, 2, 128], F32, "Ybi"))
        sc1 = P.sb([128, 2, 128], F32, "sc1")
        sc2 = P.sb([128, 2, 128], F32, "sc2")
        Mp = P.sb([128, 7, 2], F32, "Mp")
        ma, mb_ = col("ma"), col("mb")
        ts(Mp[:, 0, 0:1], Ar, 1.0, ALU.mult, r=[Pr], w=[Mp])
        ts(Mp[:, 0, 1:2], Ai, 1.0, ALU.mult, r=[Pi], w=[Mp])
        for j in range(6):
            mr_, mi_ = Mp[:, j, 0:1], Mp[:, j, 1:2]
            tt(ma[:], mr_, mr_, ALU.mult, r=[Mp], w=[ma])
            tt(mb_[:], mi_, mi_, ALU.mult, r=[Mp], w=[mb_])
            tt(Mp[:, j + 1, 0:1], ma[:], mb_[:], ALU.subtract, r=[ma, mb_], w=[Mp])
            stt(Mp[:, j + 1, 1:2], mr_, 2.0, mi_, ALU.mult, ALU.mult, r=[Mp], w=[Mp])
        W3r = Wr[:].rearrange("p (b n) -> p b n", b=2)
        W3i = Wi[:].rearrange("p (b n) -> p b n", b=2)
        src = None
        N_ = S5N
        for j in range(7):
            sft = 1 << j
            dst = Ya if j % 2 == 0 else Yb
            if src is None:
                sr, si, srl, sil = W3r, W3i, [Wr], [Wi]
            else:
                sr, si, srl, sil = src[0][:], src[1][:], [src[0]], [src[1]]
            mr_, mi_ = Mp[:, j, 0:1], Mp[:, j, 1:2]
            lo, hi = slice(0, N_ - sft), slice(sft, N_)
            ts(sc1[:, :, lo], si[:, :, lo], mi_, ALU.mult, r=sil + [Mp], w=[sc1])
            stt(sc1[:, :, lo], sr[:, :, lo], mr_, sc1[:, :, lo], ALU.mult, ALU.subtract, r=srl + [Mp, sc1], w=[sc1])
            ts(sc2[:, :, lo], sr[:, :, lo], mi_, ALU.mult, r=srl + [Mp], w=[sc2])
            stt(sc2[:, :, lo], si[:, :, lo], mr_, sc2[:, :, lo], ALU.mult, ALU.add, r=sil + [Mp, sc2], w=[sc2])
            tt(dst[0][:, :, hi], sr[:, :, hi], sc1[:, :, lo], ALU.add, r=srl + [sc1], w=[dst[0]])
            tt(dst[1][:, :, hi], si[:, :, hi], sc2[:, :, lo], ALU.add, r=sil + [sc2], w=[dst[1]])
            P.op("act", lambda e, dst=dst, sr=sr, sft=sft: e.copy(out=dst[0][:, :, 0:sft], in_=sr[:, :, 0:sft]),
                 reads=srl, writes=[dst[0]])
            P.op("act", lambda e, dst=dst, si=si, sft=sft: e.copy(out=dst[1][:, :, 0:sft], in_=si[:, :, 0:sft]),
                 reads=sil, writes=[dst[1]])
            src = dst
        P.op("act", lambda e: e.copy(out=Zr[:, :, 1:N_], in_=src[0][:, :, 0:N_ - 1]), reads=[src[0]], writes=[Zr])
        P.op("dve", lambda e: e.tensor_copy(out=Zi[:, :, 1:N_], in_=src[1][:, :, 0:N_ - 1]), reads=[src[1]], writes=[Zi])
        Yt = [P.sb([128, 256], F32, f"Yt{i}") for i in range(2)]
        ny = 0
        for gl in range(2):
            rows = slice(64 * gl, 64 * gl + 64)
            for ob in range(8):
                bk = banks[ny % 4]
                yt_ = Yt[ny % 2]
                ny += 1
                for kc in range(ob + 1):
                    P.op("pe", lambda e, bk=bk, gl=gl, kc=kc, ob=ob: e.matmul(
                        bk[:, 0:256], TZ[gl][:, kc, ob * 128:(ob + 1) * 128], Ut[gl][:, kc, :],
                        start=(kc == 0), stop=False), reads=[TZ[gl], Ut[gl]], writes=[bk])
                P.op("pe", lambda e, bk=bk, rows=rows, ob=ob: e.matmul(
                    bk[:, 0:256], fl(QCr)[rows, ob * 128:(ob + 1) * 128], Zr[rows].rearrange("p b n -> p (b n)"),
                    start=False, stop=False), reads=[QCr, Zr], writes=[bk])
                P.op("pe", lambda e, bk=bk, rows=rows, ob=ob: e.matmul(
                    bk[:, 0:256], fl(QCi)[rows, ob * 128:(ob + 1) * 128], Zi[rows].rearrange("p b n -> p (b n)"),
                    start=False, stop=True), reads=[QCi, Zi], writes=[bk])
                P.op("act", lambda e, bk=bk, yt_=yt_: e.copy(out=yt_[:], in_=bk[:, 0:256]), reads=[bk], writes=[yt_])
                P.dma("sp", lambda e, gl=gl, ob=ob, yt_=yt_: e.dma_start(out=Y[gl, :, ob, :], in_=yt_[:]), yt_,
                      reads=[yt_], is_out=True)
        P.emit()
    return nc


def s5_inputs(proj, p, l):
    kio, Wm, ident = s5_consts()
    maps = []
    for c in range(NCORES):
        Us, lam, lst, Bri, Cri = [], [], [], [], []
        for gl in range(2):
            g = 2 * c + gl
            u = proj[:, :, OFF["su"] + g * 16:OFF["su"] + (g + 1) * 16]
            u = u.reshape(BATCH, S5N, 8, 8, 16).transpose(3, 4, 2, 0, 1)
            Us.append(u.reshape(128, 8, 2 * S5N))
            lam.append(np.stack([p["s5_lam_re"][l][g], p["s5_lam_im"][l][g]], 1))
            lst.append(np.full((64, 1), p["s5_log_step"][l][g], np.float32))
            Bri.append(np.stack([p["s5_b_re"][l][g], p["s5_b_im"][l][g]], 1))
            Cri.append(np.stack([p["s5_c_re"][l][g].T, p["s5_c_im"][l][g].T], 1))
        cat = lambda xs: np.ascontiguousarray(np.concatenate(xs, 0).astype(np.float32))
        maps.append({"U": np.ascontiguousarray(np.stack(Us, 0)), "lam": cat(lam), "lst": cat(lst),
                     "Bri": cat(Bri), "Cri": cat(Cri), "kio": kio, "Wm": Wm, "ident": ident})
    return maps


def s5_gather(res):
    out = np.zeros((BATCH, T, 256), np.float32)
    for c in range(NCORES):
        Yc = res[c]["Y"]
        for gl in range(2):
            g = 2 * c + gl
            a = Yc[gl].reshape(8, 16, 8, BATCH, S5N).transpose(3, 4, 2, 0, 1)
            out[:, :, g * 16:(g + 1) * 16] = a.reshape(BATCH, T, 16)
    return out


NQB = 32
NEGB = -30000.0


def nsa_consts():
    r = np.arange(128)
    kl, ql = r[:, None], r[None, :]
    mdiag = np.where(kl <= ql, 0.0, NEGB).astype(np.float32)
    mfar = np.where(kl > ql, 0.0, NEGB).astype(np.float32)
    mall = np.full((128, 128), NEGB, np.float32)
    mzero = np.zeros((128, 128), np.float32)
    mw = [np.stack([mfar, mzero, mzero, mzero, mdiag, mall], 1),
          np.stack([mall, mfar, mzero, mzero, mzero, mdiag], 1)]
    ms = [np.stack([mdiag, mall], 1), np.stack([mzero, mdiag], 1)]
    cmpm = np.zeros((128, 16, 128), np.float32)
    for v in range(16):
        cmpm[:, v, :] = np.where(16 * (kl - 8 * v) + 31 <= ql, 0.0, NEGB)
    c = np.arange(512)[:, None]
    s = np.arange(128)[None, :]
    ovl = ((16 * c < 64 * s + 64) & (16 * c + 31 >= 64 * s) & (c < 511)).astype(np.float32)
    ovl = ovl.reshape(4, 128, 128).transpose(1, 0, 2)
    u = np.arange(255)[None, :] - 127
    curl = (r[:, None] >= 64).astype(np.int64)
    forced = (u == curl) | (u == curl - 1)
    invalid = u > curl
    W1 = np.where(forced | invalid, 0.0, 1.0)
    W2 = np.where(invalid, -1e30, np.where(forced, 1e4, 0.0))
    W12 = np.stack([W1, W2], 1).astype(np.float32)
    E = (np.arange(8192)[None, :] // 64 == r[:, None]).astype(np.float32)
    return dict(mw=mw, ms=ms, cmpm=cmpm, ovl=np.ascontiguousarray(ovl), W12=W12, E=E,
                ident=np.eye(128, dtype=np.float32), ones64=np.ones((64, 64), np.float32))


def build_nsa(bcast_rhs=True, nblk=NQB):
    nc = bass.Bass("TRN2", target_bir_lowering=False)
    din = lambda name, shape: nc.dram_tensor(name, list(shape), F32, kind="ExternalInput").ap()
    kT4 = din("kT4", [4, 64, T])
    vtok2 = din("vtok2", [2, 128, NTILE, 64])
    qTd = din("qTd", [64, 4, NQB * 128])
    gld = din("gld", [128, NQB, 12])
    gbd = din("gbd", [128, 12])
    gated = din("gated", [128, NQB, 256])
    qnd = din("qn", [64, 1])
    knd = din("kn", [64, 3])
    posd = din("posT", [64, 2, 32])
    w1d = din("w1d", [2, 64, 32, 256])
    b1d = din("b1d", [128, 2, 2])
    w2d = din("w2d", [128, 2, 2, 64])
    b2kd = din("b2k", [64, 1])
    b2vd = din("b2v", [128, 64])
    ones64d = din("ones64", [64, 64])
    Ed = din("E", [128, T])
    mwd = din("mw", [128, 6, 128])
    msd = din("ms", [128, 2, 128])
    identd = din("ident", [128, 128])
    cmpmd = din("cmpm", [128, 8, 128])
    ovld = din("ovl", [128, 4, 128])
    W12d = din("W12", [128, 2, 253])
    y = nc.dram_tensor("y", [128, NQB, 256], F32, kind="ExternalOutput").ap()
    with ExitStack() as st:
        P = Prog(nc, st)
        dq = ["sp", "pool"]
        ndq = [0]

        def load(name, shape, src, dt=F32):
            t = P.sb(shape, dt, name)
            qn_ = dq[ndq[0] % 2]
            ndq[0] += 1
            P.dma(qn_, lambda e: e.dma_start(out=t[:], in_=src), t, writes=[t])
            return t

        stg = [P.sb([128, 2048], F32, f"stg{i}") for i in range(2)]
        nst = [0]

        def load_cast(dst_ap, dst_lt, src, shape_p, ncols, eng=None):
            s = stg[nst[0] % 2]
            eng = eng or ("dve" if nst[0] % 2 == 0 else "act")
            qn_ = dq[nst[0] % 2]
            nst[0] += 1
            P.dma(qn_, lambda e: e.dma_start(out=s[0:shape_p, 0:ncols], in_=src), s, writes=[s])
            if eng == "dve":
                P.op("dve", lambda e: e.tensor_copy(out=dst_ap, in_=s[0:shape_p, 0:ncols]), reads=[s], writes=[dst_lt])
            else:
                P.op("act", lambda e: e.copy(out=dst_ap, in_=s[0:shape_p, 0:ncols]), reads=[s], writes=[dst_lt])

        banks = [P.ps([128, 512], F32, f"bk{i}") for i in range(8)]
        SB = banks[0:3]
        OC0, OC1, OS, OW, MISC = banks[3], banks[4], banks[5], banks[6], banks[7]
        MSLOT = [OC0, OC1, MISC]
        qn = load("qn", [64, 1], qnd)
        kn = load("kn", [64, 3], knd)
        b1 = load("b1", [128, 2, 2], b1d)
        b2k = load("b2k", [64, 1], b2kd)
        b2v = load("b2v", [128, 64], b2vd)
        ones64 = load("ones64", [64, 64], ones64d)
        identf = load("identf", [128, 128], identd)
        W12 = load("W12", [128, 2, 253], W12d)
        gb = load("gb", [128, 12], gbd)
        gl = load("gl", [128, NQB, 12], gld)
        epst = P.sb([128, 1], F32, "eps")
        P.op("pool", lambda e: e.memset(epst[:], RMS_EPS), writes=[epst])
        qsc = P.sb([64, 1], F32, "qsc")
        P.op("dve", lambda e: e.tensor_scalar(out=qsc[:], in0=qn[:], scalar1=64 ** -0.5, scalar2=None, op0=ALU.mult),
             reads=[qn], writes=[qsc])
        identb = P.sb([128, 128], BF16, "identb")
        P.op("dve", lambda e: e.tensor_copy(out=identb[:], in_=identf[:]), reads=[identf], writes=[identb])
        mwb = P.sb([128, 6, 128], BF16, "mwb")
        load_cast(mwb[:].rearrange("p a b -> p (a b)"), mwb, mwd.rearrange("p a b -> p (a b)"), 128, 768)
        msb = P.sb([128, 2, 128], BF16, "msb")
        load_cast(msb[:].rearrange("p a b -> p (a b)"), msb, msd.rearrange("p a b -> p (a b)"), 128, 256)
        cmpmb = P.sb([128, 8, 128], BF16, "cmpmb")
        load_cast(cmpmb[:].rearrange("p a b -> p (a b)"), cmpmb, cmpmd.rearrange("p a b -> p (a b)"), 128, 1024)
        Eb = P.sb([128, T], BF16, "Eb")
        for i in range(4):
            load_cast(Eb[:, i * 2048:(i + 1) * 2048], Eb, Ed[:, i * 2048:(i + 1) * 2048], 128, 2048)
        P.op("dve", lambda e: e.tensor_tensor(out=gl[:], in0=gl[:], in1=gb[:].unsqueeze(1).to_broadcast([128, NQB, 12]),
                                              op=ALU.add), reads=[gl, gb], writes=[gl])
        P.op("act", lambda e: e.activation(out=gl[:], in_=gl[:], func=AF.Sigmoid), reads=[gl], writes=[gl])
        V1 = [P.sb([128, NTILE, 65], BF16, f"V1_{i}") for i in range(2)]
        for j in range(2):
            P.op("pool", lambda e, j=j: e.memset(V1[j][:, :, 64:65], 1.0), writes=[V1[j]])
            for i in range(2):
                load_cast(V1[j][:, i * 32:(i + 1) * 32, 0:64], V1[j],
                          vtok2[j][:, i * 32:(i + 1) * 32, :].rearrange("p a b -> p (a b)"), 128, 2048)

        rn_sq = [P.sb([64, 512], F32, f"rn_sq{i}") for i in range(4)]
        rn_rt = [P.sb([64, 512], F32, f"rn_rt{i}") for i in range(4)]

        def rms_batch(jobs):
            for i, (src_ap, src_lt, _, _, _, _) in enumerate(jobs):
                P.op("dve", lambda e, i=i, src_ap=src_ap: e.tensor_tensor(out=rn_sq[i][:], in0=src_ap, in1=src_ap, op=ALU.mult),
                     reads=[src_lt], writes=[rn_sq[i]])
            for i in range(len(jobs)):
                P.op("pe", lambda e, i=i: e.matmul(banks[i][0:64, :], ones64[:], rn_sq[i][:], start=True, stop=True),
                     reads=[ones64, rn_sq[i]], writes=[banks[i]])
            for i in range(len(jobs)):
                P.op("act", lambda e, i=i: e.activation(out=rn_rt[i][:], in_=banks[i][0:64, :], func=AF.Ln, scale=1.0 / 64,
                                                        bias=epst[0:64, :]), reads=[banks[i], epst], writes=[rn_rt[i]])
            for i in range(len(jobs)):
                P.op("act", lambda e, i=i: e.activation(out=rn_rt[i][:], in_=rn_rt[i][:], func=AF.Exp, scale=-0.5),
                     reads=[rn_rt[i]], writes=[rn_rt[i]])
            for i, (src_ap, src_lt, scale_ap, scale_lt, dst_ap, dst_lt) in enumerate(jobs):
                P.op("dve", lambda e, i=i, src_ap=src_ap, scale_ap=scale_ap, dst_ap=dst_ap: e.scalar_tensor_tensor(
                    out=dst_ap, in0=src_ap, scalar=scale_ap, in1=rn_rt[i][:], op0=ALU.mult, op1=ALU.mult),
                    reads=[src_lt, scale_lt, rn_rt[i]], writes=[dst_lt])

        def rms_fm(src_ap, src_lt, ncol, scale_ap, scale_lt, dst_ap, dst_lt, bank):
            assert ncol == 512
            rms_batch([(src_ap, src_lt, scale_ap, scale_lt, dst_ap, dst_lt)])

        KT = [P.sb([64, T], BF16, f"KT{i}") for i in range(2)]
        nrm = 0
        for j in range(2):
            for i in range(4):
                s = stg[nst[0] % 2]
                qn_ = dq[nst[0] % 2]
                nst[0] += 1
                P.dma(qn_, lambda e, s=s, j=j, i=i: e.dma_start(out=s[0:64, :], in_=kT4[2 + j][:, i * 2048:(i + 1) * 2048]),
                      s, writes=[s])
                rms_batch([(s[0:64, t * 512:(t + 1) * 512], s, kn[:, 1 + j:2 + j], kn,
                            KT[j][:, i * 2048 + t * 512:i * 2048 + (t + 1) * 512], KT[j]) for t in range(4)])
        kcT = P.sb([64, T + 16], BF16, "kcT")
        P.op("pool", lambda e: e.memset(kcT[:, T:T + 16], 0.0), writes=[kcT])
        w1b = P.sb([64, 32, 256], BF16, "w1b")
        posb = P.sb([64, 2, 32], BF16, "posb")
        load_cast(posb[:].rearrange("p a b -> p (a b)"), posb, posd.rearrange("p a b -> p (a b)"), 64, 64)
        w2f = load("w2f", [128, 2, 2, 64], w2d)
        w2b = P.sb([128, 2, 2, 64], BF16, "w2b")
        P.op("dve", lambda e: e.tensor_copy(out=w2b[:], in_=w2f[:]), reads=[w2f], writes=[w2b])
        hidT = P.sb([128, 2, 512], BF16, "hidT")
        P.op("pool", lambda e: e.memset(hidT[:], 0.0), writes=[hidT])
        biasv = P.sb([128, 2], F32, "biasv")
        hx = P.sb([128, 512], F32, "hx")
        hu = P.sb([128, 512], F32, "hu")
        P.op("pool", lambda e: e.memset(hx[:], 0.0), writes=[hx])
        kcmpT = P.sb([64, 512], BF16, "kcmpT")
        kraw = P.sb([64, 512], F32, "kraw")
        P.op("pool", lambda e: e.memset(kraw[:], 0.0), writes=[kraw])
        Vc1 = P.sb([128, 4, 193], BF16, "Vc1")
        P.op("pool", lambda e: e.memset(Vc1[:, :, 64:65], 1.0), writes=[Vc1])
        load_cast(Vc1[:, :, 65:193], Vc1, ovld.rearrange("p a b -> p (a b)"), 128, 512, eng="dve")
        for kv in range(2):
            for i in range(4):
                load_cast(kcT[:, i * 2048:(i + 1) * 2048], kcT, kT4[kv][:, i * 2048:(i + 1) * 2048], 64, 2048)
            for i in range(4):
                load_cast(w1b[:, i * 8:(i + 1) * 8, :].rearrange("p a b -> p (a b)"), w1b,
                          w1d[kv][:, i * 8:(i + 1) * 8, :].rearrange("p a b -> p (a b)"), 64, 2048)
            for hc in range(2):
                bk = banks[hc]
                for l in range(32):
                    P.op("pe", lambda e, bk=bk, l=l, hc=hc: e.matmul(
                        bk[:, 0:511], w1b[:, l, hc * 128:(hc + 1) * 128], kcT[:, l:l + 16 * 511:16],
                        start=(l == 0), stop=(l == 31)), reads=[w1b, kcT], writes=[bk])
                pb = MISC
                for l in range(32):
                    P.op("pe", lambda e, pb=pb, l=l, hc=hc, kv=kv: e.matmul(
                        pb[:, hc:hc + 1], w1b[:, l, hc * 128:(hc + 1) * 128], posb[:, kv, l:l + 1],
                        start=(l == 0), stop=(l == 31)), reads=[w1b, posb], writes=[pb])
                P.op("dve", lambda e, hc=hc, kv=kv, pb=pb: e.tensor_tensor(
                    out=biasv[:, hc:hc + 1], in0=pb[:, hc:hc + 1], in1=b1[:, kv, hc:hc + 1], op=ALU.add),
                    reads=[pb, b1], writes=[biasv])
                P.op("act", lambda e, bk=bk, hc=hc: e.activation(
                    out=hx[:, 0:511], in_=bk[:, 0:511], func=AF.Identity, bias=biasv[:, hc:hc + 1]),
                    reads=[bk, biasv], writes=[hx])
                P.op("dve", lambda e: e.tensor_tensor(out=hu[:], in0=hx[:], in1=hx[:], op=ALU.mult), reads=[hx], writes=[hu])
                P.op("dve", lambda e: e.tensor_scalar(out=hu[:], in0=hu[:], scalar1=0.044715, scalar2=1.0, op0=ALU.mult,
                                                      op1=ALU.add), reads=[hu], writes=[hu])
                P.op("dve", lambda e: e.tensor_tensor(out=hu[:], in0=hu[:], in1=hx[:], op=ALU.mult), reads=[hu, hx], writes=[hu])
                P.op("act", lambda e: e.activation(out=hu[:], in_=hu[:], func=AF.Tanh, scale=0.7978845608028654),
                     reads=[hu], writes=[hu])
                P.op("dve", lambda e: e.tensor_scalar(out=hu[:], in0=hu[:], scalar1=0.5, scalar2=0.5, op0=ALU.mult,
                                                      op1=ALU.add), reads=[hu], writes=[hu])
                P.op("dve", lambda e, hc=hc: e.tensor_tensor(out=hidT[:, hc, 0:511], in0=hu[:, 0:511], in1=hx[:, 0:511],
                                                            op=ALU.mult), reads=[hu, hx], writes=[hidT])
            if kv == 0:
                bk = banks[2]
                for hc in range(2):
                    P.op("pe", lambda e, bk=bk, hc=hc: e.matmul(
                        bk[0:64, 0:511], w2b[:, 0, hc, :], hidT[:, hc, 0:511], start=(hc == 0), stop=(hc == 1)),
                        reads=[w2b, hidT], writes=[bk])
                P.op("act", lambda e, bk=bk: e.activation(out=kraw[:, 0:511], in_=bk[0:64, 0:511], func=AF.Identity,
                                                          bias=b2k[:, 0:1]), reads=[bk, b2k], writes=[kraw])
                rms_fm(kraw[:, :], kraw, 512, kn[:, 0:1], kn, kcmpT[:, :], kcmpT, banks[0])
            else:
                bk = banks[2]
                for cc in range(4):
                    for hc in range(2):
                        P.op("pe", lambda e, bk=bk, hc=hc, cc=cc: e.matmul(
                            bk[:, cc * 64:(cc + 1) * 64], hidT[:, hc, cc * 128:(cc + 1) * 128], w2b[:, 1, hc, :],
                            start=(hc == 0), stop=(hc == 1)), reads=[w2b, hidT], writes=[bk])
                P.op("dve", lambda e, bk=bk: e.tensor_tensor(
                    out=Vc1[:, :, 0:64], in0=bk[:, 0:256].rearrange("p (c d) -> p c d", c=4),
                    in1=b2v[:].unsqueeze(1).to_broadcast([128, 4, 64]), op=ALU.add), reads=[bk, b2v], writes=[Vc1])
        qTb = P.sb([64, NQB, 4, 128], BF16, "qTb")
        for h in range(4):
            for i in range(2):
                s = stg[nst[0] % 2]
                qn_ = dq[nst[0] % 2]
                nst[0] += 1
                P.dma(qn_, lambda e, s=s, h=h, i=i: e.dma_start(out=s[0:64, :], in_=qTd[:, h, i * 2048:(i + 1) * 2048]),
                      s, writes=[s])
                rms_batch([(s[0:64, t * 512:(t + 1) * 512], s, qsc[:, 0:1], qsc,
                            qTb[:, i * 16 + t * 4:i * 16 + t * 4 + 4, h, :], qTb) for t in range(4)])
        Pt = [P.sb([128, 512], BF16, f"Pt{i}") for i in range(3)]
        npair = [0]
        gt = [P.sb([128, 256], F32, f"gt{i}") for i in range(2)]
        yt = [P.sb([128, 256], F32, f"yt{i}") for i in range(2)]
        rec = P.sb([128, 3, 4], F32, "rec")
        imp = P.sb([128, 128], F32, "imp")
        imp2 = P.sb([128, 128], F32, "imp2")
        m8a = P.sb([128, 8], F32, "m8a")
        m8b = P.sb([128, 8], F32, "m8b")
        self_ = P.sb([128, 128], F32, "sel")
        negT = P.sb([128, 128], BF16, "negT")
        acc = P.sb([128, 4, 64], F32, "acc")
        tmp = P.sb([128, 4, 64], F32, "tmpo")

        items = []

        def pair(kT_ap, kT_lt, qb, biases, V_ap, V_lt, obanks, ow, first, last, pre=(), post=(), mmask=None):
            def front(k):
                S, pt = SB[k % 3], Pt[k % 3]
                if mmask is not None:
                    ms_ = MSLOT[k % 3]
                    P.op("pe", lambda e: e.matmul(ms_[:, 0:128], mmask[0], mmask[2], start=True, stop=True),
                         reads=mmask[1], writes=[ms_])
                P.op("pe", lambda e: e.matmul(S[:], kT_ap, qb, start=True, stop=(len(biases) == 0)),
                     reads=[kT_lt, qTb], writes=[S])
                for bi, (l_ap, lts, r_ap) in enumerate(biases):
                    lastb = bi == len(biases) - 1
                    P.op("pe", lambda e, l_ap=l_ap, r_ap=r_ap, lastb=lastb: e.matmul(
                        S[:], l_ap, r_ap.unsqueeze(1).to_broadcast([128, 4, 128]), start=False, stop=lastb),
                        reads=lts, writes=[S])
                P.op("act", lambda e: e.activation(out=pt[:], in_=S[:], func=AF.Exp), reads=[S], writes=[pt])
                if mmask is not None:
                    P.op("dve", lambda e: e.tensor_tensor(
                        out=pt[:].rearrange("p (h q) -> p h q", h=4), in0=pt[:].rearrange("p (h q) -> p h q", h=4),
                        in1=ms_[:, 0:128].unsqueeze(1).to_broadcast([128, 4, 128]), op=ALU.mult), reads=[pt, ms_], writes=[pt])

            def back(k):
                pt = Pt[k % 3]
                for h in range(4):
                    bk, c0 = obanks[h]
                    st_ = first and (h == 0 or obanks[h][0] is not obanks[h - 1][0])
                    P.op("pe", lambda e, bk=bk, c0=c0, h=h, st_=st_: e.matmul(
                        bk[:, c0:c0 + ow], pt[:, h * 128:(h + 1) * 128], V_ap, start=st_, stop=last),
                        reads=[pt, V_lt], writes=[bk])

            items.append(dict(front=front, back=back, pre=list(pre), post=list(post)))

        oc_b = [(OC0, 0), (OC0, 193), (OC1, 0), (OC1, 193)]
        os_b = [(OS, h * 65) for h in range(4)]
        ow_b = [(OW, h * 65) for h in range(4)]
        ocv = [bk[:, c0:c0 + 193] for bk, c0 in oc_b]

        def gate_load(m):
            g_ = gt[m % 2]
            P.dma("sp", lambda e: e.dma_start(out=g_[:], in_=gated[:, m, :]), g_, writes=[g_])
            P.op("act", lambda e: e.activation(out=g_[:], in_=g_[:], func=AF.Silu), reads=[g_], writes=[g_])

        def topk_chain(m):
            for h in range(4):
                P.op("dve", lambda e, h=h: e.tensor_scalar(out=rec[:, 0, h:h + 1], in0=ocv[h][:, 64:65], scalar1=1e-30,
                                                          scalar2=None, op0=ALU.max), reads=[oc_b[h][0]], writes=[rec])
            P.op("dve", lambda e: e.reciprocal(out=rec[:, 0, :], in_=rec[:, 0, :]), reads=[rec], writes=[rec])
            P.op("dve", lambda e: e.tensor_scalar(out=imp[:], in0=ocv[0][:, 65:193], scalar1=rec[:, 0, 0:1], scalar2=None,
                                                  op0=ALU.mult), reads=[OC0, rec], writes=[imp])
            for h in range(1, 4):
                P.op("dve", lambda e, h=h: e.scalar_tensor_tensor(
                    out=imp[:], in0=ocv[h][:, 65:193], scalar=rec[:, 0, h:h + 1], in1=imp[:], op0=ALU.mult, op1=ALU.add),
                    reads=[oc_b[h][0], rec, imp], writes=[imp])
            P.op("dve", lambda e: e.tensor_tensor(
                out=rec[:, 0, :], in0=rec[:, 0, :], in1=gl[:, m, :].rearrange("p (h b) -> p h b", h=4)[:, :, 0],
                op=ALU.mult), reads=[rec, gl], writes=[rec])
            for hp in range(2):
                bk = oc_b[2 * hp][0]
                P.op("dve", lambda e, bk=bk, hp=hp: e.tensor_tensor(
                    out=acc[:, 2 * hp:2 * hp + 2, :],
                    in0=bk[:, 0:386].rearrange("p (h c) -> p h c", h=2)[:, :, 0:64],
                    in1=rec[:, 0, 2 * hp:2 * hp + 2].unsqueeze(2).to_broadcast([128, 2, 64]), op=ALU.mult),
                    reads=[bk, rec], writes=[acc])
            w0 = 125 - 4 * m
            P.op("dve", lambda e: e.tensor_tensor(out=imp[:], in0=imp[:], in1=W12[:, 0, w0:w0 + 128], op=ALU.mult),
                 reads=[imp, W12], writes=[imp])
            P.op("dve", lambda e: e.tensor_tensor(out=imp[:], in0=imp[:], in1=W12[:, 1, w0:w0 + 128], op=ALU.add),
                 reads=[imp, W12], writes=[imp])
            P.op("dve", lambda e: e.memset(imp[:, 0:1], 1e4), reads=[], writes=[imp])
            P.op("dve", lambda e: e.max(out=m8a[:], in_=imp[:]), reads=[imp], writes=[m8a])
            P.op("dve", lambda e: e.match_replace(out=imp2[:], in_to_replace=m8a[:], in_values=imp[:], imm_value=-3e38),
                 reads=[imp, m8a], writes=[imp2])
            P.op("dve", lambda e: e.max(out=m8b[:], in_=imp2[:]), reads=[imp2], writes=[m8b])
            P.op("dve", lambda e: e.tensor_scalar(out=self_[:], in0=imp[:], scalar1=m8b[:, 7:8], scalar2=None, op0=ALU.is_ge),
                 reads=[imp, m8b], writes=[self_])

        def sel_mask(m):
            P.op("pe", lambda e: e.transpose(MISC[:, 0:128], self_[:], identf[:]), reads=[self_, identf], writes=[MISC])
            P.op("act", lambda e: e.copy(out=negT[:], in_=MISC[:, 0:128]), reads=[MISC], writes=[negT])

        def combine(m):
            g_, yy = gt[m % 2], yt[m % 2]
            for br, ob_ in ((1, os_b), (2, ow_b)):
                bk = ob_[0][0]
                P.op("dve", lambda e, bk=bk, br=br: e.tensor_scalar(
                    out=rec[:, br, :], in0=bk[:, 0:260].rearrange("p (h c) -> p h c", h=4)[:, :, 64],
                    scalar1=1e-30, scalar2=None, op0=ALU.max), reads=[bk], writes=[rec])
                P.op("dve", lambda e, br=br: e.reciprocal(out=rec[:, br, :], in_=rec[:, br, :]), reads=[rec], writes=[rec])
                P.op("dve", lambda e, br=br: e.tensor_tensor(
                    out=rec[:, br, :], in0=rec[:, br, :], in1=gl[:, m, :].rearrange("p (h b) -> p h b", h=4)[:, :, br],
                    op=ALU.mult), reads=[rec, gl], writes=[rec])
                P.op("dve", lambda e, bk=bk, br=br: e.tensor_tensor(
                    out=tmp[:], in0=bk[:, 0:260].rearrange("p (h c) -> p h c", h=4)[:, :, 0:64],
                    in1=rec[:, br, :].unsqueeze(2).to_broadcast([128, 4, 64]), op=ALU.mult),
                    reads=[bk, rec], writes=[tmp])
                P.op("dve", lambda e: e.tensor_tensor(out=acc[:], in0=acc[:], in1=tmp[:], op=ALU.add),
                     reads=[acc, tmp], writes=[acc])
            P.op("dve", lambda e: e.tensor_tensor(
                out=yy[:], in0=acc[:].rearrange("p h d -> p (h d)"), in1=g_[:], op=ALU.mult),
                reads=[acc, g_], writes=[yy])
            P.dma("sp", lambda e: e.dma_start(out=y[:, m, :], in_=yy[:]), yy, reads=[yy], is_out=True)

        for m in range(nblk):
            qb = qTb[:, m, :, :].rearrange("p h q -> p (h q)")
            ccs = m // 8
            for cc in range(ccs + 1):
                biases = []
                if cc == ccs:
                    biases = [(identb[:], [identb, cmpmb], cmpmb[:, m % 8, :])]
                pair(kcmpT[:, cc * 128:(cc + 1) * 128], kcmpT, qb, biases, Vc1[:, cc, :], Vc1, oc_b, 193,
                     cc == 0, cc == ccs, pre=[lambda m=m: gate_load(m)] if cc == 0 else (),
                     post=[lambda m=m: topk_chain(m)] if cc == ccs else ())
            kbs = [kb for kb in range(2 * m - 4, 2 * m + 2) if kb >= 0]
            for kb in kbs:
                o = kb - (2 * m - 4)
                biases = [] if o in (2, 3) else [(identb[:], [identb, mwb], mwb[:, o, :])]
                pair(KT[1][:, kb * 128:(kb + 1) * 128], KT[1], qb, biases, V1[1][:, kb, :], V1[1], ow_b, 65,
                     kb == kbs[0], kb == kbs[-1])
            for kb in range(2 * m + 2):
                biases = []
                if kb >= 2 * m:
                    biases.append((identb[:], [identb, msb], msb[:, kb - 2 * m, :]))
                pair(KT[0][:, kb * 128:(kb + 1) * 128], KT[0], qb, biases, V1[0][:, kb, :], V1[0], os_b, 65,
                     kb == 0, kb == 2 * m + 1, pre=[lambda m=m: sel_mask(m)] if kb == 0 else (),
                     post=[lambda m=m: combine(m)] if kb == 2 * m + 1 else (),
                     mmask=(Eb[:, kb * 128:(kb + 1) * 128], [Eb, negT], negT[:]))
        DEPTH_PIPE = 2
        n_it = len(items)
        for idx in range(n_it + DEPTH_PIPE):
            if idx < n_it:
                for f in items[idx]["pre"]:
                    f()
                items[idx]["front"](idx)
            j = idx - DEPTH_PIPE
            if j >= 0:
                items[j]["back"](j)
                for f in items[j]["post"]:
                    f()
        P.emit()
    return nc


def nsa_inputs(proj, p, l):
    C = nsa_consts()
    maps = []
    nkv = proj[:, :, OFF["nkv"]:OFF["nkv"] + 768].reshape(BATCH, T, 3, 2, 2, 64)
    for c in range(NCORES):
        b, rem = divmod(c, 4)
        g, par = divmod(rem, 2)
        blks = np.arange(NQB) * 2 + par
        tok = (blks[:, None] * 128 + np.arange(128)[None, :]).reshape(-1)
        kT4 = np.stack([nkv[b, :, 0, 0, g].T, nkv[b, :, 0, 1, g].T, nkv[b, :, 1, 0, g].T, nkv[b, :, 2, 0, g].T], 0)
        tokmaj = lambda a: a.reshape(NTILE, 128, -1).transpose(1, 0, 2)
        vtok2 = np.stack([tokmaj(nkv[b, :, 1, 1, g]), tokmaj(nkv[b, :, 2, 1, g])], 0)
        q = proj[b, tok, OFF["nq"] + g * 256:OFF["nq"] + (g + 1) * 256].reshape(-1, 4, 64)
        qTd = q.transpose(2, 1, 0)
        glg = proj[b, tok, OFF["ngl"] + g * 12:OFF["ngl"] + (g + 1) * 12].reshape(NQB, 128, 12).transpose(1, 0, 2)
        gate = proj[b, tok, OFF["ng"] + g * 256:OFF["ng"] + (g + 1) * 256].reshape(NQB, 128, 256).transpose(1, 0, 2)
        cmpm = C["cmpm"][:, par::2, :]
        W12 = C["W12"][:, :, (2 - 2 * par):(2 - 2 * par) + 253]
        f = lambda a: np.ascontiguousarray(a, dtype=np.float32)
        maps.append({
            "kT4": f(kT4), "vtok2": f(vtok2), "qTd": f(qTd), "gld": f(glg),
            "gbd": f(np.broadcast_to(p["nsa_gate_b"][l][None, g * 12:(g + 1) * 12], (128, 12))),
            "gated": f(gate), "qn": f(p["nsa_qn"][l][:, None]), "kn": f(p["nsa_kn"][l].T),
            "posT": f(p["nsa_cmp_pos"][l].transpose(2, 0, 1)),
            "w1d": f(p["nsa_cmp_w1"][l].reshape(2, 32, 64, 256).transpose(0, 2, 1, 3)),
            "b1d": f(p["nsa_cmp_b1"][l].reshape(2, 2, 128).transpose(2, 0, 1)),
            "w2d": f(p["nsa_cmp_w2"][l].reshape(2, 2, 128, 64).transpose(2, 0, 1, 3)),
            "b2k": f(p["nsa_cmp_b2"][l][0][:, None]),
            "b2v": f(np.broadcast_to(p["nsa_cmp_b2"][l][1][None, :], (128, 64))),
            "ones64": C["ones64"], "E": C["E"], "mw": f(C["mw"][par]), "ms": f(C["ms"][par]), "ident": C["ident"],
            "cmpm": f(cmpm), "ovl": C["ovl"], "W12": f(W12),
        })
    return maps


def nsa_gather(res):
    out = np.zeros((BATCH, T, 512), np.float32)
    for c in range(NCORES):
        b, rem = divmod(c, 4)
        g, par = divmod(rem, 2)
        yc = res[c]["y"].transpose(1, 0, 2)
        o = out[b].reshape(NTILE, 128, 512)
        o[par::2, :, g * 256:(g + 1) * 256] = yc
    return out


_CACHE = {}


def _prog(name, fn):
    if name not in _CACHE:
        _CACHE[name] = fn()
    return _CACHE[name]


def _x_to_T(xc):
    return np.ascontiguousarray(xc.T.reshape(KC, 128, -1).transpose(1, 0, 2))


def _fm(a, nchunk):
    af = a.reshape(BATCH * T, nchunk, 128)
    return [np.ascontiguousarray(af[c * NT:(c + 1) * NT].transpose(2, 1, 0)) for c in range(NCORES)]


def _win_layout(w):
    wp = np.zeros((D_MODEL, D_IN_PAD), np.float32)
    wp[:, :D_IN] = w
    return np.ascontiguousarray(wp.reshape(KC, 128, NFC, 128).transpose(2, 1, 0, 3))


def kernel(**p):
    p = {k: np.asarray(v, dtype=np.float32) for k, v in p.items()}
    x = p["x"]
    xT = [_x_to_T(x.reshape(-1, D_MODEL)[c * NT:(c + 1) * NT]) for c in range(NCORES)]
    proj = None
    mixers = None
    for l in range(DEPTH + 1):
        do_out, do_in = l > 0, l < DEPTH
        maps = [{"xT": xT[c]} for c in range(NCORES)]
        if do_out:
            lo = l - 1
            y_gla, y_ssm, y_nsa = mixers
            mix = np.concatenate([y_gla, np.zeros_like(y_ssm), y_nsa], -1)
            mixT = _fm(mix, 8)
            yssmT = _fm(y_ssm, 2)
            uT = _fm(proj[:, :, OFF["su"]:OFF["su"] + 256], 2)
            sgT = _fm(proj[:, :, OFF["sg"]:OFF["sg"] + 256], 2)
            wout = np.ascontiguousarray(p["w_out"][lo].reshape(KC, 128, D_MODEL).transpose(1, 0, 2))
            dsk = np.ascontiguousarray(p["s5_d"][lo].reshape(2, 128).T)
            gluw = np.ascontiguousarray(p["s5_glu_w"][lo].reshape(2, 128, 512).transpose(1, 0, 2))
            glub = np.ascontiguousarray(p["s5_glu_b"][lo].reshape(4, 128).T)
            for c in range(NCORES):
                maps[c].update({"mixT": mixT[c], "wout": wout, "yssmT": yssmT[c], "uT": uT[c], "sgT": sgT[c],
                                "dsk": dsk, "gluw": gluw, "glub": glub})
        if do_in:
            win = _win_layout(p["w_in"][l])
            gin = np.ascontiguousarray(p["norm_g"][l].reshape(KC, 128).T)
            for c in range(NCORES):
                maps[c].update({"win": win, "gin": gin})
        res = _run(_prog(f"op{int(do_out)}{int(do_in)}", lambda: build_op(do_out, do_in)), maps)
        if do_out:
            xT = [res[c]["xoT"] for c in range(NCORES)]
        if not do_in:
            break
        proj = np.concatenate([res[c]["projT"].reshape(D_IN_PAD, NT)[:D_IN].T for c in range(NCORES)], 0)
        proj = proj.reshape(BATCH, T, D_IN)
        y_gla = gla_gather(_run(_prog("gla", build_gla), gla_inputs(proj, p, l)))
        y_ssm = s5_gather(_run(_prog("s5", build_s5), s5_inputs(proj, p, l)))
        y_nsa = nsa_gather(_run(_prog("nsa", build_nsa), nsa_inputs(proj, p, l)))
        mixers = (y_gla, y_ssm, y_nsa)
    out = np.concatenate([xT[c].transpose(2, 1, 0).reshape(NT, D_MODEL) for c in range(NCORES)], 0)
    return out.reshape(BATCH, T, D_MODEL).astype(np.float32)
```

```python
from contextlib import ExitStack
import numpy as np
import concourse.bass as bass
import concourse.mybir as mybir
from concourse.bass_utils import run_bass_kernel_spmd

F32 = mybir.dt.float32
BF16 = mybir.dt.bfloat16
AF = mybir.ActivationFunctionType
ALU = mybir.AluOpType
AX = mybir.AxisListType

NCORES = 8
D_MODEL = 1024
BATCH = 2
SEQ = 8192
DEPTH = 4
D_IN = 3112
D_IN_PAD = 3200
RMS_EPS = 1e-6


class LT:
    def __init__(self, ap, name=""):
        self.ap = ap
        self.name = name
        self.w = None
        self.r = []
        self.dsem = None
        self.dcnt = 0

    def __getitem__(self, idx):
        return self.ap[idx]


class Prog:
    ENGS = ("pe", "dve", "act", "pool", "sp")

    def __init__(self, nc, stack):
        self.nc = nc
        self.stack = stack
        self.q = {e: [] for e in self.ENGS}
        self.sem = {e: stack.enter_context(nc.semaphore("s_" + e)) for e in self.ENGS}
        self.cnt = {e: 0 for e in self.ENGS}
        self.seen = {e: {} for e in self.ENGS}
        self.out_events = []
        self.nsem = 0
        self.ntile = 0

    def sb(self, shape, dt, name=None):
        self.ntile += 1
        name = "sb_" + (name or f"t{self.ntile}")
        t = self.stack.enter_context(self.nc.sbuf_tensor(name, list(shape), dt))
        return LT(t, name)

    def ps(self, shape, dt=F32, name=None):
        self.ntile += 1
        name = "ps_" + (name or f"p{self.ntile}")
        t = self.stack.enter_context(self.nc.psum_tensor(name, list(shape), dt))
        return LT(t, name)

    def sub(self, lt, idx, name=""):
        return LT(lt.ap[idx], name or lt.name)

    def _dsem(self, t):
        if t.dsem is None:
            self.nsem += 1
            t.dsem = self.stack.enter_context(self.nc.semaphore(f"d{self.nsem}"))
        return t.dsem

    def _deps(self, eng, reads, writes):
        evs = []
        for t in reads:
            if t.w is not None:
                evs.append(t.w)
        for t in writes:
            if t.w is not None:
                evs.append(t.w)
            evs.extend(t.r)
        agg = {}
        for sem, val, tile, src in evs:
            if tile is not None:
                val = max(val, tile.dcnt * 16)
            if src == "pe" and eng == "pe":
                continue
            k = id(sem)
            if k not in agg or agg[k][1] < val:
                agg[k] = (sem, val)
        waits = []
        seen = self.seen[eng]
        for k, (sem, val) in agg.items():
            if seen.get(k, 0) >= val:
                continue
            seen[k] = val
            waits.append((sem, val))
        return waits

    def op(self, eng, fn, reads=(), writes=()):
        waits = self._deps(eng, reads, writes)
        self.cnt[eng] += 1
        ev = (self.sem[eng], self.cnt[eng], None, eng)
        self.q[eng].append((waits, fn, (self.sem[eng], 1)))
        for t in reads:
            t.r.append(ev)
        for t in writes:
            t.w = ev
            t.r = []
        return ev

    def dma(self, queue, fn, sbt, reads=(), writes=(), is_out=False):
        waits = self._deps(queue, reads, writes)
        sem = self._dsem(sbt)
        sbt.dcnt += 1
        ev = (sem, sbt.dcnt * 16, sbt, "dma")
        self.q[queue].append((waits, fn, (sem, 16)))
        for t in reads:
            t.r.append(ev)
        for t in writes:
            t.w = ev
            t.r = []
        if is_out:
            self.out_events.append(ev)
        return ev

    def emit(self):
        nc = self.nc
        fin = []
        seen = {}
        for sem, val, tile, _ in self.out_events:
            v = tile.dcnt * 16
            seen[id(sem)] = (sem, v)
        fin = list(seen.values())
        engmap = {"pe": "tensor", "dve": "vector", "act": "scalar", "pool": "gpsimd", "sp": "sync"}
        with nc.Block() as block:
            for e in self.ENGS:
                items = self.q[e]
                extra = fin if e == "sp" else []

                def body(engine, items=items, extra=extra):
                    for waits, fn, inc in items:
                        for sem, val in waits:
                            engine.wait_ge(sem, val)
                        ins = fn(engine)
                        ins.then_inc(inc[0], inc[1])
                    for sem, val in extra:
                        engine.wait_ge(sem, val)

                if items or extra:
                    getattr(block, engmap[e])(body)


def _run(nc, in_maps):
    res = run_bass_kernel_spmd(nc, in_maps, core_ids=list(range(NCORES)))
    return res.results


NT = 2048
KC = 8
NFC = D_IN_PAD // 128
NF32 = 2


def build_op(do_out, do_in):
    nc = bass.Bass("TRN2", target_bir_lowering=False)
    din = lambda name, shape: nc.dram_tensor(name, list(shape), F32, kind="ExternalInput").ap()
    xT = din("xT", [128, KC, NT])
    if do_out:
        mixT = din("mixT", [128, KC, NT])
        wout = din("wout", [128, KC, D_MODEL])
        yssmT = din("yssmT", [128, 2, NT])
        uT = din("uT", [128, 2, NT])
        sgT = din("sgT", [128, 2, NT])
        dskd = din("dsk", [128, 2])
        gluwd = din("gluw", [128, 2, 512])
        glubd = din("glub", [128, 4])
        xoT = nc.dram_tensor("xoT", [128, KC, NT], F32, kind="ExternalOutput").ap()
    if do_in:
        win = din("win", [NFC, 128, KC, 128])
        gin = din("gin", [128, KC])
        projT = nc.dram_tensor("projT", [NFC, 128, NT], F32, kind="ExternalOutput").ap()
    with ExitStack() as st:
        P = Prog(nc, st)
        xs = [P.sb([128, NT], F32, f"x{k}") for k in range(KC)]
        actb = [P.sb([128, NT], BF16, f"ab{k}") for k in range(KC)]
        Fb = [P.sb([128, NT], F32, f"F{i}") for i in range(4)]
        wstg = [P.sb([128, D_MODEL], F32, f"wstg{i}") for i in range(2)]
        banks = [P.ps([128, 512], F32, f"bk{i}") for i in range(8)]
        for k in range(KC):
            P.dma("sp", lambda e, k=k: e.dma_start(out=xs[k][:], in_=xT[:, k, :]), xs[k], writes=[xs[k]])
        nb = 0
        if do_out:
            wob = [P.sb([128, D_MODEL], BF16, f"wob{k}") for k in range(KC)]
            ge = [P.sb([128, NT], BF16, f"ge{k}") for k in range(2)]
            dsk = P.sb([128, 2], F32, "dsk")
            glub = P.sb([128, 4], F32, "glub")
            gluwf = P.sb([128, 2, 512], F32, "gluwf")
            gluwb = P.sb([128, 2, 512], BF16, "gluwb")
            P.dma("sp", lambda e: e.dma_start(out=dsk[:], in_=dskd), dsk, writes=[dsk])
            P.dma("sp", lambda e: e.dma_start(out=glub[:], in_=glubd), glub, writes=[glub])
            P.dma("sp", lambda e: e.dma_start(out=gluwf[:], in_=gluwd), gluwf, writes=[gluwf])
            P.op("act", lambda e: e.copy(out=gluwb[:], in_=gluwf[:]), reads=[gluwf], writes=[gluwb])
            for k in range(KC):
                w = wstg[k % 2]
                P.dma("sp", lambda e, k=k, w=w: e.dma_start(out=w[:], in_=wout[:, k, :]), w, writes=[w])
                P.op("act", lambda e, k=k, w=w: e.copy(out=wob[k][:], in_=w[:]), reads=[w], writes=[wob[k]])
            for i, k in enumerate((0, 1, 4, 5, 6, 7)):
                s_ = Fb[i % 2]
                P.dma("pool", lambda e, k=k, s_=s_: e.dma_start(out=s_[:], in_=mixT[:, k, :]), s_, writes=[s_])
                P.op("dve", lambda e, k=k, s_=s_: e.tensor_copy(out=actb[k][:], in_=s_[:]), reads=[s_], writes=[actb[k]])
            F0, F1, F2, F3 = Fb
            for kc in range(2):
                P.dma("pool", lambda e, kc=kc: e.dma_start(out=F0[:], in_=yssmT[:, kc, :]), F0, writes=[F0])
                P.dma("sp", lambda e, kc=kc: e.dma_start(out=F1[:], in_=uT[:, kc, :]), F1, writes=[F1])
                P.op("dve", lambda e, kc=kc: e.scalar_tensor_tensor(
                    out=F0[:], in0=F1[:], scalar=dsk[:, kc:kc + 1], in1=F0[:], op0=ALU.mult, op1=ALU.add),
                    reads=[F0, F1, dsk], writes=[F0])
                P.op("act", lambda e: e.activation(out=F2[:], in_=F0[:], func=AF.Square), reads=[F0], writes=[F2])
                P.op("dve", lambda e: e.tensor_scalar(out=F2[:], in0=F2[:], scalar1=0.044715, scalar2=1.0, op0=ALU.mult,
                                                      op1=ALU.add), reads=[F2], writes=[F2])
                P.op("dve", lambda e: e.tensor_tensor(out=F2[:], in0=F2[:], in1=F0[:], op=ALU.mult), reads=[F2, F0], writes=[F2])
                P.op("act", lambda e: e.activation(out=F2[:], in_=F2[:], func=AF.Tanh, scale=0.7978845608028654),
                     reads=[F2], writes=[F2])
                P.op("dve", lambda e: e.tensor_scalar(out=F2[:], in0=F2[:], scalar1=0.5, scalar2=0.5, op0=ALU.mult,
                                                      op1=ALU.add), reads=[F2], writes=[F2])
                P.op("dve", lambda e, kc=kc: e.tensor_tensor(out=ge[kc][:], in0=F2[:], in1=F0[:], op=ALU.mult),
                     reads=[F2, F0], writes=[ge[kc]])
            for fcp in range(2):
                P.dma("pool", lambda e, fcp=fcp: e.dma_start(out=F1[:], in_=sgT[:, fcp, :]), F1, writes=[F1])
                P.op("act", lambda e: e.activation(out=F1[:], in_=F1[:], func=AF.Silu), reads=[F1], writes=[F1])
                for tt in range(4):
                    tsl = slice(tt * 512, (tt + 1) * 512)
                    bA, bB = banks[nb % 8], banks[(nb + 1) % 8]
                    nb += 2
                    for kc in range(2):
                        P.op("pe", lambda e, kc=kc, fcp=fcp, tsl=tsl, bA=bA: e.matmul(
                            bA[:], gluwb[:, kc, fcp * 128:(fcp + 1) * 128], ge[kc][:, tsl], start=(kc == 0), stop=(kc == 1)),
                            reads=[gluwb, ge[kc]], writes=[bA])
                    for kc in range(2):
                        P.op("pe", lambda e, kc=kc, fcp=fcp, tsl=tsl, bB=bB: e.matmul(
                            bB[:], gluwb[:, kc, (fcp + 2) * 128:(fcp + 3) * 128], ge[kc][:, tsl], start=(kc == 0),
                            stop=(kc == 1)), reads=[gluwb, ge[kc]], writes=[bB])
                    P.op("act", lambda e, fcp=fcp, tsl=tsl, bB=bB: e.activation(
                        out=F3[:, tsl], in_=bB[:], func=AF.Sigmoid, bias=glub[:, fcp + 2:fcp + 3]),
                        reads=[bB, glub], writes=[F3])
                    P.op("dve", lambda e, fcp=fcp, tsl=tsl, bA=bA: e.scalar_tensor_tensor(
                        out=F3[:, tsl], in0=bA[:], scalar=glub[:, fcp:fcp + 1], in1=F3[:, tsl], op0=ALU.add, op1=ALU.mult),
                        reads=[bA, glub, F3], writes=[F3])
                P.op("dve", lambda e, fcp=fcp: e.tensor_tensor(out=actb[2 + fcp][:], in0=F3[:], in1=F1[:], op=ALU.mult),
                     reads=[F3, F1], writes=[actb[2 + fcp]])
            for fo in range(KC):
                for tt in range(NT // 512):
                    bk = banks[nb % 8]
                    nb += 1
                    for k in range(KC):
                        P.op("pe", lambda e, k=k, fo=fo, tt=tt, bk=bk: e.matmul(
                            bk[:], wob[k][:, fo * 128:(fo + 1) * 128], actb[k][:, tt * 512:(tt + 1) * 512],
                            start=(k == 0), stop=(k == KC - 1)),
                            reads=[wob[k], actb[k]], writes=[bk])
                    P.op("dve", lambda e, fo=fo, tt=tt, bk=bk: e.tensor_tensor(
                        out=xs[fo][:, tt * 512:(tt + 1) * 512], in0=bk[:], in1=xs[fo][:, tt * 512:(tt + 1) * 512],
                        op=ALU.add), reads=[bk, xs[fo]], writes=[xs[fo]])
                P.dma("sp", lambda e, fo=fo: e.dma_start(out=xoT[:, fo, :], in_=xs[fo][:]), xs[fo],
                      reads=[xs[fo]], is_out=True)
        if do_in:
            gt = P.sb([128, KC], F32, "gt")
            P.dma("sp", lambda e: e.dma_start(out=gt[:], in_=gin), gt, writes=[gt])
            ones = P.sb([128, 128], F32, "ones")
            P.op("pool", lambda e: e.memset(ones[:], 1.0), writes=[ones])
            rstd = P.sb([128, NT], F32, "rstd")
            sq = Fb[0:2]
            sbk = banks[0:4]
            for k in range(KC):
                s = sq[k % 2]
                P.op("act", lambda e, k=k, s=s: e.activation(out=s[:], in_=xs[k][:], func=AF.Square),
                     reads=[xs[k]], writes=[s])
                for tt in range(4):
                    P.op("pe", lambda e, k=k, s=s, tt=tt: e.matmul(
                        sbk[tt][:], ones[:], s[:, tt * 512:(tt + 1) * 512], start=(k == 0), stop=(k == KC - 1)),
                        reads=[ones, s], writes=[sbk[tt]])
                P.op("dve", lambda e, k=k: e.tensor_scalar(
                    out=actb[k][:], in0=xs[k][:], scalar1=gt[:, k:k + 1], scalar2=None, op0=ALU.mult),
                    reads=[xs[k], gt], writes=[actb[k]])
            epst = P.sb([128, 1], F32, "eps")
            P.op("pool", lambda e: e.memset(epst[:], RMS_EPS), writes=[epst])
            for tt in range(4):
                P.op("act", lambda e, tt=tt: e.activation(
                    out=rstd[:, tt * 512:(tt + 1) * 512], in_=sbk[tt][:], func=AF.Sqrt,
                    scale=1.0 / D_MODEL, bias=epst[:]), reads=[sbk[tt], epst], writes=[rstd])
            P.op("dve", lambda e: e.reciprocal(out=rstd[:], in_=rstd[:]), reads=[rstd], writes=[rstd])
            wb = [P.sb([128, KC, 128], BF16, f"wb{i}") for i in range(2)]
            ot = Fb[2:4]
            nb = 4
            for fc in range(NFC):
                w = wstg[fc % 2]
                wbb = wb[fc % 2]
                o = ot[fc % 2]
                P.dma("pool", lambda e, fc=fc, w=w: e.dma_start(
                    out=w[:], in_=win[fc].rearrange("p k f -> p (k f)")), w, writes=[w])
                f32c = fc < NF32
                w3 = w[:].rearrange("p (k f) -> p k f", k=KC)
                if f32c:
                    P.op("dve", lambda e, w3=w3: e.tensor_tensor(
                        out=w3, in0=w3, in1=gt[:].unsqueeze(2).to_broadcast([128, KC, 128]), op=ALU.mult),
                        reads=[w, gt], writes=[w])
                else:
                    P.op("act", lambda e, w=w, wbb=wbb: e.copy(out=wbb[:].rearrange("p k f -> p (k f)"), in_=w[:]),
                         reads=[w], writes=[wbb])
                for tt in range(4):
                    bk = banks[4 + nb % 4]
                    nb += 1
                    for k in range(KC):
                        if f32c:
                            P.op("pe", lambda e, k=k, tt=tt, bk=bk, w3=w3: e.matmul(
                                bk[:], w3[:, k, :], xs[k][:, tt * 512:(tt + 1) * 512],
                                start=(k == 0), stop=(k == KC - 1)), reads=[w, xs[k]], writes=[bk])
                        else:
                            P.op("pe", lambda e, k=k, tt=tt, bk=bk, wbb=wbb: e.matmul(
                                bk[:], wbb[:, k, :], actb[k][:, tt * 512:(tt + 1) * 512],
                                start=(k == 0), stop=(k == KC - 1)), reads=[wbb, actb[k]], writes=[bk])
                    P.op("dve", lambda e, tt=tt, bk=bk, o=o: e.tensor_tensor(
                        out=o[:, tt * 512:(tt + 1) * 512], in0=bk[:], in1=rstd[:, tt * 512:(tt + 1) * 512],
                        op=ALU.mult), reads=[bk, rstd], writes=[o])
                P.dma("sp", lambda e, fc=fc, o=o: e.dma_start(out=projT[fc], in_=o[:]), o, reads=[o], is_out=True)
        P.emit()
    return nc


T = SEQ
NTILE = T // 128
NCHUNK = T // 128
GLA_G = 4


def gla_consts():
    j = np.arange(128)[:, None]
    i = np.arange(128)[None, :]
    cm = np.zeros((128, 3, 128), np.float32)
    cm[:, 0, :] = (j <= i)
    cm[:, 1, :] = (j > i)
    cm[:, 2, 0:4] = 1.0
    return cm


def build_gla():
    nc = bass.Bass("TRN2", target_bir_lowering=False)
    qT = nc.dram_tensor("qT", [32, T], F32, kind="ExternalInput").ap()
    kT = nc.dram_tensor("kT", [32, T], F32, kind="ExternalInput").ap()
    lrT1 = nc.dram_tensor("lrT1", [17, T], F32, kind="ExternalInput").ap()
    ktok = nc.dram_tensor("ktok", [128, NTILE, 32], F32, kind="ExternalInput").ap()
    vtok = nc.dram_tensor("vtok", [128, NTILE, 64], F32, kind="ExternalInput").ap()
    gtok = nc.dram_tensor("gtok", [128, NTILE, 64], F32, kind="ExternalInput").ap()
    w2b = nc.dram_tensor("w2b", [17, 32], F32, kind="ExternalInput").ap()
    onb = nc.dram_tensor("onb", [128, 64], F32, kind="ExternalInput").ap()
    cm = nc.dram_tensor("cm", [128, 3, 128], F32, kind="ExternalInput").ap()
    y = nc.dram_tensor("y", [128, NTILE, 64], F32, kind="ExternalOutput").ap()
    NG = NTILE // GLA_G
    with ExitStack() as st:
        P = Prog(nc, st)
        cmt = P.sb([128, 3, 128], F32, "cmt")
        w2t = P.sb([17, 32], F32, "w2t")
        onbt = P.sb([128, 64], F32, "onbt")
        ktokt = P.sb([128, NTILE, 32], F32, "ktokt")
        vb = P.sb([128, NTILE, 64], F32, "vb")
        qeT = P.sb([32, T], F32, "qeT")
        scm = P.sb([128, NTILE, 128], F32, "scm")
        kv_all = P.sb([32, NCHUNK, 64], F32, "kv_all")
        S_bf = P.sb([32, NCHUNK + 1, 64], F32, "S_bf")
        dec_all = P.sb([32, NCHUNK], F32, "dec_all")
        epst = P.sb([128, 1], F32, "eps")
        qTg = [P.sb([32, 512], F32, f"qTg{i}") for i in range(2)]
        kTg = [P.sb([32, 512], F32, f"kTg{i}") for i in range(2)]
        lrg = [P.sb([17, 512], F32, f"lrg{i}") for i in range(2)]
        e1 = P.sb([128, 128], F32, "e1")
        lt = P.sb([128, 128], F32, "lt")
        eqTt = P.sb([32, 512], F32, "eqTt")
        ekTt = P.sb([32, 512], F32, "ekTt")
        ekd = P.sb([128, 128], F32, "ekd")
        keT = P.sb([32, 512], F32, "keT")
        kd = P.sb([128, 4, 32], F32, "kd")
        osb = P.sb([128, 4, 64], F32, "osb")
        sq = P.sb([128, 4, 64], F32, "sq")
        ss = P.sb([128, 4], F32, "ss")
        gg = [P.sb([128, 4, 64], F32, f"gg{i}") for i in range(2)]
        yt = [P.sb([128, 4, 64], F32, f"yt{i}") for i in range(2)]
        bz = P.ps([128, 512], F32, "bz")
        zps = P.sub(bz, (slice(None), slice(0, 128)), "zps")
        sups = P.sub(bz, (slice(None), slice(128, 256)), "sups")
        bT = P.ps([32, 512], F32, "bT")
        bL = P.ps([32, 512], F32, "bL")
        bs = P.ps([128, 512], F32, "bs")
        bkv = P.ps([32, 512], F32, "bkv")
        bo = [P.ps([128, 512], F32, f"bo{i}") for i in range(2)]

        P.dma("sp", lambda e: e.dma_start(out=cmt[:], in_=cm), cmt, writes=[cmt])
        P.dma("sp", lambda e: e.dma_start(out=w2t[:], in_=w2b), w2t, writes=[w2t])
        P.dma("sp", lambda e: e.dma_start(out=onbt[:], in_=onb), onbt, writes=[onbt])
        P.dma("sp", lambda e: e.dma_start(out=ktokt[:], in_=ktok), ktokt, writes=[ktokt])
        P.op("pool", lambda e: e.memset(epst[:], RMS_EPS), writes=[epst])
        for i in range(4):
            P.dma("pool", lambda e, i=i: e.dma_start(out=vb[:, i * 16:(i + 1) * 16, :], in_=vtok[:, i * 16:(i + 1) * 16, :]),
                  vb, writes=[vb])
        LTm = cmt[:, 0, :]
        UTm = cmt[:, 1, :]
        BDs = cmt[:, 2, 0:1]
        for g in range(NG):
            qg, kg, lg = qTg[g % 2], kTg[g % 2], lrg[g % 2]
            tok = slice(g * 512, (g + 1) * 512)
            P.dma("sp", lambda e, qg=qg, tok=tok: e.dma_start(out=qg[:], in_=qT[:, tok]), qg, writes=[qg])
            P.dma("sp", lambda e, kg=kg, tok=tok: e.dma_start(out=kg[:], in_=kT[:, tok]), kg, writes=[kg])
            P.dma("sp", lambda e, lg=lg, tok=tok: e.dma_start(out=lg[:], in_=lrT1[:, tok]), lg, writes=[lg])
            for t in range(4):
                P.op("pe", lambda e, t=t, lg=lg: e.matmul(zps[:, t * 32:(t + 1) * 32], lg[:, t * 128:(t + 1) * 128],
                                                        w2t[:], start=True, stop=True), reads=[lg, w2t], writes=[zps])
            P.op("act", lambda e: e.activation(out=e1[:], in_=zps[:], func=AF.Exp, scale=-1.0), reads=[zps], writes=[e1])
            P.op("act", lambda e: e.activation(out=lt[:], in_=e1[:], func=AF.Ln, bias=1.0), reads=[e1], writes=[lt])
            for t in range(4):
                lsl = lt[:, t * 32:(t + 1) * 32]
                P.op("pe", lambda e, t=t, lsl=lsl: e.matmul(sups[:, t * 32:(t + 1) * 32], UTm, lsl, start=True, stop=True),
                     reads=[cmt, lt], writes=[sups])
                P.op("pe", lambda e, t=t, lsl=lsl: e.matmul(bT[:, t * 128:(t + 1) * 128], lsl, LTm, start=True, stop=True),
                     reads=[cmt, lt], writes=[bT])
                P.op("pe", lambda e, t=t, lsl=lsl: e.matmul(bL[:, t:t + 1], lsl, BDs, start=True, stop=True),
                     reads=[cmt, lt], writes=[bL])
            P.op("act", lambda e: e.activation(out=eqTt[:], in_=bT[:], func=AF.Exp, scale=-1.0 / 16), reads=[bT], writes=[eqTt])
            P.op("act", lambda e: e.activation(out=ekTt[:], in_=bT[:], func=AF.Exp, scale=1.0 / 16), reads=[bT], writes=[ekTt])
            P.op("act", lambda e: e.activation(out=ekd[:], in_=sups[:], func=AF.Exp, scale=-1.0 / 16), reads=[sups], writes=[ekd])
            P.op("act", lambda e, g=g: e.activation(out=dec_all[:, 4 * g:4 * g + 4], in_=bL[:, 0:4], func=AF.Exp,
                                                   scale=-1.0 / 16), reads=[bL], writes=[dec_all])
            P.op("dve", lambda e, qg=qg, tok=tok: e.scalar_tensor_tensor(
                out=qeT[:, tok], in0=qg[:], scalar=32 ** -0.5, in1=eqTt[:], op0=ALU.mult, op1=ALU.mult),
                reads=[qg, eqTt], writes=[qeT])
            P.op("dve", lambda e, kg=kg: e.tensor_tensor(out=keT[:], in0=kg[:], in1=ekTt[:], op=ALU.mult),
                 reads=[kg, ekTt], writes=[keT])
            P.op("dve", lambda e, g=g: e.tensor_tensor(
                out=kd[:], in0=ktokt[:, 4 * g:4 * g + 4, :], in1=ekd[:].rearrange("p (t d) -> p t d", t=4), op=ALU.mult),
                reads=[ktokt, ekd], writes=[kd])
            for t in range(4):
                P.op("pe", lambda e, t=t, g=g: e.matmul(
                    bs[:, t * 128:(t + 1) * 128], keT[:, t * 128:(t + 1) * 128],
                    qeT[:, (4 * g + t) * 128:(4 * g + t + 1) * 128], start=True, stop=True),
                    reads=[keT, qeT], writes=[bs])
            for t in range(4):
                P.op("pe", lambda e, t=t, g=g: e.matmul(
                    bkv[:, t * 64:(t + 1) * 64], kd[:, t, :], vb[:, 4 * g + t, :], start=True, stop=True),
                    reads=[kd, vb], writes=[bkv])
            P.op("dve", lambda e, g=g: e.tensor_tensor(
                out=scm[:, 4 * g:4 * g + 4, :], in0=bs[:].rearrange("p (t i) -> p t i", t=4),
                in1=cmt[:, 0:1, :].to_broadcast([128, 4, 128]), op=ALU.mult), reads=[bs, cmt], writes=[scm])
            P.op("act", lambda e, g=g: e.copy(out=kv_all[:, 4 * g:4 * g + 4, :],
                                             in_=bkv[:, 0:256].rearrange("p (c e) -> p c e", c=4)),
                 reads=[bkv], writes=[kv_all])
        P.op("pool", lambda e: e.memset(S_bf[:, 0, :], 0.0), writes=[S_bf])
        for ee in range(64):
            P.op("dve", lambda e, ee=ee: e.tensor_tensor_scan(
                out=S_bf[:, 1:NCHUNK + 1, ee], data0=dec_all[:], data1=kv_all[:, :, ee], initial=0.0,
                op0=ALU.mult, op1=ALU.add), reads=[dec_all, kv_all], writes=[S_bf])
        for g in range(NG):
            b = bo[g % 2]
            gt_, yy = gg[g % 2], yt[g % 2]
            P.dma("sp", lambda e, g=g, gt_=gt_: e.dma_start(out=gt_[:], in_=gtok[:, 4 * g:4 * g + 4, :]), gt_, writes=[gt_])
            for t in range(4):
                P.op("pe", lambda e, t=t, g=g, b=b: e.matmul(
                    b[:, t * 64:(t + 1) * 64], scm[:, 4 * g + t, :], vb[:, 4 * g + t, :], start=True, stop=False),
                    reads=[scm, vb], writes=[b])
                P.op("pe", lambda e, t=t, g=g, b=b: e.matmul(
                    b[:, t * 64:(t + 1) * 64], qeT[:, (4 * g + t) * 128:(4 * g + t + 1) * 128],
                    S_bf[:, 4 * g + t, :], start=False, stop=True), reads=[qeT, S_bf], writes=[b])
            P.op("act", lambda e, b=b: e.copy(out=osb[:], in_=b[:, 0:256].rearrange("p (t e) -> p t e", t=4)),
                 reads=[b], writes=[osb])
            P.op("dve", lambda e: e.tensor_tensor(out=sq[:], in0=osb[:], in1=osb[:], op=ALU.mult), reads=[osb], writes=[sq])
            P.op("dve", lambda e: e.tensor_reduce(out=ss[:], in_=sq[:], axis=AX.X, op=ALU.add), reads=[sq], writes=[ss])
            P.op("act", lambda e: e.activation(out=ss[:], in_=ss[:], func=AF.Sqrt, scale=1.0 / 64, bias=epst[:]),
                 reads=[ss, epst], writes=[ss])
            P.op("dve", lambda e: e.reciprocal(out=ss[:], in_=ss[:]), reads=[ss], writes=[ss])
            P.op("act", lambda e, gt_=gt_: e.activation(out=gt_[:], in_=gt_[:], func=AF.Silu), reads=[gt_], writes=[gt_])
            P.op("dve", lambda e, gt_=gt_: e.tensor_tensor(
                out=gt_[:], in0=gt_[:], in1=onbt[:].unsqueeze(1).to_broadcast([128, 4, 64]), op=ALU.mult),
                reads=[gt_, onbt], writes=[gt_])
            P.op("dve", lambda e: e.tensor_tensor(
                out=osb[:], in0=osb[:], in1=ss[:].unsqueeze(2).to_broadcast([128, 4, 64]), op=ALU.mult),
                reads=[osb, ss], writes=[osb])
            P.op("dve", lambda e, gt_=gt_, yy=yy: e.tensor_tensor(out=yy[:], in0=osb[:], in1=gt_[:], op=ALU.mult),
                 reads=[osb, gt_], writes=[yy])
            P.dma("sp", lambda e, g=g, yy=yy: e.dma_start(out=y[:, 4 * g:4 * g + 4, :], in_=yy[:]), yy,
                  reads=[yy], is_out=True)
        P.emit()
    return nc


OFF = dict(gq=0, gk=128, gv=256, glr=512, gg=528, su=784, sg=1040, nq=1296, nkv=1808, ngl=2576, ng=2600)


def gla_inputs(proj, p, l):
    cm = gla_consts()
    maps = []
    for c in range(NCORES):
        b, h = divmod(c, 4)
        q = proj[b, :, OFF["gq"] + h * 32:OFF["gq"] + (h + 1) * 32]
        k = proj[b, :, OFF["gk"] + h * 32:OFF["gk"] + (h + 1) * 32]
        v = proj[b, :, OFF["gv"] + h * 64:OFF["gv"] + (h + 1) * 64]
        lr = proj[b, :, OFF["glr"]:OFF["glr"] + 16]
        gt_ = proj[b, :, OFF["gg"] + h * 64:OFF["gg"] + (h + 1) * 64]
        tokmaj = lambda a: np.ascontiguousarray(a.reshape(NTILE, 128, -1).transpose(1, 0, 2))
        maps.append({
            "qT": np.ascontiguousarray(q.T), "kT": np.ascontiguousarray(k.T),
            "lrT1": np.ascontiguousarray(np.concatenate([lr.T, np.ones((1, T), np.float32)], 0)),
            "ktok": tokmaj(k), "vtok": tokmaj(v), "gtok": tokmaj(gt_),
            "w2b": np.ascontiguousarray(np.concatenate(
                [p["gla_w2"][l][:, h * 32:(h + 1) * 32], p["gla_b2"][l][None, h * 32:(h + 1) * 32]], 0)),
            "onb": np.ascontiguousarray(np.broadcast_to(p["gla_onorm"][l][None, :], (128, 64))),
            "cm": cm,
        })
    return maps


def gla_gather(res):
    out = np.zeros((BATCH, T, 256), np.float32)
    for c in range(NCORES):
        b, h = divmod(c, 4)
        out[b, :, h * 64:(h + 1) * 64] = res[c]["y"].transpose(1, 0, 2).reshape(T, 64)
    return out


S5L = 64
S5N = T // S5L


def s5_consts():
    kio = np.broadcast_to(np.arange(65, dtype=np.float32)[None, :], (128, 65)).copy()
    r = np.arange(128)
    dmask = ((r[None, :] // 16) >= (r[:, None] // 16)).astype(np.float32)
    Wm = np.concatenate([np.zeros((128, 7 * 128), np.float32), dmask, np.ones((128, 7 * 128), np.float32)], 1)
    ident = np.eye(128, dtype=np.float32)
    return kio, Wm, ident


def _fact(n):
    f = 1.0
    for i in range(2, n + 1):
        f *= i
    return f


def build_s5():
    nc = bass.Bass("TRN2", target_bir_lowering=False)
    U = nc.dram_tensor("U", [2, 128, 8, 256], F32, kind="ExternalInput").ap()
    lam = nc.dram_tensor("lam", [128, 2], F32, kind="ExternalInput").ap()
    lst = nc.dram_tensor("lst", [128, 1], F32, kind="ExternalInput").ap()
    Bri = nc.dram_tensor("Bri", [128, 2, 16], F32, kind="ExternalInput").ap()
    Cri = nc.dram_tensor("Cri", [128, 2, 16], F32, kind="ExternalInput").ap()
    kio_d = nc.dram_tensor("kio", [128, 65], F32, kind="ExternalInput").ap()
    Wm_d = nc.dram_tensor("Wm", [128, 1920], F32, kind="ExternalInput").ap()
    id_d = nc.dram_tensor("ident", [128, 128], F32, kind="ExternalInput").ap()
    Y = nc.dram_tensor("Y", [2, 128, 8, 256], F32, kind="ExternalOutput").ap()
    with ExitStack() as st:
        P = Prog(nc, st)
        col = lambda name, w=1: P.sb([128, w], F32, name)

        def load(name, shape, src):
            t = P.sb(shape, F32, name)
            P.dma("sp", lambda e: e.dma_start(out=t[:], in_=src), t, writes=[t])
            return t

        lamt = load("lamt", [128, 2], lam)
        lstt = load("lstt", [128, 1], lst)
        Bt = load("Bt", [128, 2, 16], Bri)
        Ct = load("Ct", [128, 2, 16], Cri)
        kio = load("kiot", [128, 65], kio_d)
        Wm = load("Wmt", [128, 1920], Wm_d)
        ident = load("identt", [128, 128], id_d)
        Ut = []
        for gl in range(2):
            t = P.sb([128, 8, 256], F32, f"U{gl}")
            P.dma("pool", lambda e, t=t, gl=gl: e.dma_start(out=t[:], in_=U[gl]), t, writes=[t])
            Ut.append(t)

        def ts(out, in0, s1, op0, s2=None, op1=None, r=(), w=()):
            if op1 is None:
                P.op("dve", lambda e: e.tensor_scalar(out=out, in0=in0, scalar1=s1, scalar2=None, op0=op0), reads=r, writes=w)
            else:
                P.op("dve", lambda e: e.tensor_scalar(out=out, in0=in0, scalar1=s1, scalar2=s2, op0=op0, op1=op1),
                     reads=r, writes=w)

        def tt(out, a, b, op, r=(), w=()):
            P.op("dve", lambda e: e.tensor_tensor(out=out, in0=a, in1=b, op=op), reads=r, writes=w)

        def stt(out, in0, sc, in1, op0, op1, r=(), w=()):
            P.op("dve", lambda e: e.scalar_tensor_tensor(out=out, in0=in0, scalar=sc, in1=in1, op0=op0, op1=op1),
                 reads=r, writes=w)

        def horner(name, xs, coefs):
            acc = col(name)
            P.op("pool", lambda e: e.memset(acc[:], float(coefs[-1])), writes=[acc])
            for c in reversed(coefs[:-1]):
                ts(acc[:], acc[:], xs[:, 0:1], ALU.mult, float(c), ALU.add, r=[acc, xs], w=[acc])
            return acc

        y4 = col("y4")
        ts(y4[:], lstt[:], 0.25, ALU.mult, r=[lstt], w=[y4])
        dt = horner("dt", y4, [1.0 / _fact(k) for k in range(19)])
        tt(dt[:], dt[:], dt[:], ALU.mult, r=[dt], w=[dt])
        tt(dt[:], dt[:], dt[:], ALU.mult, r=[dt], w=[dt])
        lr = col("lr")
        ts(lr[:], lamt[:, 0:1], -1e-4, ALU.min, r=[lamt], w=[lr])
        li = col("li")
        ts(li[:], lamt[:, 1:2], 1.0, ALU.mult, r=[lamt], w=[li])
        xx = col("xx")
        tt(xx[:], lr[:], dt[:], ALU.mult, r=[lr, dt], w=[xx])
        negx = col("negx")
        ts(negx[:], xx[:], -1.0, ALU.mult, r=[xx], w=[negx])
        q = horner("q", xx, [1.0 / _fact(k + 1) for k in range(10)])
        em1 = col("em1")
        tt(em1[:], q[:], xx[:], ALU.mult, r=[q, xx], w=[em1])
        mag = col("mag")
        ts(mag[:], em1[:], 1.0, ALU.add, r=[em1], w=[mag])
        phi = col("phi")
        stt(phi[:], li[:], 1.0 / 32, dt[:], ALU.mult, ALU.mult, r=[li, dt], w=[phi])
        ww = col("ww")
        tt(ww[:], phi[:], phi[:], ALU.mult, r=[phi], w=[ww])
        ps_ = horner("ps", ww, [(-1.0) ** k / _fact(2 * k + 1) for k in range(8)])
        pc_ = horner("pc", ww, [(-1.0) ** (k + 1) / _fact(2 * k + 2) for k in range(8)])
        sA = col("sA")
        tt(sA[:], ps_[:], phi[:], ALU.mult, r=[ps_, phi], w=[sA])
        cA = col("cA")
        tt(cA[:], pc_[:], ww[:], ALU.mult, r=[pc_, ww], w=[cA])
        sB, cB, a1, s2 = col("sB"), col("cB"), col("a1"), col("s2")
        cur = (cA, sA)
        nxt = (cB, sB)
        for _ in range(5):
            cm_, s_ = cur
            cn, sn = nxt
            ts(a1[:], cm_[:], 2.0, ALU.add, cm_[:, 0:1], ALU.mult, r=[cm_], w=[a1])
            tt(s2[:], s_[:], s_[:], ALU.mult, r=[s_], w=[s2])
            tt(cn[:], a1[:], s2[:], ALU.subtract, r=[a1, s2], w=[cn])
            ts(sn[:], cm_[:], 1.0, ALU.add, s_[:, 0:1], ALU.mult, r=[cm_, s_], w=[sn])
            ts(sn[:], sn[:], 2.0, ALU.mult, r=[sn], w=[sn])
            cur, nxt = nxt, cur
        cm_, s_ = cur
        cc = col("cc")
        ts(cc[:], cm_[:], 1.0, ALU.add, r=[cm_], w=[cc])
        ai = col("ai")
        tt(ai[:], mag[:], s_[:], ALU.mult, r=[mag, s_], w=[ai])
        am1r = col("am1r")
        tt(am1r[:], mag[:], cm_[:], ALU.mult, r=[mag, cm_], w=[am1r])
        tt(am1r[:], am1r[:], em1[:], ALU.add, r=[am1r, em1], w=[am1r])
        den = col("den")
        tt(den[:], lr[:], lr[:], ALU.mult, r=[lr], w=[den])
        stt(den[:], li[:], li[:, 0:1], den[:], ALU.mult, ALU.add, r=[li, den], w=[den])
        P.op("dve", lambda e: e.reciprocal(out=den[:], in_=den[:]), reads=[den], writes=[den])
        u1, u2, fr, fi = col("u1"), col("u2"), col("fr"), col("fi")
        tt(u1[:], am1r[:], lr[:], ALU.mult, r=[am1r, lr], w=[u1])
        stt(u1[:], ai[:], li[:, 0:1], u1[:], ALU.mult, ALU.add, r=[ai, li, u1], w=[u1])
        tt(fr[:], u1[:], den[:], ALU.mult, r=[u1, den], w=[fr])
        tt(u2[:], am1r[:], li[:], ALU.mult, r=[am1r, li], w=[u2])
        stt(u2[:], ai[:], lr[:, 0:1], u2[:], ALU.mult, ALU.subtract, r=[ai, lr, u2], w=[u2])
        tt(fi[:], u2[:], den[:], ALU.mult, r=[u2, den], w=[fi])
        Bb = P.sb([128, 2, 16], F32, "Bb")
        v1 = col("v1", 16)
        ts(v1[:], Bt[:, 1, :], fi[:, 0:1], ALU.mult, r=[Bt, fi], w=[v1])
        stt(Bb[:, 0, :], Bt[:, 0, :], fr[:, 0:1], v1[:], ALU.mult, ALU.subtract, r=[Bt, fr, v1], w=[Bb])
        ts(v1[:], Bt[:, 0, :], fi[:, 0:1], ALU.mult, r=[Bt, fi], w=[v1])
        stt(Bb[:, 1, :], Bt[:, 1, :], fr[:, 0:1], v1[:], ALU.mult, ALU.add, r=[Bt, fr, v1], w=[Bb])
        Er, Ei = col("Er", 65), col("Ei", 65)
        t1, t2 = col("t1", 32), col("t2", 32)
        P.op("pool", lambda e: e.memset(Er[:, 0:1], 1.0), writes=[Er])
        P.op("pool", lambda e: e.memset(Ei[:, 0:1], 0.0), writes=[Ei])
        ts(Er[:, 1:2], cc[:], 1.0, ALU.mult, r=[cc], w=[Er])
        ts(Ei[:, 1:2], s_[:], 1.0, ALU.mult, r=[s_], w=[Ei])
        m = 1
        while m <= 32:
            er, ei = Er[:, m:m + 1], Ei[:, m:m + 1]
            ts(t1[:, 0:m], Ei[:, 1:m + 1], ei, ALU.mult, r=[Ei], w=[t1])
            ts(t2[:, 0:m], Ei[:, 1:m + 1], er, ALU.mult, r=[Ei, Er], w=[t2])
            stt(Ei[:, m + 1:2 * m + 1], Er[:, 1:m + 1], ei, t2[:, 0:m], ALU.mult, ALU.add, r=[Er, Ei, t2], w=[Ei])
            stt(Er[:, m + 1:2 * m + 1], Er[:, 1:m + 1], er, t1[:, 0:m], ALU.mult, ALU.subtract, r=[Er, t1], w=[Er])
            m *= 2
        magk, imagk = col("magk", 65), col("imagk", 65)
        P.op("act", lambda e: e.activation(out=magk[:], in_=kio[:], func=AF.Exp, scale=xx[:, 0:1]), reads=[kio, xx], writes=[magk])
        P.op("act", lambda e: e.activation(out=imagk[:], in_=kio[:], func=AF.Exp, scale=negx[:, 0:1]), reads=[kio, negx],
             writes=[imagk])
        Pr, Pi, Qr, Qi = col("Pr", 65), col("Pi", 65), col("Qr", 65), col("Qi", 65)
        tt(Pr[:], Er[:], magk[:], ALU.mult, r=[Er, magk], w=[Pr])
        tt(Pi[:], Ei[:], magk[:], ALU.mult, r=[Ei, magk], w=[Pi])
        tt(Qr[:], Er[:], imagk[:], ALU.mult, r=[Er, imagk], w=[Qr])
        stt(Qi[:], Ei[:], -1.0, imagk[:], ALU.mult, ALU.mult, r=[Ei, imagk], w=[Qi])
        KBr = P.sb([128, 64, 16], F32, "KBr")
        KBi = P.sb([128, 64, 16], F32, "KBi")
        QCr = P.sb([128, 64, 16], F32, "QCr")
        QCi = P.sb([128, 64, 16], F32, "QCi")
        tmp = P.sb([128, 64, 16], F32, "tmp")
        bj = lambda tl: tl[:, 0:64].unsqueeze(2).to_broadcast([128, 64, 16])
        bc = lambda ap: ap.unsqueeze(1).to_broadcast([128, 64, 16])
        tt(KBr[:], bj(Qr), bc(Bb[:, 0, :]), ALU.mult, r=[Qr, Bb], w=[KBr])
        tt(tmp[:], bj(Qi), bc(Bb[:, 1, :]), ALU.mult, r=[Qi, Bb], w=[tmp])
        tt(KBr[:], KBr[:], tmp[:], ALU.subtract, r=[KBr, tmp], w=[KBr])
        tt(KBi[:], bj(Qr), bc(Bb[:, 1, :]), ALU.mult, r=[Qr, Bb], w=[KBi])
        tt(tmp[:], bj(Qi), bc(Bb[:, 0, :]), ALU.mult, r=[Qi, Bb], w=[tmp])
        tt(KBi[:], KBi[:], tmp[:], ALU.add, r=[KBi, tmp], w=[KBi])
        tt(QCr[:], bj(Pr), bc(Ct[:, 0, :]), ALU.mult, r=[Pr, Ct], w=[QCr])
        tt(tmp[:], bj(Pi), bc(Ct[:, 1, :]), ALU.mult, r=[Pi, Ct], w=[tmp])
        tt(QCr[:], QCr[:], tmp[:], ALU.subtract, r=[QCr, tmp], w=[QCr])
        tt(QCi[:], bj(Pr), bc(Ct[:, 1, :]), ALU.mult, r=[Pr, Ct], w=[QCi])
        tt(tmp[:], bj(Pi), bc(Ct[:, 0, :]), ALU.mult, r=[Pi, Ct], w=[tmp])
        stt(QCi[:], QCi[:], -1.0, tmp[:], ALU.mult, ALU.subtract, r=[QCi, tmp], w=[QCi])
        fl = lambda tl: tl[:].rearrange("p j c -> p (j c)")
        banks = [P.ps([128, 512], F32, f"bk{i}") for i in range(8)]
        nb = 0
        TZ = [P.sb([128, 8, 1024], F32, f"TZ{gl}") for gl in range(2)]
        for gl in range(2):
            rows = slice(64 * gl, 64 * gl + 64)
            for rc in range(8):
                for ch in range(2):
                    if 4 * ch + 3 < rc:
                        continue
                    bk = banks[nb % 4]
                    nb += 1
                    P.op("pe", lambda e, bk=bk, rows=rows, rc=rc, ch=ch: e.matmul(
                        bk[:], fl(KBr)[rows, rc * 128:(rc + 1) * 128], fl(QCr)[rows, ch * 512:(ch + 1) * 512],
                        start=True, stop=False), reads=[KBr, QCr], writes=[bk])
                    P.op("pe", lambda e, bk=bk, rows=rows, rc=rc, ch=ch: e.matmul(
                        bk[:], fl(KBi)[rows, rc * 128:(rc + 1) * 128], fl(QCi)[rows, ch * 512:(ch + 1) * 512],
                        start=False, stop=True), reads=[KBi, QCi], writes=[bk])
                    w0 = (7 - rc) * 128 + ch * 512
                    P.op("dve", lambda e, bk=bk, gl=gl, rc=rc, ch=ch, w0=w0: e.tensor_tensor(
                        out=TZ[gl][:, rc, ch * 512:(ch + 1) * 512], in0=bk[:], in1=Wm[:, w0:w0 + 512], op=ALU.mult),
                        reads=[bk, Wm], writes=[TZ[gl]])
        KBT = [[P.sb([128, 8, 64], F32, f"KBT{gl}{ri}") for ri in range(2)] for gl in range(2)]
        for gl in range(2):
            rows = slice(64 * gl, 64 * gl + 64)
            for ri, src in enumerate((KBr, KBi)):
                bk = banks[4 + (2 * gl + ri) % 2]
                for kc in range(8):
                    P.op("pe", lambda e, bk=bk, rows=rows, kc=kc, src=src: e.transpose(
                        bk[:, kc * 64:(kc + 1) * 64], fl(src)[rows, kc * 128:(kc + 1) * 128], ident[rows, rows]),
                        reads=[src, ident], writes=[bk])
                P.op("act", lambda e, bk=bk, gl=gl, ri=ri: e.copy(
                    out=KBT[gl][ri][:], in_=bk[:].rearrange("p (k c) -> p k c", k=8)), reads=[bk], writes=[KBT[gl][ri]])
        bx = banks[6]
        for gl in range(2):
            rows = slice(64 * gl, 64 * gl + 64)
            for ri in range(2):
                for kc in range(8):
                    P.op("pe", lambda e, gl=gl, rows=rows, ri=ri, kc=kc: e.matmul(
                        bx[rows, ri * 256:(ri + 1) * 256], KBT[gl][ri][:, kc, :], Ut[gl][:, kc, :],
                        start=(kc == 0), stop=(kc == 7)), reads=[KBT[gl][ri], Ut[gl]], writes=[bx])
        Wr, Wi = col("Wr", 256), col("Wi", 256)
        Xs = col("Xs", 512)
        P.op("act", lambda e: e.copy(out=Xs[:], in_=bx[:]), reads=[bx], writes=[Xs])
        Ar, Ai = Pr[:, 64:65], Pi[:, 64:65]
        ts(Wr[:], Xs[:, 256:512], Ai, ALU.mult, r=[Xs, Pi], w=[Wr])
        stt(Wr[:], Xs[:, 0:256], Ar, Wr[:], ALU.mult, ALU.subtract, r=[Xs, Pr, Wr], w=[Wr])
        ts(Wi[:], Xs[:, 0:256], Ai, ALU.mult, r=[Xs, Pi], w=[Wi])
        stt(Wi[:], Xs[:, 256:512], Ar, Wi[:], ALU.mult, ALU.add, r=[Xs, Pr, Wi], w=[Wi])
        Zr, Zi = P.sb([128, 2, 128], F32, "Zr"), P.sb([128, 2, 128], F32, "Zi")
        P.op("pool", lambda e: e.memset(Zr[:], 0.0), writes=[Zr])
        P.op("pool", lambda e: e.memset(Zi[:], 0.0), writes=[Zi])
        Ya = (P.sb([128, 2, 128], F32, "Yar"), P.sb([128, 2, 128], F32, "Yai"))
        Yb = (P.sb([128, 2, 128], F32, "Ybr"), P.sb([128, 2, 128], F32, "Ybi"))
        sc1 = P.sb([128, 2, 128], F32, "sc1")
        sc2 = P.sb([128, 2, 128], F32, "sc2")
        Mp = P.sb([128, 7, 2], F32, "Mp")
        ma, mb_ = col("ma"), col("mb")
        ts(Mp[:, 0, 0:1], Ar, 1.0, ALU.mult, r=[Pr], w=[Mp])
        ts(Mp[:, 0, 1:2], Ai, 1.0, ALU.mult, r=[Pi], w=[Mp])
        for j in range(6):
            mr_, mi_ = Mp[:, j, 0:1], Mp[:, j, 1:2]
            tt(ma[:], mr_, mr_, ALU.mult, r=[Mp], w=[ma])
            tt(mb_[:], mi_, mi_, ALU.mult, r=[Mp], w=[mb_])
            tt(Mp[:, j + 1, 0:1], ma[:], mb_[:], ALU.subtract, r=[ma, mb_], w=[Mp])
            stt(Mp[:, j + 1, 1:2], mr_, 2.0, mi_, ALU.mult, ALU.mult, r=[Mp], w=[Mp])
        W3r = Wr[:].rearrange("p (b n) -> p b n", b=2)
        W3i = Wi[:].rearrange("p (b n) -> p b n", b=2)
        src = None
        N_ = S5N
        for j in range(7):
            sft = 1 << j
            dst = Ya if j % 2 == 0 else Yb
            if src is None:
                sr, si, srl, sil = W3r, W3i, [Wr], [Wi]
            else:
                sr, si, srl, sil = src[0][:], src[1][:], [src[0]], [src[1]]
            mr_, mi_ = Mp[:, j, 0:1], Mp[:, j, 1:2]
            lo, hi = slice(0, N_ - sft), slice(sft, N_)
            ts(sc1[:, :, lo], si[:, :, lo], mi_, ALU.mult, r=sil + [Mp], w=[sc1])
            stt(sc1[:, :, lo], sr[:, :, lo], mr_, sc1[:, :, lo], ALU.mult, ALU.subtract, r=srl + [Mp, sc1], w=[sc1])
            ts(sc2[:, :, lo], sr[:, :, lo], mi_, ALU.mult, r=srl + [Mp], w=[sc2])
            stt(sc2[:, :, lo], si[:, :, lo], mr_, sc2[:, :, lo], ALU.mult, ALU.add, r=sil + [Mp, sc2], w=[sc2])
            tt(dst[0][:, :, hi], sr[:, :, hi], sc1[:, :, lo], ALU.add, r=srl + [sc1], w=[dst[0]])
            tt(dst[1][:, :, hi], si[:, :, hi], sc2[:, :, lo], ALU.add, r=sil + [sc2], w=[dst[1]])
            P.op("act", lambda e, dst=dst, sr=sr, sft=sft: e.copy(out=dst[0][:, :, 0:sft], in_=sr[:, :, 0:sft]),
                 reads=srl, writes=[dst[0]])
            P.op("act", lambda e, dst=dst, si=si, sft=sft: e.copy(out=dst[1][:, :, 0:sft], in_=si[:, :, 0:sft]),
                 reads=sil, writes=[dst[1]])
            src = dst
        P.op("act", lambda e: e.copy(out=Zr[:, :, 1:N_], in_=src[0][:, :, 0:N_ - 1]), reads=[src[0]], writes=[Zr])
        P.op("dve", lambda e: e.tensor_copy(out=Zi[:, :, 1:N_], in_=src[1][:, :, 0:N_ - 1]), reads=[src[1]], writes=[Zi])
        Yt = [P.sb([128, 256], F32, f"Yt{i}") for i in range(2)]
        ny = 0
        for gl in range(2):
            rows = slice(64 * gl, 64 * gl + 64)
            for ob in range(8):
                bk = banks[ny % 4]
                yt_ = Yt[ny % 2]
                ny += 1
                for kc in range(ob + 1):
                    P.op("pe", lambda e, bk=bk, gl=gl, kc=kc, ob=ob: e.matmul(
                        bk[:, 0:256], TZ[gl][:, kc, ob * 128:(ob + 1) * 128], Ut[gl][:, kc, :],
                        start=(kc == 0), stop=False), reads=[TZ[gl], Ut[gl]], writes=[bk])
                P.op("pe", lambda e, bk=bk, rows=rows, ob=ob: e.matmul(
                    bk[:, 0:256], fl(QCr)[rows, ob * 128:(ob + 1) * 128], Zr[rows].rearrange("p b n -> p (b n)"),
                    start=False, stop=False), reads=[QCr, Zr], writes=[bk])
                P.op("pe", lambda e, bk=bk, rows=rows, ob=ob: e.matmul(
                    bk[:, 0:256], fl(QCi)[rows, ob * 128:(ob + 1) * 128], Zi[rows].rearrange("p b n -> p (b n)"),
                    start=False, stop=True), reads=[QCi, Zi], writes=[bk])
                P.op("act", lambda e, bk=bk, yt_=yt_: e.copy(out=yt_[:], in_=bk[:, 0:256]), reads=[bk], writes=[yt_])
                P.dma("sp", lambda e, gl=gl, ob=ob, yt_=yt_: e.dma_start(out=Y[gl, :, ob, :], in_=yt_[:]), yt_,
                      reads=[yt_], is_out=True)
        P.emit()
    return nc


def s5_inputs(proj, p, l):
    kio, Wm, ident = s5_consts()
    maps = []
    for c in range(NCORES):
        Us, lam, lst, Bri, Cri = [], [], [], [], []
        for gl in range(2):
            g = 2 * c + gl
            u = proj[:, :, OFF["su"] + g * 16:OFF["su"] + (g + 1) * 16]
            u = u.reshape(BATCH, S5N, 8, 8, 16).transpose(3, 4, 2, 0, 1)
            Us.append(u.reshape(128, 8, 2 * S5N))
            lam.append(np.stack([p["s5_lam_re"][l][g], p["s5_lam_im"][l][g]], 1))
            lst.append(np.full((64, 1), p["s5_log_step"][l][g], np.float32))
            Bri.append(np.stack([p["s5_b_re"][l][g], p["s5_b_im"][l][g]], 1))
            Cri.append(np.stack([p["s5_c_re"][l][g].T, p["s5_c_im"][l][g].T], 1))
        cat = lambda xs: np.ascontiguousarray(np.concatenate(xs, 0).astype(np.float32))
        maps.append({"U": np.ascontiguousarray(np.stack(Us, 0)), "lam": cat(lam), "lst": cat(lst),
                     "Bri": cat(Bri), "Cri": cat(Cri), "kio": kio, "Wm": Wm, "ident": ident})
    return maps


def s5_gather(res):
    out = np.zeros((BATCH, T, 256), np.float32)
    for c in range(NCORES):
        Yc = res[c]["Y"]
        for gl in range(2):
            g = 2 * c + gl
            a = Yc[gl].reshape(8, 16, 8, BATCH, S5N).transpose(3, 4, 2, 0, 1)
            out[:, :, g * 16:(g + 1) * 16] = a.reshape(BATCH, T, 16)
    return out


NQB = 32
NEGB = -30000.0


def nsa_consts():
    r = np.arange(128)
    kl, ql = r[:, None], r[None, :]
    mdiag = np.where(kl <= ql, 0.0, NEGB).astype(np.float32)
    mfar = np.where(kl > ql, 0.0, NEGB).astype(np.float32)
    mall = np.full((128, 128), NEGB, np.float32)
    mzero = np.zeros((128, 128), np.float32)
    mw = [np.stack([mfar, mzero, mzero, mzero, mdiag, mall], 1),
          np.stack([mall, mfar, mzero, mzero, mzero, mdiag], 1)]
    ms = [np.stack([mdiag, mall], 1), np.stack([mzero, mdiag], 1)]
    cmpm = np.zeros((128, 16, 128), np.float32)
    for v in range(16):
        cmpm[:, v, :] = np.where(16 * (kl - 8 * v) + 31 <= ql, 0.0, NEGB)
    c = np.arange(512)[:, None]
    s = np.arange(128)[None, :]
    ovl = ((16 * c < 64 * s + 64) & (16 * c + 31 >= 64 * s) & (c < 511)).astype(np.float32)
    ovl = ovl.reshape(4, 128, 128).transpose(1, 0, 2)
    u = np.arange(255)[None, :] - 127
    curl = (r[:, None] >= 64).astype(np.int64)
    forced = (u == curl) | (u == curl - 1)
    invalid = u > curl
    W1 = np.where(forced | invalid, 0.0, 1.0)
    W2 = np.where(invalid, -1e30, np.where(forced, 1e4, 0.0))
    W12 = np.stack([W1, W2], 1).astype(np.float32)
    E = (np.arange(8192)[None, :] // 64 == r[:, None]).astype(np.float32)
    return dict(mw=mw, ms=ms, cmpm=cmpm, ovl=np.ascontiguousarray(ovl), W12=W12, E=E,
                ident=np.eye(128, dtype=np.float32), ones64=np.ones((64, 64), np.float32))


def build_nsa(bcast_rhs=True, nblk=NQB):
    nc = bass.Bass("TRN2", target_bir_lowering=False)
    din = lambda name, shape: nc.dram_tensor(name, list(shape), F32, kind="ExternalInput").ap()
    kT4 = din("kT4", [4, 64, T])
    vtok2 = din("vtok2", [2, 128, NTILE, 64])
    qTd = din("qTd", [64, 4, NQB * 128])
    gld = din("gld", [128, NQB, 12])
    gbd = din("gbd", [128, 12])
    gated = din("gated", [128, NQB, 256])
    qnd = din("qn", [64, 1])
    knd = din("kn", [64, 3])
    posd = din("posT", [64, 2, 32])
    w1d = din("w1d", [2, 64, 32, 256])
    b1d = din("b1d", [128, 2, 2])
    w2d = din("w2d", [128, 2, 2, 64])
    b2kd = din("b2k", [64, 1])
    b2vd = din("b2v", [128, 64])
    ones64d = din("ones64", [64, 64])
    Ed = din("E", [128, T])
    mwd = din("mw", [128, 6, 128])
    msd = din("ms", [128, 2, 128])
    identd = din("ident", [128, 128])
    cmpmd = din("cmpm", [128, 8, 128])
    ovld = din("ovl", [128, 4, 128])
    W12d = din("W12", [128, 2, 253])
    y = nc.dram_tensor("y", [128, NQB, 256], F32, kind="ExternalOutput").ap()
    with ExitStack() as st:
        P = Prog(nc, st)
        dq = ["sp", "pool"]
        ndq = [0]

        def load(name, shape, src, dt=F32):
            t = P.sb(shape, dt, name)
            qn_ = dq[ndq[0] % 2]
            ndq[0] += 1
            P.dma(qn_, lambda e: e.dma_start(out=t[:], in_=src), t, writes=[t])
            return t

        stg = [P.sb([128, 2048], F32, f"stg{i}") for i in range(2)]
        nst = [0]

        def load_cast(dst_ap, dst_lt, src, shape_p, ncols, eng=None):
            s = stg[nst[0] % 2]
            eng = eng or ("dve" if nst[0] % 2 == 0 else "act")
            qn_ = dq[nst[0] % 2]
            nst[0] += 1
            P.dma(qn_, lambda e: e.dma_start(out=s[0:shape_p, 0:ncols], in_=src), s, writes=[s])
            if eng == "dve":
                P.op("dve", lambda e: e.tensor_copy(out=dst_ap, in_=s[0:shape_p, 0:ncols]), reads=[s], writes=[dst_lt])
            else:
                P.op("act", lambda e: e.copy(out=dst_ap, in_=s[0:shape_p, 0:ncols]), reads=[s], writes=[dst_lt])

        banks = [P.ps([128, 512], F32, f"bk{i}") for i in range(8)]
        SB = banks[0:3]
        OC0, OC1, OS, OW, MISC = banks[3], banks[4], banks[5], banks[6], banks[7]
        MSLOT = [OC0, OC1, MISC]
        qn = load("qn", [64, 1], qnd)
        kn = load("kn", [64, 3], knd)
        b1 = load("b1", [128, 2, 2], b1d)
        b2k = load("b2k", [64, 1], b2kd)
        b2v = load("b2v", [128, 64], b2vd)
        ones64 = load("ones64", [64, 64], ones64d)
        identf = load("identf", [128, 128], identd)
        W12 = load("W12", [128, 2, 253], W12d)
        gb = load("gb", [128, 12], gbd)
        gl = load("gl", [128, NQB, 12], gld)
        epst = P.sb([128, 1], F32, "eps")
        P.op("pool", lambda e: e.memset(epst[:], RMS_EPS), writes=[epst])
        qsc = P.sb([64, 1], F32, "qsc")
        P.op("dve", lambda e: e.tensor_scalar(out=qsc[:], in0=qn[:], scalar1=64 ** -0.5, scalar2=None, op0=ALU.mult),
             reads=[qn], writes=[qsc])
        identb = P.sb([128, 128], BF16, "identb")
        P.op("dve", lambda e: e.tensor_copy(out=identb[:], in_=identf[:]), reads=[identf], writes=[identb])
        mwb = P.sb([128, 6, 128], BF16, "mwb")
        load_cast(mwb[:].rearrange("p a b -> p (a b)"), mwb, mwd.rearrange("p a b -> p (a b)"), 128, 768)
        msb = P.sb([128, 2, 128], BF16, "msb")
        load_cast(msb[:].rearrange("p a b -> p (a b)"), msb, msd.rearrange("p a b -> p (a b)"), 128, 256)
        cmpmb = P.sb([128, 8, 128], BF16, "cmpmb")
        load_cast(cmpmb[:].rearrange("p a b -> p (a b)"), cmpmb, cmpmd.rearrange("p a b -> p (a b)"), 128, 1024)
        Eb = P.sb([128, T], BF16, "Eb")
        for i in range(4):
            load_cast(Eb[:, i * 2048:(i + 1) * 2048], Eb, Ed[:, i * 2048:(i + 1) * 2048], 128, 2048)
        P.op("dve", lambda e: e.tensor_tensor(out=gl[:], in0=gl[:], in1=gb[:].unsqueeze(1).to_broadcast([128, NQB, 12]),
                                              op=ALU.add), reads=[gl, gb], writes=[gl])
        P.op("act", lambda e: e.activation(out=gl[:], in_=gl[:], func=AF.Sigmoid), reads=[gl], writes=[gl])
        V1 = [P.sb([128, NTILE, 65], BF16, f"V1_{i}") for i in range(2)]
        for j in range(2):
            P.op("pool", lambda e, j=j: e.memset(V1[j][:, :, 64:65], 1.0), writes=[V1[j]])
            for i in range(2):
                load_cast(V1[j][:, i * 32:(i + 1) * 32, 0:64], V1[j],
                          vtok2[j][:, i * 32:(i + 1) * 32, :].rearrange("p a b -> p (a b)"), 128, 2048)

        rn_sq = [P.sb([64, 512], F32, f"rn_sq{i}") for i in range(4)]
        rn_rt = [P.sb([64, 512], F32, f"rn_rt{i}") for i in range(4)]

        def rms_batch(jobs):
            for i, (src_ap, src_lt, _, _, _, _) in enumerate(jobs):
                P.op("dve", lambda e, i=i, src_ap=src_ap: e.tensor_tensor(out=rn_sq[i][:], in0=src_ap, in1=src_ap, op=ALU.mult),
                     reads=[src_lt], writes=[rn_sq[i]])
            for i in range(len(jobs)):
                P.op("pe", lambda e, i=i: e.matmul(banks[i][0:64, :], ones64[:], rn_sq[i][:], start=True, stop=True),
                     reads=[ones64, rn_sq[i]], writes=[banks[i]])
            for i in range(len(jobs)):
                P.op("act", lambda e, i=i: e.activation(out=rn_rt[i][:], in_=banks[i][0:64, :], func=AF.Ln, scale=1.0 / 64,
                                                        bias=epst[0:64, :]), reads=[banks[i], epst], writes=[rn_rt[i]])
            for i in range(len(jobs)):
                P.op("act", lambda e, i=i: e.activation(out=rn_rt[i][:], in_=rn_rt[i][:], func=AF.Exp, scale=-0.5),
                     reads=[rn_rt[i]], writes=[rn_rt[i]])
            for i, (src_ap, src_lt, scale_ap, scale_lt, dst_ap, dst_lt) in enumerate(jobs):
                P.op("dve", lambda e, i=i, src_ap=src_ap, scale_ap=scale_ap, dst_ap=dst_ap: e.scalar_tensor_tensor(
                    out=dst_ap, in0=src_ap, scalar=scale_ap, in1=rn_rt[i][:], op0=ALU.mult, op1=ALU.mult),
                    reads=[src_lt, scale_lt, rn_rt[i]], writes=[dst_lt])

        def rms_fm(src_ap, src_lt, ncol, scale_ap, scale_lt, dst_ap, dst_lt, bank):
            assert ncol == 512
            rms_batch([(src_ap, src_lt, scale_ap, scale_lt, dst_ap, dst_lt)])

        KT = [P.sb([64, T], BF16, f"KT{i}") for i in range(2)]
        nrm = 0
        for j in range(2):
            for i in range(4):
                s = stg[nst[0] % 2]
                qn_ = dq[nst[0] % 2]
                nst[0] += 1
                P.dma(qn_, lambda e, s=s, j=j, i=i: e.dma_start(out=s[0:64, :], in_=kT4[2 + j][:, i * 2048:(i + 1) * 2048]),
                      s, writes=[s])
                rms_batch([(s[0:64, t * 512:(t + 1) * 512], s, kn[:, 1 + j:2 + j], kn,
                            KT[j][:, i * 2048 + t * 512:i * 2048 + (t + 1) * 512], KT[j]) for t in range(4)])
        kcT = P.sb([64, T + 16], BF16, "kcT")
        P.op("pool", lambda e: e.memset(kcT[:, T:T + 16], 0.0), writes=[kcT])
        w1b = P.sb([64, 32, 256], BF16, "w1b")
        posb = P.sb([64, 2, 32], BF16, "posb")
        load_cast(posb[:].rearrange("p a b -> p (a b)"), posb, posd.rearrange("p a b -> p (a b)"), 64, 64)
        w2f = load("w2f", [128, 2, 2, 64], w2d)
        w2b = P.sb([128, 2, 2, 64], BF16, "w2b")
        P.op("dve", lambda e: e.tensor_copy(out=w2b[:], in_=w2f[:]), reads=[w2f], writes=[w2b])
        hidT = P.sb([128, 2, 512], BF16, "hidT")
        P.op("pool", lambda e: e.memset(hidT[:], 0.0), writes=[hidT])
        biasv = P.sb([128, 2], F32, "biasv")
        hx = P.sb([128, 512], F32, "hx")
        hu = P.sb([128, 512], F32, "hu")
        P.op("pool", lambda e: e.memset(hx[:], 0.0), writes=[hx])
        kcmpT = P.sb([64, 512], BF16, "kcmpT")
        kraw = P.sb([64, 512], F32, "kraw")
        P.op("pool", lambda e: e.memset(kraw[:], 0.0), writes=[kraw])
        Vc1 = P.sb([128, 4, 193], BF16, "Vc1")
        P.op("pool", lambda e: e.memset(Vc1[:, :, 64:65], 1.0), writes=[Vc1])
        load_cast(Vc1[:, :, 65:193], Vc1, ovld.rearrange("p a b -> p (a b)"), 128, 512, eng="dve")
        for kv in range(2):
            for i in range(4):
                load_cast(kcT[:, i * 2048:(i + 1) * 2048], kcT, kT4[kv][:, i * 2048:(i + 1) * 2048], 64, 2048)
            for i in range(4):
                load_cast(w1b[:, i * 8:(i + 1) * 8, :].rearrange("p a b -> p (a b)"), w1b,
                          w1d[kv][:, i * 8:(i + 1) * 8, :].rearrange("p a b -> p (a b)"), 64, 2048)
            for hc in range(2):
                bk = banks[hc]
                for l in range(32):
                    P.op("pe", lambda e, bk=bk, l=l, hc=hc: e.matmul(
                        bk[:, 0:511], w1b[:, l, hc * 128:(hc + 1) * 128], kcT[:, l:l + 16 * 511:16],
                        start=(l == 0), stop=(l == 31)), reads=[w1b, kcT], writes=[bk])
                pb = MISC
                for l in range(32):
                    P.op("pe", lambda e, pb=pb, l=l, hc=hc, kv=kv: e.matmul(
                        pb[:, hc:hc + 1], w1b[:, l, hc * 128:(hc + 1) * 128], posb[:, kv, l:l + 1],
                        start=(l == 0), stop=(l == 31)), reads=[w1b, posb], writes=[pb])
                P.op("dve", lambda e, hc=hc, kv=kv, pb=pb: e.tensor_tensor(
                    out=biasv[:, hc:hc + 1], in0=pb[:, hc:hc + 1], in1=b1[:, kv, hc:hc + 1], op=ALU.add),
                    reads=[pb, b1], writes=[biasv])
                P.op("act", lambda e, bk=bk, hc=hc: e.activation(
                    out=hx[:, 0:511], in_=bk[:, 0:511], func=AF.Identity, bias=biasv[:, hc:hc + 1]),
                    reads=[bk, biasv], writes=[hx])
                P.op("dve", lambda e: e.tensor_tensor(out=hu[:], in0=hx[:], in1=hx[:], op=ALU.mult), reads=[hx], writes=[hu])
                P.op("dve", lambda e: e.tensor_scalar(out=hu[:], in0=hu[:], scalar1=0.044715, scalar2=1.0, op0=ALU.mult,
                                                      op1=ALU.add), reads=[hu], writes=[hu])
                P.op("dve", lambda e: e.tensor_tensor(out=hu[:], in0=hu[:], in1=hx[:], op=ALU.mult), reads=[hu, hx], writes=[hu])
                P.op("act", lambda e: e.activation(out=hu[:], in_=hu[:], func=AF.Tanh, scale=0.7978845608028654),
                     reads=[hu], writes=[hu])
                P.op("dve", lambda e: e.tensor_scalar(out=hu[:], in0=hu[:], scalar1=0.5, scalar2=0.5, op0=ALU.mult,
                                                      op1=ALU.add), reads=[hu], writes=[hu])
                P.op("dve", lambda e, hc=hc: e.tensor_tensor(out=hidT[:, hc, 0:511], in0=hu[:, 0:511], in1=hx[:, 0:511],
                                                            op=ALU.mult), reads=[hu, hx], writes=[hidT])
            if kv == 0:
                bk = banks[2]
                for hc in range(2):
                    P.op("pe", lambda e, bk=bk, hc=hc: e.matmul(
                        bk[0:64, 0:511], w2b[:, 0, hc, :], hidT[:, hc, 0:511], start=(hc == 0), stop=(hc == 1)),
                        reads=[w2b, hidT], writes=[bk])
                P.op("act", lambda e, bk=bk: e.activation(out=kraw[:, 0:511], in_=bk[0:64, 0:511], func=AF.Identity,
                                                          bias=b2k[:, 0:1]), reads=[bk, b2k], writes=[kraw])
                rms_fm(kraw[:, :], kraw, 512, kn[:, 0:1], kn, kcmpT[:, :], kcmpT, banks[0])
            else:
                bk = banks[2]
                for cc in range(4):
                    for hc in range(2):
                        P.op("pe", lambda e, bk=bk, hc=hc, cc=cc: e.matmul(
                            bk[:, cc * 64:(cc + 1) * 64], hidT[:, hc, cc * 128:(cc + 1) * 128], w2b[:, 1, hc, :],
                            start=(hc == 0), stop=(hc == 1)), reads=[w2b, hidT], writes=[bk])
                P.op("dve", lambda e, bk=bk: e.tensor_tensor(
                    out=Vc1[:, :, 0:64], in0=bk[:, 0:256].rearrange("p (c d) -> p c d", c=4),
                    in1=b2v[:].unsqueeze(1).to_broadcast([128, 4, 64]), op=ALU.add), reads=[bk, b2v], writes=[Vc1])
        qTb = P.sb([64, NQB, 4, 128], BF16, "qTb")
        for h in range(4):
            for i in range(2):
                s = stg[nst[0] % 2]
                qn_ = dq[nst[0] % 2]
                nst[0] += 1
                P.dma(qn_, lambda e, s=s, h=h, i=i: e.dma_start(out=s[0:64, :], in_=qTd[:, h, i * 2048:(i + 1) * 2048]),
                      s, writes=[s])
                rms_batch([(s[0:64, t * 512:(t + 1) * 512], s, qsc[:, 0:1], qsc,
                            qTb[:, i * 16 + t * 4:i * 16 + t * 4 + 4, h, :], qTb) for t in range(4)])
        Pt = [P.sb([128, 512], BF16, f"Pt{i}") for i in range(3)]
        npair = [0]
        gt = [P.sb([128, 256], F32, f"gt{i}") for i in range(2)]
        yt = [P.sb([128, 256], F32, f"yt{i}") for i in range(2)]
        rec = P.sb([128, 3, 4], F32, "rec")
        imp = P.sb([128, 128], F32, "imp")
        imp2 = P.sb([128, 128], F32, "imp2")
        m8a = P.sb([128, 8], F32, "m8a")
        m8b = P.sb([128, 8], F32, "m8b")
        self_ = P.sb([128, 128], F32, "sel")
        negT = P.sb([128, 128], BF16, "negT")
        acc = P.sb([128, 4, 64], F32, "acc")
        tmp = P.sb([128, 4, 64], F32, "tmpo")

        items = []

        def pair(kT_ap, kT_lt, qb, biases, V_ap, V_lt, obanks, ow, first, last, pre=(), post=(), mmask=None):
            def front(k):
                S, pt = SB[k % 3], Pt[k % 3]
                if mmask is not None:
                    ms_ = MSLOT[k % 3]
                    P.op("pe", lambda e: e.matmul(ms_[:, 0:128], mmask[0], mmask[2], start=True, stop=True),
                         reads=mmask[1], writes=[ms_])
                P.op("pe", lambda e: e.matmul(S[:], kT_ap, qb, start=True, stop=(len(biases) == 0)),
                     reads=[kT_lt, qTb], writes=[S])
                for bi, (l_ap, lts, r_ap) in enumerate(biases):
                    lastb = bi == len(biases) - 1
                    P.op("pe", lambda e, l_ap=l_ap, r_ap=r_ap, lastb=lastb: e.matmul(
                        S[:], l_ap, r_ap.unsqueeze(1).to_broadcast([128, 4, 128]), start=False, stop=lastb),
                        reads=lts, writes=[S])
                P.op("act", lambda e: e.activation(out=pt[:], in_=S[:], func=AF.Exp), reads=[S], writes=[pt])
                if mmask is not None:
                    P.op("dve", lambda e: e.tensor_tensor(
                        out=pt[:].rearrange("p (h q) -> p h q", h=4), in0=pt[:].rearrange("p (h q) -> p h q", h=4),
                        in1=ms_[:, 0:128].unsqueeze(1).to_broadcast([128, 4, 128]), op=ALU.mult), reads=[pt, ms_], writes=[pt])

            def back(k):
                pt = Pt[k % 3]
                for h in range(4):
                    bk, c0 = obanks[h]
                    st_ = first and (h == 0 or obanks[h][0] is not obanks[h - 1][0])
                    sp_ = last and (h == 3 or obanks[h + 1][0] is not obanks[h][0])
                    P.op("pe", lambda e, bk=bk, c0=c0, h=h, st_=st_, sp_=sp_: e.matmul(
                        bk[:, c0:c0 + ow], pt[:, h * 128:(h + 1) * 128], V_ap, start=st_, stop=sp_),
                        reads=[pt, V_lt], writes=[bk])

            items.append(dict(front=front, back=back, pre=list(pre), post=list(post)))

        oc_b = [(OC0, 0), (OC0, 193), (OC1, 0), (OC1, 193)]
        os_b = [(OS, h * 65) for h in range(4)]
        ow_b = [(OW, h * 65) for h in range(4)]
        ocv = [bk[:, c0:c0 + 193] for bk, c0 in oc_b]

        def gate_load(m):
            g_ = gt[m % 2]
            P.dma("sp", lambda e: e.dma_start(out=g_[:], in_=gated[:, m, :]), g_, writes=[g_])
            P.op("act", lambda e: e.activation(out=g_[:], in_=g_[:], func=AF.Silu), reads=[g_], writes=[g_])

        def topk_chain(m):
            for h in range(4):
                P.op("dve", lambda e, h=h: e.tensor_scalar(out=rec[:, 0, h:h + 1], in0=ocv[h][:, 64:65], scalar1=1e-30,
                                                          scalar2=None, op0=ALU.max), reads=[oc_b[h][0]], writes=[rec])
            P.op("dve", lambda e: e.reciprocal(out=rec[:, 0, :], in_=rec[:, 0, :]), reads=[rec], writes=[rec])
            P.op("dve", lambda e: e.tensor_scalar(out=imp[:], in0=ocv[0][:, 65:193], scalar1=rec[:, 0, 0:1], scalar2=None,
                                                  op0=ALU.mult), reads=[OC0, rec], writes=[imp])
            for h in range(1, 4):
                P.op("dve", lambda e, h=h: e.scalar_tensor_tensor(
                    out=imp[:], in0=ocv[h][:, 65:193], scalar=rec[:, 0, h:h + 1], in1=imp[:], op0=ALU.mult, op1=ALU.add),
                    reads=[oc_b[h][0], rec, imp], writes=[imp])
            P.op("dve", lambda e: e.tensor_tensor(
                out=rec[:, 0, :], in0=rec[:, 0, :], in1=gl[:, m, :].rearrange("p (h b) -> p h b", h=4)[:, :, 0],
                op=ALU.mult), reads=[rec, gl], writes=[rec])
            for hp in range(2):
                bk = oc_b[2 * hp][0]
                P.op("dve", lambda e, bk=bk, hp=hp: e.tensor_tensor(
                    out=acc[:, 2 * hp:2 * hp + 2, :],
                    in0=bk[:, 0:386].rearrange("p (h c) -> p h c", h=2)[:, :, 0:64],
                    in1=rec[:, 0, 2 * hp:2 * hp + 2].unsqueeze(2).to_broadcast([128, 2, 64]), op=ALU.mult),
                    reads=[bk, rec], writes=[acc])
            w0 = 125 - 4 * m
            P.op("dve", lambda e: e.tensor_tensor(out=imp[:], in0=imp[:], in1=W12[:, 0, w0:w0 + 128], op=ALU.mult),
                 reads=[imp, W12], writes=[imp])
            P.op("dve", lambda e: e.tensor_tensor(out=imp[:], in0=imp[:], in1=W12[:, 1, w0:w0 + 128], op=ALU.add),
                 reads=[imp, W12], writes=[imp])
            P.op("dve", lambda e: e.memset(imp[:, 0:1], 1e4), reads=[], writes=[imp])
            P.op("dve", lambda e: e.max(out=m8a[:], in_=imp[:]), reads=[imp], writes=[m8a])
            P.op("dve", lambda e: e.match_replace(out=imp2[:], in_to_replace=m8a[:], in_values=imp[:], imm_value=-3e38),
                 reads=[imp, m8a], writes=[imp2])
            P.op("dve", lambda e: e.max(out=m8b[:], in_=imp2[:]), reads=[imp2], writes=[m8b])
            P.op("dve", lambda e: e.tensor_scalar(out=self_[:], in0=imp[:], scalar1=m8b[:, 7:8], scalar2=None, op0=ALU.is_ge),
                 reads=[imp, m8b], writes=[self_])

        def sel_mask(m):
            P.op("pe", lambda e: e.transpose(MISC[:, 0:128], self_[:], identf[:]), reads=[self_, identf], writes=[MISC])
            P.op("act", lambda e: e.copy(out=negT[:], in_=MISC[:, 0:128]), reads=[MISC], writes=[negT])

        def combine(m):
            g_, yy = gt[m % 2], yt[m % 2]
            for br, ob_ in ((1, os_b), (2, ow_b)):
                bk = ob_[0][0]
                P.op("dve", lambda e, bk=bk, br=br: e.tensor_scalar(
                    out=rec[:, br, :], in0=bk[:, 0:260].rearrange("p (h c) -> p h c", h=4)[:, :, 64],
                    scalar1=1e-30, scalar2=None, op0=ALU.max), reads=[bk], writes=[rec])
                P.op("dve", lambda e, br=br: e.reciprocal(out=rec[:, br, :], in_=rec[:, br, :]), reads=[rec], writes=[rec])
                P.op("dve", lambda e, br=br: e.tensor_tensor(
                    out=rec[:, br, :], in0=rec[:, br, :], in1=gl[:, m, :].rearrange("p (h b) -> p h b", h=4)[:, :, br],
                    op=ALU.mult), reads=[rec, gl], writes=[rec])
                P.op("dve", lambda e, bk=bk, br=br: e.tensor_tensor(
                    out=tmp[:], in0=bk[:, 0:260].rearrange("p (h c) -> p h c", h=4)[:, :, 0:64],
                    in1=rec[:, br, :].unsqueeze(2).to_broadcast([128, 4, 64]), op=ALU.mult),
                    reads=[bk, rec], writes=[tmp])
                P.op("dve", lambda e: e.tensor_tensor(out=acc[:], in0=acc[:], in1=tmp[:], op=ALU.add),
                     reads=[acc, tmp], writes=[acc])
            P.op("dve", lambda e: e.tensor_tensor(
                out=yy[:], in0=acc[:].rearrange("p h d -> p (h d)"), in1=g_[:], op=ALU.mult),
                reads=[acc, g_], writes=[yy])
            P.dma("sp", lambda e: e.dma_start(out=y[:, m, :], in_=yy[:]), yy, reads=[yy], is_out=True)

        for m in range(nblk):
            qb = qTb[:, m, :, :].rearrange("p h q -> p (h q)")
            ccs = m // 8
            for cc in range(ccs + 1):
                biases = []
                if cc == ccs:
                    biases = [(identb[:], [identb, cmpmb], cmpmb[:, m % 8, :])]
                pair(kcmpT[:, cc * 128:(cc + 1) * 128], kcmpT, qb, biases, Vc1[:, cc, :], Vc1, oc_b, 193,
                     cc == 0, cc == ccs, pre=[lambda m=m: gate_load(m)] if cc == 0 else (),
                     post=[lambda m=m: topk_chain(m)] if cc == ccs else ())
            kbs = [kb for kb in range(2 * m - 4, 2 * m + 2) if kb >= 0]
            for kb in kbs:
                o = kb - (2 * m - 4)
                biases = [] if o in (2, 3) else [(identb[:], [identb, mwb], mwb[:, o, :])]
                pair(KT[1][:, kb * 128:(kb + 1) * 128], KT[1], qb, biases, V1[1][:, kb, :], V1[1], ow_b, 65,
                     kb == kbs[0], kb == kbs[-1])
            for kb in range(2 * m + 2):
                biases = []
                if kb >= 2 * m:
                    biases.append((identb[:], [identb, msb], msb[:, kb - 2 * m, :]))
                pair(KT[0][:, kb * 128:(kb + 1) * 128], KT[0], qb, biases, V1[0][:, kb, :], V1[0], os_b, 65,
                     kb == 0, kb == 2 * m + 1, pre=[lambda m=m: sel_mask(m)] if kb == 0 else (),
                     post=[lambda m=m: combine(m)] if kb == 2 * m + 1 else (),
                     mmask=(Eb[:, kb * 128:(kb + 1) * 128], [Eb, negT], negT[:]))
        DEPTH_PIPE = 2
        n_it = len(items)
        for idx in range(n_it + DEPTH_PIPE):
            if idx < n_it:
                for f in items[idx]["pre"]:
                    f()
                items[idx]["front"](idx)
            j = idx - DEPTH_PIPE
            if j >= 0:
                items[j]["back"](j)
                for f in items[j]["post"]:
                    f()
        P.emit()
    return nc


def nsa_inputs(proj, p, l):
    C = nsa_consts()
    maps = []
    nkv = proj[:, :, OFF["nkv"]:OFF["nkv"] + 768].reshape(BATCH, T, 3, 2, 2, 64)
    for c in range(NCORES):
        b, rem = divmod(c, 4)
        g, par = divmod(rem, 2)
        blks = np.arange(NQB) * 2 + par
        tok = (blks[:, None] * 128 + np.arange(128)[None, :]).reshape(-1)
        kT4 = np.stack([nkv[b, :, 0, 0, g].T, nkv[b, :, 0, 1, g].T, nkv[b, :, 1, 0, g].T, nkv[b, :, 2, 0, g].T], 0)
        tokmaj = lambda a: a.reshape(NTILE, 128, -1).transpose(1, 0, 2)
        vtok2 = np.stack([tokmaj(nkv[b, :, 1, 1, g]), tokmaj(nkv[b, :, 2, 1, g])], 0)
        q = proj[b, tok, OFF["nq"] + g * 256:OFF["nq"] + (g + 1) * 256].reshape(-1, 4, 64)
        qTd = q.transpose(2, 1, 0)
        glg = proj[b, tok, OFF["ngl"] + g * 12:OFF["ngl"] + (g + 1) * 12].reshape(NQB, 128, 12).transpose(1, 0, 2)
        gate = proj[b, tok, OFF["ng"] + g * 256:OFF["ng"] + (g + 1) * 256].reshape(NQB, 128, 256).transpose(1, 0, 2)
        cmpm = C["cmpm"][:, par::2, :]
        W12 = C["W12"][:, :, (2 - 2 * par):(2 - 2 * par) + 253]
        f = lambda a: np.ascontiguousarray(a, dtype=np.float32)
        maps.append({
            "kT4": f(kT4), "vtok2": f(vtok2), "qTd": f(qTd), "gld": f(glg),
            "gbd": f(np.broadcast_to(p["nsa_gate_b"][l][None, g * 12:(g + 1) * 12], (128, 12))),
            "gated": f(gate), "qn": f(p["nsa_qn"][l][:, None]), "kn": f(p["nsa_kn"][l].T),
            "posT": f(p["nsa_cmp_pos"][l].transpose(2, 0, 1)),
            "w1d": f(p["nsa_cmp_w1"][l].reshape(2, 32, 64, 256).transpose(0, 2, 1, 3)),
            "b1d": f(p["nsa_cmp_b1"][l].reshape(2, 2, 128).transpose(2, 0, 1)),
            "w2d": f(p["nsa_cmp_w2"][l].reshape(2, 2, 128, 64).transpose(2, 0, 1, 3)),
            "b2k": f(p["nsa_cmp_b2"][l][0][:, None]),
            "b2v": f(np.broadcast_to(p["nsa_cmp_b2"][l][1][None, :], (128, 64))),
            "ones64": C["ones64"], "E": C["E"], "mw": f(C["mw"][par]), "ms": f(C["ms"][par]), "ident": C["ident"],
            "cmpm": f(cmpm), "ovl": C["ovl"], "W12": f(W12),
        })
    return maps


def nsa_gather(res):
    out = np.zeros((BATCH, T, 512), np.float32)
    for c in range(NCORES):
        b, rem = divmod(c, 4)
        g, par = divmod(rem, 2)
        yc = res[c]["y"].transpose(1, 0, 2)
        o = out[b].reshape(NTILE, 128, 512)
        o[par::2, :, g * 256:(g + 1) * 256] = yc
    return out


_CACHE = {}


def _prog(name, fn):
    if name not in _CACHE:
        _CACHE[name] = fn()
    return _CACHE[name]


def _x_to_T(xc):
    return np.ascontiguousarray(xc.T.reshape(KC, 128, -1).transpose(1, 0, 2))


def _fm(a, nchunk):
    af = a.reshape(BATCH * T, nchunk, 128)
    return [np.ascontiguousarray(af[c * NT:(c + 1) * NT].transpose(2, 1, 0)) for c in range(NCORES)]


def _win_layout(w):
    wp = np.zeros((D_MODEL, D_IN_PAD), np.float32)
    wp[:, :D_IN] = w
    return np.ascontiguousarray(wp.reshape(KC, 128, NFC, 128).transpose(2, 1, 0, 3))


def kernel(**p):
    p = {k: np.asarray(v, dtype=np.float32) for k, v in p.items()}
    x = p["x"]
    xT = [_x_to_T(x.reshape(-1, D_MODEL)[c * NT:(c + 1) * NT]) for c in range(NCORES)]
    proj = None
    mixers = None
    for l in range(DEPTH + 1):
        do_out, do_in = l > 0, l < DEPTH
        maps = [{"xT": xT[c]} for c in range(NCORES)]
        if do_out:
            lo = l - 1
            y_gla, y_ssm, y_nsa = mixers
            mix = np.concatenate([y_gla, np.zeros_like(y_ssm), y_nsa], -1)
            mixT = _fm(mix, 8)
            yssmT = _fm(y_ssm, 2)
            uT = _fm(proj[:, :, OFF["su"]:OFF["su"] + 256], 2)
            sgT = _fm(proj[:, :, OFF["sg"]:OFF["sg"] + 256], 2)
            wout = np.ascontiguousarray(p["w_out"][lo].reshape(KC, 128, D_MODEL).transpose(1, 0, 2))
            dsk = np.ascontiguousarray(p["s5_d"][lo].reshape(2, 128).T)
            gluw = np.ascontiguousarray(p["s5_glu_w"][lo].reshape(2, 128, 512).transpose(1, 0, 2))
            glub = np.ascontiguousarray(p["s5_glu_b"][lo].reshape(4, 128).T)
            for c in range(NCORES):
                maps[c].update({"mixT": mixT[c], "wout": wout, "yssmT": yssmT[c], "uT": uT[c], "sgT": sgT[c],
                                "dsk": dsk, "gluw": gluw, "glub": glub})
        if do_in:
            win = _win_layout(p["w_in"][l])
            gin = np.ascontiguousarray(p["norm_g"][l].reshape(KC, 128).T)
            for c in range(NCORES):
                maps[c].update({"win": win, "gin": gin})
        res = _run(_prog(f"op{int(do_out)}{int(do_in)}", lambda: build_op(do_out, do_in)), maps)
        if do_out:
            xT = [res[c]["xoT"] for c in range(NCORES)]
        if not do_in:
            break
        proj = np.concatenate([res[c]["projT"].reshape(D_IN_PAD, NT)[:D_IN].T for c in range(NCORES)], 0)
        proj = proj.reshape(BATCH, T, D_IN)
        y_gla = gla_gather(_run(_prog("gla", build_gla), gla_inputs(proj, p, l)))
        y_ssm = s5_gather(_run(_prog("s5", build_s5), s5_inputs(proj, p, l)))
        y_nsa = nsa_gather(_run(_prog("nsa", build_nsa), nsa_inputs(proj, p, l)))
        mixers = (y_gla, y_ssm, y_nsa)
    out = np.concatenate([xT[c].transpose(2, 1, 0).reshape(NT, D_MODEL) for c in range(NCORES)], 0)
    return out.reshape(BATCH, T, D_MODEL).astype(np.float32)
```

```python
from contextlib import ExitStack
import numpy as np
import concourse.bass as bass
import concourse.mybir as mybir
from concourse.bass_utils import run_bass_kernel_spmd

F32 = mybir.dt.float32
BF16 = mybir.dt.bfloat16
AF = mybir.ActivationFunctionType
ALU = mybir.AluOpType
AX = mybir.AxisListType

NCORES = 8
D_MODEL = 1024
BATCH = 2
SEQ = 8192
DEPTH = 4
D_IN = 3112
D_IN_PAD = 3200
RMS_EPS = 1e-6


class LT:
    def __init__(self, ap, name=""):
        self.ap = ap
        self.name = name
        self.w = None
        self.r = []
        self.dsem = None
        self.dcnt = 0

    def __getitem__(self, idx):
        return self.ap[idx]


class Prog:
    ENGS = ("pe", "dve", "act", "pool", "sp")

    def __init__(self, nc, stack):
        self.nc = nc
        self.stack = stack
        self.q = {e: [] for e in self.ENGS}
        self.sem = {e: stack.enter_context(nc.semaphore("s_" + e)) for e in self.ENGS}
        self.cnt = {e: 0 for e in self.ENGS}
        self.seen = {e: {} for e in self.ENGS}
        self.out_events = []
        self.nsem = 0
        self.ntile = 0

    def sb(self, shape, dt, name=None):
        self.ntile += 1
        name = "sb_" + (name or f"t{self.ntile}")
        t = self.stack.enter_context(self.nc.sbuf_tensor(name, list(shape), dt))
        return LT(t, name)

    def ps(self, shape, dt=F32, name=None):
        self.ntile += 1
        name = "ps_" + (name or f"p{self.ntile}")
        t = self.stack.enter_context(self.nc.psum_tensor(name, list(shape), dt))
        return LT(t, name)

    def sub(self, lt, idx, name=""):
        return LT(lt.ap[idx], name or lt.name)

    def _dsem(self, t):
        if t.dsem is None:
            self.nsem += 1
            t.dsem = self.stack.enter_context(self.nc.semaphore(f"d{self.nsem}"))
        return t.dsem

    def _deps(self, eng, reads, writes):
        evs = []
        for t in reads:
            if t.w is not None:
                evs.append(t.w)
        for t in writes:
            if t.w is not None:
                evs.append(t.w)
            evs.extend(t.r)
        agg = {}
        for sem, val, tile, src in evs:
            if tile is not None:
                val = max(val, tile.dcnt * 16)
            if src == "pe" and eng == "pe":
                continue
            k = id(sem)
            if k not in agg or agg[k][1] < val:
                agg[k] = (sem, val)
        waits = []
        seen = self.seen[eng]
        for k, (sem, val) in agg.items():
            if seen.get(k, 0) >= val:
                continue
            seen[k] = val
            waits.append((sem, val))
        return waits

    def op(self, eng, fn, reads=(), writes=()):
        waits = self._deps(eng, reads, writes)
        self.cnt[eng] += 1
        ev = (self.sem[eng], self.cnt[eng], None, eng)
        self.q[eng].append((waits, fn, (self.sem[eng], 1)))
        for t in reads:
            t.r.append(ev)
        for t in writes:
            t.w = ev
            t.r = []
        return ev

    def dma(self, queue, fn, sbt, reads=(), writes=(), is_out=False):
        waits = self._deps(queue, reads, writes)
        sem = self._dsem(sbt)
        sbt.dcnt += 1
        ev = (sem, sbt.dcnt * 16, sbt, "dma")
        self.q[queue].append((waits, fn, (sem, 16)))
        for t in reads:
            t.r.append(ev)
        for t in writes:
            t.w = ev
            t.r = []
        if is_out:
            self.out_events.append(ev)
        return ev

    def emit(self):
        nc = self.nc
        fin = []
        seen = {}
        for sem, val, tile, _ in self.out_events:
            v = tile.dcnt * 16
            seen[id(sem)] = (sem, v)
        fin = list(seen.values())
        engmap = {"pe": "tensor", "dve": "vector", "act": "scalar", "pool": "gpsimd", "sp": "sync"}
        with nc.Block() as block:
            for e in self.ENGS:
                items = self.q[e]
                extra = fin if e == "sp" else []

                def body(engine, items=items, extra=extra):
                    for waits, fn, inc in items:
                        for sem, val in waits:
                            engine.wait_ge(sem, val)
                        ins = fn(engine)
                        ins.then_inc(inc[0], inc[1])
                    for sem, val in extra:
                        engine.wait_ge(sem, val)

                if items or extra:
                    getattr(block, engmap[e])(body)


def _run(nc, in_maps):
    res = run_bass_kernel_spmd(nc, in_maps, core_ids=list(range(NCORES)))
    return res.results


NT = 2048
KC = 8
NFC = D_IN_PAD // 128
NF32 = 2


def build_op(do_out, do_in):
    nc = bass.Bass("TRN2", target_bir_lowering=False)
    din = lambda name, shape: nc.dram_tensor(name, list(shape), F32, kind="ExternalInput").ap()
    xT = din("xT", [128, KC, NT])
    if do_out:
        mixT = din("mixT", [128, KC, NT])
        wout = din("wout", [128, KC, D_MODEL])
        yssmT = din("yssmT", [128, 2, NT])
        uT = din("uT", [128, 2, NT])
        sgT = din("sgT", [128, 2, NT])
        dskd = din("dsk", [128, 2])
        gluwd = din("gluw", [128, 2, 512])
        glubd = din("glub", [128, 4])
        xoT = nc.dram_tensor("xoT", [128, KC, NT], F32, kind="ExternalOutput").ap()
    if do_in:
        win = din("win", [NFC, 128, KC, 128])
        gin = din("gin", [128, KC])
        projT = nc.dram_tensor("projT", [NFC, 128, NT], F32, kind="ExternalOutput").ap()
    with ExitStack() as st:
        P = Prog(nc, st)
        xs = [P.sb([128, NT], F32, f"x{k}") for k in range(KC)]
        actb = [P.sb([128, NT], BF16, f"ab{k}") for k in range(KC)]
        Fb = [P.sb([128, NT], F32, f"F{i}") for i in range(4)]
        wstg = [P.sb([128, D_MODEL], F32, f"wstg{i}") for i in range(2)]
        banks = [P.ps([128, 512], F32, f"bk{i}") for i in range(8)]
        for k in range(KC):
            P.dma("sp", lambda e, k=k: e.dma_start(out=xs[k][:], in_=xT[:, k, :]), xs[k], writes=[xs[k]])
        nb = 0
        if do_out:
            wob = [P.sb([128, D_MODEL], BF16, f"wob{k}") for k in range(KC)]
            ge = [P.sb([128, NT], BF16, f"ge{k}") for k in range(2)]
            dsk = P.sb([128, 2], F32, "dsk")
            glub = P.sb([128, 4], F32, "glub")
            gluwf = P.sb([128, 2, 512], F32, "gluwf")
            gluwb = P.sb([128, 2, 512], BF16, "gluwb")
            P.dma("sp", lambda e: e.dma_start(out=dsk[:], in_=dskd), dsk, writes=[dsk])
            P.dma("sp", lambda e: e.dma_start(out=glub[:], in_=glubd), glub, writes=[glub])
            P.dma("sp", lambda e: e.dma_start(out=gluwf[:], in_=gluwd), gluwf, writes=[gluwf])
            P.op("act", lambda e: e.copy(out=gluwb[:], in_=gluwf[:]), reads=[gluwf], writes=[gluwb])
            for k in range(KC):
                w = wstg[k % 2]
                P.dma("sp", lambda e, k=k, w=w: e.dma_start(out=w[:], in_=wout[:, k, :]), w, writes=[w])
                P.op("act", lambda e, k=k, w=w: e.copy(out=wob[k][:], in_=w[:]), reads=[w], writes=[wob[k]])
            for i, k in enumerate((0, 1, 4, 5, 6, 7)):
                s_ = Fb[i % 2]
                P.dma("pool", lambda e, k=k, s_=s_: e.dma_start(out=s_[:], in_=mixT[:, k, :]), s_, writes=[s_])
                P.op("dve", lambda e, k=k, s_=s_: e.tensor_copy(out=actb[k][:], in_=s_[:]), reads=[s_], writes=[actb[k]])
            F0, F1, F2, F3 = Fb
            for kc in range(2):
                P.dma("pool", lambda e, kc=kc: e.dma_start(out=F0[:], in_=yssmT[:, kc, :]), F0, writes=[F0])
                P.dma("sp", lambda e, kc=kc: e.dma_start(out=F1[:], in_=uT[:, kc, :]), F1, writes=[F1])
                P.op("dve", lambda e, kc=kc: e.scalar_tensor_tensor(
                    out=F0[:], in0=F1[:], scalar=dsk[:, kc:kc + 1], in1=F0[:], op0=ALU.mult, op1=ALU.add),
                    reads=[F0, F1, dsk], writes=[F0])
                P.op("act", lambda e: e.activation(out=F2[:], in_=F0[:], func=AF.Square), reads=[F0], writes=[F2])
                P.op("dve", lambda e: e.tensor_scalar(out=F2[:], in0=F2[:], scalar1=0.044715, scalar2=1.0, op0=ALU.mult,
                                                      op1=ALU.add), reads=[F2], writes=[F2])
                P.op("dve", lambda e: e.tensor_tensor(out=F2[:], in0=F2[:], in1=F0[:], op=ALU.mult), reads=[F2, F0], writes=[F2])
                P.op("act", lambda e: e.activation(out=F2[:], in_=F2[:], func=AF.Tanh, scale=0.7978845608028654),
                     reads=[F2], writes=[F2])
                P.op("dve", lambda e: e.tensor_scalar(out=F2[:], in0=F2[:], scalar1=0.5, scalar2=0.5, op0=ALU.mult,
                                                      op1=ALU.add), reads=[F2], writes=[F2])
                P.op("dve", lambda e, kc=kc: e.tensor_tensor(out=ge[kc][:], in0=F2[:], in1=F0[:], op=ALU.mult),
                     reads=[F2, F0], writes=[ge[kc]])
            for fcp in range(2):
                P.dma("pool", lambda e, fcp=fcp: e.dma_start(out=F1[:], in_=sgT[:, fcp, :]), F1, writes=[F1])
                P.op("act", lambda e: e.activation(out=F1[:], in_=F1[:], func=AF.Silu), reads=[F1], writes=[F1])
                for tt in range(4):
                    tsl = slice(tt * 512, (tt + 1) * 512)
                    bA, bB = banks[nb % 8], banks[(nb + 1) % 8]
                    nb += 2
                    for kc in range(2):
                        P.op("pe", lambda e, kc=kc, fcp=fcp, tsl=tsl, bA=bA: e.matmul(
                            bA[:], gluwb[:, kc, fcp * 128:(fcp + 1) * 128], ge[kc][:, tsl], start=(kc == 0), stop=(kc == 1)),
                            reads=[gluwb, ge[kc]], writes=[bA])
                    for kc in range(2):
                        P.op("pe", lambda e, kc=kc, fcp=fcp, tsl=tsl, bB=bB: e.matmul(
                            bB[:], gluwb[:, kc, (fcp + 2) * 128:(fcp + 3) * 128], ge[kc][:, tsl], start=(kc == 0),
                            stop=(kc == 1)), reads=[gluwb, ge[kc]], writes=[bB])
                    P.op("act", lambda e, fcp=fcp, tsl=tsl, bB=bB: e.activation(
                        out=F3[:, tsl], in_=bB[:], func=AF.Sigmoid, bias=glub[:, fcp + 2:fcp + 3]),
                        reads=[bB, glub], writes=[F3])
                    P.op("dve", lambda e, fcp=fcp, tsl=tsl, bA=bA: e.scalar_tensor_tensor(
                        out=F3[:, tsl], in0=bA[:], scalar=glub[:, fcp:fcp + 1], in1=F3[:, tsl], op0=ALU.add, op1=ALU.mult),
                        reads=[bA, glub, F3], writes=[F3])
                P.op("dve", lambda e, fcp=fcp: e.tensor_tensor(out=actb[2 + fcp][:], in0=F3[:], in1=F1[:], op=ALU.mult),
                     reads=[F3, F1], writes=[actb[2 + fcp]])
            for fo in range(KC):
                for tt in range(NT // 512):
                    bk = banks[nb % 8]
                    nb += 1
                    for k in range(KC):
                        P.op("pe", lambda e, k=k, fo=fo, tt=tt, bk=bk: e.matmul(
                            bk[:], wob[k][:, fo * 128:(fo + 1) * 128], actb[k][:, tt * 512:(tt + 1) * 512],
                            start=(k == 0), stop=(k == KC - 1)),
                            reads=[wob[k], actb[k]], writes=[bk])
                    P.op("dve", lambda e, fo=fo, tt=tt, bk=bk: e.tensor_tensor(
                        out=xs[fo][:, tt * 512:(tt + 1) * 512], in0=bk[:], in1=xs[fo][:, tt * 512:(tt + 1) * 512],
                        op=ALU.add), reads=[bk, xs[fo]], writes=[xs[fo]])
                P.dma("sp", lambda e, fo=fo: e.dma_start(out=xoT[:, fo, :], in_=xs[fo][:]), xs[fo],
                      reads=[xs[fo]], is_out=True)
        if do_in:
            gt = P.sb([128, KC], F32, "gt")
            P.dma("sp", lambda e: e.dma_start(out=gt[:], in_=gin), gt, writes=[gt])
            ones = P.sb([128, 128], F32, "ones")
            P.op("pool", lambda e: e.memset(ones[:], 1.0), writes=[ones])
            rstd = P.sb([128, NT], F32, "rstd")
            sq = Fb[0:2]
            sbk = banks[0:4]
            for k in range(KC):
                s = sq[k % 2]
                P.op("act", lambda e, k=k, s=s: e.activation(out=s[:], in_=xs[k][:], func=AF.Square),
                     reads=[xs[k]], writes=[s])
                for tt in range(4):
                    P.op("pe", lambda e, k=k, s=s, tt=tt: e.matmul(
                        sbk[tt][:], ones[:], s[:, tt * 512:(tt + 1) * 512], start=(k == 0), stop=(k == KC - 1)),
                        reads=[ones, s], writes=[sbk[tt]])
                P.op("dve", lambda e, k=k: e.tensor_scalar(
                    out=actb[k][:], in0=xs[k][:], scalar1=gt[:, k:k + 1], scalar2=None, op0=ALU.mult),
                    reads=[xs[k], gt], writes=[actb[k]])
            epst = P.sb([128, 1], F32, "eps")
            P.op("pool", lambda e: e.memset(epst[:], RMS_EPS), writes=[epst])
            for tt in range(4):
                P.op("act", lambda e, tt=tt: e.activation(
                    out=rstd[:, tt * 512:(tt + 1) * 512], in_=sbk[tt][:], func=AF.Sqrt,
                    scale=1.0 / D_MODEL, bias=epst[:]), reads=[sbk[tt], epst], writes=[rstd])
            P.op("dve", lambda e: e.reciprocal(out=rstd[:], in_=rstd[:]), reads=[rstd], writes=[rstd])
            wb = [P.sb([128, KC, 128], BF16, f"wb{i}") for i in range(2)]
            ot = Fb[2:4]
            nb = 4
            for fc in range(NFC):
                w = wstg[fc % 2]
                wbb = wb[fc % 2]
                o = ot[fc % 2]
                P.dma("pool", lambda e, fc=fc, w=w: e.dma_start(
                    out=w[:], in_=win[fc].rearrange("p k f -> p (k f)")), w, writes=[w])
                f32c = fc < NF32
                w3 = w[:].rearrange("p (k f) -> p k f", k=KC)
                if f32c:
                    P.op("dve", lambda e, w3=w3: e.tensor_tensor(
                        out=w3, in0=w3, in1=gt[:].unsqueeze(2).to_broadcast([128, KC, 128]), op=ALU.mult),
                        reads=[w, gt], writes=[w])
                else:
                    P.op("act", lambda e, w=w, wbb=wbb: e.copy(out=wbb[:].rearrange("p k f -> p (k f)"), in_=w[:]),
                         reads=[w], writes=[wbb])
                for tt in range(4):
                    bk = banks[4 + nb % 4]
                    nb += 1
                    for k in range(KC):
                        if f32c:
                            P.op("pe", lambda e, k=k, tt=tt, bk=bk, w3=w3: e.matmul(
                                bk[:], w3[:, k, :], xs[k][:, tt * 512:(tt + 1) * 512],
                                start=(k == 0), stop=(k == KC - 1)), reads=[w, xs[k]], writes=[bk])
                        else:
                            P.op("pe", lambda e, k=k, tt=tt, bk=bk, wbb=wbb: e.matmul(
                                bk[:], wbb[:, k, :], actb[k][:, tt * 512:(tt + 1) * 512],
                                start=(k == 0), stop=(k == KC - 1)), reads=[wbb, actb[k]], writes=[bk])
                    P.op("dve", lambda e, tt=tt, bk=bk, o=o: e.tensor_tensor(
                        out=o[:, tt * 512:(tt + 1) * 512], in0=bk[:], in1=rstd[:, tt * 512:(tt + 1) * 512],
                        op=ALU.mult), reads=[bk, rstd], writes=[o])
                P.dma("sp", lambda e, fc=fc, o=o: e.dma_start(out=projT[fc], in_=o[:]), o, reads=[o], is_out=True)
        P.emit()
    return nc


T = SEQ
NTILE = T // 128
NCHUNK = T // 128
GLA_G = 4


def gla_consts():
    j = np.arange(128)[:, None]
    i = np.arange(128)[None, :]
    cm = np.zeros((128, 3, 128), np.float32)
    cm[:, 0, :] = (j <= i)
    cm[:, 1, :] = (j > i)
    cm[:, 2, 0:4] = 1.0
    return cm


def build_gla():
    nc = bass.Bass("TRN2", target_bir_lowering=False)
    qT = nc.dram_tensor("qT", [32, T], F32, kind="ExternalInput").ap()
    kT = nc.dram_tensor("kT", [32, T], F32, kind="ExternalInput").ap()
    lrT1 = nc.dram_tensor("lrT1", [17, T], F32, kind="ExternalInput").ap()
    ktok = nc.dram_tensor("ktok", [128, NTILE, 32], F32, kind="ExternalInput").ap()
    vtok = nc.dram_tensor("vtok", [128, NTILE, 64], F32, kind="ExternalInput").ap()
    gtok = nc.dram_tensor("gtok", [128, NTILE, 64], F32, kind="ExternalInput").ap()
    w2b = nc.dram_tensor("w2b", [17, 32], F32, kind="ExternalInput").ap()
    onb = nc.dram_tensor("onb", [128, 64], F32, kind="ExternalInput").ap()
    cm = nc.dram_tensor("cm", [128, 3, 128], F32, kind="ExternalInput").ap()
    y = nc.dram_tensor("y", [128, NTILE, 64], F32, kind="ExternalOutput").ap()
    NG = NTILE // GLA_G
    with ExitStack() as st:
        P = Prog(nc, st)
        cmt = P.sb([128, 3, 128], F32, "cmt")
        w2t = P.sb([17, 32], F32, "w2t")
        onbt = P.sb([128, 64], F32, "onbt")
        ktokt = P.sb([128, NTILE, 32], F32, "ktokt")
        vb = P.sb([128, NTILE, 64], F32, "vb")
        qeT = P.sb([32, T], F32, "qeT")
        scm = P.sb([128, NTILE, 128], F32, "scm")
        kv_all = P.sb([32, NCHUNK, 64], F32, "kv_all")
        S_bf = P.sb([32, NCHUNK + 1, 64], F32, "S_bf")
        dec_all = P.sb([32, NCHUNK], F32, "dec_all")
        epst = P.sb([128, 1], F32, "eps")
        qTg = [P.sb([32, 512], F32, f"qTg{i}") for i in range(2)]
        kTg = [P.sb([32, 512], F32, f"kTg{i}") for i in range(2)]
        lrg = [P.sb([17, 512], F32, f"lrg{i}") for i in range(2)]
        e1 = P.sb([128, 128], F32, "e1")
        lt = P.sb([128, 128], F32, "lt")
        eqTt = P.sb([32, 512], F32, "eqTt")
        ekTt = P.sb([32, 512], F32, "ekTt")
        ekd = P.sb([128, 128], F32, "ekd")
        keT = P.sb([32, 512], F32, "keT")
        kd = P.sb([128, 4, 32], F32, "kd")
        osb = P.sb([128, 4, 64], F32, "osb")
        sq = P.sb([128, 4, 64], F32, "sq")
        ss = P.sb([128, 4], F32, "ss")
        gg = [P.sb([128, 4, 64], F32, f"gg{i}") for i in range(2)]
        yt = [P.sb([128, 4, 64], F32, f"yt{i}") for i in range(2)]
        bz = P.ps([128, 512], F32, "bz")
        zps = P.sub(bz, (slice(None), slice(0, 128)), "zps")
        sups = P.sub(bz, (slice(None), slice(128, 256)), "sups")
        bT = P.ps([32, 512], F32, "bT")
        bL = P.ps([32, 512], F32, "bL")
        bs = P.ps([128, 512], F32, "bs")
        bkv = P.ps([32, 512], F32, "bkv")
        bo = [P.ps([128, 512], F32, f"bo{i}") for i in range(2)]

        P.dma("sp", lambda e: e.dma_start(out=cmt[:], in_=cm), cmt, writes=[cmt])
        P.dma("sp", lambda e: e.dma_start(out=w2t[:], in_=w2b), w2t, writes=[w2t])
        P.dma("sp", lambda e: e.dma_start(out=onbt[:], in_=onb), onbt, writes=[onbt])
        P.dma("sp", lambda e: e.dma_start(out=ktokt[:], in_=ktok), ktokt, writes=[ktokt])
        P.op("pool", lambda e: e.memset(epst[:], RMS_EPS), writes=[epst])
        for i in range(4):
            P.dma("pool", lambda e, i=i: e.dma_start(out=vb[:, i * 16:(i + 1) * 16, :], in_=vtok[:, i * 16:(i + 1) * 16, :]),
                  vb, writes=[vb])
        LTm = cmt[:, 0, :]
        UTm = cmt[:, 1, :]
        BDs = cmt[:, 2, 0:1]
        for g in range(NG):
            qg, kg, lg = qTg[g % 2], kTg[g % 2], lrg[g % 2]
            tok = slice(g * 512, (g + 1) * 512)
            P.dma("sp", lambda e, qg=qg, tok=tok: e.dma_start(out=qg[:], in_=qT[:, tok]), qg, writes=[qg])
            P.dma("sp", lambda e, kg=kg, tok=tok: e.dma_start(out=kg[:], in_=kT[:, tok]), kg, writes=[kg])
            P.dma("sp", lambda e, lg=lg, tok=tok: e.dma_start(out=lg[:], in_=lrT1[:, tok]), lg, writes=[lg])
            for t in range(4):
                P.op("pe", lambda e, t=t, lg=lg: e.matmul(zps[:, t * 32:(t + 1) * 32], lg[:, t * 128:(t + 1) * 128],
                                                        w2t[:], start=True, stop=True), reads=[lg, w2t], writes=[zps])
            P.op("act", lambda e: e.activation(out=e1[:], in_=zps[:], func=AF.Exp, scale=-1.0), reads=[zps], writes=[e1])
            P.op("act", lambda e: e.activation(out=lt[:], in_=e1[:], func=AF.Ln, bias=1.0), reads=[e1], writes=[lt])
            for t in range(4):
                lsl = lt[:, t * 32:(t + 1) * 32]
                P.op("pe", lambda e, t=t, lsl=lsl: e.matmul(sups[:, t * 32:(t + 1) * 32], UTm, lsl, start=True, stop=True),
                     reads=[cmt, lt], writes=[sups])
                P.op("pe", lambda e, t=t, lsl=lsl: e.matmul(bT[:, t * 128:(t + 1) * 128], lsl, LTm, start=True, stop=True),
                     reads=[cmt, lt], writes=[bT])
                P.op("pe", lambda e, t=t, lsl=lsl: e.matmul(bL[:, t:t + 1], lsl, BDs, start=True, stop=True),
                     reads=[cmt, lt], writes=[bL])
            P.op("act", lambda e: e.activation(out=eqTt[:], in_=bT[:], func=AF.Exp, scale=-1.0 / 16), reads=[bT], writes=[eqTt])
            P.op("act", lambda e: e.activation(out=ekTt[:], in_=bT[:], func=AF.Exp, scale=1.0 / 16), reads=[bT], writes=[ekTt])
            P.op("act", lambda e: e.activation(out=ekd[:], in_=sups[:], func=AF.Exp, scale=-1.0 / 16), reads=[sups], writes=[ekd])
            P.op("act", lambda e, g=g: e.activation(out=dec_all[:, 4 * g:4 * g + 4], in_=bL[:, 0:4], func=AF.Exp,
                                                   scale=-1.0 / 16), reads=[bL], writes=[dec_all])
            P.op("dve", lambda e, qg=qg, tok=tok: e.scalar_tensor_tensor(
                out=qeT[:, tok], in0=qg[:], scalar=32 ** -0.5, in1=eqTt[:], op0=ALU.mult, op1=ALU.mult),
                reads=[qg, eqTt], writes=[qeT])
            P.op("dve", lambda e, kg=kg: e.tensor_tensor(out=keT[:], in0=kg[:], in1=ekTt[:], op=ALU.mult),
                 reads=[kg, ekTt], writes=[keT])
            P.op("dve", lambda e, g=g: e.tensor_tensor(
                out=kd[:], in0=ktokt[:, 4 * g:4 * g + 4, :], in1=ekd[:].rearrange("p (t d) -> p t d", t=4), op=ALU.mult),
                reads=[ktokt, ekd], writes=[kd])
            for t in range(4):
                P.op("pe", lambda e, t=t, g=g: e.matmul(
                    bs[:, t * 128:(t + 1) * 128], keT[:, t * 128:(t + 1) * 128],
                    qeT[:, (4 * g + t) * 128:(4 * g + t + 1) * 128], start=True, stop=True),
                    reads=[keT, qeT], writes=[bs])
            for t in range(4):
                P.op("pe", lambda e, t=t, g=g: e.matmul(
                    bkv[:, t * 64:(t + 1) * 64], kd[:, t, :], vb[:, 4 * g + t, :], start=True, stop=True),
                    reads=[kd, vb], writes=[bkv])
            P.op("dve", lambda e, g=g: e.tensor_tensor(
                out=scm[:, 4 * g:4 * g + 4, :], in0=bs[:].rearrange("p (t i) -> p t i", t=4),
                in1=cmt[:, 0:1, :].to_broadcast([128, 4, 128]), op=ALU.mult), reads=[bs, cmt], writes=[scm])
            P.op("act", lambda e, g=g: e.copy(out=kv_all[:, 4 * g:4 * g + 4, :],
                                             in_=bkv[:, 0:256].rearrange("p (c e) -> p c e", c=4)),
                 reads=[bkv], writes=[kv_all])
        P.op("pool", lambda e: e.memset(S_bf[:, 0, :], 0.0), writes=[S_bf])
        for ee in range(64):
            P.op("dve", lambda e, ee=ee: e.tensor_tensor_scan(
                out=S_bf[:, 1:NCHUNK + 1, ee], data0=dec_all[:], data1=kv_all[:, :, ee], initial=0.0,
                op0=ALU.mult, op1=ALU.add), reads=[dec_all, kv_all], writes=[S_bf])
        for g in range(NG):
            b = bo[g % 2]
            gt_, yy = gg[g % 2], yt[g % 2]
            P.dma("sp", lambda e, g=g, gt_=gt_: e.dma_start(out=gt_[:], in_=gtok[:, 4 * g:4 * g + 4, :]), gt_, writes=[gt_])
            for t in range(4):
                P.op("pe", lambda e, t=t, g=g, b=b: e.matmul(
                    b[:, t * 64:(t + 1) * 64], scm[:, 4 * g + t, :], vb[:, 4 * g + t, :], start=True, stop=False),
                    reads=[scm, vb], writes=[b])
                P.op("pe", lambda e, t=t, g=g, b=b: e.matmul(
                    b[:, t * 64:(t + 1) * 64], qeT[:, (4 * g + t) * 128:(4 * g + t + 1) * 128],
                    S_bf[:, 4 * g + t, :], start=False, stop=True), reads=[qeT, S_bf], writes=[b])
            P.op("act", lambda e, b=b: e.copy(out=osb[:], in_=b[:, 0:256].rearrange("p (t e) -> p t e", t=4)),
                 reads=[b], writes=[osb])
            P.op("dve", lambda e: e.tensor_tensor(out=sq[:], in0=osb[:], in1=osb[:], op=ALU.mult), reads=[osb], writes=[sq])
            P.op("dve", lambda e: e.tensor_reduce(out=ss[:], in_=sq[:], axis=AX.X, op=ALU.add), reads=[sq], writes=[ss])
            P.op("act", lambda e: e.activation(out=ss[:], in_=ss[:], func=AF.Sqrt, scale=1.0 / 64, bias=epst[:]),
                 reads=[ss, epst], writes=[ss])
            P.op("dve", lambda e: e.reciprocal(out=ss[:], in_=ss[:]), reads=[ss], writes=[ss])
            P.op("act", lambda e, gt_=gt_: e.activation(out=gt_[:], in_=gt_[:], func=AF.Silu), reads=[gt_], writes=[gt_])
            P.op("dve", lambda e, gt_=gt_: e.tensor_tensor(
                out=gt_[:], in0=gt_[:], in1=onbt[:].unsqueeze(1).to_broadcast([128, 4, 64]), op=ALU.mult),
                reads=[gt_, onbt], writes=[gt_])
            P.op("dve", lambda e: e.tensor_tensor(
                out=osb[:], in0=osb[:], in1=ss[:].unsqueeze(2).to_broadcast([128, 4, 64]), op=ALU.mult),
                reads=[osb, ss], writes=[osb])
            P.op("dve", lambda e, gt_=gt_, yy=yy: e.tensor_tensor(out=yy[:], in0=osb[:], in1=gt_[:], op=ALU.mult),
                 reads=[osb, gt_], writes=[yy])
            P.dma("sp", lambda e, g=g, yy=yy: e.dma_start(out=y[:, 4 * g:4 * g + 4, :], in_=yy[:]), yy,
                  reads=[yy], is_out=True)
        P.emit()
    return nc


OFF = dict(gq=0, gk=128, gv=256, glr=512, gg=528, su=784, sg=1040, nq=1296, nkv=1808, ngl=2576, ng=2600)


def gla_inputs(proj, p, l):
    cm = gla_consts()
    maps = []
    for c in range(NCORES):
        b, h = divmod(c, 4)
        q = proj[b, :, OFF["gq"] + h * 32:OFF["gq"] + (h + 1) * 32]
        k = proj[b, :, OFF["gk"] + h * 32:OFF["gk"] + (h + 1) * 32]
        v = proj[b, :, OFF["gv"] + h * 64:OFF["gv"] + (h + 1) * 64]
        lr = proj[b, :, OFF["glr"]:OFF["glr"] + 16]
        gt_ = proj[b, :, OFF["gg"] + h * 64:OFF["gg"] + (h + 1) * 64]
        tokmaj = lambda a: np.ascontiguousarray(a.reshape(NTILE, 128, -1).transpose(1, 0, 2))
        maps.append({
            "qT": np.ascontiguousarray(q.T), "kT": np.ascontiguousarray(k.T),
            "lrT1": np.ascontiguousarray(np.concatenate([lr.T, np.ones((1, T), np.float32)], 0)),
            "ktok": tokmaj(k), "vtok": tokmaj(v), "gtok": tokmaj(gt_),
            "w2b": np.ascontiguousarray(np.concatenate(
                [p["gla_w2"][l][:, h * 32:(h + 1) * 32], p["gla_b2"][l][None, h * 32:(h + 1) * 32]], 0)),
            "onb": np.ascontiguousarray(np.broadcast_to(p["gla_onorm"][l][None, :], (128, 64))),
            "cm": cm,
        })
    return maps


def gla_gather(res):
    out = np.zeros((BATCH, T, 256), np.float32)
    for c in range(NCORES):
        b, h = divmod(c, 4)
        out[b, :, h * 64:(h + 1) * 64] = res[c]["y"].transpose(1, 0, 2).reshape(T, 64)
    return out


S5L = 64
S5N = T // S5L


def s5_consts():
    kio = np.broadcast_to(np.arange(65, dtype=np.float32)[None, :], (128, 65)).copy()
    r = np.arange(128)
    dmask = ((r[None, :] // 16) >= (r[:, None] // 16)).astype(np.float32)
    Wm = np.concatenate([np.zeros((128, 7 * 128), np.float32), dmask, np.ones((128, 7 * 128), np.float32)], 1)
    ident = np.eye(128, dtype=np.float32)
    return kio, Wm, ident


def _fact(n):
    f = 1.0
    for i in range(2, n + 1):
        f *= i
    return f


def build_s5():
    nc = bass.Bass("TRN2", target_bir_lowering=False)
    U = nc.dram_tensor("U", [2, 128, 8, 256], F32, kind="ExternalInput").ap()
    lam = nc.dram_tensor("lam", [128, 2], F32, kind="ExternalInput").ap()
    lst = nc.dram_tensor("lst", [128, 1], F32, kind="ExternalInput").ap()
    Bri = nc.dram_tensor("Bri", [128, 2, 16], F32, kind="ExternalInput").ap()
    Cri = nc.dram_tensor("Cri", [128, 2, 16], F32, kind="ExternalInput").ap()
    kio_d = nc.dram_tensor("kio", [128, 65], F32, kind="ExternalInput").ap()
    Wm_d = nc.dram_tensor("Wm", [128, 1920], F32, kind="ExternalInput").ap()
    id_d = nc.dram_tensor("ident", [128, 128], F32, kind="ExternalInput").ap()
    Y = nc.dram_tensor("Y", [2, 128, 8, 256], F32, kind="ExternalOutput").ap()
    with ExitStack() as st:
        P = Prog(nc, st)
        col = lambda name, w=1: P.sb([128, w], F32, name)

        def load(name, shape, src):
            t = P.sb(shape, F32, name)
            P.dma("sp", lambda e: e.dma_start(out=t[:], in_=src), t, writes=[t])
            return t

        lamt = load("lamt", [128, 2], lam)
        lstt = load("lstt", [128, 1], lst)
        Bt = load("Bt", [128, 2, 16], Bri)
        Ct = load("Ct", [128, 2, 16], Cri)
        kio = load("kiot", [128, 65], kio_d)
        Wm = load("Wmt", [128, 1920], Wm_d)
        ident = load("identt", [128, 128], id_d)
        Ut = []
        for gl in range(2):
            t = P.sb([128, 8, 256], F32, f"U{gl}")
            P.dma("pool", lambda e, t=t, gl=gl: e.dma_start(out=t[:], in_=U[gl]), t, writes=[t])
            Ut.append(t)

        def ts(out, in0, s1, op0, s2=None, op1=None, r=(), w=()):
            if op1 is None:
                P.op("dve", lambda e: e.tensor_scalar(out=out, in0=in0, scalar1=s1, scalar2=None, op0=op0), reads=r, writes=w)
            else:
                P.op("dve", lambda e: e.tensor_scalar(out=out, in0=in0, scalar1=s1, scalar2=s2, op0=op0, op1=op1),
                     reads=r, writes=w)

        def tt(out, a, b, op, r=(), w=()):
            P.op("dve", lambda e: e.tensor_tensor(out=out, in0=a, in1=b, op=op), reads=r, writes=w)

        def stt(out, in0, sc, in1, op0, op1, r=(), w=()):
            P.op("dve", lambda e: e.scalar_tensor_tensor(out=out, in0=in0, scalar=sc, in1=in1, op0=op0, op1=op1),
                 reads=r, writes=w)

        def horner(name, xs, coefs):
            acc = col(name)
            P.op("pool", lambda e: e.memset(acc[:], float(coefs[-1])), writes=[acc])
            for c in reversed(coefs[:-1]):
                ts(acc[:], acc[:], xs[:, 0:1], ALU.mult, float(c), ALU.add, r=[acc, xs], w=[acc])
            return acc

        y4 = col("y4")
        ts(y4[:], lstt[:], 0.25, ALU.mult, r=[lstt], w=[y4])
        dt = horner("dt", y4, [1.0 / _fact(k) for k in range(19)])
        tt(dt[:], dt[:], dt[:], ALU.mult, r=[dt], w=[dt])
        tt(dt[:], dt[:], dt[:], ALU.mult, r=[dt], w=[dt])
        lr = col("lr")
        ts(lr[:], lamt[:, 0:1], -1e-4, ALU.min, r=[lamt], w=[lr])
        li = col("li")
        ts(li[:], lamt[:, 1:2], 1.0, ALU.mult, r=[lamt], w=[li])
        xx = col("xx")
        tt(xx[:], lr[:], dt[:], ALU.mult, r=[lr, dt], w=[xx])
        negx = col("negx")
        ts(negx[:], xx[:], -1.0, ALU.mult, r=[xx], w=[negx])
        q = horner("q", xx, [1.0 / _fact(k + 1) for k in range(10)])
        em1 = col("em1")
        tt(em1[:], q[:], xx[:], ALU.mult, r=[q, xx], w=[em1])
        mag = col("mag")
        ts(mag[:], em1[:], 1.0, ALU.add, r=[em1], w=[mag])
        phi = col("phi")
        stt(phi[:], li[:], 1.0 / 32, dt[:], ALU.mult, ALU.mult, r=[li, dt], w=[phi])
        ww = col("ww")
        tt(ww[:], phi[:], phi[:], ALU.mult, r=[phi], w=[ww])
        ps_ = horner("ps", ww, [(-1.0) ** k / _fact(2 * k + 1) for k in range(8)])
        pc_ = horner("pc", ww, [(-1.0) ** (k + 1) / _fact(2 * k + 2) for k in range(8)])
        sA = col("sA")
        tt(sA[:], ps_[:], phi[:], ALU.mult, r=[ps_, phi], w=[sA])
        cA = col("cA")
        tt(cA[:], pc_[:], ww[:], ALU.mult, r=[pc_, ww], w=[cA])
        sB, cB, a1, s2 = col("sB"), col("cB"), col("a1"), col("s2")
        cur = (cA, sA)
        nxt = (cB, sB)
        for _ in range(5):
            cm_, s_ = cur
            cn, sn = nxt
            ts(a1[:], cm_[:], 2.0, ALU.add, cm_[:, 0:1], ALU.mult, r=[cm_], w=[a1])
            tt(s2[:], s_[:], s_[:], ALU.mult, r=[s_], w=[s2])
            tt(cn[:], a1[:], s2[:], ALU.subtract, r=[a1, s2], w=[cn])
            ts(sn[:], cm_[:], 1.0, ALU.add, s_[:, 0:1], ALU.mult, r=[cm_, s_], w=[sn])
            ts(sn[:], sn[:], 2.0, ALU.mult, r=[sn], w=[sn])
            cur, nxt = nxt, cur
        cm_, s_ = cur
        cc = col("cc")
        ts(cc[:], cm_[:], 1.0, ALU.add, r=[cm_], w=[cc])
        ai = col("ai")
        tt(ai[:], mag[:], s_[:], ALU.mult, r=[mag, s_], w=[ai])
        am1r = col("am1r")
        tt(am1r[:], mag[:], cm_[:], ALU.mult, r=[mag, cm_], w=[am1r])
        tt(am1r[:], am1r[:], em1[:], ALU.add, r=[am1r, em1], w=[am1r])
        den = col("den")
        tt(den[:], lr[:], lr[:], ALU.mult, r=[lr], w=[den])
        stt(den[:], li[:], li[:, 0:1], den[:], ALU.mult, ALU.add, r=[li, den], w=[den])
        P.op("dve", lambda e: e.reciprocal(out=den[:], in_=den[:]), reads=[den], writes=[den])
        u1, u2, fr, fi = col("u1"), col("u2"), col("fr"), col("fi")
        tt(u1[:], am1r[:], lr[:], ALU.mult, r=[am1r, lr], w=[u1])
        stt(u1[:], ai[:], li[:, 0:1], u1[:], ALU.mult, ALU.add, r=[ai, li, u1], w=[u1])
        tt(fr[:], u1[:], den[:], ALU.mult, r=[u1, den], w=[fr])
        tt(u2[:], am1r[:], li[:], ALU.mult, r=[am1r, li], w=[u2])
        stt(u2[:], ai[:], lr[:, 0:1], u2[:], ALU.mult, ALU.subtract, r=[ai, lr, u2], w=[u2])
        tt(fi[:], u2[:], den[:], ALU.mult, r=[u2, den], w=[fi])
        Bb = P.sb([128, 2, 16], F32, "Bb")
        v1 = col("v1", 16)
        ts(v1[:], Bt[:, 1, :], fi[:, 0:1], ALU.mult, r=[Bt, fi], w=[v1])
        stt(Bb[:, 0, :], Bt[:, 0, :], fr[:, 0:1], v1[:], ALU.mult, ALU.subtract, r=[Bt, fr, v1], w=[Bb])
        ts(v1[:], Bt[:, 0, :], fi[:, 0:1], ALU.mult, r=[Bt, fi], w=[v1])
        stt(Bb[:, 1, :], Bt[:, 1, :], fr[:, 0:1], v1[:], ALU.mult, ALU.add, r=[Bt, fr, v1], w=[Bb])
        Er, Ei = col("Er", 65), col("Ei", 65)
        t1, t2 = col("t1", 32), col("t2", 32)
        P.op("pool", lambda e: e.memset(Er[:, 0:1], 1.0), writes=[Er])
        P.op("pool", lambda e: e.memset(Ei[:, 0:1], 0.0), writes=[Ei])
        ts(Er[:, 1:2], cc[:], 1.0, ALU.mult, r=[cc], w=[Er])
        ts(Ei[:, 1:2], s_[:], 1.0, ALU.mult, r=[s_], w=[Ei])
        m = 1
        while m <= 32:
            er, ei = Er[:, m:m + 1], Ei[:, m:m + 1]
            ts(t1[:, 0:m], Ei[:, 1:m + 1], ei, ALU.mult, r=[Ei], w=[t1])
            ts(t2[:, 0:m], Ei[:, 1:m + 1], er, ALU.mult, r=[Ei, Er], w=[t2])
            stt(Ei[:, m + 1:2 * m + 1], Er[:, 1:m + 1], ei, t2[:, 0:m], ALU.mult, ALU.add, r=[Er, Ei, t2], w=[Ei])
            stt(Er[:, m + 1:2 * m + 1], Er[:, 1:m + 1], er, t1[:, 0:m], ALU.mult, ALU.subtract, r=[Er, t1], w=[Er])
            m *= 2
        magk, imagk = col("magk", 65), col("imagk", 65)
        P.op("act", lambda e: e.activation(out=magk[:], in_=kio[:], func=AF.Exp, scale=xx[:, 0:1]), reads=[kio, xx], writes=[magk])
        P.op("act", lambda e: e.activation(out=imagk[:], in_=kio[:], func=AF.Exp, scale=negx[:, 0:1]), reads=[kio, negx],
             writes=[imagk])
        Pr, Pi, Qr, Qi = col("Pr", 65), col("Pi", 65), col("Qr", 65), col("Qi", 65)
        tt(Pr[:], Er[:], magk[:], ALU.mult, r=[Er, magk], w=[Pr])
        tt(Pi[:], Ei[:], magk[:], ALU.mult, r=[Ei, magk], w=[Pi])
        tt(Qr[:], Er[:], imagk[:], ALU.mult, r=[Er, imagk], w=[Qr])
        stt(Qi[:], Ei[:], -1.0, imagk[:], ALU.mult, ALU.mult, r=[Ei, imagk], w=[Qi])
        KBr = P.sb([128, 64, 16], F32, "KBr")
        KBi = P.sb([128, 64, 16], F32, "KBi")
        QCr = P.sb([128, 64, 16], F32, "QCr")
        QCi = P.sb([128, 64, 16], F32, "QCi")
        tmp = P.sb([128, 64, 16], F32, "tmp")
        bj = lambda tl: tl[:, 0:64].unsqueeze(2).to_broadcast([128, 64, 16])
        bc = lambda ap: ap.unsqueeze(1).to_broadcast([128, 64, 16])
        tt(KBr[:], bj(Qr), bc(Bb[:, 0, :]), ALU.mult, r=[Qr, Bb], w=[KBr])
        tt(tmp[:], bj(Qi), bc(Bb[:, 1, :]), ALU.mult, r=[Qi, Bb], w=[tmp])
        tt(KBr[:], KBr[:], tmp[:], ALU.subtract, r=[KBr, tmp], w=[KBr])
        tt(KBi[:], bj(Qr), bc(Bb[:, 1, :]), ALU.mult, r=[Qr, Bb], w=[KBi])
        tt(tmp[:], bj(Qi), bc(Bb[:, 0, :]), ALU.mult, r=[Qi, Bb], w=[tmp])
        tt(KBi[:], KBi[:], tmp[:], ALU.add, r=[KBi, tmp], w=[KBi])
        tt(QCr[:], bj(Pr), bc(Ct[:, 0, :]), ALU.mult, r=[Pr, Ct], w=[QCr])
        tt(tmp[:], bj(Pi), bc(Ct[:, 1, :]), ALU.mult, r=[Pi, Ct], w=[tmp])
        tt(QCr[:], QCr[:], tmp[:], ALU.subtract, r=[QCr, tmp], w=[QCr])
        tt(QCi[:], bj(Pr), bc(Ct[:, 1, :]), ALU.mult, r=[Pr, Ct], w=[QCi])
        tt(tmp[:], bj(Pi), bc(Ct[:, 0, :]), ALU.mult, r=[Pi, Ct], w=[tmp])
        stt(QCi[:], QCi[:], -1.0, tmp[:], ALU.mult, ALU.subtract, r=[QCi, tmp], w=[QCi])
        fl = lambda tl: tl[:].rearrange("p j c -> p (j c)")
        banks = [P.ps([128, 512], F32, f"bk{i}") for i in range(8)]
        nb = 0
        TZ = [P.sb([128, 8, 1024], F32, f"TZ{gl}") for gl in range(2)]
        for gl in range(2):
            rows = slice(64 * gl, 64 * gl + 64)
            for rc in range(8):
                for ch in range(2):
                    if 4 * ch + 3 < rc:
                        continue
                    bk = banks[nb % 4]
                    nb += 1
                    P.op("pe", lambda e, bk=bk, rows=rows, rc=rc, ch=ch: e.matmul(
                        bk[:], fl(KBr)[rows, rc * 128:(rc + 1) * 128], fl(QCr)[rows, ch * 512:(ch + 1) * 512],
                        start=True, stop=False), reads=[KBr, QCr], writes=[bk])
                    P.op("pe", lambda e, bk=bk, rows=rows, rc=rc, ch=ch: e.matmul(
                        bk[:], fl(KBi)[rows, rc * 128:(rc + 1) * 128], fl(QCi)[rows, ch * 512:(ch + 1) * 512],
                        start=False, stop=True), reads=[KBi, QCi], writes=[bk])
                    w0 = (7 - rc) * 128 + ch * 512
                    P.op("dve", lambda e, bk=bk, gl=gl, rc=rc, ch=ch, w0=w0: e.tensor_tensor(
                        out=TZ[gl][:, rc, ch * 512:(ch + 1) * 512], in0=bk[:], in1=Wm[:, w0:w0 + 512], op=ALU.mult),
                        reads=[bk, Wm], writes=[TZ[gl]])
        KBT = [[P.sb([128, 8, 64], F32, f"KBT{gl}{ri}") for ri in range(2)] for gl in range(2)]
        for gl in range(2):
            rows = slice(64 * gl, 64 * gl + 64)
            for ri, src in enumerate((KBr, KBi)):
                bk = banks[4 + (2 * gl + ri) % 2]
                for kc in range(8):
                    P.op("pe", lambda e, bk=bk, rows=rows, kc=kc, src=src: e.transpose(
                        bk[:, kc * 64:(kc + 1) * 64], fl(src)[rows, kc * 128:(kc + 1) * 128], ident[rows, rows]),
                        reads=[src, ident], writes=[bk])
                P.op("act", lambda e, bk=bk, gl=gl, ri=ri: e.copy(
                    out=KBT[gl][ri][:], in_=bk[:].rearrange("p (k c) -> p k c", k=8)), reads=[bk], writes=[KBT[gl][ri]])
        bx = banks[6]
        for gl in range(2):
            rows = slice(64 * gl, 64 * gl + 64)
            for ri in range(2):
                for kc in range(8):
                    P.op("pe", lambda e, gl=gl, rows=rows, ri=ri, kc=kc: e.matmul(
                        bx[rows, ri * 256:(ri + 1) * 256], KBT[gl][ri][:, kc, :], Ut[gl][:, kc, :],
                        start=(kc == 0), stop=(kc == 7)), reads=[KBT[gl][ri], Ut[gl]], writes=[bx])
        Wr, Wi = col("Wr", 256), col("Wi", 256)
        Xs = col("Xs", 512)
        P.op("act", lambda e: e.copy(out=Xs[:], in_=bx[:]), reads=[bx], writes=[Xs])
        Ar, Ai = Pr[:, 64:65], Pi[:, 64:65]
        ts(Wr[:], Xs[:, 256:512], Ai, ALU.mult, r=[Xs, Pi], w=[Wr])
        stt(Wr[:], Xs[:, 0:256], Ar, Wr[:], ALU.mult, ALU.subtract, r=[Xs, Pr, Wr], w=[Wr])
        ts(Wi[:], Xs[:, 0:256], Ai, ALU.mult, r=[Xs, Pi], w=[Wi])
        stt(Wi[:], Xs[:, 256:512], Ar, Wi[:], ALU.mult, ALU.add, r=[Xs, Pr, Wi], w=[Wi])
        Zr, Zi = P.sb([128, 2, 128], F32, "Zr"), P.sb([128, 2, 128], F32, "Zi")
        P.op("pool", lambda e: e.memset(Zr[:], 0.0), writes=[Zr])
        P.op("pool", lambda e: e.memset(Zi[:], 0.0), writes=[Zi])
        Ya = (P.sb([128, 2, 128], F32, "Yar"), P.sb([128, 2, 128], F32, "Yai"))
        Yb = (P.sb([128, 2, 128], F32, "Ybr"), P.sb([128, 2, 128], F32, "Ybi"))
        sc1 = P.sb([128, 2, 128], F32, "sc1")
        sc2 = P.sb([128, 2, 128], F32, "sc2")
        Mp = P.sb([128, 7, 2], F32, "Mp")
        ma, mb_ = col("ma"), col("mb")
        ts(Mp[:, 0, 0:1], Ar, 1.0, ALU.mult, r=[Pr], w=[Mp])
        ts(Mp[:, 0, 1:2], Ai, 1.0, ALU.mult, r=[Pi], w=[Mp])
        for j in range(6):
            mr_, mi_ = Mp[:, j, 0:1], Mp[:, j, 1:2]
            tt(ma[:], mr_, mr_, ALU.mult, r=[Mp], w=[ma])
            tt(mb_[:], mi_, mi_, ALU.mult, r=[Mp], w=[mb_])
            tt(Mp[:, j + 1, 0:1], ma[:], mb_[:], ALU.subtract, r=[ma, mb_], w=[Mp])
            stt(Mp[:, j + 1, 1:2], mr_, 2.0, mi_, ALU.mult, ALU.mult, r=[Mp], w=[Mp])
        W3r = Wr[:].rearrange("p (b n) -> p b n", b=2)
        W3i = Wi[:].rearrange("p (b n) -> p b n", b=2)
        src = None
        N_ = S5N
        for j in range(7):
            sft = 1 << j
            dst = Ya if j % 2 == 0 else Yb
            if src is None:
                sr, si, srl, sil = W3r, W3i, [Wr], [Wi]
            else:
                sr, si, srl, sil = src[0][:], src[1][:], [src[0]], [src[1]]
            mr_, mi_ = Mp[:, j, 0:1], Mp[:, j, 1:2]
            lo, hi = slice(0, N_ - sft), slice(sft, N_)
            ts(sc1[:, :, lo], si[:, :, lo], mi_, ALU.mult, r=sil + [Mp], w=[sc1])
            stt(sc1[:, :, lo], sr[:, :, lo], mr_, sc1[:, :, lo], ALU.mult, ALU.subtract, r=srl + [Mp, sc1], w=[sc1])
            ts(sc2[:, :, lo], sr[:, :, lo], mi_, ALU.mult, r=srl + [Mp], w=[sc2])
            stt(sc2[:, :, lo], si[:, :, lo], mr_, sc2[:, :, lo], ALU.mult, ALU.add, r=sil + [Mp, sc2], w=[sc2])
            tt(dst[0][:, :, hi], sr[:, :, hi], sc1[:, :, lo], ALU.add, r=srl + [sc1], w=[dst[0]])
            tt(dst[1][:, :, hi], si[:, :, hi], sc2[:, :, lo], ALU.add, r=sil + [sc2], w=[dst[1]])
            P.op("act", lambda e, dst=dst, sr=sr, sft=sft: e.copy(out=dst[0][:, :, 0:sft], in_=sr[:, :, 0:sft]),
                 reads=srl, writes=[dst[0]])
            P.op("act", lambda e, dst=dst, si=si, sft=sft: e.copy(out=dst[1][:, :, 0:sft], in_=si[:, :, 0:sft]),
                 reads=sil, writes=[dst[1]])
            src = dst
        P.op("act", lambda e: e.copy(out=Zr[:, :, 1:N_], in_=src[0][:, :, 0:N_ - 1]), reads=[src[0]], writes=[Zr])
        P.op("dve", lambda e: e.tensor_copy(out=Zi[:, :, 1:N_], in_=src[1][:, :, 0:N_ - 1]), reads=[src[1]], writes=[Zi])
        Yt = [P.sb([128, 256], F32, f"Yt{i}") for i in range(2)]
        ny = 0
        for gl in range(2):
            rows = slice(64 * gl, 64 * gl + 64)
            for ob in range(8):
                bk = banks[ny % 4]
                yt_ = Yt[ny % 2]
                ny += 1
                for kc in range(ob + 1):
                    P.op("pe", lambda e, bk=bk, gl=gl, kc=kc, ob=ob: e.matmul(
                        bk[:, 0:256], TZ[gl][:, kc, ob * 128:(ob + 1) * 128], Ut[gl][:, kc, :],
                        start=(kc == 0), stop=False), reads=[TZ[gl], Ut[gl]], writes=[bk])
                P.op("pe", lambda e, bk=bk, rows=rows, ob=ob: e.matmul(
                    bk[:, 0:256], fl(QCr)[rows, ob * 128:(ob + 1) * 128], Zr[rows].rearrange("p b n -> p (b n)"),
                    start=False, stop=False), reads=[QCr, Zr], writes=[bk])
                P.op("pe", lambda e, bk=bk, rows=rows, ob=ob: e.matmul(
                    bk[:, 0:256], fl(QCi)[rows, ob * 128:(ob + 1) * 128], Zi[rows].rearrange("p b n -> p (b n)"),
                    start=False, stop=True), reads=[QCi, Zi], writes=[bk])
                P.op("act", lambda e, bk=bk, yt_=yt_: e.copy(out=yt_[:], in_=bk[:, 0:256]), reads=[bk], writes=[yt_])
                P.dma("sp", lambda e, gl=gl, ob=ob, yt_=yt_: e.dma_start(out=Y[gl, :, ob, :], in_=yt_[:]), yt_,
                      reads=[yt_], is_out=True)
        P.emit()
    return nc


def s5_inputs(proj, p, l):
    kio, Wm, ident = s5_consts()
    maps = []
    for c in range(NCORES):
        Us, lam, lst, Bri, Cri = [], [], [], [], []
        for gl in range(2):
            g = 2 * c + gl
            u = proj[:, :, OFF["su"] + g * 16:OFF["su"] + (g + 1) * 16]
            u = u.reshape(BATCH, S5N, 8, 8, 16).transpose(3, 4, 2, 0, 1)
            Us.append(u.reshape(128, 8, 2 * S5N))
            lam.append(np.stack([p["s5_lam_re"][l][g], p["s5_lam_im"][l][g]], 1))
            lst.append(np.full((64, 1), p["s5_log_step"][l][g], np.float32))
            Bri.append(np.stack([p["s5_b_re"][l][g], p["s5_b_im"][l][g]], 1))
            Cri.append(np.stack([p["s5_c_re"][l][g].T, p["s5_c_im"][l][g].T], 1))
        cat = lambda xs: np.ascontiguousarray(np.concatenate(xs, 0).astype(np.float32))
        maps.append({"U": np.ascontiguousarray(np.stack(Us, 0)), "lam": cat(lam), "lst": cat(lst),
                     "Bri": cat(Bri), "Cri": cat(Cri), "kio": kio, "Wm": Wm, "ident": ident})
    return maps


def s5_gather(res):
    out = np.zeros((BATCH, T, 256), np.float32)
    for c in range(NCORES):
        Yc = res[c]["Y"]
        for gl in range(2):
            g = 2 * c + gl
            a = Yc[gl].reshape(8, 16, 8, BATCH, S5N).transpose(3, 4, 2, 0, 1)
            out[:, :, g * 16:(g + 1) * 16] = a.reshape(BATCH, T, 16)
    return out


NQB = 32
NEGB = -30000.0


def nsa_consts():
    r = np.arange(128)
    kl, ql = r[:, None], r[None, :]
    mdiag = np.where(kl <= ql, 0.0, NEGB).astype(np.float32)
    mfar = np.where(kl > ql, 0.0, NEGB).astype(np.float32)
    mall = np.full((128, 128), NEGB, np.float32)
    mzero = np.zeros((128, 128), np.float32)
    mw = [np.stack([mfar, mzero, mzero, mzero, mdiag, mall], 1),
          np.stack([mall, mfar, mzero, mzero, mzero, mdiag], 1)]
    ms = [np.stack([mdiag, mall], 1), np.stack([mzero, mdiag], 1)]
    cmpm = np.zeros((128, 16, 128), np.float32)
    for v in range(16):
        cmpm[:, v, :] = np.where(16 * (kl - 8 * v) + 31 <= ql, 0.0, NEGB)
    c = np.arange(512)[:, None]
    s = np.arange(128)[None, :]
    ovl = ((16 * c < 64 * s + 64) & (16 * c + 31 >= 64 * s) & (c < 511)).astype(np.float32)
    ovl = ovl.reshape(4, 128, 128).transpose(1, 0, 2)
    u = np.arange(255)[None, :] - 127
    curl = (r[:, None] >= 64).astype(np.int64)
    forced = (u == curl) | (u == curl - 1)
    invalid = u > curl
    W1 = np.where(forced | invalid, 0.0, 1.0)
    W2 = np.where(invalid, -1e30, np.where(forced, 1e4, 0.0))
    W12 = np.stack([W1, W2], 1).astype(np.float32)
    E = (np.arange(8192)[None, :] // 64 == r[:, None]).astype(np.float32)
    return dict(mw=mw, ms=ms, cmpm=cmpm, ovl=np.ascontiguousarray(ovl), W12=W12, E=E,
                ident=np.eye(128, dtype=np.float32), ones64=np.ones((64, 64), np.float32))


def build_nsa(bcast_rhs=True, nblk=NQB):
    nc = bass.Bass("TRN2", target_bir_lowering=False)
    din = lambda name, shape: nc.dram_tensor(name, list(shape), F32, kind="ExternalInput").ap()
    kT4 = din("kT4", [4, 64, T])
    vtok2 = din("vtok2", [2, 128, NTILE, 64])
    qTd = din("qTd", [64, 4, NQB * 128])
    gld = din("gld", [128, NQB, 12])
    gbd = din("gbd", [128, 12])
    gated = din("gated", [128, NQB, 256])
    qnd = din("qn", [64, 1])
    knd = din("kn", [64, 3])
    posd = din("posT", [64, 2, 32])
    w1d = din("w1d", [2, 64, 32, 256])
    b1d = din("b1d", [128, 2, 2])
    w2d = din("w2d", [128, 2, 2, 64])
    b2kd = din("b2k", [64, 1])
    b2vd = din("b2v", [128, 64])
    ones64d = din("ones64", [64, 64])
    Ed = din("E", [128, T])
    mwd = din("mw", [128, 6, 128])
    msd = din("ms", [128, 2, 128])
    identd = din("ident", [128, 128])
    cmpmd = din("cmpm", [128, 8, 128])
    ovld = din("ovl", [128, 4, 128])
    W12d = din("W12", [128, 2, 253])
    y = nc.dram_tensor("y", [128, NQB, 256], F32, kind="ExternalOutput").ap()
    with ExitStack() as st:
        P = Prog(nc, st)
        dq = ["sp", "pool"]
        ndq = [0]

        def load(name, shape, src, dt=F32):
            t = P.sb(shape, dt, name)
            qn_ = dq[ndq[0] % 2]
            ndq[0] += 1
            P.dma(qn_, lambda e: e.dma_start(out=t[:], in_=src), t, writes=[t])
            return t

        stg = [P.sb([128, 2048], F32, f"stg{i}") for i in range(2)]
        nst = [0]

        def load_cast(dst_ap, dst_lt, src, shape_p, ncols, eng=None):
            s = stg[nst[0] % 2]
            eng = eng or ("dve" if nst[0] % 2 == 0 else "act")
            qn_ = dq[nst[0] % 2]
            nst[0] += 1
            P.dma(qn_, lambda e: e.dma_start(out=s[0:shape_p, 0:ncols], in_=src), s, writes=[s])
            if eng == "dve":
                P.op("dve", lambda e: e.tensor_copy(out=dst_ap, in_=s[0:shape_p, 0:ncols]), reads=[s], writes=[dst_lt])
            else:
                P.op("act", lambda e: e.copy(out=dst_ap, in_=s[0:shape_p, 0:ncols]), reads=[s], writes=[dst_lt])

        banks = [P.ps([128, 512], F32, f"bk{i}") for i in range(8)]
        SB = banks[0:3]
        OC0, OC1, OS, OW, MISC = banks[3], banks[4], banks[5], banks[6], banks[7]
        MSLOT = [OC0, OC1, MISC]
        qn = load("qn", [64, 1], qnd)
        kn = load("kn", [64, 3], knd)
        b1 = load("b1", [128, 2, 2], b1d)
        b2k = load("b2k", [64, 1], b2kd)
        b2v = load("b2v", [128, 64], b2vd)
        ones64 = load("ones64", [64, 64], ones64d)
        identf = load("identf", [128, 128], identd)
        W12 = load("W12", [128, 2, 253], W12d)
        gb = load("gb", [128, 12], gbd)
        gl = load("gl", [128, NQB, 12], gld)
        epst = P.sb([128, 1], F32, "eps")
        P.op("pool", lambda e: e.memset(epst[:], RMS_EPS), writes=[epst])
        qsc = P.sb([64, 1], F32, "qsc")
        P.op("dve", lambda e: e.tensor_scalar(out=qsc[:], in0=qn[:], scalar1=64 ** -0.5, scalar2=None, op0=ALU.mult),
             reads=[qn], writes=[qsc])
        identb = P.sb([128, 128], BF16, "identb")
        P.op("dve", lambda e: e.tensor_copy(out=identb[:], in_=identf[:]), reads=[identf], writes=[identb])
        mwb = P.sb([128, 6, 128], BF16, "mwb")
        load_cast(mwb[:].rearrange("p a b -> p (a b)"), mwb, mwd.rearrange("p a b -> p (a b)"), 128, 768)
        msb = P.sb([128, 2, 128], BF16, "msb")
        load_cast(msb[:].rearrange("p a b -> p (a b)"), msb, msd.rearrange("p a b -> p (a b)"), 128, 256)
        cmpmb = P.sb([128, 8, 128], BF16, "cmpmb")
        load_cast(cmpmb[:].rearrange("p a b -> p (a b)"), cmpmb, cmpmd.rearrange("p a b -> p (a b)"), 128, 1024)
        Eb = P.sb([128, T], BF16, "Eb")
        for i in range(4):
            load_cast(Eb[:, i * 2048:(i + 1) * 2048], Eb, Ed[:, i * 2048:(i + 1) * 2048], 128, 2048)
        P.op("dve", lambda e: e.tensor_tensor(out=gl[:], in0=gl[:], in1=gb[:].unsqueeze(1).to_broadcast([128, NQB, 12]),
                                              op=ALU.add), reads=[gl, gb], writes=[gl])
        P.op("act", lambda e: e.activation(out=gl[:], in_=gl[:], func=AF.Sigmoid), reads=[gl], writes=[gl])
        V1 = [P.sb([128, NTILE, 65], BF16, f"V1_{i}") for i in range(2)]
        for j in range(2):
            P.op("pool", lambda e, j=j: e.memset(V1[j][:, :, 64:65], 1.0), writes=[V1[j]])
            for i in range(2):
                load_cast(V1[j][:, i * 32:(i + 1) * 32, 0:64], V1[j],
                          vtok2[j][:, i * 32:(i + 1) * 32, :].rearrange("p a b -> p (a b)"), 128, 2048)

        rn_sq = [P.sb([64, 512], F32, f"rn_sq{i}") for i in range(4)]
        rn_rt = [P.sb([64, 512], F32, f"rn_rt{i}") for i in range(4)]

        def rms_batch(jobs):
            for i, (src_ap, src_lt, _, _, _, _) in enumerate(jobs):
                P.op("dve", lambda e, i=i, src_ap=src_ap: e.tensor_tensor(out=rn_sq[i][:], in0=src_ap, in1=src_ap, op=ALU.mult),
                     reads=[src_lt], writes=[rn_sq[i]])
            for i in range(len(jobs)):
                P.op("pe", lambda e, i=i: e.matmul(banks[i][0:64, :], ones64[:], rn_sq[i][:], start=True, stop=True),
                     reads=[ones64, rn_sq[i]], writes=[banks[i]])
            for i in range(len(jobs)):
                P.op("act", lambda e, i=i: e.activation(out=rn_rt[i][:], in_=banks[i][0:64, :], func=AF.Ln, scale=1.0 / 64,
                                                        bias=epst[0:64, :]), reads=[banks[i], epst], writes=[rn_rt[i]])
            for i in range(len(jobs)):
                P.op("act", lambda e, i=i: e.activation(out=rn_rt[i][:], in_=rn_rt[i][:], func=AF.Exp, scale=-0.5),
                     reads=[rn_rt[i]], writes=[rn_rt[i]])
            for i, (src_ap, src_lt, scale_ap, scale_lt, dst_ap, dst_lt) in enumerate(jobs):
                P.op("dve", lambda e, i=i, src_ap=src_ap, scale_ap=scale_ap, dst_ap=dst_ap: e.scalar_tensor_tensor(
                    out=dst_ap, in0=src_ap, scalar=scale_ap, in1=rn_rt[i][:], op0=ALU.mult, op1=ALU.mult),
                    reads=[src_lt, scale_lt, rn_rt[i]], writes=[dst_lt])

        def rms_fm(src_ap, src_lt, ncol, scale_ap, scale_lt, dst_ap, dst_lt, bank):
            assert ncol == 512
            rms_batch([(src_ap, src_lt, scale_ap, scale_lt, dst_ap, dst_lt)])

        KT = [P.sb([64, T], BF16, f"KT{i}") for i in range(2)]
        nrm = 0
        for j in range(2):
            for i in range(4):
                s = stg[nst[0] % 2]
                qn_ = dq[nst[0] % 2]
                nst[0] += 1
                P.dma(qn_, lambda e, s=s, j=j, i=i: e.dma_start(out=s[0:64, :], in_=kT4[2 + j][:, i * 2048:(i + 1) * 2048]),
                      s, writes=[s])
                rms_batch([(s[0:64, t * 512:(t + 1) * 512], s, kn[:, 1 + j:2 + j], kn,
                            KT[j][:, i * 2048 + t * 512:i * 2048 + (t + 1) * 512], KT[j]) for t in range(4)])
        kcT = P.sb([64, 16, 512], BF16, "kcT")
        w1b = P.sb([64, 32, 256], BF16, "w1b")
        posb = P.sb([64, 2, 32], BF16, "posb")
        load_cast(posb[:].rearrange("p a b -> p (a b)"), posb, posd.rearrange("p a b -> p (a b)"), 64, 64)
        w2f = load("w2f", [128, 2, 2, 64], w2d)
        w2b = P.sb([128, 2, 2, 64], BF16, "w2b")
        P.op("dve", lambda e: e.tensor_copy(out=w2b[:], in_=w2f[:]), reads=[w2f], writes=[w2b])
        hidT = P.sb([128, 2, 512], BF16, "hidT")
        P.op("pool", lambda e: e.memset(hidT[:], 0.0), writes=[hidT])
        biasv = P.sb([128, 2], F32, "biasv")
        hx = P.sb([128, 512], F32, "hx")
        hu = P.sb([128, 512], F32, "hu")
        P.op("pool", lambda e: e.memset(hx[:], 0.0), writes=[hx])
        kcmpT = P.sb([64, 512], BF16, "kcmpT")
        kraw = P.sb([64, 512], F32, "kraw")
        P.op("pool", lambda e: e.memset(kraw[:], 0.0), writes=[kraw])
        Vc1 = P.sb([128, 4, 193], BF16, "Vc1")
        P.op("pool", lambda e: e.memset(Vc1[:, :, 64:65], 1.0), writes=[Vc1])
        load_cast(Vc1[:, :, 65:193], Vc1, ovld.rearrange("p a b -> p (a b)"), 128, 512, eng="dve")
        for kv in range(2):
            for i in range(4):
                load_cast(kcT[:, :, i * 128:(i + 1) * 128].rearrange("p r j -> p j r"), kcT,
                          kT4[kv][:, i * 2048:(i + 1) * 2048], 64, 2048)
            for i in range(4):
                load_cast(w1b[:, i * 8:(i + 1) * 8, :].rearrange("p a b -> p (a b)"), w1b,
                          w1d[kv][:, i * 8:(i + 1) * 8, :].rearrange("p a b -> p (a b)"), 64, 2048)
            for hc in range(2):
                bk = banks[hc]
                for l in range(32):
                    P.op("pe", lambda e, bk=bk, l=l, hc=hc: e.matmul(
                        bk[:, 0:511], w1b[:, l, hc * 128:(hc + 1) * 128], kcT[:, l % 16, l // 16:l // 16 + 511],
                        start=(l == 0), stop=(l == 31)), reads=[w1b, kcT], writes=[bk])
                pb = MISC
                for l in range(32):
                    P.op("pe", lambda e, pb=pb, l=l, hc=hc, kv=kv: e.matmul(
                        pb[:, hc:hc + 1], w1b[:, l, hc * 128:(hc + 1) * 128], posb[:, kv, l:l + 1],
                        start=(l == 0), stop=(l == 31)), reads=[w1b, posb], writes=[pb])
                P.op("dve", lambda e, hc=hc, kv=kv, pb=pb: e.tensor_tensor(
                    out=biasv[:, hc:hc + 1], in0=pb[:, hc:hc + 1], in1=b1[:, kv, hc:hc + 1], op=ALU.add),
                    reads=[pb, b1], writes=[biasv])
                P.op("act", lambda e, bk=bk, hc=hc: e.activation(
                    out=hx[:, 0:511], in_=bk[:, 0:511], func=AF.Identity, bias=biasv[:, hc:hc + 1]),
                    reads=[bk, biasv], writes=[hx])
                P.op("dve", lambda e: e.tensor_tensor(out=hu[:], in0=hx[:], in1=hx[:], op=ALU.mult), reads=[hx], writes=[hu])
                P.op("dve", lambda e: e.tensor_scalar(out=hu[:], in0=hu[:], scalar1=0.044715, scalar2=1.0, op0=ALU.mult,
                                                      op1=ALU.add), reads=[hu], writes=[hu])
                P.op("dve", lambda e: e.tensor_tensor(out=hu[:], in0=hu[:], in1=hx[:], op=ALU.mult), reads=[hu, hx], writes=[hu])
                P.op("act", lambda e: e.activation(out=hu[:], in_=hu[:], func=AF.Tanh, scale=0.7978845608028654),
                     reads=[hu], writes=[hu])
                P.op("dve", lambda e: e.tensor_scalar(out=hu[:], in0=hu[:], scalar1=0.5, scalar2=0.5, op0=ALU.mult,
                                                      op1=ALU.add), reads=[hu], writes=[hu])
                P.op("dve", lambda e, hc=hc: e.tensor_tensor(out=hidT[:, hc, 0:511], in0=hu[:, 0:511], in1=hx[:, 0:511],
                                                            op=ALU.mult), reads=[hu, hx], writes=[hidT])
            if kv == 0:
                bk = banks[2]
                for hc in range(2):
                    P.op("pe", lambda e, bk=bk, hc=hc: e.matmul(
                        bk[0:64, 0:511], w2b[:, 0, hc, :], hidT[:, hc, 0:511], start=(hc == 0), stop=(hc == 1)),
                        reads=[w2b, hidT], writes=[bk])
                P.op("act", lambda e, bk=bk: e.activation(out=kraw[:, 0:511], in_=bk[0:64, 0:511], func=AF.Identity,
                                                          bias=b2k[:, 0:1]), reads=[bk, b2k], writes=[kraw])
                rms_fm(kraw[:, :], kraw, 512, kn[:, 0:1], kn, kcmpT[:, :], kcmpT, banks[0])
            else:
                bk = banks[2]
                for cc in range(4):
                    for hc in range(2):
                        P.op("pe", lambda e, bk=bk, hc=hc, cc=cc: e.matmul(
                            bk[:, cc * 64:(cc + 1) * 64], hidT[:, hc, cc * 128:(cc + 1) * 128], w2b[:, 1, hc, :],
                            start=(hc == 0), stop=(hc == 1)), reads=[w2b, hidT], writes=[bk])
                P.op("dve", lambda e, bk=bk: e.tensor_tensor(
                    out=Vc1[:, :, 0:64], in0=bk[:, 0:256].rearrange("p (c d) -> p c d", c=4),
                    in1=b2v[:].unsqueeze(1).to_broadcast([128, 4, 64]), op=ALU.add), reads=[bk, b2v], writes=[Vc1])
        qTb = P.sb([64, NQB, 4, 128], BF16, "qTb")
        for h in range(4):
            for i in range(2):
                s = stg[nst[0] % 2]
                qn_ = dq[nst[0] % 2]
                nst[0] += 1
                P.dma(qn_, lambda e, s=s, h=h, i=i: e.dma_start(out=s[0:64, :], in_=qTd[:, h, i * 2048:(i + 1) * 2048]),
                      s, writes=[s])
                rms_batch([(s[0:64, t * 512:(t + 1) * 512], s, qsc[:, 0:1], qsc,
                            qTb[:, i * 16 + t * 4:i * 16 + t * 4 + 4, h, :], qTb) for t in range(4)])
        Pt = [P.sb([128, 512], BF16, f"Pt{i}") for i in range(3)]
        npair = [0]
        gt = [P.sb([128, 256], F32, f"gt{i}") for i in range(2)]
        yt = [P.sb([128, 256], F32, f"yt{i}") for i in range(2)]
        rec = P.sb([128, 3, 4], F32, "rec")
        imp = P.sb([128, 128], F32, "imp")
        imp2 = P.sb([128, 128], F32, "imp2")
        m8a = P.sb([128, 8], F32, "m8a")
        m8b = P.sb([128, 8], F32, "m8b")
        self_ = P.sb([128, 128], F32, "sel")
        negT = P.sb([128, 128], BF16, "negT")
        acc = P.sb([128, 4, 64], F32, "acc")
        tmp = P.sb([128, 4, 64], F32, "tmpo")

        items = []

        def pair(kT_ap, kT_lt, qb, biases, V_ap, V_lt, obanks, ow, first, last, pre=(), post=(), mmask=None):
            def front(k):
                S, pt = SB[k % 3], Pt[k % 3]
                if mmask is not None:
                    ms_ = MSLOT[k % 3]
                    P.op("pe", lambda e: e.matmul(ms_[:, 0:128], mmask[0], mmask[2], start=True, stop=True),
                         reads=mmask[1], writes=[ms_])
                P.op("pe", lambda e: e.matmul(S[:], kT_ap, qb, start=True, stop=(len(biases) == 0)),
                     reads=[kT_lt, qTb], writes=[S])
                for bi, (l_ap, lts, r_ap) in enumerate(biases):
                    lastb = bi == len(biases) - 1
                    P.op("pe", lambda e, l_ap=l_ap, r_ap=r_ap, lastb=lastb: e.matmul(
                        S[:], l_ap, r_ap.unsqueeze(1).to_broadcast([128, 4, 128]), start=False, stop=lastb),
                        reads=lts, writes=[S])
                P.op("act", lambda e: e.activation(out=pt[:], in_=S[:], func=AF.Exp), reads=[S], writes=[pt])
                if mmask is not None:
                    P.op("dve", lambda e: e.tensor_tensor(
                        out=pt[:].rearrange("p (h q) -> p h q", h=4), in0=pt[:].rearrange("p (h q) -> p h q", h=4),
                        in1=ms_[:, 0:128].unsqueeze(1).to_broadcast([128, 4, 128]), op=ALU.mult), reads=[pt, ms_], writes=[pt])

            def back(k):
                pt = Pt[k % 3]
                for h in range(4):
                    bk, c0 = obanks[h]
                    st_ = first and (h == 0 or obanks[h][0] is not obanks[h - 1][0])
                    sp_ = last and (h == 3 or obanks[h + 1][0] is not obanks[h][0])
                    P.op("pe", lambda e, bk=bk, c0=c0, h=h, st_=st_, sp_=sp_: e.matmul(
                        bk[:, c0:c0 + ow], pt[:, h * 128:(h + 1) * 128], V_ap, start=st_, stop=sp_),
                        reads=[pt, V_lt], writes=[bk])

            items.append(dict(front=front, back=back, pre=list(pre), post=list(post)))

        oc_b = [(OC0, 0), (OC0, 193), (OC1, 0), (OC1, 193)]
        os_b = [(OS, h * 65) for h in range(4)]
        ow_b = [(OW, h * 65) for h in range(4)]
        ocv = [bk[:, c0:c0 + 193] for bk, c0 in oc_b]

        def gate_load(m):
            g_ = gt[m % 2]
            P.dma("sp", lambda e: e.dma_start(out=g_[:], in_=gated[:, m, :]), g_, writes=[g_])
            P.op("act", lambda e: e.activation(out=g_[:], in_=g_[:], func=AF.Silu), reads=[g_], writes=[g_])

        def topk_chain(m):
            for h in range(4):
                P.op("dve", lambda e, h=h: e.tensor_scalar(out=rec[:, 0, h:h + 1], in0=ocv[h][:, 64:65], scalar1=1e-30,
                                                          scalar2=None, op0=ALU.max), reads=[oc_b[h][0]], writes=[rec])
            P.op("dve", lambda e: e.reciprocal(out=rec[:, 0, :], in_=rec[:, 0, :]), reads=[rec], writes=[rec])
            P.op("dve", lambda e: e.tensor_scalar(out=imp[:], in0=ocv[0][:, 65:193], scalar1=rec[:, 0, 0:1], scalar2=None,
                                                  op0=ALU.mult), reads=[OC0, rec], writes=[imp])
            for h in range(1, 4):
                P.op("dve", lambda e, h=h: e.scalar_tensor_tensor(
                    out=imp[:], in0=ocv[h][:, 65:193], scalar=rec[:, 0, h:h + 1], in1=imp[:], op0=ALU.mult, op1=ALU.add),
                    reads=[oc_b[h][0], rec, imp], writes=[imp])
            P.op("dve", lambda e: e.tensor_tensor(
                out=rec[:, 0, :], in0=rec[:, 0, :], in1=gl[:, m, :].rearrange("p (h b) -> p h b", h=4)[:, :, 0],
                op=ALU.mult), reads=[rec, gl], writes=[rec])
            for hp in range(2):
                bk = oc_b[2 * hp][0]
                P.op("dve", lambda e, bk=bk, hp=hp: e.tensor_tensor(
                    out=acc[:, 2 * hp:2 * hp + 2, :],
                    in0=bk[:, 0:386].rearrange("p (h c) -> p h c", h=2)[:, :, 0:64],
                    in1=rec[:, 0, 2 * hp:2 * hp + 2].unsqueeze(2).to_broadcast([128, 2, 64]), op=ALU.mult),
                    reads=[bk, rec], writes=[acc])
            w0 = 125 - 4 * m
            P.op("dve", lambda e: e.tensor_tensor(out=imp[:], in0=imp[:], in1=W12[:, 0, w0:w0 + 128], op=ALU.mult),
                 reads=[imp, W12], writes=[imp])
            P.op("dve", lambda e: e.tensor_tensor(out=imp[:], in0=imp[:], in1=W12[:, 1, w0:w0 + 128], op=ALU.add),
                 reads=[imp, W12], writes=[imp])
            P.op("dve", lambda e: e.memset(imp[:, 0:1], 1e4), reads=[], writes=[imp])
            P.op("dve", lambda e: e.max(out=m8a[:], in_=imp[:]), reads=[imp], writes=[m8a])
            P.op("dve", lambda e: e.match_replace(out=imp2[:], in_to_replace=m8a[:], in_values=imp[:], imm_value=-3e38),
                 reads=[imp, m8a], writes=[imp2])
            P.op("dve", lambda e: e.max(out=m8b[:], in_=imp2[:]), reads=[imp2], writes=[m8b])
            P.op("dve", lambda e: e.tensor_scalar(out=self_[:], in0=imp[:], scalar1=m8b[:, 7:8], scalar2=None, op0=ALU.is_ge),
                 reads=[imp, m8b], writes=[self_])

        def sel_mask(m):
            P.op("pe", lambda e: e.transpose(MISC[:, 0:128], self_[:], identf[:]), reads=[self_, identf], writes=[MISC])
            P.op("act", lambda e: e.copy(out=negT[:], in_=MISC[:, 0:128]), reads=[MISC], writes=[negT])

        def combine(m):
            g_, yy = gt[m % 2], yt[m % 2]
            for br, ob_ in ((1, os_b), (2, ow_b)):
                bk = ob_[0][0]
                P.op("dve", lambda e, bk=bk, br=br: e.tensor_scalar(
                    out=rec[:, br, :], in0=bk[:, 0:260].rearrange("p (h c) -> p h c", h=4)[:, :, 64],
                    scalar1=1e-30, scalar2=None, op0=ALU.max), reads=[bk], writes=[rec])
                P.op("dve", lambda e, br=br: e.reciprocal(out=rec[:, br, :], in_=rec[:, br, :]), reads=[rec], writes=[rec])
                P.op("dve", lambda e, br=br: e.tensor_tensor(
                    out=rec[:, br, :], in0=rec[:, br, :], in1=gl[:, m, :].rearrange("p (h b) -> p h b", h=4)[:, :, br],
                    op=ALU.mult), reads=[rec, gl], writes=[rec])
                P.op("dve", lambda e, bk=bk, br=br: e.tensor_tensor(
                    out=tmp[:], in0=bk[:, 0:260].rearrange("p (h c) -> p h c", h=4)[:, :, 0:64],
                    in1=rec[:, br, :].unsqueeze(2).to_broadcast([128, 4, 64]), op=ALU.mult),
                    reads=[bk, rec], writes=[tmp])
                P.op("dve", lambda e: e.tensor_tensor(out=acc[:], in0=acc[:], in1=tmp[:], op=ALU.add),
                     reads=[acc, tmp], writes=[acc])
            P.op("dve", lambda e: e.tensor_tensor(
                out=yy[:], in0=acc[:].rearrange("p h d -> p (h d)"), in1=g_[:], op=ALU.mult),
                reads=[acc, g_], writes=[yy])
            P.dma("sp", lambda e: e.dma_start(out=y[:, m, :], in_=yy[:]), yy, reads=[yy], is_out=True)

        for m in range(nblk):
            qb = qTb[:, m, :, :].rearrange("p h q -> p (h q)")
            ccs = m // 8
            for cc in range(ccs + 1):
                biases = []
                if cc == ccs:
                    biases = [(identb[:], [identb, cmpmb], cmpmb[:, m % 8, :])]
                pair(kcmpT[:, cc * 128:(cc + 1) * 128], kcmpT, qb, biases, Vc1[:, cc, :], Vc1, oc_b, 193,
                     cc == 0, cc == ccs, pre=[lambda m=m: gate_load(m)] if cc == 0 else (),
                     post=[lambda m=m: topk_chain(m)] if cc == ccs else ())
            kbs = [kb for kb in range(2 * m - 4, 2 * m + 2) if kb >= 0]
            for kb in kbs:
                o = kb - (2 * m - 4)
                biases = [] if o in (2, 3) else [(identb[:], [identb, mwb], mwb[:, o, :])]
                pair(KT[1][:, kb * 128:(kb + 1) * 128], KT[1], qb, biases, V1[1][:, kb, :], V1[1], ow_b, 65,
                     kb == kbs[0], kb == kbs[-1])
            for kb in range(2 * m + 2):
                biases = []
                if kb >= 2 * m:
                    biases.append((identb[:], [identb, msb], msb[:, kb - 2 * m, :]))
                pair(KT[0][:, kb * 128:(kb + 1) * 128], KT[0], qb, biases, V1[0][:, kb, :], V1[0], os_b, 65,
                     kb == 0, kb == 2 * m + 1, pre=[lambda m=m: sel_mask(m)] if kb == 0 else (),
                     post=[lambda m=m: combine(m)] if kb == 2 * m + 1 else (),
                     mmask=(Eb[:, kb * 128:(kb + 1) * 128], [Eb, negT], negT[:]))
        DEPTH_PIPE = 2
        n_it = len(items)
        for idx in range(n_it + DEPTH_PIPE):
            if idx < n_it:
                for f in items[idx]["pre"]:
                    f()
                items[idx]["front"](idx)
            j = idx - DEPTH_PIPE
            if j >= 0:
                items[j]["back"](j)
                for f in items[j]["post"]:
                    f()
        P.emit()
    return nc


def nsa_inputs(proj, p, l):
    C = nsa_consts()
    maps = []
    nkv = proj[:, :, OFF["nkv"]:OFF["nkv"] + 768].reshape(BATCH, T, 3, 2, 2, 64)
    for c in range(NCORES):
        b, rem = divmod(c, 4)
        g, par = divmod(rem, 2)
        blks = np.arange(NQB) * 2 + par
        tok = (blks[:, None] * 128 + np.arange(128)[None, :]).reshape(-1)
        kT4 = np.stack([nkv[b, :, 0, 0, g].T, nkv[b, :, 0, 1, g].T, nkv[b, :, 1, 0, g].T, nkv[b, :, 2, 0, g].T], 0)
        tokmaj = lambda a: a.reshape(NTILE, 128, -1).transpose(1, 0, 2)
        vtok2 = np.stack([tokmaj(nkv[b, :, 1, 1, g]), tokmaj(nkv[b, :, 2, 1, g])], 0)
        q = proj[b, tok, OFF["nq"] + g * 256:OFF["nq"] + (g + 1) * 256].reshape(-1, 4, 64)
        qTd = q.transpose(2, 1, 0)
        glg = proj[b, tok, OFF["ngl"] + g * 12:OFF["ngl"] + (g + 1) * 12].reshape(NQB, 128, 12).transpose(1, 0, 2)
        gate = proj[b, tok, OFF["ng"] + g * 256:OFF["ng"] + (g + 1) * 256].reshape(NQB, 128, 256).transpose(1, 0, 2)
        cmpm = C["cmpm"][:, par::2, :]
        W12 = C["W12"][:, :, (2 - 2 * par):(2 - 2 * par) + 253]
        f = lambda a: np.ascontiguousarray(a, dtype=np.float32)
        maps.append({
            "kT4": f(kT4), "vtok2": f(vtok2), "qTd": f(qTd), "gld": f(glg),
            "gbd": f(np.broadcast_to(p["nsa_gate_b"][l][None, g * 12:(g + 1) * 12], (128, 12))),
            "gated": f(gate), "qn": f(p["nsa_qn"][l][:, None]), "kn": f(p["nsa_kn"][l].T),
            "posT": f(p["nsa_cmp_pos"][l].transpose(2, 0, 1)),
            "w1d": f(p["nsa_cmp_w1"][l].reshape(2, 32, 64, 256).transpose(0, 2, 1, 3)),
            "b1d": f(p["nsa_cmp_b1"][l].reshape(2, 2, 128).transpose(2, 0, 1)),
            "w2d": f(p["nsa_cmp_w2"][l].reshape(2, 2, 128, 64).transpose(2, 0, 1, 3)),
            "b2k": f(p["nsa_cmp_b2"][l][0][:, None]),
            "b2v": f(np.broadcast_to(p["nsa_cmp_b2"][l][1][None, :], (128, 64))),
            "ones64": C["ones64"], "E": C["E"], "mw": f(C["mw"][par]), "ms": f(C["ms"][par]), "ident": C["ident"],
            "cmpm": f(cmpm), "ovl": C["ovl"], "W12": f(W12),
        })
    return maps


def nsa_gather(res):
    out = np.zeros((BATCH, T, 512), np.float32)
    for c in range(NCORES):
        b, rem = divmod(c, 4)
        g, par = divmod(rem, 2)
        yc = res[c]["y"].transpose(1, 0, 2)
        o = out[b].reshape(NTILE, 128, 512)
        o[par::2, :, g * 256:(g + 1) * 256] = yc
    return out


_CACHE = {}


def _prog(name, fn):
    if name not in _CACHE:
        _CACHE[name] = fn()
    return _CACHE[name]


def _x_to_T(xc):
    return np.ascontiguousarray(xc.T.reshape(KC, 128, -1).transpose(1, 0, 2))


def _fm(a, nchunk):
    af = a.reshape(BATCH * T, nchunk, 128)
    return [np.ascontiguousarray(af[c * NT:(c + 1) * NT].transpose(2, 1, 0)) for c in range(NCORES)]


def _win_layout(w):
    wp = np.zeros((D_MODEL, D_IN_PAD), np.float32)
    wp[:, :D_IN] = w
    return np.ascontiguousarray(wp.reshape(KC, 128, NFC, 128).transpose(2, 1, 0, 3))


def kernel(**p):
    p = {k: np.asarray(v, dtype=np.float32) for k, v in p.items()}
    x = p["x"]
    xT = [_x_to_T(x.reshape(-1, D_MODEL)[c * NT:(c + 1) * NT]) for c in range(NCORES)]
    proj = None
    mixers = None
    for l in range(DEPTH + 1):
        do_out, do_in = l > 0, l < DEPTH
        maps = [{"xT": xT[c]} for c in range(NCORES)]
        if do_out:
            lo = l - 1
            y_gla, y_ssm, y_nsa = mixers
            mix = np.concatenate([y_gla, np.zeros_like(y_ssm), y_nsa], -1)
            mixT = _fm(mix, 8)
            yssmT = _fm(y_ssm, 2)
            uT = _fm(proj[:, :, OFF["su"]:OFF["su"] + 256], 2)
            sgT = _fm(proj[:, :, OFF["sg"]:OFF["sg"] + 256], 2)
            wout = np.ascontiguousarray(p["w_out"][lo].reshape(KC, 128, D_MODEL).transpose(1, 0, 2))
            dsk = np.ascontiguousarray(p["s5_d"][lo].reshape(2, 128).T)
            gluw = np.ascontiguousarray(p["s5_glu_w"][lo].reshape(2, 128, 512).transpose(1, 0, 2))
            glub = np.ascontiguousarray(p["s5_glu_b"][lo].reshape(4, 128).T)
            for c in range(NCORES):
                maps[c].update({"mixT": mixT[c], "wout": wout, "yssmT": yssmT[c], "uT": uT[c], "sgT": sgT[c],
                                "dsk": dsk, "gluw": gluw, "glub": glub})
        if do_in:
            win = _win_layout(p["w_in"][l])
            gin = np.ascontiguousarray(p["norm_g"][l].reshape(KC, 128).T)
            for c in range(NCORES):
                maps[c].update({"win": win, "gin": gin})
        res = _run(_prog(f"op{int(do_out)}{int(do_in)}", lambda: build_op(do_out, do_in)), maps)
        if do_out:
            xT = [res[c]["xoT"] for c in range(NCORES)]
        if not do_in:
            break
        proj = np.concatenate([res[c]["projT"].reshape(D_IN_PAD, NT)[:D_IN].T for c in range(NCORES)], 0)
        proj = proj.reshape(BATCH, T, D_IN)
        y_gla = gla_gather(_run(_prog("gla", build_gla), gla_inputs(proj, p, l)))
        y_ssm = s5_gather(_run(_prog("s5", build_s5), s5_inputs(proj, p, l)))
        y_nsa = nsa_gather(_run(_prog("nsa", build_nsa), nsa_inputs(proj, p, l)))
        mixers = (y_gla, y_ssm, y_nsa)
    out = np.concatenate([xT[c].transpose(2, 1, 0).reshape(NT, D_MODEL) for c in range(NCORES)], 0)
    return out.reshape(BATCH, T, D_MODEL).astype(np.float32)
```
